# Optimizing a Trainium2 kernel written in Bass

```python
import math
import jax, jax.numpy as jnp
from jax import lax
import numpy as np

D_MODEL = 2048
BATCH = 4
SEQ = 8192
DEPTH = 4

N_MIXERS = 2
N_MLSTM = (DEPTH + 1) // 2
N_FOX = DEPTH // 2
MLSTM_HEADS = 4
MLSTM_DQK = D_MODEL // (2 * MLSTM_HEADS)
MLSTM_DV = D_MODEL // MLSTM_HEADS
MLSTM_QK = MLSTM_HEADS * MLSTM_DQK
MLSTM_V = MLSTM_HEADS * MLSTM_DV
MLSTM_IN = 2 * MLSTM_QK + 2 * MLSTM_V + 2 * MLSTM_HEADS
MLSTM_CHUNK = 64
GATE_SOFTCAP = 15.0
FOX_HEAD_DIM = 128
FOX_HEADS = D_MODEL // FOX_HEAD_DIM
FOX_IN = 3 * D_MODEL + FOX_HEADS
Q_BLOCK = 128
FFN_MULT = 256
D_FF = ((8 * D_MODEL + 3 * FFN_MULT - 1) // (3 * FFN_MULT)) * FFN_MULT
EPS = 1e-6

kernel_name = "hybrid_mlstm_fox_adaln_trunk"


def rms_norm(x, w):
    xf = x.astype(jnp.float32)
    y = xf * lax.rsqrt(jnp.mean(xf * xf, axis=-1, keepdims=True) + EPS)
    return (y * w.astype(jnp.float32)).astype(x.dtype)


def mlstm_chunkwise(q, k, v, log_i, log_f):
    B, H, T, dqk = q.shape
    dv = v.shape[-1]
    L = MLSTM_CHUNK
    nc = T // L

    def to_chunks(a):
        a = a.reshape((B, H, nc, L) + a.shape[3:])
        return jnp.moveaxis(a, 2, 0)

    causal = jnp.arange(L)[:, None] >= jnp.arange(L)[None, :]

    def step(carry, inp):
        C, n, m = carry
        qc, kc, vc, lic, lfc = inp
        b = jnp.cumsum(lfc, axis=-1)
        g = b[..., -1]
        D = b[..., :, None] - b[..., None, :] + lic[..., None, :]
        D = jnp.where(causal, D, -jnp.inf)
        inter_log = b + m[..., None]
        m_t = jnp.maximum(inter_log, jnp.max(D, axis=-1))
        w_intra = jnp.exp(D - m_t[..., None])
        w_inter = jnp.exp(inter_log - m_t)
        s = jnp.einsum('bhtd,bhsd->bhts', qc, kc) * w_intra
        num = w_inter[..., None] * jnp.einsum('bhtd,bhde->bhte', qc, C) + jnp.einsum('bhts,bhse->bhte', s, vc)
        den = w_inter * jnp.einsum('bhtd,bhd->bht', qc, n) + jnp.sum(s, axis=-1)
        h = num / jnp.maximum(jnp.abs(den), jnp.exp(-m_t))[..., None]
        tail = g[..., None] - b + lic
        m_new = jnp.maximum(g + m, jnp.max(tail, axis=-1))
        w_tail = jnp.exp(tail - m_new[..., None])
        decay = jnp.exp(g + m - m_new)
        C_new = decay[..., None, None] * C + jnp.einsum('bhs,bhsd,bhse->bhde', w_tail, kc, vc)
        n_new = decay[..., None] * n + jnp.einsum('bhs,bhsd->bhd', w_tail, kc)
        return (C_new, n_new, m_new), h

    init = (jnp.zeros((B, H, dqk, dv), jnp.float32),
            jnp.zeros((B, H, dqk), jnp.float32),
            jnp.zeros((B, H), jnp.float32))
    _, hs = lax.scan(step, init, (to_chunks(q), to_chunks(k), to_chunks(v), to_chunks(log_i), to_chunks(log_f)))
    return jnp.moveaxis(hs, 0, 2).reshape(B, H, T, dv)


def mlstm_mixer(h, w_in, b_gates, norm_w, w_out):
    B, T, _ = h.shape
    Hh = MLSTM_HEADS
    proj = h @ w_in
    q, k, v, o, gates = jnp.split(proj, [MLSTM_QK, 2 * MLSTM_QK, 2 * MLSTM_QK + MLSTM_V, 2 * MLSTM_QK + 2 * MLSTM_V], axis=-1)
    heads = lambda a, d: a.astype(jnp.float32).reshape(B, T, Hh, d).transpose(0, 2, 1, 3)
    q = heads(q, MLSTM_DQK) * (MLSTM_DQK ** -0.5)
    k = heads(k, MLSTM_DQK)
    v = heads(v, MLSTM_DV)
    gates = gates.astype(jnp.float32) + b_gates.astype(jnp.float32)
    gates = GATE_SOFTCAP * jnp.tanh(gates / GATE_SOFTCAP)
    log_i = gates[..., :Hh].transpose(0, 2, 1)
    log_f = jax.nn.log_sigmoid(gates[..., Hh:]).transpose(0, 2, 1)
    hs = mlstm_chunkwise(q, k, v, log_i, log_f)
    hs = hs * lax.rsqrt(jnp.mean(hs * hs, axis=-1, keepdims=True) + EPS)
    hs = hs.transpose(0, 2, 1, 3).reshape(B, T, MLSTM_V) * norm_w.astype(jnp.float32)
    y = jax.nn.sigmoid(o.astype(jnp.float32)) * hs
    return y.astype(h.dtype) @ w_out


def forgetting_attention(q, k, v, log_f):
    B, H, T, dh = q.shape
    nb = T // Q_BLOCK
    cum = jnp.cumsum(log_f, axis=-1)
    kf = k.astype(jnp.float32)
    vf = v.astype(jnp.float32)
    qb = jnp.moveaxis(q.astype(jnp.float32).reshape(B, H, nb, Q_BLOCK, dh), 2, 0)
    cqb = jnp.moveaxis(cum.reshape(B, H, nb, Q_BLOCK), 2, 0)
    kpos = jnp.arange(T)
    scale = dh ** -0.5

    def block(args):
        qi, cqi, bi = args
        s = jnp.einsum('bhqd,bhkd->bhqk', qi, kf) * scale + cqi[..., None] - cum[..., None, :]
        qpos = bi * Q_BLOCK + jnp.arange(Q_BLOCK)
        s = jnp.where(kpos[None, :] <= qpos[:, None], s, -jnp.inf)
        p = jax.nn.softmax(s, axis=-1)
        return jnp.einsum('bhqk,bhkd->bhqd', p, vf)

    out = lax.map(block, (qb, cqb, jnp.arange(nb)))
    return jnp.moveaxis(out, 0, 2).reshape(B, H, T, dh)


def fox_mixer(h, w_in, b_f, w_out):
    B, T, _ = h.shape
    Hh = FOX_HEADS
    proj = h @ w_in
    q, k, v, fl = jnp.split(proj, [D_MODEL, 2 * D_MODEL, 3 * D_MODEL], axis=-1)
    heads = lambda a: a.reshape(B, T, Hh, FOX_HEAD_DIM).transpose(0, 2, 1, 3)
    log_f = jax.nn.log_sigmoid(fl.astype(jnp.float32) + b_f.astype(jnp.float32)).transpose(0, 2, 1)
    o = forgetting_attention(heads(q), heads(k), heads(v), log_f)
    o = o.transpose(0, 2, 1, 3).reshape(B, T, D_MODEL).astype(h.dtype)
    return o @ w_out


def swiglu(h, w_gate_up, w_down):
    g, u = jnp.split(h @ w_gate_up, 2, axis=-1)
    return (jax.nn.silu(g) * u) @ w_down


def setup_inputs(seed: int = 0) -> dict:
    key = jax.random.key(seed)
    ks = jax.random.split(key, 20)
    f32 = jnp.float32
    nrm = lambda k, shape, s: jax.random.normal(k, shape, f32) * s
    x = nrm(ks[0], (BATCH, SEQ, D_MODEL), 1.0)
    c = nrm(ks[1], (BATCH, D_MODEL), 1.0)
    ada_w = nrm(ks[2], (DEPTH, D_MODEL, 6 * D_MODEL), 0.5 * D_MODEL ** -0.5)
    ada_b = nrm(ks[3], (DEPTH, 6 * D_MODEL), 0.02)
    norm_mix_w = 1.0 + nrm(ks[4], (DEPTH, D_MODEL), 0.05)
    norm_ffn_w = 1.0 + nrm(ks[5], (DEPTH, D_MODEL), 0.05)
    mlstm_w_in = nrm(ks[6], (N_MLSTM, D_MODEL, MLSTM_IN), D_MODEL ** -0.5)
    b_i = nrm(ks[7], (N_MLSTM, MLSTM_HEADS), 0.1)
    b_f = jnp.linspace(3.0, 6.0, MLSTM_HEADS, dtype=f32)[None, :] + nrm(ks[8], (N_MLSTM, MLSTM_HEADS), 0.1)
    mlstm_b_gates = jnp.concatenate([b_i, b_f], axis=-1)
    mlstm_norm_w = 1.0 + nrm(ks[9], (N_MLSTM, MLSTM_V), 0.05)
    mlstm_w_out = nrm(ks[10], (N_MLSTM, MLSTM_V, D_MODEL), MLSTM_V ** -0.5)
    fox_w_in = nrm(ks[11], (N_FOX, D_MODEL, FOX_IN), D_MODEL ** -0.5)
    fox_b_f = 4.0 + nrm(ks[12], (N_FOX, FOX_HEADS), 0.5)
    fox_w_out = nrm(ks[13], (N_FOX, D_MODEL, D_MODEL), D_MODEL ** -0.5)
    ffn_w_gate_up = nrm(ks[14], (DEPTH, D_MODEL, 2 * D_FF), D_MODEL ** -0.5)
    ffn_w_down = nrm(ks[15], (DEPTH, D_FF, D_MODEL), D_FF ** -0.5)
    final_norm_w = 1.0 + nrm(ks[16], (D_MODEL,), 0.05)
    return {"x": x, "c": c, "ada_w": ada_w, "ada_b": ada_b,
            "norm_mix_w": norm_mix_w, "norm_ffn_w": norm_ffn_w,
            "mlstm_w_in": mlstm_w_in, "mlstm_b_gates": mlstm_b_gates,
            "mlstm_norm_w": mlstm_norm_w, "mlstm_w_out": mlstm_w_out,
            "fox_w_in": fox_w_in, "fox_b_f": fox_b_f, "fox_w_out": fox_w_out,
            "ffn_w_gate_up": ffn_w_gate_up, "ffn_w_down": ffn_w_down,
            "final_norm_w": final_norm_w}


def reference(x, c, ada_w, ada_b, norm_mix_w, norm_ffn_w, mlstm_w_in, mlstm_b_gates,
              mlstm_norm_w, mlstm_w_out, fox_w_in, fox_b_f, fox_w_out,
              ffn_w_gate_up, ffn_w_down, final_norm_w):
    cs = jax.nn.silu(c)
    for layer in range(DEPTH):
        mod = cs @ ada_w[layer] + ada_b[layer]
        sh1, sc1, g1, sh2, sc2, g2 = [m[:, None, :] for m in jnp.split(mod, 6, axis=-1)]
        h = rms_norm(x, norm_mix_w[layer]) * (1.0 + sc1) + sh1
        j = layer // N_MIXERS
        if layer % N_MIXERS == 0:
            y = mlstm_mixer(h, mlstm_w_in[j], mlstm_b_gates[j], mlstm_norm_w[j], mlstm_w_out[j])
        else:
            y = fox_mixer(h, fox_w_in[j], fox_b_f[j], fox_w_out[j])
        x = x + g1 * y
        h = rms_norm(x, norm_ffn_w[layer]) * (1.0 + sc2) + sh2
        x = x + g2 * swiglu(h, ffn_w_gate_up[layer], ffn_w_down[layer])
    return rms_norm(x, final_norm_w)
```

```python
from contextlib import ExitStack
import numpy as np
import ml_dtypes
import concourse.bass as bass
import concourse.mybir as mybir
from concourse.bass_utils import run_bass_kernel_spmd

F32 = mybir.dt.float32
BF16 = mybir.dt.bfloat16
AF = mybir.ActivationFunctionType
ALU = mybir.AluOpType
AX = mybir.AxisListType

D = 2048
NC_ = 16
DFF = 5632
NFC = 44
EPS = 1e-6
TT = 512
NEG = -30000.0
R_DMA = 8


class LoopVar:
    cur = None


LV = LoopVar()


def TS(idx, size, off=0):
    if isinstance(idx, LoopVar):
        if isinstance(idx.cur, int):
            return slice(idx.cur * size + off, idx.cur * size + off + size)
        assert off == 0
        return bass.ts(idx.cur, size)
    return slice(idx * size + off, idx * size + off + size)


def DS(idx, stride, off, size):
    if isinstance(idx, LoopVar):
        return bass.ds(idx.cur * stride + off, size)
    return slice(idx * stride + off, idx * stride + off + size)


class Ring:
    registry = []

    def __init__(self, items):
        self.items = items
        self.i = 0
        Ring.registry.append(self)

    def next(self):
        x = self.items[self.i % len(self.items)]
        self.i += 1
        return x


ENGS = ["pe", "act", "dve", "pool", "sp"]
QUEUES = ["sp", "pool", "act"]


class Sch:
    ARENA = 188 * 1024

    def __init__(self, nc, stack):
        self.nc = nc
        self.stack = stack
        self.ops = []
        self.lw = {}
        self.lr = {}
        self.barrier_deps = set()
        self.last_on = {}
        self.dma_hist = {q: [] for q in QUEUES}
        self.n_t = 0
        self.regions = []
        self._after_barrier = set(ENGS)
        Ring.registry.clear()

    def sb(self, shape, dt, name=None):
        if not hasattr(self, "arena"):
            self.arena = self.stack.enter_context(self.nc.sbuf_tensor("arena", [128, self.ARENA], mybir.dt.uint8))
            self.top = 0
        esz = 4 if dt == F32 else 2
        n = 1
        for d in shape[1:]:
            n *= d
        nbytes = (n * esz + 63) // 64 * 64
        off = self.top
        self.top += nbytes
        assert self.top <= self.ARENA, f"SBUF arena overflow: {self.top} ({name})"
        ap = self.arena[0:shape[0], off:off + n * esz].bitcast(dt)
        if len(shape) == 3:
            ap = ap.rearrange("p (a b) -> p a b", a=shape[1], b=shape[2])
        return ap

    def mark(self):
        return self.top

    def reset(self, m):
        self.top = m

    def ps(self, shape, dt=F32, name=None):
        self.n_t += 1
        return self.stack.enter_context(self.nc.psum_tensor(name or f"p{self.n_t}", list(shape), dt))

    def op(self, eng, fn, r=(), w=(), dma=False):
        idx = len(self.ops)
        deps = set(self.barrier_deps) if eng not in self._after_barrier else set()
        self._after_barrier.add(eng)
        for k in r:
            x = self.lw.get(k)
            if x is not None:
                deps.add(x)
        for k in w:
            x = self.lw.get(k)
            if x is not None:
                deps.add(x)
            for y in self.lr.get(k, ()):
                deps.add(y)
        for k in r:
            self.lr.setdefault(k, []).append(idx)
        for k in w:
            self.lw[k] = idx
            self.lr[k] = []
        deps.discard(idx)
        self.ops.append(dict(eng=eng, fn=fn, deps=deps, dma=dma))
        if dma:
            self.dma_hist[eng].append(idx)
        else:
            self.last_on[eng] = idx
        return idx

    def barrier(self):
        deps = set(self.last_on.values())
        for q, h in self.dma_hist.items():
            deps.update(h[-R_DMA:])
        self.barrier_deps = deps
        self._after_barrier = set()

    def loop(self, N, body, reset_fn=None):
        if N == 1:
            body(0)
            return
        self.barrier()
        reg = dict(N=N, s0=len(self.ops), bdeps=set(self.barrier_deps))
        for rg in Ring.registry:
            rg.i = 0
        if reset_fn:
            reset_fn()
        body(LV)
        reg["s1"] = len(self.ops)
        for rg in Ring.registry:
            rg.i = 0
        if reset_fn:
            reset_fn()
        body(LV)
        reg["s2"] = len(self.ops)
        assert reg["s2"] - reg["s1"] == reg["s1"] - reg["s0"], "loop body not iteration-invariant"
        for a, b in zip(range(reg["s0"], reg["s1"]), range(reg["s1"], reg["s2"])):
            assert self.ops[a]["eng"] == self.ops[b]["eng"] and self.ops[a]["dma"] == self.ops[b]["dma"]
        self.regions.append(reg)
        self.barrier()

    def emit(self):
        nc = self.nc
        ops = self.ops
        n = len(ops)
        reg_of = [None] * n
        copy_of = [0] * n
        for reg in self.regions:
            for i in range(reg["s0"], reg["s1"]):
                reg_of[i] = reg
                copy_of[i] = 1
            for i in range(reg["s1"], reg["s2"]):
                reg_of[i] = reg
                copy_of[i] = 2

        def twin(i):
            reg = reg_of[i]
            return i + (reg["s1"] - reg["s0"]) if copy_of[i] == 1 else i

        def pe_pe(d, i):
            return ops[d]["eng"] == "pe" and ops[i]["eng"] == "pe" and not ops[d]["dma"] and not ops[i]["dma"]

        sig = [False] * n
        for i, o in enumerate(ops):
            for d in o["deps"]:
                if pe_pe(d, i):
                    continue
                sig[twin(d)] = True
        cnt = {e: 0 for e in ENGS}
        rr = {q: 0 for q in QUEUES}
        tot = {q: [0] * R_DMA for q in QUEUES}
        i = 0
        while i < n:
            reg = reg_of[i]
            if reg is None:
                o = ops[i]
                e = o["eng"]
                if o["dma"]:
                    s = rr[e] % R_DMA
                    rr[e] += 1
                    tot[e][s] += 1
                    o["slot"] = s
                    o["c"], o["k"] = 16 * tot[e][s], 0
                elif sig[i]:
                    cnt[e] += 1
                    o["c"], o["k"] = cnt[e], 0
                i += 1
                continue
            N = reg["N"]
            body = range(reg["s1"], reg["s2"])
            delta = {e: 0 for e in ENGS}
            for j in body:
                if not ops[j]["dma"] and sig[j]:
                    delta[ops[j]["eng"]] += 1
            run_c = {e: 0 for e in ENGS}
            rr0 = dict(rr)
            cslot = {q: [0] * R_DMA for q in QUEUES}
            for j in body:
                if ops[j]["dma"]:
                    q = ops[j]["eng"]
                    s = rr0[q] % R_DMA
                    rr0[q] += 1
                    ops[j]["slot"] = s
                    ops[j]["m"] = cslot[q][s]
                    cslot[q][s] += 1
            for j in body:
                o = ops[j]
                e = o["eng"]
                if o["dma"]:
                    s = o["slot"]
                    o["c"] = 16 * (tot[e][s] + o["m"] + 1)
                    o["k"] = 16 * cslot[e][s]
                elif sig[j]:
                    run_c[e] += 1
                    o["c"], o["k"] = cnt[e] + run_c[e], delta[e]
            for e in ENGS:
                cnt[e] += N * delta[e]
            for q in QUEUES:
                for s in range(R_DMA):
                    tot[q][s] += N * cslot[q][s]
            reg["cslot"] = cslot
            off = reg["s1"] - reg["s0"]
            for j0 in range(reg["s0"], reg["s1"]):
                t = ops[j0 + off]
                if "c" in t:
                    ops[j0]["c"], ops[j0]["k"] = t["c"], 0
                if "slot" in t:
                    ops[j0]["slot"] = t["slot"]
            i = reg["s2"]

        csem = {e: self.stack.enter_context(nc.semaphore(f"c_{e}")) for e in ENGS}
        dsem = {q: [self.stack.enter_context(nc.semaphore(f"d_{q}{s}")) for s in range(R_DMA)] for q in QUEUES}

        def semkey(p):
            return ("d", p["eng"], p["slot"]) if p["dma"] else ("c", p["eng"])

        def dep_target(d, i):
            t = twin(d)
            p = ops[t]
            key = semkey(p)
            if reg_of[i] is not None and reg_of[i] is reg_of[d]:
                if copy_of[i] == 1:
                    return key, p["c"], 0
                if copy_of[d] == 1:
                    return key, p["c"] - p["k"], p["k"]
                return key, p["c"], p["k"]
            if reg_of[d] is not None:
                return key, p["c"] + (reg_of[d]["N"] - 1) * p["k"], 0
            return key, p["c"], 0

        tmpregs = {}
        itregs = {}

        def emit_waits(eobj, waits, waited, it):
            for (key, k), c in waits.items():
                prev = waited.get((key, k))
                if prev is not None and prev >= c:
                    continue
                waited[(key, k)] = c
                sem = csem[key[1]] if key[0] == "c" else dsem[key[1]][key[2]]
                if k == 0 or it is None:
                    eobj.wait_ge(sem, c)
                else:
                    rg = tmpregs.get(id(eobj))
                    if rg is None:
                        rg = eobj.alloc_register()
                        tmpregs[id(eobj)] = rg
                    itr = waited.get("__itr")
                    if itr is None:
                        itr = eobj.to_reg(it)
                        waited["__itr"] = itr
                    eobj.reg_mul(rg, itr, k)
                    eobj.reg_add(rg, rg, c)
                    eobj.wait_ge(sem, rg)

        def collect(i, ename):
            o = ops[i]
            waits = {}
            for d in o["deps"]:
                if pe_pe(d, i):
                    continue
                key, c, k = dep_target(d, i)
                if waits.get((key, k), -10 ** 9) < c:
                    waits[(key, k)] = c
            if o["dma"]:
                key = ("d", ename, o["slot"])
                c, k = o["c"] - 16, o["k"]
                if copy_of[i] == 1:
                    k = 0
                if c > 0 or k > 0:
                    if waits.get((key, k), -10 ** 9) < c:
                        waits[(key, k)] = c
            return waits

        def emit_op(i, ename, eobj, waited, it):
            o = ops[i]
            emit_waits(eobj, collect(i, ename), waited, it)
            if o["fn"] is None:
                return
            ins = o["fn"](eobj)
            if o["dma"]:
                ins.then_inc(dsem[ename][o["slot"]], 16)
            elif sig[twin(i)]:
                ins.then_inc(csem[ename], 1)

        def run(ename, eobj):
            waited = {}
            i = 0
            while i < n:
                reg = reg_of[i]
                if reg is None:
                    if ops[i]["eng"] == ename:
                        emit_op(i, ename, eobj, waited, None)
                    i += 1
                    continue
                LV.cur = 0
                for j in range(reg["s0"], reg["s1"]):
                    if ops[j]["eng"] == ename:
                        emit_op(j, ename, eobj, waited, None)
                LV.cur = None
                mine = [j for j in range(reg["s1"], reg["s2"]) if ops[j]["eng"] == ename]
                if mine:
                    with eobj.Fori(1, reg["N"]) as iv:
                        LV.cur = iv
                        w2 = {}
                        for j in mine:
                            emit_op(j, ename, eobj, w2, iv)
                    LV.cur = None
                i = reg["s2"]

        with nc.Block() as block:
            @block.tensor
            def _(e):
                run("pe", e)

            @block.scalar
            def _(e):
                run("act", e)

            @block.vector
            def _(e):
                run("dve", e)

            @block.gpsimd
            def _(e):
                run("pool", e)

            @block.sync
            def _(e):
                run("sp", e)


def build(T, depth, dbg=False):
    NT = T // TT
    NQ = T // 128
    n_ml = (depth + 1) // 2
    n_fx = depth // 2
    nc = bass.Bass("TRN2", target_bir_lowering=False)
    stack = ExitStack()
    S = Sch(nc, stack)

    def dram(name, shape, dt, kind="Internal"):
        return nc.dram_tensor(name, list(shape), dt, kind=kind).ap()

    xT_in = dram("xT", [D, T], F32, "ExternalInput")
    cT = dram("cT", [128, 16], F32, "ExternalInput")
    ada_w = dram("ada_w", [depth, D, 6 * D], F32, "ExternalInput")
    ada_bT = dram("ada_bT", [128, depth * 96], F32, "ExternalInput")
    nmw = dram("nmw", [128, depth * 16], F32, "ExternalInput")
    nfw = dram("nfw", [128, depth * 16], F32, "ExternalInput")
    fnw = dram("fnw", [128, 16], F32, "ExternalInput")
    mw_in = dram("mw_in", [max(n_ml, 1), D, 6152], F32, "ExternalInput")
    mbg = dram("mbg", [4, max(n_ml, 1) * 2], F32, "ExternalInput")
    mnw = dram("mnw", [128, max(n_ml, 1) * 2048], F32, "ExternalInput")
    mw_out = dram("mw_out", [max(n_ml, 1), D, D], F32, "ExternalInput")
    fw_in = dram("fw_in", [max(n_fx, 1), D, 6160], F32, "ExternalInput")
    fbf = dram("fbf", [16 * max(n_fx, 1), 1], F32, "ExternalInput")
    fw_out = dram("fw_out", [max(n_fx, 1), D, D], F32, "ExternalInput")
    w_gu = dram("w_gu", [depth, D, 2 * DFF], F32, "ExternalInput")
    w_dn = dram("w_dn", [depth, DFF, D], F32, "ExternalInput")
    consts = dram("consts", [128, 128 * 4 + 16 * 128], F32, "ExternalInput")
    outT = dram("outT", [D, T], F32, "ExternalOutput")

    xs_d = dram("xs_d", [D, T], F32)
    qkT_d = dram("qkT_d", [2 * D, T], BF16)
    tm_d = dram("tm_d", [T, 2048], BF16)
    tmb = [dram(f"tmb{i}", [T, 512], BF16) for i in range(10)]
    gate_d = dram("gate_d", [16, T], F32)
    gate2_d = dram("gate2_d", [4, T], F32)
    oT_d = dram("oT_d", [D, T], BF16)
    ncum_d = dram("ncum_d", [16, T], F32)
    if dbg:
        dbg_h = dram("dbg_h", [D, T], F32, "ExternalOutput")

    ident_f = S.sb([128, 128], F32, "ident_f")
    ident_b = S.sb([128, 128], BF16, "ident_b")
    mask01 = S.sb([128, 128], F32, "mask01")
    maskneg = S.sb([128, 128], F32, "maskneg")
    onesm = S.sb([128, 128], BF16, "onesm")
    ones1 = S.sb([128, 128], BF16, "ones1")
    sel16 = S.sb([16, 4 * 128], F32, "sel16")
    modsb = S.sb([128, depth * 96], F32, "modsb")
    gam1 = S.sb([128, depth * 16], F32, "gam1")
    gam2 = S.sb([128, depth * 16], F32, "gam2")
    nmw_s = S.sb([128, depth * 16], F32, "nmw_s")
    nfw_s = S.sb([128, depth * 16], F32, "nfw_s")
    fnw_s = S.sb([128, 16], F32, "fnw_s")
    zero_c = S.sb([128, 16], F32, "zero_c")
    epsc = S.sb([128, 1], F32, "epsc")
    onec = S.sb([128, 1], F32, "onec")

    S.op("pool", lambda e: e.dma_start(out=ident_f[:], in_=consts[:, 0:128]), w=["ident_f"], dma=True)
    S.op("pool", lambda e: e.dma_start(out=ident_b[:], in_=consts[:, 0:128]), w=["ident_b"], dma=True)
    S.op("pool", lambda e: e.dma_start(out=mask01[:], in_=consts[:, 128:256]), w=["mask01"], dma=True)
    S.op("pool", lambda e: e.dma_start(out=maskneg[:], in_=consts[:, 256:384]), w=["maskneg"], dma=True)
    S.op("pool", lambda e: e.dma_start(out=sel16[:], in_=consts[0:16, 512:512 + 512]), w=["sel16"], dma=True)
    S.op("pool", lambda e: e.dma_start(out=nmw_s[:], in_=nmw[:, :]), w=["nmw_s"], dma=True)
    S.op("pool", lambda e: e.dma_start(out=nfw_s[:], in_=nfw[:, :]), w=["nfw_s"], dma=True)
    S.op("pool", lambda e: e.dma_start(out=fnw_s[:], in_=fnw[:, :]), w=["fnw_s"], dma=True)
    S.op("dve", lambda e: e.memset(onesm[:], 1.0 / D), w=["onesm"])
    S.op("dve", lambda e: e.memset(ones1[:], 1.0), w=["ones1"])
    S.op("dve", lambda e: e.memset(zero_c[:], 0.0), w=["zero_c"])
    S.op("dve", lambda e: e.memset(epsc[:], EPS), w=["epsc"])
    S.op("dve", lambda e: e.memset(onec[:], 1.0), w=["onec"])

    m0 = S.mark()
    NWB = 3
    wbufs = [S.sb([128, 16, 512], BF16, f"wb{i}") for i in range(NWB)]
    wstate = dict(n=0)

    def wload(src2d, k0, nk, c0, ncols, dcol=0):
        i = wstate["n"] % NWB
        wstate["n"] += 1
        b = wbufs[i]
        src = src2d[k0 * 128:(k0 + nk) * 128, c0:c0 + ncols].rearrange("(k p) c -> p k c", p=128)
        S.op("pool", lambda e: e.dma_start(out=b[:, 0:nk, dcol:dcol + ncols], in_=src), w=[("wb", i)], dma=True)
        return b, ("wb", i)

    def wload2(src2d, k0, nk, c0, c1, ncols):
        i = wstate["n"] % NWB
        wstate["n"] += 1
        b = wbufs[i]
        s0 = src2d[k0 * 128:(k0 + nk) * 128, c0:c0 + ncols].rearrange("(k p) c -> p k c", p=128)
        s1 = src2d[k0 * 128:(k0 + nk) * 128, c1:c1 + ncols].rearrange("(k p) c -> p k c", p=128)
        S.op("pool", lambda e: e.dma_start(out=b[:, 0:nk, 0:ncols], in_=s0), w=[("wb", i)], dma=True)
        S.op("pool", lambda e: e.dma_start(out=b[:, 0:nk, ncols:2 * ncols], in_=s1), w=[("wb", i, 1)], r=[("wb", i)], dma=True)
        return b, ("wb", i, 1)

    def reset_w():
        wstate["n"] = 0

    def wstream(blocks, consume, la=NWB - 1):
        issued = []
        for j in range(len(blocks)):
            while len(issued) < min(len(blocks), j + la + 1):
                issued.append(blocks[len(issued)]())
            b, key = issued[j]
            consume(j, b, key)

    psb = [S.ps([128, 512], F32, f"psb{i}") for i in range(8)]
    psring = Ring(list(range(8)))

    def psum():
        i = psring.next()
        return psb[i], ("ps", i)

    cs32 = S.sb([128, 16], F32, "cs32")
    csb = S.sb([128, 16], BF16, "csb")
    adab_s = S.sb([128, depth * 96], F32, "adab_s")
    S.op("pool", lambda e: e.dma_start(out=cs32[:], in_=cT[:, :]), w=["cs32"], dma=True)
    S.op("pool", lambda e: e.dma_start(out=adab_s[:], in_=ada_bT[:, :]), w=["adab_s"], dma=True)
    S.op("act", lambda e: e.activation(out=csb[:], in_=cs32[:], func=AF.Silu), r=["cs32"], w=["csb"])
    for l in range(depth):
        pm, pmk = psum()
        blocks = [(lambda l=l, bi=bi: wload(ada_w[l], 0, 16, bi * 512, 512)) for bi in range(24)]

        def cons(j, b, key, pm=pm, pmk=pmk):
            for j4 in range(4):
                col = j * 4 + j4
                for kc in range(16):
                    S.op("pe", lambda e, b=b, kc=kc, j4=j4, col=col: e.matmul(
                        pm[:, col:col + 1], b[:, kc, j4 * 128:(j4 + 1) * 128], csb[:, kc:kc + 1],
                        start=(kc == 0), stop=(kc == 15)), r=[key, "csb"], w=[pmk])
        wstream(blocks, cons)
        S.op("dve", lambda e, l=l, pm=pm: e.tensor_tensor(out=modsb[:, l * 96:(l + 1) * 96], in0=pm[:, 0:96],
                                                       in1=adab_s[:, l * 96:(l + 1) * 96], op=ALU.add),
             r=[pmk, "adab_s"], w=[("mod", l)])
        S.op("dve", lambda e, l=l: e.scalar_tensor_tensor(out=gam1[:, l * 16:(l + 1) * 16],
                                                         in0=modsb[:, l * 96 + 16:l * 96 + 32], scalar=1.0,
                                                         in1=nmw_s[:, l * 16:(l + 1) * 16], op0=ALU.add, op1=ALU.mult),
             r=[("mod", l), "nmw_s"], w=[("gam1", l)])
        S.op("dve", lambda e, l=l: e.scalar_tensor_tensor(out=gam2[:, l * 16:(l + 1) * 16],
                                                         in0=modsb[:, l * 96 + 64:l * 96 + 80], scalar=1.0,
                                                         in1=nfw_s[:, l * 16:(l + 1) * 16], op0=ALU.add, op1=ALU.mult),
             r=[("mod", l), "nfw_s"], w=[("gam2", l)])

    def modcol(l, which, c):
        base = l * 96 + which * 16 + c
        return modsb[:, base:base + 1]

    xs = S.sb([128, 16, TT], F32, "xs")
    hb = S.sb([128, 16, TT], BF16, "hb")
    act = S.sb([128, NFC, TT], BF16, "act")
    rstd = S.sb([128, TT], F32, "rstd")
    tmpr = Ring([(S.sb([128, TT], F32, f"tmp{i}"), ("tmp", i)) for i in range(3)])
    vst3 = S.sb([128, 4, 2048], BF16, "vst3")
    stgf = Ring([(S.sb([128, TT], F32, f"stgf{i}"), ("stgf", i)) for i in range(2)])

    def norm_tile(gam_ap, sh_ap, l, rdeps, out_fn=None):
        for g in range(4):
            S.op("act", lambda e, g=g: e.activation(out=act[:, 4 * g:4 * g + 4, :], in_=xs[:, 4 * g:4 * g + 4, :],
                                                   func=AF.Square),
                 r=[("xs", c) for c in range(4 * g, 4 * g + 4)], w=[("act", c) for c in range(4 * g, 4 * g + 4)])
        pss, pssk = psum()
        for c in range(16):
            S.op("pe", lambda e, c=c: e.matmul(pss[:, :], onesm[:, :], act[:, c, :], start=(c == 0), stop=(c == 15)),
                 r=[("act", c), "onesm"], w=[pssk])
        S.op("act", lambda e: e.activation(out=rstd[:], in_=pss[:, :], func=AF.Ln, bias=epsc[:, 0:1], scale=1.0),
             r=[pssk, "epsc"], w=["rstd"])
        S.op("act", lambda e: e.activation(out=rstd[:], in_=rstd[:], func=AF.Exp, scale=-0.5), r=["rstd"], w=["rstd"])
        for c in range(16):
            t, tk = tmpr.next()
            S.op("dve", lambda e, c=c, t=t: e.tensor_tensor(out=t[:], in0=xs[:, c, :], in1=rstd[:], op=ALU.mult),
                 r=[("xs", c), "rstd"], w=[tk])
            if out_fn is None:
                S.op("act", lambda e, c=c, t=t: e.activation(out=hb[:, c, :], in_=t[:], func=AF.Identity,
                                                            bias=sh_ap[:, c:c + 1], scale=gam_ap[:, c:c + 1]),
                     r=[tk] + rdeps, w=[("hb", c)])
            else:
                out_fn(c, t, tk)

    def load_x(src, tt):
        S.op("sp", lambda e: e.dma_start(out=xs[:, :, :],
                                         in_=src[:, TS(tt, TT)].rearrange("(c p) t -> p c t", p=128)),
             r=[("xd",)], w=[("xs", c) for c in range(16)], dma=True)

    evac_rr = Ring(["act", "dve"])

    def evac_bf16(ps_ap, psk, dst_ap, dstk, scale=None, func=None, extra_r=()):
        if func is not None:
            S.op("act", lambda e: e.activation(out=dst_ap, in_=ps_ap, func=func), r=[psk] + list(extra_r), w=[dstk])
            return
        eng = evac_rr.next()
        if eng == "act":
            S.op("act", lambda e: e.activation(out=dst_ap, in_=ps_ap, func=AF.Identity,
                                               scale=(1.0 if scale is None else scale)), r=[psk] + list(extra_r), w=[dstk])
        else:
            if scale is None:
                S.op("dve", lambda e: e.tensor_copy(out=dst_ap, in_=ps_ap), r=[psk] + list(extra_r), w=[dstk])
            else:
                S.op("dve", lambda e: e.tensor_scalar(out=dst_ap, in0=ps_ap, scalar1=scale, scalar2=None, op0=ALU.mult),
                     r=[psk] + list(extra_r), w=[dstk])

    def phase_inproj(l, x_src):
        mixer = l % 2
        j = l // 2
        if mixer == 0:
            W = mw_in[j]
            fm_blocks = 4
            qscale = 256 ** -0.5
            nq_chunks = 8
            tm_c0, tm_blocks = 1024, 10
        else:
            W = fw_in[j]
            fm_blocks = 8
            qscale = 128 ** -0.5
            nq_chunks = 16
            tm_c0, tm_blocks = 4096, 4
        def body(tt):
            load_x(x_src, tt)
            norm_tile(gam1[:, l * 16:(l + 1) * 16], modsb[:, l * 96:l * 96 + 16], l, [("gam1", l), ("mod", l)])
            blocks = []
            for bi in range(fm_blocks):
                blocks.append(lambda bi=bi: wload(W, 0, 16, bi * 512, 512))
            for bi in range(tm_blocks):
                blocks.append(lambda bi=bi: wload(W, 0, 16, tm_c0 + bi * 512, 512))
            blocks.append(lambda: wload(W, 0, 16, 6144, 8 if mixer == 0 else 16))

            def cons(jb, b, key):
                if jb < fm_blocks:
                    for oc4 in range(4):
                        oc = jb * 4 + oc4
                        p, pk = psum()
                        for kc in range(16):
                            S.op("pe", lambda e, p=p, b=b, kc=kc, oc4=oc4: e.matmul(
                                p[:, :], b[:, kc, oc4 * 128:(oc4 + 1) * 128], hb[:, kc, :],
                                start=(kc == 0), stop=(kc == 15)), r=[key, ("hb", kc)], w=[pk])
                        evac_bf16(p[:, :], pk, act[:, oc, :], ("act", oc), scale=(qscale if oc < nq_chunks else None))
                    if jb == fm_blocks - 1:
                        nfm = fm_blocks * 4
                        S.op("sp", lambda e: e.dma_start(
                            out=qkT_d.rearrange("(c p) t -> p c t", p=128)[:, 0:nfm, TS(tt, TT)], in_=act[:, 0:nfm, :]),
                            r=[("act", c) for c in range(nfm)], w=[("qkT", c) for c in range(32)], dma=True)
                elif jb < fm_blocks + tm_blocks:
                    nb = jb - fm_blocks
                    for m in range(4):
                        p, pk = psum()
                        for kc in range(16):
                            S.op("pe", lambda e, p=p, b=b, kc=kc, m=m: e.matmul(
                                p[:, :], hb[:, kc, m * 128:(m + 1) * 128], b[:, kc, :],
                                start=(kc == 0), stop=(kc == 15)), r=[key, ("hb", kc)], w=[pk])
                        is_o = (mixer == 0 and nb >= 6)
                        s4 = nb % 4
                        evac_bf16(p[:, :], pk, vst3[:, m, s4 * 512:(s4 + 1) * 512], ("vst", m, s4),
                                  func=(AF.Sigmoid if is_o else None))
                    if mixer == 0:
                        s4 = nb % 4
                        S.op("sp", lambda e, nb=nb, s4=s4: e.dma_start(
                            out=tmb[nb].rearrange("(a m p) c -> a p m c", m=4, p=128)[TS(tt, 1)]
                            .rearrange("a p m c -> (a p) m c"), in_=vst3[:, :, s4 * 512:(s4 + 1) * 512]),
                            r=[("vst", m, s4) for m in range(4)], w=[("tm",)], dma=True)
                    elif nb == tm_blocks - 1:
                        S.op("sp", lambda e: e.dma_start(
                            out=tm_d.rearrange("(a m p) c -> a p m c", m=4, p=128)[TS(tt, 1)]
                            .rearrange("a p m c -> (a p) m c"), in_=vst3[:, :, :]),
                            r=[("vst", m, x) for m in range(4) for x in range(4)], w=[("tm",)], dma=True)
                else:
                    if mixer == 0:
                        for gi in range(2):
                            p, pk = psum()
                            for kc in range(16):
                                S.op("pe", lambda e, p=p, b=b, kc=kc, gi=gi: e.matmul(
                                    p[0:4, :], b[:, kc, gi * 4:gi * 4 + 4], hb[:, kc, :],
                                    start=(kc == 0), stop=(kc == 15)), r=[key, ("hb", kc)], w=[pk])
                            sf, sfk = stgf.next()
                            S.op("dve", lambda e, p=p, sf=sf: e.tensor_copy(out=sf[0:4, :], in_=p[0:4, :]), r=[pk], w=[sfk])
                            dst = gate_d if gi == 0 else gate2_d
                            S.op("sp", lambda e, sf=sf, dst=dst: e.dma_start(
                                out=dst[0:4, TS(tt, TT)], in_=sf[0:4, :]), r=[sfk], w=[("gate", gi)], dma=True)
                    else:
                        p, pk = psum()
                        for kc in range(16):
                            S.op("pe", lambda e, p=p, b=b, kc=kc: e.matmul(
                                p[0:16, :], b[:, kc, 0:16], hb[:, kc, :],
                                start=(kc == 0), stop=(kc == 15)), r=[key, ("hb", kc)], w=[pk])
                        sf, sfk = stgf.next()
                        S.op("dve", lambda e, p=p, sf=sf: e.tensor_copy(out=sf[0:16, :], in_=p[0:16, :]), r=[pk], w=[sfk])
                        S.op("sp", lambda e, sf=sf: e.dma_start(
                            out=gate_d[0:16, TS(tt, TT)], in_=sf[0:16, :]), r=[sfk], w=[("gate", 0)], dma=True)
            wstream(blocks, cons)
        S.loop(NT, body, reset_w)

    def phase_post(l, x_src, last):
        mixer = l % 2
        j = l // 2
        Wo = mw_out[j] if mixer == 0 else fw_out[j]
        def fin_factory(tt):
            def fin(c, t, tk):
                S.op("act", lambda e, c=c, t=t: e.activation(out=xs[:, c, :], in_=t[:], func=AF.Identity,
                                                            scale=fnw_s[:, c:c + 1]), r=[tk, "fnw_s"], w=[("xs", c)])
                if c == 15:
                    S.op("sp", lambda e: e.dma_start(
                        out=outT[:, TS(tt, TT)].rearrange("(c p) t -> p c t", p=128), in_=xs[:, :, :]),
                        r=[("xs", x) for x in range(16)], w=[("out",)], dma=True)
            return fin

        def body(tt):
            load_x(x_src, tt)
            S.op("sp", lambda e: e.dma_start(
                out=act[:, 16:32, :], in_=oT_d[:, TS(tt, TT)].rearrange("(c p) t -> p c t", p=128)),
                r=[("oT",)], w=[("act", c) for c in range(16, 32)], dma=True)
            blocks = [(lambda ob=ob: wload(Wo, 0, 16, ob * 512, 512)) for ob in range(4)]

            def cons_o(ob, b, key):
                for oc4 in range(4):
                    oc = ob * 4 + oc4
                    p, pk = psum()
                    for kc in range(16):
                        S.op("pe", lambda e, p=p, b=b, kc=kc, oc4=oc4: e.matmul(
                            p[:, :], b[:, kc, oc4 * 128:(oc4 + 1) * 128], act[:, 16 + kc, :],
                            start=(kc == 0), stop=(kc == 15)), r=[key, ("act", 16 + kc)], w=[pk])
                    S.op("dve", lambda e, p=p, oc=oc: e.scalar_tensor_tensor(
                        out=xs[:, oc, :], in0=p[:, :], scalar=modcol(l, 2, oc), in1=xs[:, oc, :],
                        op0=ALU.mult, op1=ALU.add), r=[pk, ("mod", l), ("xs", oc)], w=[("xs", oc)])
            wstream(blocks, cons_o)
            import os as _os
            if _os.environ.get("DBG_SKIP_FFN"):
                norm_tile(None, None, l, [], out_fn=fin_factory(tt))
                return
            norm_tile(gam2[:, l * 16:(l + 1) * 16], modsb[:, l * 96 + 48:l * 96 + 64], l, [("gam2", l), ("mod", l)])
            blocks = []
            for fb in range(22):
                blocks.append(lambda fb=fb: wload(w_gu[l], 0, 16, fb * 256, 256))
                blocks.append(lambda fb=fb: wload(w_gu[l], 0, 16, DFF + fb * 256, 256))
            hold = {}

            def cons_gu(jb, b, key):
                if jb % 2 == 0:
                    hold["g"] = (b, key)
                    return
                fb = jb // 2
                gb, gk = hold["g"]
                ub, uk = b, key
                for f2 in range(2):
                    fc = fb * 2 + f2
                    pg, pgk = psum()
                    pu, puk = psum()
                    for kc in range(16):
                        S.op("pe", lambda e, pg=pg, gb=gb, kc=kc, f2=f2: e.matmul(
                            pg[:, :], gb[:, kc, f2 * 128:(f2 + 1) * 128], hb[:, kc, :],
                            start=(kc == 0), stop=(kc == 15)), r=[gk, ("hb", kc)], w=[pgk])
                    for kc in range(16):
                        S.op("pe", lambda e, pu=pu, ub=ub, kc=kc, f2=f2: e.matmul(
                            pu[:, :], ub[:, kc, f2 * 128:(f2 + 1) * 128], hb[:, kc, :],
                            start=(kc == 0), stop=(kc == 15)), r=[uk, ("hb", kc)], w=[puk])
                    t, tk = tmpr.next()
                    S.op("act", lambda e, pg=pg, t=t: e.activation(out=t[:], in_=pg[:, :], func=AF.Silu), r=[pgk], w=[tk])
                    S.op("dve", lambda e, pu=pu, t=t, fc=fc: e.tensor_tensor(out=act[:, fc, :], in0=t[:], in1=pu[:, :],
                                                                           op=ALU.mult), r=[tk, puk], w=[("act", fc)])
            wstream(blocks, cons_gu, la=1)
            for cb in range(4):
                accs = [psum() for _ in range(4)]
                blocks = [(lambda kg=kg, cb=cb: wload(w_dn[l], kg * 16, (16 if kg < 2 else 12), cb * 512, 512)) for kg in range(3)]

                def cons_d(kg, b, key, accs=accs, cb=cb):
                    nk = 16 if kg < 2 else 12
                    for oc4 in range(4):
                        p, pk = accs[oc4]
                        for kc in range(nk):
                            S.op("pe", lambda e, p=p, b=b, kc=kc, oc4=oc4, kg=kg, nk=nk: e.matmul(
                                p[:, :], b[:, kc, oc4 * 128:(oc4 + 1) * 128], act[:, kg * 16 + kc, :],
                                start=(kg == 0 and kc == 0), stop=(kg == 2 and kc == nk - 1)),
                                r=[key, ("act", kg * 16 + kc)], w=[pk])
                wstream(blocks, cons_d)
                for oc4 in range(4):
                    oc = cb * 4 + oc4
                    p, pk = accs[oc4]
                    S.op("dve", lambda e, p=p, oc=oc: e.scalar_tensor_tensor(
                        out=xs[:, oc, :], in0=p[:, :], scalar=modcol(l, 5, oc), in1=xs[:, oc, :],
                        op0=ALU.mult, op1=ALU.add), r=[pk, ("mod", l), ("xs", oc)], w=[("xs", oc)])
            if not last:
                S.op("sp", lambda e: e.dma_start(
                    out=xs_d[:, TS(tt, TT)].rearrange("(c p) t -> p c t", p=128), in_=xs[:, :, :]),
                    r=[("xs", c) for c in range(16)], w=[("xd",)], dma=True)
            else:
                norm_tile(None, None, l, [], out_fn=fin_factory(tt))
        S.loop(NT, body, reset_w)

    def phase_fox(l):
        j = l // 2
        if True:
            S.reset(m0)

            def sb2(shape, dt, name):
                return S.sb(shape, dt, name)

            rS, rT, rO, rM = Ring([0, 1]), Ring([2, 3]), Ring([4, 5]), Ring([6, 7])

            def bank(ring):
                i_ = ring.next()
                return psb[i_], ("ps", i_)
            fz = sb2([16, T], F32, "fz")
            nbf = sb2([16, 1], F32, "nbf")
            bfs = sb2([16, 1], F32, "bfs")
            ncb = sb2([128, T], F32, "ncb")
            qh = [sb2([128, T], BF16, f"qh{i}") for i in range(1)]
            kh = [sb2([128, T], BF16, f"kh{i}") for i in range(1)]
            vh = [sb2([128, NQ, 129], BF16, f"vh{i}") for i in range(1)]
            sq = sb2([128, T], BF16, "sqb")
            q2 = sb2([128, NQ], F32, "q2")
            km = sb2([128, 16], F32, "km")
            km1 = sb2([128, 1], F32, "km1")
            negm = sb2([128, NQ], F32, "negm")
            ssb = Ring([(sb2([128, 512], F32, f"ssb{i}"), ("ssb", i)) for i in range(3)])
            pbf = Ring([(sb2([128, 512], BF16, f"pbf{i}"), ("pbf", i)) for i in range(3)])
            ptb = Ring([(sb2([128, 512], BF16, f"ptb{i}"), ("ptb", i)) for i in range(3)])
            oq = Ring([(sb2([128, 128], BF16, f"oq{i}"), ("oq", i)) for i in range(2)])
            rinv = Ring([(sb2([128, 1], F32, f"rinv{i}"), ("rinv", i)) for i in range(2)])
            oTh = [sb2([128, T], BF16, f"oTh{i}") for i in range(1)]

            S.op("sp", lambda e: e.dma_start(out=fz[:, :], in_=gate_d[0:16, :]),
                 r=[("gate", 0)], w=["fz"], dma=True)
            S.op("sp", lambda e: e.dma_start(out=bfs[:, :], in_=fbf[j * 16:(j + 1) * 16, 0:1]), w=["bfs"], dma=True)
            S.op("dve", lambda e: e.tensor_scalar(out=nbf[:], in0=bfs[:], scalar1=-1.0, scalar2=None, op0=ALU.mult),
                 r=["bfs"], w=["nbf"])
            S.op("act", lambda e: e.activation(out=fz[:, :], in_=fz[:, :], func=AF.Exp, bias=nbf[:, 0:1], scale=-1.0),
                 r=["fz", "nbf"], w=["fz"])
            S.op("act", lambda e: e.activation(out=fz[:, :], in_=fz[:, :], func=AF.Ln, bias=onec[0:16, 0:1], scale=1.0),
                 r=["fz", "onec"], w=["fz"])
            S.op("dve", lambda e: e.memset(ncb[0:16, :], 1.0), w=["ncb"])
            SEG = 1024
            for sg in range(T // SEG):
                a0, a1 = sg * SEG, (sg + 1) * SEG
                S.op("dve", lambda e, a0=a0, a1=a1, sg=sg: e.tensor_tensor_scan(
                    out=fz[:, a0:a1], data0=ncb[0:16, a0:a1], data1=fz[:, a0:a1],
                    initial=(0.0 if sg == 0 else fz[:, a0 - 1:a0]), op0=ALU.mult, op1=ALU.add),
                    r=["ncb", "fz"], w=["fz"])
            S.op("sp", lambda e: e.dma_start(out=ncum_d[:, :], in_=fz[:, :]), r=["fz"], w=["ncum_d"], dma=True)
            nqt = sb2([NQ, 128], F32, "nqt")
            ncq = sb2([128, NQ], F32, "ncq")

            def load_head(h):
                i = 0
                S.op("act", lambda e: e.dma_start(out=qh[i][:, :], in_=qkT_d[TS(h, 128), :]),
                     r=[("qkT", oc) for oc in range(32)], w=[("qh", i)], dma=True)
                S.op("act", lambda e: e.dma_start(out=kh[i][:, :], in_=qkT_d[D:2 * D, :][TS(h, 128), :]),
                     r=[("qkT", oc) for oc in range(32)], w=[("kh", i)], dma=True)
                S.op("act", lambda e: e.dma_start(
                    out=vh[i][:, :, 0:128],
                    in_=tm_d[:, TS(h, 128)].rearrange("(j p) d -> p j d", p=128)),
                    r=[("tm",)], w=[("vh", i)], dma=True)
                S.op("pool", lambda e: e.memset(vh[i][:, :, 128:129], 1.0), r=[], w=[("vh1", i)])
                S.op("act", lambda e: e.dma_start(out=ncb[:, :], in_=ncum_d[TS(h, 1), :].to_broadcast([128, T])),
                     r=["ncum_d"], w=["ncb"], dma=True)
                S.op("act", lambda e: e.dma_start(out=nqt[:, :],
                                                 in_=ncum_d[TS(h, 1), :].rearrange("o (q p) -> (o q) p", p=128)),
                     r=["ncum_d"], w=["nqt"], dma=True)
                p, pk = bank(rM)
                S.op("pe", lambda e, p=p: e.transpose(p[:, 0:NQ], nqt[0:NQ, :], ident_f[0:NQ, 0:NQ]),
                     r=["nqt", "ident_f"], w=[pk])
                S.op("dve", lambda e, p=p: e.tensor_scalar(out=ncq[:, :], in0=p[:, 0:NQ], scalar1=-1.0, scalar2=None,
                                                          op0=ALU.mult), r=[pk], w=["ncq"])

            def head_body(h):
                i = 0
                load_head(h)
                S.op("act", lambda e, i=i: e.activation(out=sq[:, :], in_=kh[i][:, :], func=AF.Square),
                     r=[("kh", i)], w=["sq"])
                for t4 in range(T // 512):
                    p, pk = bank(rM)
                    S.op("pe", lambda e, p=p, t4=t4: e.matmul(p[:, :], ones1[:, :], sq[:, t4 * 512:(t4 + 1) * 512],
                                                             start=True, stop=True), r=["sq", "ones1"], w=[pk])
                    S.op("dve", lambda e, p=p, t4=t4: e.tensor_reduce(out=km[:, t4:t4 + 1], in_=p[:, :], axis=AX.X,
                                                                     op=ALU.max), r=[pk], w=[("km", t4)])
                S.op("dve", lambda e: e.tensor_reduce(out=km1[:, 0:1], in_=km[:, 0:T // 512], axis=AX.X, op=ALU.max),
                     r=[("km", t4) for t4 in range(T // 512)], w=["km1"])
                S.op("dve", lambda e: e.tensor_scalar(out=km1[:, 0:1], in0=km1[:, 0:1], scalar1=1.05, scalar2=None,
                                                      op0=ALU.mult), r=["km1"], w=["km1"])
                S.op("act", lambda e, i=i: e.activation(out=sq[:, :], in_=qh[i][:, :], func=AF.Square),
                     r=[("qh", i)], w=["sq"])
                p2, p2k = bank(rM)
                for qi in range(NQ):
                    S.op("pe", lambda e, qi=qi, p2=p2: e.matmul(p2[:, qi:qi + 1], sq[:, qi * 128:(qi + 1) * 128],
                                                               ones1[:, 0:1], start=True, stop=True),
                         r=["sq", "ones1"], w=[p2k])
                S.op("act", lambda e, p2=p2: e.activation(out=negm[:, :], in_=p2[:, 0:NQ], func=AF.Sqrt,
                                                         scale=km1[:, 0:1]), r=[p2k, "km1"], w=["negm"])
                S.op("dve", lambda e: e.tensor_scalar(out=negm[:, :], in0=negm[:, :], scalar1=-1.0, scalar2=None,
                                                      op0=ALU.mult), r=["negm"], w=["negm"])
                steps = []
                for qi in range(NQ):
                    ks = (qi + 1) * 128
                    nkt = (ks + 511) // 512
                    for kt in range(nkt):
                        w = min(512, ks - kt * 512)
                        steps.append(dict(qi=qi, kt=kt, w=w, first=(kt == 0), last=(kt == nkt - 1)))
                po = {}

                def st_qk(s):
                    p, pk = bank(rS)
                    s["ps"], s["psk"] = p, pk
                    qi, kt, w = s["qi"], s["kt"], s["w"]
                    S.op("pe", lambda e: e.matmul(p[:, 0:w], qh[i][:, qi * 128:(qi + 1) * 128],
                                                  kh[i][:, kt * 512:kt * 512 + w], start=True, stop=True),
                         r=[("qh", i), ("kh", i)], w=[pk])
                    sb_, sbk = ssb.next()
                    s["ssb"], s["ssbk"] = sb_, sbk
                    S.op("dve", lambda e: e.scalar_tensor_tensor(
                        out=sb_[:, 0:w], in0=p[:, 0:w], scalar=ncq[:, qi:qi + 1],
                        in1=ncb[:, kt * 512:kt * 512 + w], op0=ALU.add, op1=ALU.add),
                        r=[pk, "ncq", "ncb"], w=[sbk])
                    if s["last"]:
                        S.op("pool", lambda e: e.tensor_tensor(out=sb_[:, w - 128:w], in0=sb_[:, w - 128:w],
                                                               in1=maskneg[:, :], op=ALU.add),
                             r=[sbk, "maskneg"], w=[sbk])
                    pb, pbk = pbf.next()
                    s["pb"], s["pbk"] = pb, pbk
                    S.op("act", lambda e: e.activation(out=pb[:, 0:w], in_=sb_[:, 0:w], func=AF.Exp,
                                                       bias=negm[:, qi:qi + 1], scale=1.0),
                         r=[sbk, "negm"], w=[pbk])

                def st_tr(s):
                    p, pk = bank(rT)
                    w = s["w"]
                    pv = p.bitcast(BF16) if False else p
                    s["pt"], s["ptk"] = p, pk
                    pb = s["pb"]
                    ptv = p[:, 0:256].bitcast(BF16)
                    for jj in range(w // 128):
                        S.op("pe", lambda e, jj=jj: e.transpose(ptv[:, jj * 128:(jj + 1) * 128],
                                                                pb[:, jj * 128:(jj + 1) * 128], ident_b[:, :]),
                             r=[s["pbk"], "ident_b"], w=[pk])
                    tb, tbk = ptb.next()
                    s["tb"], s["tbk"] = tb, tbk
                    S.op("dve", lambda e: e.tensor_copy(out=tb[:, 0:w], in_=ptv[:, 0:w]), r=[pk], w=[tbk])

                def st_pv(s):
                    qi, kt, w = s["qi"], s["kt"], s["w"]
                    if s["first"]:
                        po["p"], po["k"] = bank(rO)
                    p, pk = po["p"], po["k"]
                    tb = s["tb"]
                    for jj in range(w // 128):
                        S.op("pe", lambda e, jj=jj: e.matmul(
                            p[:, 0:129], tb[:, jj * 128:(jj + 1) * 128], vh[i][:, kt * 4 + jj, 0:129],
                            start=(s["first"] and jj == 0), stop=(s["last"] and jj == w // 128 - 1)),
                            r=[s["tbk"], ("vh", i), ("vh1", i)], w=[pk])
                    if s["last"]:
                        ri, rik = rinv.next()
                        S.op("dve", lambda e: e.reciprocal(out=ri[:, 0:1], in_=p[:, 128:129]), r=[pk], w=[rik])
                        o_, ok = oq.next()
                        S.op("dve", lambda e: e.tensor_scalar(out=o_[:, :], in0=p[:, 0:128], scalar1=ri[:, 0:1],
                                                              scalar2=None, op0=ALU.mult), r=[pk, rik], w=[ok])
                        p3, p3k = bank(rM)
                        p3v = p3[:, 0:64].bitcast(BF16)
                        S.op("pe", lambda e: e.transpose(p3v[:, 0:128], o_[:, :], ident_b[:, :]),
                             r=[ok, "ident_b"], w=[p3k])
                        S.op("act", lambda e: e.activation(out=oTh[i][:, qi * 128:(qi + 1) * 128], in_=p3v[:, 0:128],
                                                           func=AF.Identity), r=[p3k], w=[("oTh", i, qi)])

                ns = len(steps)
                for n in range(ns + 3):
                    if n < ns:
                        st_qk(steps[n])
                    if 0 <= n - 1 < ns:
                        st_tr(steps[n - 1])
                    if 0 <= n - 3 < ns:
                        st_pv(steps[n - 3])
                S.op("act", lambda e, i=i: e.dma_start(out=oT_d[TS(h, 128), :], in_=oTh[i][:, :]),
                     r=[("oTh", i, qi) for qi in range(NQ)], w=[("oT",)], dma=True)
            S.loop(16, head_body)
            S.barrier()

    def phase_mlstm(l):
        j = l // 2
        S.reset(m0)
        A = S.sb([4, T], F32, "gA")
        B = S.sb([4, T], F32, "gB")
        bg = S.sb([4, 2], F32, "bg")
        bg15 = S.sb([4, 2], F32, "bg15")
        ref = S.sb([4, NQ], F32, "ref")
        Gn = S.sb([4, NQ], F32, "Gn")
        Gp = S.sb([4, NQ], F32, "Gp")
        r1 = S.sb([4, NQ], F32, "r1")
        eT = S.sb([128, NQ * 4], F32, "eT")
        flT = S.sb([128, NQ * 4], F32, "flT")
        r1b = S.sb([128, 4 * NQ], F32, "r1b")
        nwb = S.sb([128, 2048], F32, "nwb")
        Cs = S.sb([128, 2, 512], F32, "Cs")
        Cb = S.sb([128, 2, 512], BF16, "Cb")
        ns = S.sb([128, 2], F32, "ns")
        nb_ = S.sb([128, 2], BF16, "nb_")
        qTb = [S.sb([128, 2, 512], BF16, f"qTb{i}") for i in range(2)]
        kTb = [S.sb([128, 2, 512], BF16, f"kTb{i}") for i in range(2)]
        ktm = [S.sb([128, 4, 256], BF16, f"ktm{i}") for i in range(2)]
        vb = [S.sb([128, 4, 512], BF16, f"vb{i}") for i in range(2)]
        ob = [S.sb([128, 4, 512], BF16, f"ob{i}") for i in range(2)]
        yTb = [S.sb([128, 4, 512], BF16, f"yTb{i}") for i in range(2)]
        PT = Ring([(S.sb([128, 128], BF16, f"PT{i}"), ("PT", i)) for i in range(2)])
        Kt = Ring([(S.sb([128, 256], BF16, f"Kt{i}"), ("Kt", i)) for i in range(2)])
        d1 = Ring([(S.sb([128, 4], F32, f"d1{i}"), ("d1", i)) for i in range(2)])
        junk = S.sb([128, 512], BF16, "junk")
        ytmp = Ring([(S.sb([128, 512], F32, f"ytmp{i}"), ("ytmp", i)) for i in range(2)])
        ybf = Ring([(S.sb([128, 512], BF16, f"ybf{i}"), ("ybf", i)) for i in range(2)])

        gdeps = [("gate", gi) for gi in range(2)]
        S.op("sp", lambda e: e.dma_start(out=A[:, :], in_=gate_d[0:4, :]), r=gdeps, w=["gA"], dma=True)
        S.op("sp", lambda e: e.dma_start(out=B[:, :], in_=gate2_d[0:4, :]), r=gdeps, w=["gB"], dma=True)
        S.op("sp", lambda e: e.dma_start(out=bg[:, :], in_=mbg[:, 2 * j:2 * j + 2]), w=["bg"], dma=True)
        S.op("sp", lambda e: e.dma_start(out=nwb[:, :], in_=mnw[:, j * 2048:(j + 1) * 2048]), w=["nwb"], dma=True)
        S.op("dve", lambda e: e.tensor_scalar(out=bg15[:, :], in0=bg[:, :], scalar1=1.0 / 15.0, scalar2=None,
                                              op0=ALU.mult), r=["bg"], w=["bg15"])
        S.op("act", lambda e: e.activation(out=A[:, :], in_=A[:, :], func=AF.Tanh, bias=bg15[:, 0:1], scale=1.0 / 15.0),
             r=["gA", "bg15"], w=["gA"])
        S.op("act", lambda e: e.activation(out=B[:, :], in_=B[:, :], func=AF.Tanh, bias=bg15[:, 1:2], scale=1.0 / 15.0),
             r=["gB", "bg15"], w=["gB"])
        S.op("act", lambda e: e.activation(out=B[:, :], in_=B[:, :], func=AF.Exp, scale=-15.0), r=["gB"], w=["gB"])
        S.op("act", lambda e: e.activation(out=B[:, :], in_=B[:, :], func=AF.Ln, bias=onec[0:4, 0:1], scale=1.0), r=["gB", "onec"], w=["gB"])
        onesT = S.sb([4, T], F32, "onesT")
        S.op("pool", lambda e: e.memset(onesT[:, :], 1.0), w=["onesT"])
        SEG = 1024
        for sg in range(T // SEG):
            a0, a1 = sg * SEG, (sg + 1) * SEG
            S.op("dve", lambda e, a0=a0, a1=a1, sg=sg: e.tensor_tensor_scan(
                out=B[:, a0:a1], data0=onesT[:, a0:a1], data1=B[:, a0:a1],
                initial=(0.0 if sg == 0 else B[:, a0 - 1:a0]), op0=ALU.mult, op1=ALU.add),
                r=["gB", "onesT"], w=["gB"])
        S.op("dve", lambda e: e.scalar_tensor_tensor(out=A[:, :], in0=A[:, :], scalar=15.0, in1=B[:, :],
                                                    op0=ALU.mult, op1=ALU.add), r=["gA", "gB"], w=["gA"])
        A3 = A.rearrange("p (c s) -> p c s", c=NQ, s=128)
        B3 = B.rearrange("p (c s) -> p c s", c=NQ, s=128)
        S.op("dve", lambda e: e.tensor_reduce(out=ref[:, :], in_=A3, axis=AX.X, op=ALU.max), r=["gA"], w=["ref"])
        S.op("dve", lambda e: e.tensor_tensor_scan(out=Gn[:, :], data0=ref[:, :], data1=ref[:, :], initial=0.0,
                                                  op0=ALU.max, op1=ALU.max), r=["ref"], w=["Gn"])
        S.op("dve", lambda e: e.memset(Gp[:, 0:1], 0.0), w=["Gp0"])
        if NQ > 1:
            S.op("dve", lambda e: e.tensor_copy(out=Gp[:, 1:NQ], in_=Gn[:, 0:NQ - 1]), r=["Gn"], w=["Gp1"])
        S.op("dve", lambda e: e.tensor_tensor(out=r1[:, :], in0=Gp[:, :], in1=Gn[:, :], op=ALU.subtract),
             r=["Gn", "Gp0", "Gp1"], w=["r1"])
        S.op("act", lambda e: e.activation(out=r1[:, :], in_=r1[:, :], func=AF.Exp), r=["r1"], w=["r1"])
        for c in range(NQ):
            S.op("dve", lambda e, c=c: e.tensor_scalar(out=A[:, c * 128:(c + 1) * 128], in0=A[:, c * 128:(c + 1) * 128],
                                                      scalar1=Gn[:, c:c + 1], scalar2=None, op0=ALU.subtract),
                 r=["gA", "Gn"], w=["gA"])
            S.op("pool", lambda e, c=c: e.tensor_scalar(out=B[:, c * 128:(c + 1) * 128], in0=B[:, c * 128:(c + 1) * 128],
                                                       scalar1=Gn[:, c:c + 1], scalar2=None, op0=ALU.subtract),
                 r=["gB", "Gn"], w=["gB"])
        S.op("act", lambda e: e.activation(out=A[:, :], in_=A[:, :], func=AF.Exp), r=["gA"], w=["gA"])
        S.op("act", lambda e: e.activation(out=B[:, :], in_=B[:, :], func=AF.Exp), r=["gB"], w=["gB"])
        for (src, dst, dk, sk) in ((A, eT, "eT", "gA"), (B, flT, "flT", "gB")):
            for g0 in range(0, NQ, 64):
                p, pk = psum()
                n_in = min(64, NQ - g0)
                for c in range(g0, g0 + n_in):
                    S.op("pe", lambda e, p=p, c=c, g0=g0, src=src: e.transpose(
                        p[:, (c - g0) * 4:(c - g0 + 1) * 4], src[0:4, c * 128:(c + 1) * 128], ident_f[0:4, 0:4]),
                        r=[sk, "ident_f"], w=[pk])
                S.op("dve", lambda e, p=p, g0=g0, n_in=n_in, dst=dst: e.tensor_copy(
                    out=dst[:, g0 * 4:(g0 + n_in) * 4], in_=p[:, 0:n_in * 4]), r=[pk], w=[dk])
        p, pk = psum()
        for h in range(4):
            S.op("pe", lambda e, p=p, h=h: e.matmul(p[:, h * NQ:(h + 1) * NQ], sel16[0:4, h * 128:(h + 1) * 128],
                                                   r1[0:4, 0:NQ], start=True, stop=True), r=["sel16", "r1"], w=[pk])
        S.op("dve", lambda e, p=p: e.tensor_copy(out=r1b[:, :], in_=p[:, 0:4 * NQ]), r=[pk], w=["r1b"])

        NB = T // 512
        for h in range(4):
            S.op("dve", lambda e: e.memset(Cs[:, :, :], 0.0), w=["Cs"])
            S.op("dve", lambda e: e.memset(ns[:, :], 0.0), w=["ns"])
            for tb in range(NB):
                bi = (h * NB + tb) % 2
                t0 = tb * 512
                S.op("sp", lambda e, bi=bi, t0=t0, h=h: e.dma_start(
                    out=qTb[bi][:, :, :], in_=qkT_d[h * 256:(h + 1) * 256, t0:t0 + 512].rearrange("(c p) t -> p c t", p=128)),
                    r=[("qkT", oc) for oc in range(16)], w=[("qTb", bi)], dma=True)
                S.op("sp", lambda e, bi=bi, t0=t0, h=h: e.dma_start(
                    out=kTb[bi][:, :, :], in_=qkT_d[1024 + h * 256:1024 + (h + 1) * 256, t0:t0 + 512].rearrange("(c p) t -> p c t", p=128)),
                    r=[("qkT", oc) for oc in range(16)], w=[("kTb", bi)], dma=True)
                S.op("sp", lambda e, bi=bi, t0=t0, h=h: e.dma_start(
                    out=ktm[bi][:, :, :], in_=tmb[h // 2][t0:t0 + 512, (h % 2) * 256:(h % 2 + 1) * 256].rearrange("(j p) d -> p j d", p=128)),
                    r=[("tm",)], w=[("ktm", bi)], dma=True)
                S.op("sp", lambda e, bi=bi, t0=t0, h=h: e.dma_start(
                    out=vb[bi][:, :, :], in_=tmb[2 + h][t0:t0 + 512, :].rearrange("(j p) d -> p j d", p=128)),
                    r=[("tm",)], w=[("vb", bi)], dma=True)
                S.op("sp", lambda e, bi=bi, t0=t0, h=h: e.dma_start(
                    out=ob[bi][:, :, :], in_=tmb[6 + h][t0:t0 + 512, :].rearrange("(j p) d -> p j d", p=128)),
                    r=[("tm",)], w=[("ob", bi)], dma=True)
                for cc in range(4):
                    c = tb * 4 + cc
                    ecol = eT[:, c * 4 + h:c * 4 + h + 1]
                    fcol = flT[:, c * 4 + h:c * 4 + h + 1]
                    rcol = r1b[:, h * NQ + c:h * NQ + c + 1]
                    tsl = slice(cc * 128, (cc + 1) * 128)
                    p1, p1k = psum()
                    for c2 in range(2):
                        S.op("pe", lambda e, c2=c2, p1=p1, bi=bi, tsl=tsl: e.matmul(
                            p1[:, 0:128], kTb[bi][:, c2, tsl], qTb[bi][:, c2, tsl], start=(c2 == 0), stop=(c2 == 1)),
                            r=[("kTb", bi), ("qTb", bi)], w=[p1k])
                    pt, ptk = PT.next()
                    S.op("dve", lambda e, p1=p1, pt=pt, ecol=ecol: e.scalar_tensor_tensor(
                        out=pt[:, :], in0=p1[:, 0:128], scalar=ecol, in1=mask01[:, :], op0=ALU.mult, op1=ALU.mult),
                        r=[p1k, "eT", "mask01"], w=[ptk])
                    S.op("dve", lambda e, rcol=rcol: e.tensor_scalar(out=Cs[:, :, :], in0=Cs[:, :, :], scalar1=rcol,
                                                                    scalar2=None, op0=ALU.mult), r=["Cs", "r1b"], w=["Cs"])
                    S.op("act", lambda e: e.activation(out=Cb[:, :, :], in_=Cs[:, :, :], func=AF.Identity), r=["Cs"], w=["Cb"])
                    S.op("dve", lambda e, rcol=rcol: e.tensor_scalar(out=ns[:, :], in0=ns[:, :], scalar1=rcol,
                                                                    scalar2=None, op0=ALU.mult), r=["ns", "r1b"], w=["ns"])
                    S.op("dve", lambda e: e.tensor_copy(out=nb_[:, :], in_=ns[:, :]), r=["ns"], w=["nb_"])
                    p2, p2k = psum()
                    for c2 in range(2):
                        S.op("pe", lambda e, c2=c2, p2=p2, bi=bi, tsl=tsl: e.matmul(
                            p2[:, :], qTb[bi][:, c2, tsl], Cb[:, c2, :], start=(c2 == 0), stop=False),
                            r=[("qTb", bi), "Cb"], w=[p2k])
                    S.op("pe", lambda e, p2=p2, pt=pt, bi=bi, cc=cc: e.matmul(
                        p2[:, :], pt[:, :], vb[bi][:, cc, :], start=False, stop=True), r=[ptk, ("vb", bi)], w=[p2k])
                    p3, p3k = psum()
                    for c2 in range(2):
                        S.op("pe", lambda e, c2=c2, p3=p3, bi=bi, tsl=tsl: e.matmul(
                            p3[:, 0:1], qTb[bi][:, c2, tsl], nb_[:, c2:c2 + 1], start=(c2 == 0), stop=False),
                            r=[("qTb", bi), "nb_"], w=[p3k])
                    S.op("pe", lambda e, p3=p3, pt=pt: e.matmul(p3[:, 0:1], pt[:, :], ones1[:, 0:1], start=False, stop=True),
                         r=[ptk, "ones1"], w=[p3k])
                    kt_, ktk = Kt.next()
                    S.op("pool", lambda e, kt_=kt_, bi=bi, cc=cc, ecol=ecol: e.tensor_scalar(
                        out=kt_[:, :], in0=ktm[bi][:, cc, :], scalar1=ecol, scalar2=None, op0=ALU.mult),
                        r=[("ktm", bi), "eT"], w=[ktk])
                    p5, p5k = psum()
                    for c2 in range(2):
                        p4, p4k = psum()
                        S.op("pe", lambda e, c2=c2, p4=p4, kt_=kt_, bi=bi, cc=cc: e.matmul(
                            p4[:, :], kt_[:, c2 * 128:(c2 + 1) * 128], vb[bi][:, cc, :], start=True, stop=True),
                            r=[ktk, ("vb", bi)], w=[p4k])
                        S.op("pe", lambda e, c2=c2, p5=p5, kt_=kt_: e.matmul(
                            p5[:, c2:c2 + 1], kt_[:, c2 * 128:(c2 + 1) * 128], ones1[:, 0:1], start=True, stop=True),
                            r=[ktk, "ones1"], w=[p5k])
                        S.op("dve", lambda e, c2=c2, p4=p4: e.tensor_tensor(out=Cs[:, c2, :], in0=Cs[:, c2, :], in1=p4[:, :],
                                                                          op=ALU.add), r=[p4k, "Cs", "Cb"], w=["Cs"])
                    S.op("dve", lambda e, p5=p5: e.tensor_tensor(out=ns[:, :], in0=ns[:, :], in1=p5[:, 0:2], op=ALU.add),
                         r=[p5k, "ns", "nb_"], w=["ns"])
                    dd, ddk = d1.next()
                    S.op("act", lambda e, dd=dd, p3=p3: e.activation(out=dd[:, 0:1], in_=p3[:, 0:1], func=AF.Abs),
                         r=[p3k], w=[(ddk, 0)])
                    S.op("dve", lambda e, dd=dd, fcol=fcol: e.tensor_scalar(
                        out=dd[:, 0:1], in0=dd[:, 0:1], scalar1=fcol, scalar2=None, op0=ALU.max),
                        r=[(ddk, 0), "flT"], w=[(ddk, 0)])
                    S.op("dve", lambda e, dd=dd: e.reciprocal(out=dd[:, 1:2], in_=dd[:, 0:1]), r=[(ddk, 0)], w=[(ddk, 1)])
                    S.op("pool", lambda e, dd=dd: e.memset(dd[:, 2:3], 0.0), w=[(ddk, 2)])
                    S.op("act", lambda e, dd=dd, p2=p2: e.activation(out=junk[:, :], in_=p2[:, :], func=AF.Square,
                                                                    scale=dd[:, 1:2], accum_out=dd[:, 2:3]),
                         r=[p2k, (ddk, 1), (ddk, 2)], w=[(ddk, 2), "junk"])
                    S.op("act", lambda e, dd=dd: e.activation(out=dd[:, 3:4], in_=dd[:, 2:3], func=AF.Ln,
                                                             bias=epsc[:, 0:1], scale=1.0 / 512.0),
                         r=[(ddk, 2), "epsc"], w=[(ddk, 3)])
                    S.op("act", lambda e, dd=dd: e.activation(out=dd[:, 3:4], in_=dd[:, 3:4], func=AF.Exp, scale=-0.5),
                         r=[(ddk, 3)], w=[(ddk, 3)])
                    S.op("dve", lambda e, dd=dd: e.tensor_tensor(out=dd[:, 3:4], in0=dd[:, 3:4], in1=dd[:, 1:2],
                                                                op=ALU.mult), r=[(ddk, 3), (ddk, 1)], w=[(ddk, 3)])
                    yt, ytk = ytmp.next()
                    S.op("dve", lambda e, yt=yt, p2=p2, dd=dd, h=h: e.scalar_tensor_tensor(
                        out=yt[:, :], in0=p2[:, :], scalar=dd[:, 3:4], in1=nwb[:, h * 512:(h + 1) * 512],
                        op0=ALU.mult, op1=ALU.mult), r=[p2k, (ddk, 3), "nwb"], w=[ytk])
                    yb, ybk = ybf.next()
                    S.op("pool", lambda e, yb=yb, yt=yt, bi=bi, cc=cc: e.tensor_tensor(
                        out=yb[:, :], in0=yt[:, :], in1=ob[bi][:, cc, :], op=ALU.mult), r=[ytk, ("ob", bi)], w=[ybk])
                    p6, p6k = psum()
                    p6v = p6[:, 0:256].bitcast(BF16)
                    for d4 in range(4):
                        S.op("pe", lambda e, d4=d4, p6v=p6v, yb=yb: e.transpose(
                            p6v[:, d4 * 128:(d4 + 1) * 128], yb[:, d4 * 128:(d4 + 1) * 128], ident_b[:, :]),
                            r=[ybk, "ident_b"], w=[p6k])
                    S.op("act", lambda e, p6v=p6v, bi=bi, tsl=tsl: e.activation(
                        out=yTb[bi][:, :, tsl], in_=p6v.rearrange("p (d t) -> p d t", d=4, t=128), func=AF.Identity),
                        r=[p6k], w=[("yTb", bi, cc)])
                S.op("sp", lambda e, bi=bi, t0=t0, h=h: e.dma_start(
                    out=oT_d[h * 512:(h + 1) * 512, t0:t0 + 512].rearrange("(d p) t -> p d t", p=128), in_=yTb[bi][:, :, :]),
                    r=[("yTb", bi, cc) for cc in range(4)], w=[("oT",)], dma=True)
        S.barrier()

    x_src = xT_in
    for l in range(depth):
        phase_inproj(l, x_src)
        S.barrier()
        if l % 2 == 1:
            phase_fox(l)
        else:
            phase_mlstm(l)
        phase_post(l, x_src, last=(l == depth - 1))
        S.barrier()
        x_src = xs_d
    S.op("sp", None)
    S.emit()
    return nc, stack


def _consts():
    c = np.zeros((128, 512 + 2048), np.float32)
    c[:, 0:128] = np.eye(128, dtype=np.float32)
    s = np.arange(128)
    c[:, 128:256] = (s[:, None] <= s[None, :]).astype(np.float32)
    c[:, 256:384] = np.where(s[None, :] <= s[:, None], 0.0, NEG)
    for h in range(16):
        c[h, 512 + h * 128:512 + (h + 1) * 128] = 1.0
    return c


def _col(v):
    v = np.asarray(v, np.float32)
    return np.ascontiguousarray(v.reshape(-1, 128).T)


def make_in_map(b, T, depth, x, c, ada_w, ada_b, norm_mix_w, norm_ffn_w, mlstm_w_in, mlstm_b_gates,
                mlstm_norm_w, mlstm_w_out, fox_w_in, fox_b_f, fox_w_out, ffn_w_gate_up, ffn_w_down, final_norm_w):
    n_ml = (depth + 1) // 2
    n_fx = depth // 2
    m = {}
    m["xT"] = np.ascontiguousarray(np.asarray(x[b], np.float32).T)
    m["cT"] = _col(c[b])
    m["ada_w"] = np.ascontiguousarray(ada_w[:depth], dtype=np.float32)
    m["ada_bT"] = np.concatenate([_col(ada_b[l]) for l in range(depth)], axis=1)
    m["nmw"] = np.concatenate([_col(norm_mix_w[l]) for l in range(depth)], axis=1)
    m["nfw"] = np.concatenate([_col(norm_ffn_w[l]) for l in range(depth)], axis=1)
    m["fnw"] = _col(final_norm_w)
    m["mw_in"] = np.ascontiguousarray(mlstm_w_in[:max(n_ml, 1)], dtype=np.float32)
    bgs = np.asarray(mlstm_b_gates, np.float32)[:max(n_ml, 1)]
    m["mbg"] = np.ascontiguousarray(np.concatenate([np.stack([bg[0:4], bg[4:8]], axis=1) for bg in bgs], axis=1))
    m["mnw"] = np.ascontiguousarray(np.concatenate(
        [np.broadcast_to(np.asarray(w, np.float32)[None, :], (128, 2048)) for w in mlstm_norm_w[:max(n_ml, 1)]], axis=1))
    m["mw_out"] = np.ascontiguousarray(mlstm_w_out[:max(n_ml, 1)], dtype=np.float32)
    m["fw_in"] = np.ascontiguousarray(fox_w_in[:max(n_fx, 1)], dtype=np.float32)
    m["fbf"] = np.ascontiguousarray(np.asarray(fox_b_f, np.float32)[:max(n_fx, 1)].reshape(-1, 1))
    m["fw_out"] = np.ascontiguousarray(fox_w_out[:max(n_fx, 1)], dtype=np.float32)
    m["w_gu"] = np.ascontiguousarray(ffn_w_gate_up[:depth], dtype=np.float32)
    m["w_dn"] = np.ascontiguousarray(ffn_w_down[:depth], dtype=np.float32)
    m["consts"] = _consts()
    return m


_CACHE = {}


def kernel(**inputs):
    x = np.asarray(inputs["x"])
    Bn, T, _ = x.shape
    depth = np.asarray(inputs["ada_w"]).shape[0]
    key = (T, depth)
    if key not in _CACHE:
        _CACHE[key] = build(T, depth)
    nc, _stack = _CACHE[key]
    arrs = {k: np.asarray(v) for k, v in inputs.items()}
    in_maps = [make_in_map(b, T, depth, **arrs) for b in range(Bn)]
    res = run_bass_kernel_spmd(nc, in_maps, core_ids=list(range(Bn)))
    out = np.stack([np.ascontiguousarray(r["outT"].T) for r in res.results], axis=0)
    return out.astype(np.float32)
```

```python
from contextlib import ExitStack
import numpy as np
import ml_dtypes
import concourse.bass as bass
import concourse.mybir as mybir
from concourse.bass_utils import run_bass_kernel_spmd

F32 = mybir.dt.float32
BF16 = mybir.dt.bfloat16
AF = mybir.ActivationFunctionType
ALU = mybir.AluOpType
AX = mybir.AxisListType

D = 2048
NC_ = 16
DFF = 5632
NFC = 44
EPS = 1e-6
TT = 512
NEG = -30000.0
R_DMA = 8


class LoopVar:
    cur = None


LV = LoopVar()


def TS(idx, size, off=0):
    if isinstance(idx, LoopVar):
        if isinstance(idx.cur, int):
            return slice(idx.cur * size + off, idx.cur * size + off + size)
        assert off == 0
        return bass.ts(idx.cur, size)
    return slice(idx * size + off, idx * size + off + size)


def DS(idx, stride, off, size):
    if isinstance(idx, LoopVar):
        return bass.ds(idx.cur * stride + off, size)
    return slice(idx * stride + off, idx * stride + off + size)


class Ring:
    registry = []

    def __init__(self, items):
        self.items = items
        self.i = 0
        Ring.registry.append(self)

    def next(self):
        x = self.items[self.i % len(self.items)]
        self.i += 1
        return x


ENGS = ["pe", "act", "dve", "pool", "sp"]
QUEUES = ["sp", "pool", "act"]


class Sch:
    ARENA = 188 * 1024

    def __init__(self, nc, stack):
        self.nc = nc
        self.stack = stack
        self.ops = []
        self.lw = {}
        self.lr = {}
        self.barrier_deps = set()
        self.last_on = {}
        self.dma_hist = {q: [] for q in QUEUES}
        self.n_t = 0
        self.regions = []
        self._after_barrier = set(ENGS)
        Ring.registry.clear()

    def sb(self, shape, dt, name=None):
        if not hasattr(self, "arena"):
            self.arena = self.stack.enter_context(self.nc.sbuf_tensor("arena", [128, self.ARENA], mybir.dt.uint8))
            self.top = 0
        esz = 4 if dt == F32 else 2
        n = 1
        for d in shape[1:]:
            n *= d
        nbytes = (n * esz + 63) // 64 * 64
        off = self.top
        self.top += nbytes
        assert self.top <= self.ARENA, f"SBUF arena overflow: {self.top} ({name})"
        ap = self.arena[0:shape[0], off:off + n * esz].bitcast(dt)
        if len(shape) == 3:
            ap = ap.rearrange("p (a b) -> p a b", a=shape[1], b=shape[2])
        return ap

    def mark(self):
        return self.top

    def reset(self, m):
        self.top = m

    def ps(self, shape, dt=F32, name=None):
        self.n_t += 1
        return self.stack.enter_context(self.nc.psum_tensor(name or f"p{self.n_t}", list(shape), dt))

    def op(self, eng, fn, r=(), w=(), dma=False):
        idx = len(self.ops)
        deps = set(self.barrier_deps) if eng not in self._after_barrier else set()
        self._after_barrier.add(eng)
        for k in r:
            x = self.lw.get(k)
            if x is not None:
                deps.add(x)
        for k in w:
            x = self.lw.get(k)
            if x is not None:
                deps.add(x)
            for y in self.lr.get(k, ()):
                deps.add(y)
        for k in r:
            self.lr.setdefault(k, []).append(idx)
        for k in w:
            self.lw[k] = idx
            self.lr[k] = []
        deps.discard(idx)
        self.ops.append(dict(eng=eng, fn=fn, deps=deps, dma=dma))
        if dma:
            self.dma_hist[eng].append(idx)
        else:
            self.last_on[eng] = idx
        return idx

    def barrier(self):
        deps = set(self.last_on.values())
        for q, h in self.dma_hist.items():
            deps.update(h[-R_DMA:])
        self.barrier_deps = deps
        self._after_barrier = set()

    def loop(self, N, body, reset_fn=None):
        if N == 1:
            body(0)
            return
        self.barrier()
        reg = dict(N=N, s0=len(self.ops), bdeps=set(self.barrier_deps))
        for rg in Ring.registry:
            rg.i = 0
        if reset_fn:
            reset_fn()
        body(LV)
        reg["s1"] = len(self.ops)
        for rg in Ring.registry:
            rg.i = 0
        if reset_fn:
            reset_fn()
        body(LV)
        reg["s2"] = len(self.ops)
        assert reg["s2"] - reg["s1"] == reg["s1"] - reg["s0"], "loop body not iteration-invariant"
        for a, b in zip(range(reg["s0"], reg["s1"]), range(reg["s1"], reg["s2"])):
            assert self.ops[a]["eng"] == self.ops[b]["eng"] and self.ops[a]["dma"] == self.ops[b]["dma"]
        self.regions.append(reg)
        self.barrier()

    def emit(self):
        nc = self.nc
        ops = self.ops
        n = len(ops)
        reg_of = [None] * n
        copy_of = [0] * n
        for reg in self.regions:
            for i in range(reg["s0"], reg["s1"]):
                reg_of[i] = reg
                copy_of[i] = 1
            for i in range(reg["s1"], reg["s2"]):
                reg_of[i] = reg
                copy_of[i] = 2

        def twin(i):
            reg = reg_of[i]
            return i + (reg["s1"] - reg["s0"]) if copy_of[i] == 1 else i

        def pe_pe(d, i):
            return ops[d]["eng"] == "pe" and ops[i]["eng"] == "pe" and not ops[d]["dma"] and not ops[i]["dma"]

        sig = [False] * n
        for i, o in enumerate(ops):
            for d in o["deps"]:
                if pe_pe(d, i):
                    continue
                sig[twin(d)] = True
        cnt = {e: 0 for e in ENGS}
        rr = {q: 0 for q in QUEUES}
        tot = {q: [0] * R_DMA for q in QUEUES}
        i = 0
        while i < n:
            reg = reg_of[i]
            if reg is None:
                o = ops[i]
                e = o["eng"]
                if o["dma"]:
                    s = rr[e] % R_DMA
                    rr[e] += 1
                    tot[e][s] += 1
                    o["slot"] = s
                    o["c"], o["k"] = 16 * tot[e][s], 0
                elif sig[i]:
                    cnt[e] += 1
                    o["c"], o["k"] = cnt[e], 0
                i += 1
                continue
            N = reg["N"]
            body = range(reg["s1"], reg["s2"])
            delta = {e: 0 for e in ENGS}
            for j in body:
                if not ops[j]["dma"] and sig[j]:
                    delta[ops[j]["eng"]] += 1
            run_c = {e: 0 for e in ENGS}
            rr0 = dict(rr)
            cslot = {q: [0] * R_DMA for q in QUEUES}
            for j in body:
                if ops[j]["dma"]:
                    q = ops[j]["eng"]
                    s = rr0[q] % R_DMA
                    rr0[q] += 1
                    ops[j]["slot"] = s
                    ops[j]["m"] = cslot[q][s]
                    cslot[q][s] += 1
            for j in body:
                o = ops[j]
                e = o["eng"]
                if o["dma"]:
                    s = o["slot"]
                    o["c"] = 16 * (tot[e][s] + o["m"] + 1)
                    o["k"] = 16 * cslot[e][s]
                elif sig[j]:
                    run_c[e] += 1
                    o["c"], o["k"] = cnt[e] + run_c[e], delta[e]
            for e in ENGS:
                cnt[e] += N * delta[e]
            for q in QUEUES:
                for s in range(R_DMA):
                    tot[q][s] += N * cslot[q][s]
            reg["cslot"] = cslot
            off = reg["s1"] - reg["s0"]
            for j0 in range(reg["s0"], reg["s1"]):
                t = ops[j0 + off]
                if "c" in t:
                    ops[j0]["c"], ops[j0]["k"] = t["c"], 0
                if "slot" in t:
                    ops[j0]["slot"] = t["slot"]
            i = reg["s2"]

        csem = {e: self.stack.enter_context(nc.semaphore(f"c_{e}")) for e in ENGS}
        dsem = {q: [self.stack.enter_context(nc.semaphore(f"d_{q}{s}")) for s in range(R_DMA)] for q in QUEUES}

        def semkey(p):
            return ("d", p["eng"], p["slot"]) if p["dma"] else ("c", p["eng"])

        def dep_target(d, i):
            t = twin(d)
            p = ops[t]
            key = semkey(p)
            if reg_of[i] is not None and reg_of[i] is reg_of[d]:
                if copy_of[i] == 1:
                    return key, p["c"], 0
                if copy_of[d] == 1:
                    return key, p["c"] - p["k"], p["k"]
                return key, p["c"], p["k"]
            if reg_of[d] is not None:
                return key, p["c"] + (reg_of[d]["N"] - 1) * p["k"], 0
            return key, p["c"], 0

        tmpregs = {}
        itregs = {}

        def emit_waits(eobj, waits, waited, it):
            for (key, k), c in waits.items():
                prev = waited.get((key, k))
                if prev is not None and prev >= c:
                    continue
                waited[(key, k)] = c
                sem = csem[key[1]] if key[0] == "c" else dsem[key[1]][key[2]]
                if k == 0 or it is None:
                    eobj.wait_ge(sem, c)
                else:
                    rg = tmpregs.get(id(eobj))
                    if rg is None:
                        rg = eobj.alloc_register()
                        tmpregs[id(eobj)] = rg
                    itr = waited.get("__itr")
                    if itr is None:
                        itr = eobj.to_reg(it)
                        waited["__itr"] = itr
                    eobj.reg_mul(rg, itr, k)
                    eobj.reg_add(rg, rg, c)
                    eobj.wait_ge(sem, rg)

        def collect(i, ename):
            o = ops[i]
            waits = {}
            for d in o["deps"]:
                if pe_pe(d, i):
                    continue
                key, c, k = dep_target(d, i)
                if waits.get((key, k), -10 ** 9) < c:
                    waits[(key, k)] = c
            if o["dma"]:
                key = ("d", ename, o["slot"])
                c, k = o["c"] - 16, o["k"]
                if copy_of[i] == 1:
                    k = 0
                if c > 0 or k > 0:
                    if waits.get((key, k), -10 ** 9) < c:
                        waits[(key, k)] = c
            return waits

        def emit_op(i, ename, eobj, waited, it):
            o = ops[i]
            emit_waits(eobj, collect(i, ename), waited, it)
            if o["fn"] is None:
                return
            ins = o["fn"](eobj)
            if o["dma"]:
                ins.then_inc(dsem[ename][o["slot"]], 16)
            elif sig[twin(i)]:
                ins.then_inc(csem[ename], 1)

        def run(ename, eobj):
            waited = {}
            i = 0
            while i < n:
                reg = reg_of[i]
                if reg is None:
                    if ops[i]["eng"] == ename:
                        emit_op(i, ename, eobj, waited, None)
                    i += 1
                    continue
                LV.cur = 0
                for j in range(reg["s0"], reg["s1"]):
                    if ops[j]["eng"] == ename:
                        emit_op(j, ename, eobj, waited, None)
                LV.cur = None
                mine = [j for j in range(reg["s1"], reg["s2"]) if ops[j]["eng"] == ename]
                if mine:
                    with eobj.Fori(1, reg["N"]) as iv:
                        LV.cur = iv
                        w2 = {}
                        for j in mine:
                            emit_op(j, ename, eobj, w2, iv)
                    LV.cur = None
                i = reg["s2"]

        with nc.Block() as block:
            @block.tensor
            def _(e):
                run("pe", e)

            @block.scalar
            def _(e):
                run("act", e)

            @block.vector
            def _(e):
                run("dve", e)

            @block.gpsimd
            def _(e):
                run("pool", e)

            @block.sync
            def _(e):
                run("sp", e)


def build(T, depth, dbg=False):
    NT = T // TT
    NQ = T // 128
    n_ml = (depth + 1) // 2
    n_fx = depth // 2
    nc = bass.Bass("TRN2", target_bir_lowering=False)
    stack = ExitStack()
    S = Sch(nc, stack)

    def dram(name, shape, dt, kind="Internal"):
        return nc.dram_tensor(name, list(shape), dt, kind=kind).ap()

    xT_in = dram("xT", [D, T], F32, "ExternalInput")
    cT = dram("cT", [128, 16], F32, "ExternalInput")
    ada_w = dram("ada_w", [depth, D, 6 * D], F32, "ExternalInput")
    ada_bT = dram("ada_bT", [128, depth * 96], F32, "ExternalInput")
    nmw = dram("nmw", [128, depth * 16], F32, "ExternalInput")
    nfw = dram("nfw", [128, depth * 16], F32, "ExternalInput")
    fnw = dram("fnw", [128, 16], F32, "ExternalInput")
    mw_in = dram("mw_in", [max(n_ml, 1), D, 6152], F32, "ExternalInput")
    mbg = dram("mbg", [4, max(n_ml, 1) * 2], F32, "ExternalInput")
    mnw = dram("mnw", [128, max(n_ml, 1) * 2048], F32, "ExternalInput")
    mw_out = dram("mw_out", [max(n_ml, 1), D, D], F32, "ExternalInput")
    fw_in = dram("fw_in", [max(n_fx, 1), D, 6160], F32, "ExternalInput")
    fbf = dram("fbf", [16 * max(n_fx, 1), 1], F32, "ExternalInput")
    fw_out = dram("fw_out", [max(n_fx, 1), D, D], F32, "ExternalInput")
    w_gu = dram("w_gu", [depth, D, 2 * DFF], F32, "ExternalInput")
    w_dn = dram("w_dn", [depth, DFF, D], F32, "ExternalInput")
    consts = dram("consts", [128, 128 * 4 + 16 * 128], F32, "ExternalInput")
    outT = dram("outT", [D, T], F32, "ExternalOutput")

    xs_d = dram("xs_d", [D, T], F32)
    qkT_d = dram("qkT_d", [2 * D, T], BF16)
    tm_d = dram("tm_d", [T, 2048], BF16)
    tmb = [dram(f"tmb{i}", [T, 512], BF16) for i in range(10)]
    gate_d = dram("gate_d", [16, T], F32)
    gate2_d = dram("gate2_d", [4, T], F32)
    oT_d = dram("oT_d", [D, T], BF16)
    ncum_d = dram("ncum_d", [16, T], F32)
    if dbg:
        dbg_h = dram("dbg_h", [D, T], F32, "ExternalOutput")

    ident_f = S.sb([128, 128], F32, "ident_f")
    ident_b = S.sb([128, 128], BF16, "ident_b")
    mask01 = S.sb([128, 128], F32, "mask01")
    maskneg = S.sb([128, 128], F32, "maskneg")
    onesm = S.sb([128, 128], BF16, "onesm")
    ones1 = S.sb([128, 128], BF16, "ones1")
    sel16 = S.sb([16, 4 * 128], F32, "sel16")
    modsb = S.sb([128, depth * 96], F32, "modsb")
    gam1 = S.sb([128, depth * 16], F32, "gam1")
    gam2 = S.sb([128, depth * 16], F32, "gam2")
    nmw_s = S.sb([128, depth * 16], F32, "nmw_s")
    nfw_s = S.sb([128, depth * 16], F32, "nfw_s")
    fnw_s = S.sb([128, 16], F32, "fnw_s")
    zero_c = S.sb([128, 16], F32, "zero_c")
    epsc = S.sb([128, 1], F32, "epsc")
    onec = S.sb([128, 1], F32, "onec")

    S.op("pool", lambda e: e.dma_start(out=ident_f[:], in_=consts[:, 0:128]), w=["ident_f"], dma=True)
    S.op("pool", lambda e: e.dma_start(out=ident_b[:], in_=consts[:, 0:128]), w=["ident_b"], dma=True)
    S.op("pool", lambda e: e.dma_start(out=mask01[:], in_=consts[:, 128:256]), w=["mask01"], dma=True)
    S.op("pool", lambda e: e.dma_start(out=maskneg[:], in_=consts[:, 256:384]), w=["maskneg"], dma=True)
    S.op("pool", lambda e: e.dma_start(out=sel16[:], in_=consts[0:16, 512:512 + 512]), w=["sel16"], dma=True)
    S.op("pool", lambda e: e.dma_start(out=nmw_s[:], in_=nmw[:, :]), w=["nmw_s"], dma=True)
    S.op("pool", lambda e: e.dma_start(out=nfw_s[:], in_=nfw[:, :]), w=["nfw_s"], dma=True)
    S.op("pool", lambda e: e.dma_start(out=fnw_s[:], in_=fnw[:, :]), w=["fnw_s"], dma=True)
    S.op("dve", lambda e: e.memset(onesm[:], 1.0 / D), w=["onesm"])
    S.op("dve", lambda e: e.memset(ones1[:], 1.0), w=["ones1"])
    S.op("dve", lambda e: e.memset(zero_c[:], 0.0), w=["zero_c"])
    S.op("dve", lambda e: e.memset(epsc[:], EPS), w=["epsc"])
    S.op("dve", lambda e: e.memset(onec[:], 1.0), w=["onec"])

    wblk = {}

    class WRef:
        def __init__(self, name, i):
            self.name, self.i = name, i

    class WMat:
        def __init__(self, name):
            self.name = name

        def __getitem__(self, i):
            return WRef(self.name, i)

    def conv(name, src, n, blocks, slot):
        dst = dram(name, [n * len(blocks), 128, slot], BF16)
        for i in range(n):
            for bi_, (k0, nk, c0, ncb_) in enumerate(blocks):
                d = dst[i * len(blocks) + bi_, :, 0:nk * ncb_]
                wblk[(name, i, k0, c0)] = (d, nk, ncb_)
                S.op("pool", lambda e, i=i, k0=k0, nk=nk, c0=c0, ncb_=ncb_, d=d: e.dma_start(
                    out=d.rearrange("p (k c) -> p k c", k=nk, c=ncb_),
                    in_=src[i, k0 * 128:(k0 + nk) * 128, c0:c0 + ncb_].rearrange("(k p) c -> p k c", p=128)),
                    w=[("wconv", name, i)], dma=True)
        return WMat(name)

    in_blocks = [(0, 16, c0, 512) for c0 in range(0, 6144, 512)]
    mw_in = conv("mw_in_b", mw_in, max(n_ml, 1), in_blocks + [(0, 16, 6144, 8)], 16 * 512)
    fw_in = conv("fw_in_b", fw_in, max(n_fx, 1), in_blocks + [(0, 16, 6144, 16)], 16 * 512)
    out_blocks = [(0, 16, c0, 512) for c0 in range(0, D, 512)]
    mw_out = conv("mw_out_b", mw_out, max(n_ml, 1), out_blocks, 16 * 512)
    fw_out = conv("fw_out_b", fw_out, max(n_fx, 1), out_blocks, 16 * 512)
    gu_blocks = [(0, 16, c0, 256) for c0 in range(0, 2 * DFF, 256)]
    w_gu = conv("w_gu_b", w_gu, depth, gu_blocks, 16 * 256)
    dn_blocks = [(kg * 16, (16 if kg < 2 else 12), cb * 512, 512) for kg in range(3) for cb in range(4)]
    w_dn = conv("w_dn_b", w_dn, depth, dn_blocks, 16 * 512)
    S.barrier()

    m0 = S.mark()
    NWB = 3
    wbufs = [S.sb([128, 16, 512], BF16, f"wb{i}") for i in range(NWB)]
    wstate = dict(n=0)

    def wload(src2d, k0, nk, c0, ncols, dcol=0):
        i = wstate["n"] % NWB
        wstate["n"] += 1
        b = wbufs[i]
        if isinstance(src2d, WRef):
            d, nk_, nc_ = wblk[(src2d.name, src2d.i, k0, c0)]
            assert nk_ == nk and nc_ == ncols, (src2d.name, k0, c0, nk, ncols)
            src = d.rearrange("p (k c) -> p k c", k=nk, c=ncols)
        else:
            src = src2d[k0 * 128:(k0 + nk) * 128, c0:c0 + ncols].rearrange("(k p) c -> p k c", p=128)
        S.op("pool", lambda e: e.dma_start(out=b[:, 0:nk, dcol:dcol + ncols], in_=src), w=[("wb", i)], dma=True)
        return b, ("wb", i)

    def wload2(src2d, k0, nk, c0, c1, ncols):
        i = wstate["n"] % NWB
        wstate["n"] += 1
        b = wbufs[i]
        s0 = src2d[k0 * 128:(k0 + nk) * 128, c0:c0 + ncols].rearrange("(k p) c -> p k c", p=128)
        s1 = src2d[k0 * 128:(k0 + nk) * 128, c1:c1 + ncols].rearrange("(k p) c -> p k c", p=128)
        S.op("pool", lambda e: e.dma_start(out=b[:, 0:nk, 0:ncols], in_=s0), w=[("wb", i)], dma=True)
        S.op("pool", lambda e: e.dma_start(out=b[:, 0:nk, ncols:2 * ncols], in_=s1), w=[("wb", i, 1)], r=[("wb", i)], dma=True)
        return b, ("wb", i, 1)

    def reset_w():
        wstate["n"] = 0

    def wstream(blocks, consume, la=NWB - 1):
        issued = []
        for j in range(len(blocks)):
            while len(issued) < min(len(blocks), j + la + 1):
                issued.append(blocks[len(issued)]())
            b, key = issued[j]
            consume(j, b, key)

    psb = [S.ps([128, 512], F32, f"psb{i}") for i in range(8)]
    psring = Ring(list(range(8)))

    def psum():
        i = psring.next()
        return psb[i], ("ps", i)

    cs32 = S.sb([128, 16], F32, "cs32")
    csb = S.sb([128, 16], BF16, "csb")
    adab_s = S.sb([128, depth * 96], F32, "adab_s")
    S.op("pool", lambda e: e.dma_start(out=cs32[:], in_=cT[:, :]), w=["cs32"], dma=True)
    S.op("pool", lambda e: e.dma_start(out=adab_s[:], in_=ada_bT[:, :]), w=["adab_s"], dma=True)
    S.op("act", lambda e: e.activation(out=csb[:], in_=cs32[:], func=AF.Silu), r=["cs32"], w=["csb"])
    for l in range(depth):
        pm, pmk = psum()
        blocks = [(lambda l=l, bi=bi: wload(ada_w[l], 0, 16, bi * 512, 512)) for bi in range(24)]

        def cons(j, b, key, pm=pm, pmk=pmk):
            for j4 in range(4):
                col = j * 4 + j4
                for kc in range(16):
                    S.op("pe", lambda e, b=b, kc=kc, j4=j4, col=col: e.matmul(
                        pm[:, col:col + 1], b[:, kc, j4 * 128:(j4 + 1) * 128], csb[:, kc:kc + 1],
                        start=(kc == 0), stop=(kc == 15)), r=[key, "csb"], w=[pmk])
        wstream(blocks, cons)
        S.op("dve", lambda e, l=l, pm=pm: e.tensor_tensor(out=modsb[:, l * 96:(l + 1) * 96], in0=pm[:, 0:96],
                                                       in1=adab_s[:, l * 96:(l + 1) * 96], op=ALU.add),
             r=[pmk, "adab_s"], w=[("mod", l)])
        S.op("dve", lambda e, l=l: e.scalar_tensor_tensor(out=gam1[:, l * 16:(l + 1) * 16],
                                                         in0=modsb[:, l * 96 + 16:l * 96 + 32], scalar=1.0,
                                                         in1=nmw_s[:, l * 16:(l + 1) * 16], op0=ALU.add, op1=ALU.mult),
             r=[("mod", l), "nmw_s"], w=[("gam1", l)])
        S.op("dve", lambda e, l=l: e.scalar_tensor_tensor(out=gam2[:, l * 16:(l + 1) * 16],
                                                         in0=modsb[:, l * 96 + 64:l * 96 + 80], scalar=1.0,
                                                         in1=nfw_s[:, l * 16:(l + 1) * 16], op0=ALU.add, op1=ALU.mult),
             r=[("mod", l), "nfw_s"], w=[("gam2", l)])

    def modcol(l, which, c):
        base = l * 96 + which * 16 + c
        return modsb[:, base:base + 1]

    xs = S.sb([128, 16, TT], F32, "xs")
    hb = S.sb([128, 16, TT], BF16, "hb")
    act = S.sb([128, NFC, TT], BF16, "act")
    rstd = S.sb([128, TT], F32, "rstd")
    tmpr = Ring([(S.sb([128, TT], F32, f"tmp{i}"), ("tmp", i)) for i in range(3)])
    vst3 = S.sb([128, 4, 2048], BF16, "vst3")
    stgf = Ring([(S.sb([128, TT], F32, f"stgf{i}"), ("stgf", i)) for i in range(2)])

    def norm_tile(gam_ap, sh_ap, l, rdeps, out_fn=None):
        for g in range(4):
            S.op("act", lambda e, g=g: e.activation(out=act[:, 4 * g:4 * g + 4, :], in_=xs[:, 4 * g:4 * g + 4, :],
                                                   func=AF.Square),
                 r=[("xs", c) for c in range(4 * g, 4 * g + 4)], w=[("act", c) for c in range(4 * g, 4 * g + 4)])
        pss, pssk = psum()
        for c in range(16):
            S.op("pe", lambda e, c=c: e.matmul(pss[:, :], onesm[:, :], act[:, c, :], start=(c == 0), stop=(c == 15)),
                 r=[("act", c), "onesm"], w=[pssk])
        S.op("act", lambda e: e.activation(out=rstd[:], in_=pss[:, :], func=AF.Ln, bias=epsc[:, 0:1], scale=1.0),
             r=[pssk, "epsc"], w=["rstd"])
        S.op("act", lambda e: e.activation(out=rstd[:], in_=rstd[:], func=AF.Exp, scale=-0.5), r=["rstd"], w=["rstd"])
        for c in range(16):
            t, tk = tmpr.next()
            S.op("dve", lambda e, c=c, t=t: e.tensor_tensor(out=t[:], in0=xs[:, c, :], in1=rstd[:], op=ALU.mult),
                 r=[("xs", c), "rstd"], w=[tk])
            if out_fn is None:
                S.op("act", lambda e, c=c, t=t: e.activation(out=hb[:, c, :], in_=t[:], func=AF.Identity,
                                                            bias=sh_ap[:, c:c + 1], scale=gam_ap[:, c:c + 1]),
                     r=[tk] + rdeps, w=[("hb", c)])
            else:
                out_fn(c, t, tk)

    def load_x(src, tt):
        S.op("sp", lambda e: e.dma_start(out=xs[:, :, :],
                                         in_=src[:, TS(tt, TT)].rearrange("(c p) t -> p c t", p=128)),
             r=[("xd",)], w=[("xs", c) for c in range(16)], dma=True)

    evac_rr = Ring(["act", "dve"])

    def evac_bf16(ps_ap, psk, dst_ap, dstk, scale=None, func=None, extra_r=()):
        if func is not None:
            S.op("act", lambda e: e.activation(out=dst_ap, in_=ps_ap, func=func), r=[psk] + list(extra_r), w=[dstk])
            return
        eng = evac_rr.next()
        if eng == "act":
            S.op("act", lambda e: e.activation(out=dst_ap, in_=ps_ap, func=AF.Identity,
                                               scale=(1.0 if scale is None else scale)), r=[psk] + list(extra_r), w=[dstk])
        else:
            if scale is None:
                S.op("dve", lambda e: e.tensor_copy(out=dst_ap, in_=ps_ap), r=[psk] + list(extra_r), w=[dstk])
            else:
                S.op("dve", lambda e: e.tensor_scalar(out=dst_ap, in0=ps_ap, scalar1=scale, scalar2=None, op0=ALU.mult),
                     r=[psk] + list(extra_r), w=[dstk])

    def phase_inproj(l, x_src):
        mixer = l % 2
        j = l // 2
        if mixer == 0:
            W = mw_in[j]
            fm_blocks = 4
            qscale = 256 ** -0.5
            nq_chunks = 8
            tm_c0, tm_blocks = 1024, 10
        else:
            W = fw_in[j]
            fm_blocks = 8
            qscale = 128 ** -0.5
            nq_chunks = 16
            tm_c0, tm_blocks = 4096, 4
        def body(tt):
            load_x(x_src, tt)
            norm_tile(gam1[:, l * 16:(l + 1) * 16], modsb[:, l * 96:l * 96 + 16], l, [("gam1", l), ("mod", l)])
            blocks = []
            for bi in range(fm_blocks):
                blocks.append(lambda bi=bi: wload(W, 0, 16, bi * 512, 512))
            for bi in range(tm_blocks):
                blocks.append(lambda bi=bi: wload(W, 0, 16, tm_c0 + bi * 512, 512))
            blocks.append(lambda: wload(W, 0, 16, 6144, 8 if mixer == 0 else 16))

            def cons(jb, b, key):
                if jb < fm_blocks:
                    for oc4 in range(4):
                        oc = jb * 4 + oc4
                        p, pk = psum()
                        for kc in range(16):
                            S.op("pe", lambda e, p=p, b=b, kc=kc, oc4=oc4: e.matmul(
                                p[:, :], b[:, kc, oc4 * 128:(oc4 + 1) * 128], hb[:, kc, :],
                                start=(kc == 0), stop=(kc == 15)), r=[key, ("hb", kc)], w=[pk])
                        evac_bf16(p[:, :], pk, act[:, oc, :], ("act", oc), scale=(qscale if oc < nq_chunks else None))
                    if jb == fm_blocks - 1:
                        nfm = fm_blocks * 4
                        S.op("sp", lambda e: e.dma_start(
                            out=qkT_d.rearrange("(c p) t -> p c t", p=128)[:, 0:nfm, TS(tt, TT)], in_=act[:, 0:nfm, :]),
                            r=[("act", c) for c in range(nfm)], w=[("qkT", c) for c in range(32)], dma=True)
                elif jb < fm_blocks + tm_blocks:
                    nb = jb - fm_blocks
                    for m in range(4):
                        p, pk = psum()
                        for kc in range(16):
                            S.op("pe", lambda e, p=p, b=b, kc=kc, m=m: e.matmul(
                                p[:, :], hb[:, kc, m * 128:(m + 1) * 128], b[:, kc, :],
                                start=(kc == 0), stop=(kc == 15)), r=[key, ("hb", kc)], w=[pk])
                        is_o = (mixer == 0 and nb >= 6)
                        s4 = nb % 4
                        evac_bf16(p[:, :], pk, vst3[:, m, s4 * 512:(s4 + 1) * 512], ("vst", m, s4),
                                  func=(AF.Sigmoid if is_o else None))
                    if mixer == 0:
                        s4 = nb % 4
                        S.op("sp", lambda e, nb=nb, s4=s4: e.dma_start(
                            out=tmb[nb].rearrange("(a m p) c -> a p m c", m=4, p=128)[TS(tt, 1)]
                            .rearrange("a p m c -> (a p) m c"), in_=vst3[:, :, s4 * 512:(s4 + 1) * 512]),
                            r=[("vst", m, s4) for m in range(4)], w=[("tm",)], dma=True)
                    elif nb == tm_blocks - 1:
                        S.op("sp", lambda e: e.dma_start(
                            out=tm_d.rearrange("(a m p) c -> a p m c", m=4, p=128)[TS(tt, 1)]
                            .rearrange("a p m c -> (a p) m c"), in_=vst3[:, :, :]),
                            r=[("vst", m, x) for m in range(4) for x in range(4)], w=[("tm",)], dma=True)
                else:
                    if mixer == 0:
                        for gi in range(2):
                            p, pk = psum()
                            for kc in range(16):
                                S.op("pe", lambda e, p=p, b=b, kc=kc, gi=gi: e.matmul(
                                    p[0:4, :], b[:, kc, gi * 4:gi * 4 + 4], hb[:, kc, :],
                                    start=(kc == 0), stop=(kc == 15)), r=[key, ("hb", kc)], w=[pk])
                            sf, sfk = stgf.next()
                            S.op("dve", lambda e, p=p, sf=sf: e.tensor_copy(out=sf[0:4, :], in_=p[0:4, :]), r=[pk], w=[sfk])
                            dst = gate_d if gi == 0 else gate2_d
                            S.op("sp", lambda e, sf=sf, dst=dst: e.dma_start(
                                out=dst[0:4, TS(tt, TT)], in_=sf[0:4, :]), r=[sfk], w=[("gate", gi)], dma=True)
                    else:
                        p, pk = psum()
                        for kc in range(16):
                            S.op("pe", lambda e, p=p, b=b, kc=kc: e.matmul(
                                p[0:16, :], b[:, kc, 0:16], hb[:, kc, :],
                                start=(kc == 0), stop=(kc == 15)), r=[key, ("hb", kc)], w=[pk])
                        sf, sfk = stgf.next()
                        S.op("dve", lambda e, p=p, sf=sf: e.tensor_copy(out=sf[0:16, :], in_=p[0:16, :]), r=[pk], w=[sfk])
                        S.op("sp", lambda e, sf=sf: e.dma_start(
                            out=gate_d[0:16, TS(tt, TT)], in_=sf[0:16, :]), r=[sfk], w=[("gate", 0)], dma=True)
            wstream(blocks, cons)
        S.loop(NT, body, reset_w)

    def phase_post(l, x_src, last):
        mixer = l % 2
        j = l // 2
        Wo = mw_out[j] if mixer == 0 else fw_out[j]
        def fin_factory(tt):
            def fin(c, t, tk):
                S.op("act", lambda e, c=c, t=t: e.activation(out=xs[:, c, :], in_=t[:], func=AF.Identity,
                                                            scale=fnw_s[:, c:c + 1]), r=[tk, "fnw_s"], w=[("xs", c)])
                if c == 15:
                    S.op("sp", lambda e: e.dma_start(
                        out=outT[:, TS(tt, TT)].rearrange("(c p) t -> p c t", p=128), in_=xs[:, :, :]),
                        r=[("xs", x) for x in range(16)], w=[("out",)], dma=True)
            return fin

        def body(tt):
            load_x(x_src, tt)
            S.op("sp", lambda e: e.dma_start(
                out=act[:, 16:32, :], in_=oT_d[:, TS(tt, TT)].rearrange("(c p) t -> p c t", p=128)),
                r=[("oT",)], w=[("act", c) for c in range(16, 32)], dma=True)
            blocks = [(lambda ob=ob: wload(Wo, 0, 16, ob * 512, 512)) for ob in range(4)]

            def cons_o(ob, b, key):
                for oc4 in range(4):
                    oc = ob * 4 + oc4
                    p, pk = psum()
                    for kc in range(16):
                        S.op("pe", lambda e, p=p, b=b, kc=kc, oc4=oc4: e.matmul(
                            p[:, :], b[:, kc, oc4 * 128:(oc4 + 1) * 128], act[:, 16 + kc, :],
                            start=(kc == 0), stop=(kc == 15)), r=[key, ("act", 16 + kc)], w=[pk])
                    S.op("dve", lambda e, p=p, oc=oc: e.scalar_tensor_tensor(
                        out=xs[:, oc, :], in0=p[:, :], scalar=modcol(l, 2, oc), in1=xs[:, oc, :],
                        op0=ALU.mult, op1=ALU.add), r=[pk, ("mod", l), ("xs", oc)], w=[("xs", oc)])
            wstream(blocks, cons_o)
            import os as _os
            if _os.environ.get("DBG_SKIP_FFN"):
                norm_tile(None, None, l, [], out_fn=fin_factory(tt))
                return
            norm_tile(gam2[:, l * 16:(l + 1) * 16], modsb[:, l * 96 + 48:l * 96 + 64], l, [("gam2", l), ("mod", l)])
            blocks = []
            for fb in range(22):
                blocks.append(lambda fb=fb: wload(w_gu[l], 0, 16, fb * 256, 256))
                blocks.append(lambda fb=fb: wload(w_gu[l], 0, 16, DFF + fb * 256, 256))
            hold = {}

            def cons_gu(jb, b, key):
                if jb % 2 == 0:
                    hold["g"] = (b, key)
                    return
                fb = jb // 2
                gb, gk = hold["g"]
                ub, uk = b, key
                for f2 in range(2):
                    fc = fb * 2 + f2
                    pg, pgk = psum()
                    pu, puk = psum()
                    for kc in range(16):
                        S.op("pe", lambda e, pg=pg, gb=gb, kc=kc, f2=f2: e.matmul(
                            pg[:, :], gb[:, kc, f2 * 128:(f2 + 1) * 128], hb[:, kc, :],
                            start=(kc == 0), stop=(kc == 15)), r=[gk, ("hb", kc)], w=[pgk])
                    for kc in range(16):
                        S.op("pe", lambda e, pu=pu, ub=ub, kc=kc, f2=f2: e.matmul(
                            pu[:, :], ub[:, kc, f2 * 128:(f2 + 1) * 128], hb[:, kc, :],
                            start=(kc == 0), stop=(kc == 15)), r=[uk, ("hb", kc)], w=[puk])
                    t, tk = tmpr.next()
                    S.op("act", lambda e, pg=pg, t=t: e.activation(out=t[:], in_=pg[:, :], func=AF.Silu), r=[pgk], w=[tk])
                    S.op("dve", lambda e, pu=pu, t=t, fc=fc: e.tensor_tensor(out=act[:, fc, :], in0=t[:], in1=pu[:, :],
                                                                           op=ALU.mult), r=[tk, puk], w=[("act", fc)])
            wstream(blocks, cons_gu, la=1)
            for cb in range(4):
                accs = [psum() for _ in range(4)]
                blocks = [(lambda kg=kg, cb=cb: wload(w_dn[l], kg * 16, (16 if kg < 2 else 12), cb * 512, 512)) for kg in range(3)]

                def cons_d(kg, b, key, accs=accs, cb=cb):
                    nk = 16 if kg < 2 else 12
                    for oc4 in range(4):
                        p, pk = accs[oc4]
                        for kc in range(nk):
                            S.op("pe", lambda e, p=p, b=b, kc=kc, oc4=oc4, kg=kg, nk=nk: e.matmul(
                                p[:, :], b[:, kc, oc4 * 128:(oc4 + 1) * 128], act[:, kg * 16 + kc, :],
                                start=(kg == 0 and kc == 0), stop=(kg == 2 and kc == nk - 1)),
                                r=[key, ("act", kg * 16 + kc)], w=[pk])
                wstream(blocks, cons_d)
                for oc4 in range(4):
                    oc = cb * 4 + oc4
                    p, pk = accs[oc4]
                    S.op("dve", lambda e, p=p, oc=oc: e.scalar_tensor_tensor(
                        out=xs[:, oc, :], in0=p[:, :], scalar=modcol(l, 5, oc), in1=xs[:, oc, :],
                        op0=ALU.mult, op1=ALU.add), r=[pk, ("mod", l), ("xs", oc)], w=[("xs", oc)])
            if not last:
                S.op("sp", lambda e: e.dma_start(
                    out=xs_d[:, TS(tt, TT)].rearrange("(c p) t -> p c t", p=128), in_=xs[:, :, :]),
                    r=[("xs", c) for c in range(16)], w=[("xd",)], dma=True)
            else:
                norm_tile(None, None, l, [], out_fn=fin_factory(tt))
        S.loop(NT, body, reset_w)

    def phase_fox(l):
        j = l // 2
        if True:
            S.reset(m0)

            def sb2(shape, dt, name):
                return S.sb(shape, dt, name)

            rS, rT, rO, rM = Ring([0, 1]), Ring([2, 3]), Ring([4, 5]), Ring([6, 7])

            def bank(ring):
                i_ = ring.next()
                return psb[i_], ("ps", i_)
            fz = sb2([16, T], F32, "fz")
            nbf = sb2([16, 1], F32, "nbf")
            bfs = sb2([16, 1], F32, "bfs")
            ncb = sb2([128, T], F32, "ncb")
            qh = [sb2([128, T], BF16, f"qh{i}") for i in range(1)]
            kh = [sb2([128, T], BF16, f"kh{i}") for i in range(1)]
            vh = [sb2([128, NQ, 129], BF16, f"vh{i}") for i in range(1)]
            sq = sb2([128, T], BF16, "sqb")
            q2 = sb2([128, NQ], F32, "q2")
            km = sb2([128, 16], F32, "km")
            km1 = sb2([128, 1], F32, "km1")
            negm = sb2([128, NQ], F32, "negm")
            ssb = Ring([(sb2([128, 512], F32, f"ssb{i}"), ("ssb", i)) for i in range(3)])
            pbf = Ring([(sb2([128, 512], BF16, f"pbf{i}"), ("pbf", i)) for i in range(3)])
            ptb = Ring([(sb2([128, 512], BF16, f"ptb{i}"), ("ptb", i)) for i in range(3)])
            oq = Ring([(sb2([128, 128], BF16, f"oq{i}"), ("oq", i)) for i in range(2)])
            rinv = Ring([(sb2([128, 1], F32, f"rinv{i}"), ("rinv", i)) for i in range(2)])
            oTh = [sb2([128, T], BF16, f"oTh{i}") for i in range(1)]

            S.op("sp", lambda e: e.dma_start(out=fz[:, :], in_=gate_d[0:16, :]),
                 r=[("gate", 0)], w=["fz"], dma=True)
            S.op("sp", lambda e: e.dma_start(out=bfs[:, :], in_=fbf[j * 16:(j + 1) * 16, 0:1]), w=["bfs"], dma=True)
            S.op("dve", lambda e: e.tensor_scalar(out=nbf[:], in0=bfs[:], scalar1=-1.0, scalar2=None, op0=ALU.mult),
                 r=["bfs"], w=["nbf"])
            S.op("act", lambda e: e.activation(out=fz[:, :], in_=fz[:, :], func=AF.Exp, bias=nbf[:, 0:1], scale=-1.0),
                 r=["fz", "nbf"], w=["fz"])
            S.op("act", lambda e: e.activation(out=fz[:, :], in_=fz[:, :], func=AF.Ln, bias=onec[0:16, 0:1], scale=1.0),
                 r=["fz", "onec"], w=["fz"])
            S.op("dve", lambda e: e.memset(ncb[0:16, :], 1.0), w=["ncb"])
            SEG = 1024
            for sg in range(T // SEG):
                a0, a1 = sg * SEG, (sg + 1) * SEG
                S.op("dve", lambda e, a0=a0, a1=a1, sg=sg: e.tensor_tensor_scan(
                    out=fz[:, a0:a1], data0=ncb[0:16, a0:a1], data1=fz[:, a0:a1],
                    initial=(0.0 if sg == 0 else fz[:, a0 - 1:a0]), op0=ALU.mult, op1=ALU.add),
                    r=["ncb", "fz"], w=["fz"])
            S.op("sp", lambda e: e.dma_start(out=ncum_d[:, :], in_=fz[:, :]), r=["fz"], w=["ncum_d"], dma=True)
            nqt = sb2([NQ, 128], F32, "nqt")
            ncq = sb2([128, NQ], F32, "ncq")

            def load_head(h):
                i = 0
                S.op("act", lambda e: e.dma_start(out=qh[i][:, :], in_=qkT_d[TS(h, 128), :]),
                     r=[("qkT", oc) for oc in range(32)], w=[("qh", i)], dma=True)
                S.op("act", lambda e: e.dma_start(out=kh[i][:, :], in_=qkT_d[D:2 * D, :][TS(h, 128), :]),
                     r=[("qkT", oc) for oc in range(32)], w=[("kh", i)], dma=True)
                S.op("act", lambda e: e.dma_start(
                    out=vh[i][:, :, 0:128],
                    in_=tm_d[:, TS(h, 128)].rearrange("(j p) d -> p j d", p=128)),
                    r=[("tm",)], w=[("vh", i)], dma=True)
                S.op("pool", lambda e: e.memset(vh[i][:, :, 128:129], 1.0), r=[], w=[("vh1", i)])
                S.op("act", lambda e: e.dma_start(out=ncb[:, :], in_=ncum_d[TS(h, 1), :].to_broadcast([128, T])),
                     r=["ncum_d"], w=["ncb"], dma=True)
                S.op("act", lambda e: e.dma_start(out=nqt[:, :],
                                                 in_=ncum_d[TS(h, 1), :].rearrange("o (q p) -> (o q) p", p=128)),
                     r=["ncum_d"], w=["nqt"], dma=True)
                p, pk = bank(rM)
                S.op("pe", lambda e, p=p: e.transpose(p[:, 0:NQ], nqt[0:NQ, :], ident_f[0:NQ, 0:NQ]),
                     r=["nqt", "ident_f"], w=[pk])
                S.op("dve", lambda e, p=p: e.tensor_scalar(out=ncq[:, :], in0=p[:, 0:NQ], scalar1=-1.0, scalar2=None,
                                                          op0=ALU.mult), r=[pk], w=["ncq"])

            def head_body(h):
                i = 0
                load_head(h)
                S.op("act", lambda e, i=i: e.activation(out=sq[:, :], in_=kh[i][:, :], func=AF.Square),
                     r=[("kh", i)], w=["sq"])
                for t4 in range(T // 512):
                    p, pk = bank(rM)
                    S.op("pe", lambda e, p=p, t4=t4: e.matmul(p[:, :], ones1[:, :], sq[:, t4 * 512:(t4 + 1) * 512],
                                                             start=True, stop=True), r=["sq", "ones1"], w=[pk])
                    S.op("dve", lambda e, p=p, t4=t4: e.tensor_reduce(out=km[:, t4:t4 + 1], in_=p[:, :], axis=AX.X,
                                                                     op=ALU.max), r=[pk], w=[("km", t4)])
                S.op("dve", lambda e: e.tensor_reduce(out=km1[:, 0:1], in_=km[:, 0:T // 512], axis=AX.X, op=ALU.max),
                     r=[("km", t4) for t4 in range(T // 512)], w=["km1"])
                S.op("dve", lambda e: e.tensor_scalar(out=km1[:, 0:1], in0=km1[:, 0:1], scalar1=1.05, scalar2=None,
                                                      op0=ALU.mult), r=["km1"], w=["km1"])
                S.op("act", lambda e, i=i: e.activation(out=sq[:, :], in_=qh[i][:, :], func=AF.Square),
                     r=[("qh", i)], w=["sq"])
                p2, p2k = bank(rM)
                for qi in range(NQ):
                    S.op("pe", lambda e, qi=qi, p2=p2: e.matmul(p2[:, qi:qi + 1], sq[:, qi * 128:(qi + 1) * 128],
                                                               ones1[:, 0:1], start=True, stop=True),
                         r=["sq", "ones1"], w=[p2k])
                S.op("act", lambda e, p2=p2: e.activation(out=negm[:, :], in_=p2[:, 0:NQ], func=AF.Sqrt,
                                                         scale=km1[:, 0:1]), r=[p2k, "km1"], w=["negm"])
                S.op("dve", lambda e: e.tensor_scalar(out=negm[:, :], in0=negm[:, :], scalar1=-1.0, scalar2=None,
                                                      op0=ALU.mult), r=["negm"], w=["negm"])
                steps = []
                for qi in range(NQ):
                    ks = (qi + 1) * 128
                    nkt = (ks + 511) // 512
                    for kt in range(nkt):
                        w = min(512, ks - kt * 512)
                        steps.append(dict(qi=qi, kt=kt, w=w, first=(kt == 0), last=(kt == nkt - 1)))
                po = {}

                def st_qk(s):
                    p, pk = bank(rS)
                    s["ps"], s["psk"] = p, pk
                    qi, kt, w = s["qi"], s["kt"], s["w"]
                    S.op("pe", lambda e: e.matmul(p[:, 0:w], qh[i][:, qi * 128:(qi + 1) * 128],
                                                  kh[i][:, kt * 512:kt * 512 + w], start=True, stop=True),
                         r=[("qh", i), ("kh", i)], w=[pk])
                    sb_, sbk = ssb.next()
                    s["ssb"], s["ssbk"] = sb_, sbk
                    S.op("dve", lambda e: e.scalar_tensor_tensor(
                        out=sb_[:, 0:w], in0=p[:, 0:w], scalar=ncq[:, qi:qi + 1],
                        in1=ncb[:, kt * 512:kt * 512 + w], op0=ALU.add, op1=ALU.add),
                        r=[pk, "ncq", "ncb"], w=[sbk])
                    if s["last"]:
                        S.op("pool", lambda e: e.tensor_tensor(out=sb_[:, w - 128:w], in0=sb_[:, w - 128:w],
                                                               in1=maskneg[:, :], op=ALU.add),
                             r=[sbk, "maskneg"], w=[sbk])
                    pb, pbk = pbf.next()
                    s["pb"], s["pbk"] = pb, pbk
                    S.op("act", lambda e: e.activation(out=pb[:, 0:w], in_=sb_[:, 0:w], func=AF.Exp,
                                                       bias=negm[:, qi:qi + 1], scale=1.0),
                         r=[sbk, "negm"], w=[pbk])

                def st_tr(s):
                    p, pk = bank(rT)
                    w = s["w"]
                    pv = p.bitcast(BF16) if False else p
                    s["pt"], s["ptk"] = p, pk
                    pb = s["pb"]
                    ptv = p[:, 0:256].bitcast(BF16)
                    for jj in range(w // 128):
                        S.op("pe", lambda e, jj=jj: e.transpose(ptv[:, jj * 128:(jj + 1) * 128],
                                                                pb[:, jj * 128:(jj + 1) * 128], ident_b[:, :]),
                             r=[s["pbk"], "ident_b"], w=[pk])
                    tb, tbk = ptb.next()
                    s["tb"], s["tbk"] = tb, tbk
                    S.op("dve", lambda e: e.tensor_copy(out=tb[:, 0:w], in_=ptv[:, 0:w]), r=[pk], w=[tbk])

                def st_pv(s):
                    qi, kt, w = s["qi"], s["kt"], s["w"]
                    if s["first"]:
                        po["p"], po["k"] = bank(rO)
                    p, pk = po["p"], po["k"]
                    tb = s["tb"]
                    for jj in range(w // 128):
                        S.op("pe", lambda e, jj=jj: e.matmul(
                            p[:, 0:129], tb[:, jj * 128:(jj + 1) * 128], vh[i][:, kt * 4 + jj, 0:129],
                            start=(s["first"] and jj == 0), stop=(s["last"] and jj == w // 128 - 1)),
                            r=[s["tbk"], ("vh", i), ("vh1", i)], w=[pk])
                    if s["last"]:
                        ri, rik = rinv.next()
                        S.op("dve", lambda e: e.reciprocal(out=ri[:, 0:1], in_=p[:, 128:129]), r=[pk], w=[rik])
                        o_, ok = oq.next()
                        S.op("dve", lambda e: e.tensor_scalar(out=o_[:, :], in0=p[:, 0:128], scalar1=ri[:, 0:1],
                                                              scalar2=None, op0=ALU.mult), r=[pk, rik], w=[ok])
                        p3, p3k = bank(rM)
                        p3v = p3[:, 0:64].bitcast(BF16)
                        S.op("pe", lambda e: e.transpose(p3v[:, 0:128], o_[:, :], ident_b[:, :]),
                             r=[ok, "ident_b"], w=[p3k])
                        S.op("act", lambda e: e.activation(out=oTh[i][:, qi * 128:(qi + 1) * 128], in_=p3v[:, 0:128],
                                                           func=AF.Identity), r=[p3k], w=[("oTh", i, qi)])

                ns = len(steps)
                for n in range(ns + 3):
                    if n < ns:
                        st_qk(steps[n])
                    if 0 <= n - 1 < ns:
                        st_tr(steps[n - 1])
                    if 0 <= n - 3 < ns:
                        st_pv(steps[n - 3])
                S.op("act", lambda e, i=i: e.dma_start(out=oT_d[TS(h, 128), :], in_=oTh[i][:, :]),
                     r=[("oTh", i, qi) for qi in range(NQ)], w=[("oT",)], dma=True)
            S.loop(16, head_body)
            S.barrier()

    def phase_mlstm(l):
        j = l // 2
        S.reset(m0)
        A = S.sb([4, T], F32, "gA")
        B = S.sb([4, T], F32, "gB")
        bg = S.sb([4, 2], F32, "bg")
        bg15 = S.sb([4, 2], F32, "bg15")
        ref = S.sb([4, NQ], F32, "ref")
        Gn = S.sb([4, NQ], F32, "Gn")
        Gp = S.sb([4, NQ], F32, "Gp")
        r1 = S.sb([4, NQ], F32, "r1")
        eT = S.sb([128, NQ * 4], F32, "eT")
        flT = S.sb([128, NQ * 4], F32, "flT")
        r1b = S.sb([128, 4 * NQ], F32, "r1b")
        nwb = S.sb([128, 2048], F32, "nwb")
        Cs = S.sb([128, 2, 512], F32, "Cs")
        Cb = S.sb([128, 2, 512], BF16, "Cb")
        ns = S.sb([128, 2], F32, "ns")
        nb_ = S.sb([128, 2], BF16, "nb_")
        qTb = [S.sb([128, 2, 512], BF16, f"qTb{i}") for i in range(2)]
        kTb = [S.sb([128, 2, 512], BF16, f"kTb{i}") for i in range(2)]
        ktm = [S.sb([128, 4, 256], BF16, f"ktm{i}") for i in range(2)]
        vb = [S.sb([128, 4, 512], BF16, f"vb{i}") for i in range(2)]
        ob = [S.sb([128, 4, 512], BF16, f"ob{i}") for i in range(2)]
        yTb = [S.sb([128, 4, 512], BF16, f"yTb{i}") for i in range(2)]
        PT = Ring([(S.sb([128, 128], BF16, f"PT{i}"), ("PT", i)) for i in range(2)])
        Kt = Ring([(S.sb([128, 256], BF16, f"Kt{i}"), ("Kt", i)) for i in range(2)])
        d1 = Ring([(S.sb([128, 4], F32, f"d1{i}"), ("d1", i)) for i in range(2)])
        junk = S.sb([128, 512], BF16, "junk")
        ytmp = Ring([(S.sb([128, 512], F32, f"ytmp{i}"), ("ytmp", i)) for i in range(2)])
        ybf = Ring([(S.sb([128, 512], BF16, f"ybf{i}"), ("ybf", i)) for i in range(2)])

        gdeps = [("gate", gi) for gi in range(2)]
        S.op("sp", lambda e: e.dma_start(out=A[:, :], in_=gate_d[0:4, :]), r=gdeps, w=["gA"], dma=True)
        S.op("sp", lambda e: e.dma_start(out=B[:, :], in_=gate2_d[0:4, :]), r=gdeps, w=["gB"], dma=True)
        S.op("sp", lambda e: e.dma_start(out=bg[:, :], in_=mbg[:, 2 * j:2 * j + 2]), w=["bg"], dma=True)
        S.op("sp", lambda e: e.dma_start(out=nwb[:, :], in_=mnw[:, j * 2048:(j + 1) * 2048]), w=["nwb"], dma=True)
        S.op("dve", lambda e: e.tensor_scalar(out=bg15[:, :], in0=bg[:, :], scalar1=1.0 / 15.0, scalar2=None,
                                              op0=ALU.mult), r=["bg"], w=["bg15"])
        S.op("act", lambda e: e.activation(out=A[:, :], in_=A[:, :], func=AF.Tanh, bias=bg15[:, 0:1], scale=1.0 / 15.0),
             r=["gA", "bg15"], w=["gA"])
        S.op("act", lambda e: e.activation(out=B[:, :], in_=B[:, :], func=AF.Tanh, bias=bg15[:, 1:2], scale=1.0 / 15.0),
             r=["gB", "bg15"], w=["gB"])
        S.op("act", lambda e: e.activation(out=B[:, :], in_=B[:, :], func=AF.Exp, scale=-15.0), r=["gB"], w=["gB"])
        S.op("act", lambda e: e.activation(out=B[:, :], in_=B[:, :], func=AF.Ln, bias=onec[0:4, 0:1], scale=1.0), r=["gB", "onec"], w=["gB"])
        onesT = S.sb([4, T], F32, "onesT")
        S.op("pool", lambda e: e.memset(onesT[:, :], 1.0), w=["onesT"])
        SEG = 1024
        for sg in range(T // SEG):
            a0, a1 = sg * SEG, (sg + 1) * SEG
            S.op("dve", lambda e, a0=a0, a1=a1, sg=sg: e.tensor_tensor_scan(
                out=B[:, a0:a1], data0=onesT[:, a0:a1], data1=B[:, a0:a1],
                initial=(0.0 if sg == 0 else B[:, a0 - 1:a0]), op0=ALU.mult, op1=ALU.add),
                r=["gB", "onesT"], w=["gB"])
        S.op("dve", lambda e: e.scalar_tensor_tensor(out=A[:, :], in0=A[:, :], scalar=15.0, in1=B[:, :],
                                                    op0=ALU.mult, op1=ALU.add), r=["gA", "gB"], w=["gA"])
        A3 = A.rearrange("p (c s) -> p c s", c=NQ, s=128)
        B3 = B.rearrange("p (c s) -> p c s", c=NQ, s=128)
        S.op("dve", lambda e: e.tensor_reduce(out=ref[:, :], in_=A3, axis=AX.X, op=ALU.max), r=["gA"], w=["ref"])
        S.op("dve", lambda e: e.tensor_tensor_scan(out=Gn[:, :], data0=ref[:, :], data1=ref[:, :], initial=0.0,
                                                  op0=ALU.max, op1=ALU.max), r=["ref"], w=["Gn"])
        S.op("dve", lambda e: e.memset(Gp[:, 0:1], 0.0), w=["Gp0"])
        if NQ > 1:
            S.op("dve", lambda e: e.tensor_copy(out=Gp[:, 1:NQ], in_=Gn[:, 0:NQ - 1]), r=["Gn"], w=["Gp1"])
        S.op("dve", lambda e: e.tensor_tensor(out=r1[:, :], in0=Gp[:, :], in1=Gn[:, :], op=ALU.subtract),
             r=["Gn", "Gp0", "Gp1"], w=["r1"])
        S.op("act", lambda e: e.activation(out=r1[:, :], in_=r1[:, :], func=AF.Exp), r=["r1"], w=["r1"])
        for c in range(NQ):
            S.op("dve", lambda e, c=c: e.tensor_scalar(out=A[:, c * 128:(c + 1) * 128], in0=A[:, c * 128:(c + 1) * 128],
                                                      scalar1=Gn[:, c:c + 1], scalar2=None, op0=ALU.subtract),
                 r=["gA", "Gn"], w=["gA"])
            S.op("pool", lambda e, c=c: e.tensor_scalar(out=B[:, c * 128:(c + 1) * 128], in0=B[:, c * 128:(c + 1) * 128],
                                                       scalar1=Gn[:, c:c + 1], scalar2=None, op0=ALU.subtract),
                 r=["gB", "Gn"], w=["gB"])
        S.op("act", lambda e: e.activation(out=A[:, :], in_=A[:, :], func=AF.Exp), r=["gA"], w=["gA"])
        S.op("act", lambda e: e.activation(out=B[:, :], in_=B[:, :], func=AF.Exp), r=["gB"], w=["gB"])
        for (src, dst, dk, sk) in ((A, eT, "eT", "gA"), (B, flT, "flT", "gB")):
            for g0 in range(0, NQ, 64):
                p, pk = psum()
                n_in = min(64, NQ - g0)
                for c in range(g0, g0 + n_in):
                    S.op("pe", lambda e, p=p, c=c, g0=g0, src=src: e.transpose(
                        p[:, (c - g0) * 4:(c - g0 + 1) * 4], src[0:4, c * 128:(c + 1) * 128], ident_f[0:4, 0:4]),
                        r=[sk, "ident_f"], w=[pk])
                S.op("dve", lambda e, p=p, g0=g0, n_in=n_in, dst=dst: e.tensor_copy(
                    out=dst[:, g0 * 4:(g0 + n_in) * 4], in_=p[:, 0:n_in * 4]), r=[pk], w=[dk])
        p, pk = psum()
        for h in range(4):
            S.op("pe", lambda e, p=p, h=h: e.matmul(p[:, h * NQ:(h + 1) * NQ], sel16[0:4, h * 128:(h + 1) * 128],
                                                   r1[0:4, 0:NQ], start=True, stop=True), r=["sel16", "r1"], w=[pk])
        S.op("dve", lambda e, p=p: e.tensor_copy(out=r1b[:, :], in_=p[:, 0:4 * NQ]), r=[pk], w=["r1b"])

        NB = T // 512
        for h in range(4):
            S.op("dve", lambda e: e.memset(Cs[:, :, :], 0.0), w=["Cs"])
            S.op("dve", lambda e: e.memset(ns[:, :], 0.0), w=["ns"])
            for tb in range(NB):
                bi = (h * NB + tb) % 2
                t0 = tb * 512
                S.op("sp", lambda e, bi=bi, t0=t0, h=h: e.dma_start(
                    out=qTb[bi][:, :, :], in_=qkT_d[h * 256:(h + 1) * 256, t0:t0 + 512].rearrange("(c p) t -> p c t", p=128)),
                    r=[("qkT", oc) for oc in range(16)], w=[("qTb", bi)], dma=True)
                S.op("sp", lambda e, bi=bi, t0=t0, h=h: e.dma_start(
                    out=kTb[bi][:, :, :], in_=qkT_d[1024 + h * 256:1024 + (h + 1) * 256, t0:t0 + 512].rearrange("(c p) t -> p c t", p=128)),
                    r=[("qkT", oc) for oc in range(16)], w=[("kTb", bi)], dma=True)
                S.op("sp", lambda e, bi=bi, t0=t0, h=h: e.dma_start(
                    out=ktm[bi][:, :, :], in_=tmb[h // 2][t0:t0 + 512, (h % 2) * 256:(h % 2 + 1) * 256].rearrange("(j p) d -> p j d", p=128)),
                    r=[("tm",)], w=[("ktm", bi)], dma=True)
                S.op("sp", lambda e, bi=bi, t0=t0, h=h: e.dma_start(
                    out=vb[bi][:, :, :], in_=tmb[2 + h][t0:t0 + 512, :].rearrange("(j p) d -> p j d", p=128)),
                    r=[("tm",)], w=[("vb", bi)], dma=True)
                S.op("sp", lambda e, bi=bi, t0=t0, h=h: e.dma_start(
                    out=ob[bi][:, :, :], in_=tmb[6 + h][t0:t0 + 512, :].rearrange("(j p) d -> p j d", p=128)),
                    r=[("tm",)], w=[("ob", bi)], dma=True)
                for cc in range(4):
                    c = tb * 4 + cc
                    ecol = eT[:, c * 4 + h:c * 4 + h + 1]
                    fcol = flT[:, c * 4 + h:c * 4 + h + 1]
                    rcol = r1b[:, h * NQ + c:h * NQ + c + 1]
                    tsl = slice(cc * 128, (cc + 1) * 128)
                    p1, p1k = psum()
                    for c2 in range(2):
                        S.op("pe", lambda e, c2=c2, p1=p1, bi=bi, tsl=tsl: e.matmul(
                            p1[:, 0:128], kTb[bi][:, c2, tsl], qTb[bi][:, c2, tsl], start=(c2 == 0), stop=(c2 == 1)),
                            r=[("kTb", bi), ("qTb", bi)], w=[p1k])
                    pt, ptk = PT.next()
                    S.op("dve", lambda e, p1=p1, pt=pt, ecol=ecol: e.scalar_tensor_tensor(
                        out=pt[:, :], in0=p1[:, 0:128], scalar=ecol, in1=mask01[:, :], op0=ALU.mult, op1=ALU.mult),
                        r=[p1k, "eT", "mask01"], w=[ptk])
                    S.op("dve", lambda e, rcol=rcol: e.tensor_scalar(out=Cs[:, :, :], in0=Cs[:, :, :], scalar1=rcol,
                                                                    scalar2=None, op0=ALU.mult), r=["Cs", "r1b"], w=["Cs"])
                    S.op("act", lambda e: e.activation(out=Cb[:, :, :], in_=Cs[:, :, :], func=AF.Identity), r=["Cs"], w=["Cb"])
                    S.op("dve", lambda e, rcol=rcol: e.tensor_scalar(out=ns[:, :], in0=ns[:, :], scalar1=rcol,
                                                                    scalar2=None, op0=ALU.mult), r=["ns", "r1b"], w=["ns"])
                    S.op("dve", lambda e: e.tensor_copy(out=nb_[:, :], in_=ns[:, :]), r=["ns"], w=["nb_"])
                    p2, p2k = psum()
                    for c2 in range(2):
                        S.op("pe", lambda e, c2=c2, p2=p2, bi=bi, tsl=tsl: e.matmul(
                            p2[:, :], qTb[bi][:, c2, tsl], Cb[:, c2, :], start=(c2 == 0), stop=False),
                            r=[("qTb", bi), "Cb"], w=[p2k])
                    S.op("pe", lambda e, p2=p2, pt=pt, bi=bi, cc=cc: e.matmul(
                        p2[:, :], pt[:, :], vb[bi][:, cc, :], start=False, stop=True), r=[ptk, ("vb", bi)], w=[p2k])
                    p3, p3k = psum()
                    for c2 in range(2):
                        S.op("pe", lambda e, c2=c2, p3=p3, bi=bi, tsl=tsl: e.matmul(
                            p3[:, 0:1], qTb[bi][:, c2, tsl], nb_[:, c2:c2 + 1], start=(c2 == 0), stop=False),
                            r=[("qTb", bi), "nb_"], w=[p3k])
                    S.op("pe", lambda e, p3=p3, pt=pt: e.matmul(p3[:, 0:1], pt[:, :], ones1[:, 0:1], start=False, stop=True),
                         r=[ptk, "ones1"], w=[p3k])
                    kt_, ktk = Kt.next()
                    S.op("pool", lambda e, kt_=kt_, bi=bi, cc=cc, ecol=ecol: e.tensor_scalar(
                        out=kt_[:, :], in0=ktm[bi][:, cc, :], scalar1=ecol, scalar2=None, op0=ALU.mult),
                        r=[("ktm", bi), "eT"], w=[ktk])
                    p5, p5k = psum()
                    for c2 in range(2):
                        p4, p4k = psum()
                        S.op("pe", lambda e, c2=c2, p4=p4, kt_=kt_, bi=bi, cc=cc: e.matmul(
                            p4[:, :], kt_[:, c2 * 128:(c2 + 1) * 128], vb[bi][:, cc, :], start=True, stop=True),
                            r=[ktk, ("vb", bi)], w=[p4k])
                        S.op("pe", lambda e, c2=c2, p5=p5, kt_=kt_: e.matmul(
                            p5[:, c2:c2 + 1], kt_[:, c2 * 128:(c2 + 1) * 128], ones1[:, 0:1], start=True, stop=True),
                            r=[ktk, "ones1"], w=[p5k])
                        S.op("dve", lambda e, c2=c2, p4=p4: e.tensor_tensor(out=Cs[:, c2, :], in0=Cs[:, c2, :], in1=p4[:, :],
                                                                          op=ALU.add), r=[p4k, "Cs", "Cb"], w=["Cs"])
                    S.op("dve", lambda e, p5=p5: e.tensor_tensor(out=ns[:, :], in0=ns[:, :], in1=p5[:, 0:2], op=ALU.add),
                         r=[p5k, "ns", "nb_"], w=["ns"])
                    dd, ddk = d1.next()
                    S.op("act", lambda e, dd=dd, p3=p3: e.activation(out=dd[:, 0:1], in_=p3[:, 0:1], func=AF.Abs),
                         r=[p3k], w=[(ddk, 0)])
                    S.op("dve", lambda e, dd=dd, fcol=fcol: e.tensor_scalar(
                        out=dd[:, 0:1], in0=dd[:, 0:1], scalar1=fcol, scalar2=None, op0=ALU.max),
                        r=[(ddk, 0), "flT"], w=[(ddk, 0)])
                    S.op("dve", lambda e, dd=dd: e.reciprocal(out=dd[:, 1:2], in_=dd[:, 0:1]), r=[(ddk, 0)], w=[(ddk, 1)])
                    S.op("pool", lambda e, dd=dd: e.memset(dd[:, 2:3], 0.0), w=[(ddk, 2)])
                    S.op("act", lambda e, dd=dd, p2=p2: e.activation(out=junk[:, :], in_=p2[:, :], func=AF.Square,
                                                                    scale=dd[:, 1:2], accum_out=dd[:, 2:3]),
                         r=[p2k, (ddk, 1), (ddk, 2)], w=[(ddk, 2), "junk"])
                    S.op("act", lambda e, dd=dd: e.activation(out=dd[:, 3:4], in_=dd[:, 2:3], func=AF.Ln,
                                                             bias=epsc[:, 0:1], scale=1.0 / 512.0),
                         r=[(ddk, 2), "epsc"], w=[(ddk, 3)])
                    S.op("act", lambda e, dd=dd: e.activation(out=dd[:, 3:4], in_=dd[:, 3:4], func=AF.Exp, scale=-0.5),
                         r=[(ddk, 3)], w=[(ddk, 3)])
                    S.op("dve", lambda e, dd=dd: e.tensor_tensor(out=dd[:, 3:4], in0=dd[:, 3:4], in1=dd[:, 1:2],
                                                                op=ALU.mult), r=[(ddk, 3), (ddk, 1)], w=[(ddk, 3)])
                    yt, ytk = ytmp.next()
                    S.op("dve", lambda e, yt=yt, p2=p2, dd=dd, h=h: e.scalar_tensor_tensor(
                        out=yt[:, :], in0=p2[:, :], scalar=dd[:, 3:4], in1=nwb[:, h * 512:(h + 1) * 512],
                        op0=ALU.mult, op1=ALU.mult), r=[p2k, (ddk, 3), "nwb"], w=[ytk])
                    yb, ybk = ybf.next()
                    S.op("pool", lambda e, yb=yb, yt=yt, bi=bi, cc=cc: e.tensor_tensor(
                        out=yb[:, :], in0=yt[:, :], in1=ob[bi][:, cc, :], op=ALU.mult), r=[ytk, ("ob", bi)], w=[ybk])
                    p6, p6k = psum()
                    p6v = p6[:, 0:256].bitcast(BF16)
                    for d4 in range(4):
                        S.op("pe", lambda e, d4=d4, p6v=p6v, yb=yb: e.transpose(
                            p6v[:, d4 * 128:(d4 + 1) * 128], yb[:, d4 * 128:(d4 + 1) * 128], ident_b[:, :]),
                            r=[ybk, "ident_b"], w=[p6k])
                    S.op("act", lambda e, p6v=p6v, bi=bi, tsl=tsl: e.activation(
                        out=yTb[bi][:, :, tsl], in_=p6v.rearrange("p (d t) -> p d t", d=4, t=128), func=AF.Identity),
                        r=[p6k], w=[("yTb", bi, cc)])
                S.op("sp", lambda e, bi=bi, t0=t0, h=h: e.dma_start(
                    out=oT_d[h * 512:(h + 1) * 512, t0:t0 + 512].rearrange("(d p) t -> p d t", p=128), in_=yTb[bi][:, :, :]),
                    r=[("yTb", bi, cc) for cc in range(4)], w=[("oT",)], dma=True)
        S.barrier()

    x_src = xT_in
    for l in range(depth):
        phase_inproj(l, x_src)
        S.barrier()
        if l % 2 == 1:
            phase_fox(l)
        else:
            phase_mlstm(l)
        phase_post(l, x_src, last=(l == depth - 1))
        S.barrier()
        x_src = xs_d
    S.op("sp", None)
    S.emit()
    return nc, stack


def _consts():
    c = np.zeros((128, 512 + 2048), np.float32)
    c[:, 0:128] = np.eye(128, dtype=np.float32)
    s = np.arange(128)
    c[:, 128:256] = (s[:, None] <= s[None, :]).astype(np.float32)
    c[:, 256:384] = np.where(s[None, :] <= s[:, None], 0.0, NEG)
    for h in range(16):
        c[h, 512 + h * 128:512 + (h + 1) * 128] = 1.0
    return c


def _col(v):
    v = np.asarray(v, np.float32)
    return np.ascontiguousarray(v.reshape(-1, 128).T)


def make_in_map(b, T, depth, x, c, ada_w, ada_b, norm_mix_w, norm_ffn_w, mlstm_w_in, mlstm_b_gates,
                mlstm_norm_w, mlstm_w_out, fox_w_in, fox_b_f, fox_w_out, ffn_w_gate_up, ffn_w_down, final_norm_w):
    n_ml = (depth + 1) // 2
    n_fx = depth // 2
    m = {}
    m["xT"] = np.ascontiguousarray(np.asarray(x[b], np.float32).T)
    m["cT"] = _col(c[b])
    m["ada_w"] = np.ascontiguousarray(ada_w[:depth], dtype=np.float32)
    m["ada_bT"] = np.concatenate([_col(ada_b[l]) for l in range(depth)], axis=1)
    m["nmw"] = np.concatenate([_col(norm_mix_w[l]) for l in range(depth)], axis=1)
    m["nfw"] = np.concatenate([_col(norm_ffn_w[l]) for l in range(depth)], axis=1)
    m["fnw"] = _col(final_norm_w)
    m["mw_in"] = np.ascontiguousarray(mlstm_w_in[:max(n_ml, 1)], dtype=np.float32)
    bgs = np.asarray(mlstm_b_gates, np.float32)[:max(n_ml, 1)]
    m["mbg"] = np.ascontiguousarray(np.concatenate([np.stack([bg[0:4], bg[4:8]], axis=1) for bg in bgs], axis=1))
    m["mnw"] = np.ascontiguousarray(np.concatenate(
        [np.broadcast_to(np.asarray(w, np.float32)[None, :], (128, 2048)) for w in mlstm_norm_w[:max(n_ml, 1)]], axis=1))
    m["mw_out"] = np.ascontiguousarray(mlstm_w_out[:max(n_ml, 1)], dtype=np.float32)
    m["fw_in"] = np.ascontiguousarray(fox_w_in[:max(n_fx, 1)], dtype=np.float32)
    m["fbf"] = np.ascontiguousarray(np.asarray(fox_b_f, np.float32)[:max(n_fx, 1)].reshape(-1, 1))
    m["fw_out"] = np.ascontiguousarray(fox_w_out[:max(n_fx, 1)], dtype=np.float32)
    m["w_gu"] = np.ascontiguousarray(ffn_w_gate_up[:depth], dtype=np.float32)
    m["w_dn"] = np.ascontiguousarray(ffn_w_down[:depth], dtype=np.float32)
    m["consts"] = _consts()
    return m


_CACHE = {}


def kernel(**inputs):
    x = np.asarray(inputs["x"])
    Bn, T, _ = x.shape
    depth = np.asarray(inputs["ada_w"]).shape[0]
    key = (T, depth)
    if key not in _CACHE:
        _CACHE[key] = build(T, depth)
    nc, _stack = _CACHE[key]
    arrs = {k: np.asarray(v) for k, v in inputs.items()}
    in_maps = [make_in_map(b, T, depth, **arrs) for b in range(Bn)]
    res = run_bass_kernel_spmd(nc, in_maps, core_ids=list(range(Bn)))
    out = np.stack([np.ascontiguousarray(r["outT"].T) for r in res.results], axis=0)
    return out.astype(np.float32)
```

```python
from contextlib import ExitStack
import numpy as np
import ml_dtypes
import concourse.bass as bass
import concourse.mybir as mybir
from concourse.bass_utils import run_bass_kernel_spmd

F32 = mybir.dt.float32
BF16 = mybir.dt.bfloat16
AF = mybir.ActivationFunctionType
ALU = mybir.AluOpType
AX = mybir.AxisListType

D = 2048
NC_ = 16
DFF = 5632
NFC = 44
EPS = 1e-6
TT = 512
NEG = -30000.0
R_DMA = 8


class LoopVar:
    cur = None


LV = LoopVar()


def TS(idx, size, off=0):
    if isinstance(idx, LoopVar):
        if isinstance(idx.cur, int):
            return slice(idx.cur * size + off, idx.cur * size + off + size)
        assert off == 0
        return bass.ts(idx.cur, size)
    return slice(idx * size + off, idx * size + off + size)


def DS(idx, stride, off, size):
    if isinstance(idx, LoopVar):
        return bass.ds(idx.cur * stride + off, size)
    return slice(idx * stride + off, idx * stride + off + size)


class Ring:
    registry = []

    def __init__(self, items):
        self.items = items
        self.i = 0
        Ring.registry.append(self)

    def next(self):
        x = self.items[self.i % len(self.items)]
        self.i += 1
        return x


ENGS = ["pe", "act", "dve", "pool", "sp"]
QUEUES = ["sp", "pool", "act"]


class Sch:
    ARENA = 188 * 1024

    def __init__(self, nc, stack):
        self.nc = nc
        self.stack = stack
        self.ops = []
        self.lw = {}
        self.lr = {}
        self.barrier_deps = set()
        self.last_on = {}
        self.dma_hist = {q: [] for q in QUEUES}
        self.n_t = 0
        self.regions = []
        self._after_barrier = set(ENGS)
        Ring.registry.clear()

    def sb(self, shape, dt, name=None):
        if not hasattr(self, "arena"):
            self.arena = self.stack.enter_context(self.nc.sbuf_tensor("arena", [128, self.ARENA], mybir.dt.uint8))
            self.top = 0
        esz = 4 if dt == F32 else 2
        n = 1
        for d in shape[1:]:
            n *= d
        nbytes = (n * esz + 63) // 64 * 64
        off = self.top
        self.top += nbytes
        assert self.top <= self.ARENA, f"SBUF arena overflow: {self.top} ({name})"
        ap = self.arena[0:shape[0], off:off + n * esz].bitcast(dt)
        if len(shape) == 3:
            ap = ap.rearrange("p (a b) -> p a b", a=shape[1], b=shape[2])
        return ap

    def mark(self):
        return self.top

    def reset(self, m):
        self.top = m

    def ps(self, shape, dt=F32, name=None):
        self.n_t += 1
        return self.stack.enter_context(self.nc.psum_tensor(name or f"p{self.n_t}", list(shape), dt))

    def op(self, eng, fn, r=(), w=(), dma=False):
        idx = len(self.ops)
        deps = set(self.barrier_deps) if eng not in self._after_barrier else set()
        self._after_barrier.add(eng)
        for k in r:
            x = self.lw.get(k)
            if x is not None:
                deps.add(x)
        for k in w:
            x = self.lw.get(k)
            if x is not None:
                deps.add(x)
            for y in self.lr.get(k, ()):
                deps.add(y)
        for k in r:
            self.lr.setdefault(k, []).append(idx)
        for k in w:
            self.lw[k] = idx
            self.lr[k] = []
        deps.discard(idx)
        self.ops.append(dict(eng=eng, fn=fn, deps=deps, dma=dma))
        if dma:
            self.dma_hist[eng].append(idx)
        else:
            self.last_on[eng] = idx
        return idx

    def barrier(self):
        deps = set(self.last_on.values())
        for q, h in self.dma_hist.items():
            deps.update(h[-R_DMA:])
        self.barrier_deps = deps
        self._after_barrier = set()

    def loop(self, N, body, reset_fn=None):
        if N == 1:
            body(0)
            return
        self.barrier()
        reg = dict(N=N, s0=len(self.ops), bdeps=set(self.barrier_deps))
        for rg in Ring.registry:
            rg.i = 0
        if reset_fn:
            reset_fn()
        body(LV)
        reg["s1"] = len(self.ops)
        for rg in Ring.registry:
            rg.i = 0
        if reset_fn:
            reset_fn()
        body(LV)
        reg["s2"] = len(self.ops)
        assert reg["s2"] - reg["s1"] == reg["s1"] - reg["s0"], "loop body not iteration-invariant"
        for a, b in zip(range(reg["s0"], reg["s1"]), range(reg["s1"], reg["s2"])):
            assert self.ops[a]["eng"] == self.ops[b]["eng"] and self.ops[a]["dma"] == self.ops[b]["dma"]
        self.regions.append(reg)
        self.barrier()

    def emit(self):
        nc = self.nc
        ops = self.ops
        n = len(ops)
        reg_of = [None] * n
        copy_of = [0] * n
        for reg in self.regions:
            for i in range(reg["s0"], reg["s1"]):
                reg_of[i] = reg
                copy_of[i] = 1
            for i in range(reg["s1"], reg["s2"]):
                reg_of[i] = reg
                copy_of[i] = 2

        def twin(i):
            reg = reg_of[i]
            return i + (reg["s1"] - reg["s0"]) if copy_of[i] == 1 else i

        def pe_pe(d, i):
            return ops[d]["eng"] == "pe" and ops[i]["eng"] == "pe" and not ops[d]["dma"] and not ops[i]["dma"]

        sig = [False] * n
        for i, o in enumerate(ops):
            for d in o["deps"]:
                if pe_pe(d, i):
                    continue
                sig[twin(d)] = True
        cnt = {e: 0 for e in ENGS}
        rr = {q: 0 for q in QUEUES}
        tot = {q: [0] * R_DMA for q in QUEUES}
        i = 0
        while i < n:
            reg = reg_of[i]
            if reg is None:
                o = ops[i]
                e = o["eng"]
                if o["dma"]:
                    s = rr[e] % R_DMA
                    rr[e] += 1
                    tot[e][s] += 1
                    o["slot"] = s
                    o["c"], o["k"] = 16 * tot[e][s], 0
                elif sig[i]:
                    cnt[e] += 1
                    o["c"], o["k"] = cnt[e], 0
                i += 1
                continue
            N = reg["N"]
            body = range(reg["s1"], reg["s2"])
            delta = {e: 0 for e in ENGS}
            for j in body:
                if not ops[j]["dma"] and sig[j]:
                    delta[ops[j]["eng"]] += 1
            run_c = {e: 0 for e in ENGS}
            rr0 = dict(rr)
            cslot = {q: [0] * R_DMA for q in QUEUES}
            for j in body:
                if ops[j]["dma"]:
                    q = ops[j]["eng"]
                    s = rr0[q] % R_DMA
                    rr0[q] += 1
                    ops[j]["slot"] = s
                    ops[j]["m"] = cslot[q][s]
                    cslot[q][s] += 1
            for j in body:
                o = ops[j]
                e = o["eng"]
                if o["dma"]:
                    s = o["slot"]
                    o["c"] = 16 * (tot[e][s] + o["m"] + 1)
                    o["k"] = 16 * cslot[e][s]
                elif sig[j]:
                    run_c[e] += 1
                    o["c"], o["k"] = cnt[e] + run_c[e], delta[e]
            for e in ENGS:
                cnt[e] += N * delta[e]
            for q in QUEUES:
                for s in range(R_DMA):
                    tot[q][s] += N * cslot[q][s]
            reg["cslot"] = cslot
            off = reg["s1"] - reg["s0"]
            for j0 in range(reg["s0"], reg["s1"]):
                t = ops[j0 + off]
                if "c" in t:
                    ops[j0]["c"], ops[j0]["k"] = t["c"], 0
                if "slot" in t:
                    ops[j0]["slot"] = t["slot"]
            i = reg["s2"]

        csem = {e: self.stack.enter_context(nc.semaphore(f"c_{e}")) for e in ENGS}
        dsem = {q: [self.stack.enter_context(nc.semaphore(f"d_{q}{s}")) for s in range(R_DMA)] for q in QUEUES}

        def semkey(p):
            return ("d", p["eng"], p["slot"]) if p["dma"] else ("c", p["eng"])

        def dep_target(d, i):
            t = twin(d)
            p = ops[t]
            key = semkey(p)
            if reg_of[i] is not None and reg_of[i] is reg_of[d]:
                if copy_of[i] == 1:
                    return key, p["c"], 0
                if copy_of[d] == 1:
                    return key, p["c"] - p["k"], p["k"]
                return key, p["c"], p["k"]
            if reg_of[d] is not None:
                return key, p["c"] + (reg_of[d]["N"] - 1) * p["k"], 0
            return key, p["c"], 0

        tmpregs = {}
        itregs = {}

        def emit_waits(eobj, waits, waited, it):
            for (key, k), c in waits.items():
                prev = waited.get((key, k))
                if prev is not None and prev >= c:
                    continue
                waited[(key, k)] = c
                sem = csem[key[1]] if key[0] == "c" else dsem[key[1]][key[2]]
                if k == 0 or it is None:
                    eobj.wait_ge(sem, c)
                else:
                    rg = tmpregs.get(id(eobj))
                    if rg is None:
                        rg = eobj.alloc_register()
                        tmpregs[id(eobj)] = rg
                    itr = waited.get("__itr")
                    if itr is None:
                        itr = eobj.to_reg(it)
                        waited["__itr"] = itr
                    eobj.reg_mul(rg, itr, k)
                    eobj.reg_add(rg, rg, c)
                    eobj.wait_ge(sem, rg)

        def collect(i, ename):
            o = ops[i]
            waits = {}
            for d in o["deps"]:
                if pe_pe(d, i):
                    continue
                key, c, k = dep_target(d, i)
                if waits.get((key, k), -10 ** 9) < c:
                    waits[(key, k)] = c
            if o["dma"]:
                key = ("d", ename, o["slot"])
                c, k = o["c"] - 16, o["k"]
                if copy_of[i] == 1:
                    k = 0
                if c > 0 or k > 0:
                    if waits.get((key, k), -10 ** 9) < c:
                        waits[(key, k)] = c
            return waits

        def emit_op(i, ename, eobj, waited, it):
            o = ops[i]
            emit_waits(eobj, collect(i, ename), waited, it)
            if o["fn"] is None:
                return
            ins = o["fn"](eobj)
            if o["dma"]:
                ins.then_inc(dsem[ename][o["slot"]], 16)
            elif sig[twin(i)]:
                ins.then_inc(csem[ename], 1)

        def run(ename, eobj):
            waited = {}
            i = 0
            while i < n:
                reg = reg_of[i]
                if reg is None:
                    if ops[i]["eng"] == ename:
                        emit_op(i, ename, eobj, waited, None)
                    i += 1
                    continue
                LV.cur = 0
                for j in range(reg["s0"], reg["s1"]):
                    if ops[j]["eng"] == ename:
                        emit_op(j, ename, eobj, waited, None)
                LV.cur = None
                mine = [j for j in range(reg["s1"], reg["s2"]) if ops[j]["eng"] == ename]
                if mine:
                    with eobj.Fori(1, reg["N"]) as iv:
                        LV.cur = iv
                        w2 = {}
                        for j in mine:
                            emit_op(j, ename, eobj, w2, iv)
                    LV.cur = None
                i = reg["s2"]

        with nc.Block() as block:
            @block.tensor
            def _(e):
                run("pe", e)

            @block.scalar
            def _(e):
                run("act", e)

            @block.vector
            def _(e):
                run("dve", e)

            @block.gpsimd
            def _(e):
                run("pool", e)

            @block.sync
            def _(e):
                run("sp", e)


def build(T, depth, dbg=False):
    NT = T // TT
    NQ = T // 128
    n_ml = (depth + 1) // 2
    n_fx = depth // 2
    nc = bass.Bass("TRN2", target_bir_lowering=False)
    stack = ExitStack()
    S = Sch(nc, stack)

    def dram(name, shape, dt, kind="Internal"):
        return nc.dram_tensor(name, list(shape), dt, kind=kind).ap()

    xT_in = dram("xT", [D, T], F32, "ExternalInput")
    cT = dram("cT", [128, 16], F32, "ExternalInput")
    ada_w = dram("ada_w", [depth, D, 6 * D], F32, "ExternalInput")
    ada_bT = dram("ada_bT", [128, depth * 96], F32, "ExternalInput")
    nmw = dram("nmw", [128, depth * 16], F32, "ExternalInput")
    nfw = dram("nfw", [128, depth * 16], F32, "ExternalInput")
    fnw = dram("fnw", [128, 16], F32, "ExternalInput")
    mw_in = dram("mw_in", [max(n_ml, 1), D, 6152], F32, "ExternalInput")
    mbg = dram("mbg", [4, max(n_ml, 1) * 2], F32, "ExternalInput")
    mnw = dram("mnw", [128, max(n_ml, 1) * 2048], F32, "ExternalInput")
    mw_out = dram("mw_out", [max(n_ml, 1), D, D], F32, "ExternalInput")
    fw_in = dram("fw_in", [max(n_fx, 1), D, 6160], F32, "ExternalInput")
    fbf = dram("fbf", [16 * max(n_fx, 1), 1], F32, "ExternalInput")
    fw_out = dram("fw_out", [max(n_fx, 1), D, D], F32, "ExternalInput")
    w_gu = dram("w_gu", [depth, D, 2 * DFF], F32, "ExternalInput")
    w_dn = dram("w_dn", [depth, DFF, D], F32, "ExternalInput")
    consts = dram("consts", [128, 2560 + 2048], F32, "ExternalInput")
    outT = dram("outT", [D, T], F32, "ExternalOutput")

    xs_d = dram("xs_d", [D, T], F32)
    qkT_d = dram("qkT_d", [2 * D, T], BF16)
    tm_d = dram("tm_d", [T, 2048], BF16)
    tmb = [dram(f"tmb{i}", [T, 512], BF16) for i in range(10)]
    gate_d = dram("gate_d", [16, T], F32)
    gate2_d = dram("gate2_d", [4, T], F32)
    oT_d = dram("oT_d", [D, T], BF16)
    ncum_d = dram("ncum_d", [16, T], F32)
    if dbg:
        dbg_h = dram("dbg_h", [D, T], F32, "ExternalOutput")

    ident_f = S.sb([128, 128], F32, "ident_f")
    ident_b = S.sb([128, 128], BF16, "ident_b")
    mask01 = S.sb([128, 128], F32, "mask01")
    maskneg = S.sb([128, 128], F32, "maskneg")
    onesm = S.sb([128, 128], BF16, "onesm")
    ones1 = S.sb([128, 128], BF16, "ones1")
    sel16 = S.sb([16, 4 * 128], F32, "sel16")
    modsb = S.sb([128, depth * 96], F32, "modsb")
    gam1 = S.sb([128, depth * 16], F32, "gam1")
    gam2 = S.sb([128, depth * 16], F32, "gam2")
    nmw_s = S.sb([128, depth * 16], F32, "nmw_s")
    nfw_s = S.sb([128, depth * 16], F32, "nfw_s")
    fnw_s = S.sb([128, 16], F32, "fnw_s")
    zero_c = S.sb([128, 16], F32, "zero_c")
    epsc = S.sb([128, 1], F32, "epsc")
    onec = S.sb([128, 1], F32, "onec")

    S.op("pool", lambda e: e.dma_start(out=ident_f[:], in_=consts[:, 0:128]), w=["ident_f"], dma=True)
    S.op("pool", lambda e: e.dma_start(out=ident_b[:], in_=consts[:, 0:128]), w=["ident_b"], dma=True)
    S.op("pool", lambda e: e.dma_start(out=mask01[:], in_=consts[:, 128:256]), w=["mask01"], dma=True)
    S.op("pool", lambda e: e.dma_start(out=maskneg[:], in_=consts[:, 256:384]), w=["maskneg"], dma=True)
    S.op("pool", lambda e: e.dma_start(out=sel16[:], in_=consts[0:16, 512:512 + 512]), w=["sel16"], dma=True)
    S.op("pool", lambda e: e.dma_start(out=nmw_s[:], in_=nmw[:, :]), w=["nmw_s"], dma=True)
    S.op("pool", lambda e: e.dma_start(out=nfw_s[:], in_=nfw[:, :]), w=["nfw_s"], dma=True)
    S.op("pool", lambda e: e.dma_start(out=fnw_s[:], in_=fnw[:, :]), w=["fnw_s"], dma=True)
    S.op("dve", lambda e: e.memset(onesm[:], 1.0 / D), w=["onesm"])
    S.op("dve", lambda e: e.memset(ones1[:], 1.0), w=["ones1"])
    S.op("dve", lambda e: e.memset(zero_c[:], 0.0), w=["zero_c"])
    S.op("dve", lambda e: e.memset(epsc[:], EPS), w=["epsc"])
    S.op("dve", lambda e: e.memset(onec[:], 1.0), w=["onec"])

    wblk = {}

    class WRef:
        def __init__(self, name, i):
            self.name, self.i = name, i

    class WMat:
        def __init__(self, name):
            self.name = name

        def __getitem__(self, i):
            return WRef(self.name, i)

    def conv(name, src, n, blocks, slot):
        dst = dram(name, [n * len(blocks), 128, slot], BF16)
        for i in range(n):
            for bi_, (k0, nk, c0, ncb_) in enumerate(blocks):
                d = dst[i * len(blocks) + bi_, :, 0:nk * ncb_]
                wblk[(name, i, k0, c0)] = (d, nk, ncb_)
                S.op("pool", lambda e, i=i, k0=k0, nk=nk, c0=c0, ncb_=ncb_, d=d: e.dma_start(
                    out=d.rearrange("p (k c) -> p k c", k=nk, c=ncb_),
                    in_=src[i, k0 * 128:(k0 + nk) * 128, c0:c0 + ncb_].rearrange("(k p) c -> p k c", p=128)),
                    w=[("wconv", name, i)], dma=True)
        return WMat(name)

    in_blocks = [(0, 16, c0, 512) for c0 in range(0, 6144, 512)]
    mw_in = conv("mw_in_b", mw_in, max(n_ml, 1), in_blocks + [(0, 16, 6144, 8)], 16 * 512)
    fw_in = conv("fw_in_b", fw_in, max(n_fx, 1), in_blocks + [(0, 16, 6144, 16)], 16 * 512)
    out_blocks = [(0, 16, c0, 512) for c0 in range(0, D, 512)]
    mw_out = conv("mw_out_b", mw_out, max(n_ml, 1), out_blocks, 16 * 512)
    fw_out = conv("fw_out_b", fw_out, max(n_fx, 1), out_blocks, 16 * 512)
    gu_blocks = [(0, 16, c0, 256) for c0 in range(0, 2 * DFF, 256)]
    w_gu = conv("w_gu_b", w_gu, depth, gu_blocks, 16 * 256)
    dn_blocks = [(kg * 16, (16 if kg < 2 else 12), cb * 512, 512) for kg in range(3) for cb in range(4)]
    w_dn = conv("w_dn_b", w_dn, depth, dn_blocks, 16 * 512)
    S.barrier()

    m0 = S.mark()
    NWB = 3
    wbufs = [S.sb([128, 16, 512], BF16, f"wb{i}") for i in range(NWB)]
    wstate = dict(n=0)

    def wload(src2d, k0, nk, c0, ncols, dcol=0):
        i = wstate["n"] % NWB
        wstate["n"] += 1
        b = wbufs[i]
        if isinstance(src2d, WRef):
            d, nk_, nc_ = wblk[(src2d.name, src2d.i, k0, c0)]
            assert nk_ == nk and nc_ == ncols, (src2d.name, k0, c0, nk, ncols)
            src = d.rearrange("p (k c) -> p k c", k=nk, c=ncols)
        else:
            src = src2d[k0 * 128:(k0 + nk) * 128, c0:c0 + ncols].rearrange("(k p) c -> p k c", p=128)
        S.op("pool", lambda e: e.dma_start(out=b[:, 0:nk, dcol:dcol + ncols], in_=src), w=[("wb", i)], dma=True)
        return b, ("wb", i)

    def wload2(src2d, k0, nk, c0, c1, ncols):
        i = wstate["n"] % NWB
        wstate["n"] += 1
        b = wbufs[i]
        s0 = src2d[k0 * 128:(k0 + nk) * 128, c0:c0 + ncols].rearrange("(k p) c -> p k c", p=128)
        s1 = src2d[k0 * 128:(k0 + nk) * 128, c1:c1 + ncols].rearrange("(k p) c -> p k c", p=128)
        S.op("pool", lambda e: e.dma_start(out=b[:, 0:nk, 0:ncols], in_=s0), w=[("wb", i)], dma=True)
        S.op("pool", lambda e: e.dma_start(out=b[:, 0:nk, ncols:2 * ncols], in_=s1), w=[("wb", i, 1)], r=[("wb", i)], dma=True)
        return b, ("wb", i, 1)

    def reset_w():
        wstate["n"] = 0

    def wstream(blocks, consume, la=NWB - 1):
        issued = []
        for j in range(len(blocks)):
            while len(issued) < min(len(blocks), j + la + 1):
                issued.append(blocks[len(issued)]())
            b, key = issued[j]
            consume(j, b, key)

    psb = [S.ps([128, 512], F32, f"psb{i}") for i in range(8)]
    psring = Ring(list(range(8)))

    def psum():
        i = psring.next()
        return psb[i], ("ps", i)

    cs32 = S.sb([128, 16], F32, "cs32")
    csb = S.sb([128, 16], BF16, "csb")
    adab_s = S.sb([128, depth * 96], F32, "adab_s")
    S.op("pool", lambda e: e.dma_start(out=cs32[:], in_=cT[:, :]), w=["cs32"], dma=True)
    S.op("pool", lambda e: e.dma_start(out=adab_s[:], in_=ada_bT[:, :]), w=["adab_s"], dma=True)
    S.op("act", lambda e: e.activation(out=csb[:], in_=cs32[:], func=AF.Silu), r=["cs32"], w=["csb"])
    for l in range(depth):
        pm, pmk = psum()
        blocks = [(lambda l=l, bi=bi: wload(ada_w[l], 0, 16, bi * 512, 512)) for bi in range(24)]

        def cons(j, b, key, pm=pm, pmk=pmk):
            for j4 in range(4):
                col = j * 4 + j4
                for kc in range(16):
                    S.op("pe", lambda e, b=b, kc=kc, j4=j4, col=col: e.matmul(
                        pm[:, col:col + 1], b[:, kc, j4 * 128:(j4 + 1) * 128], csb[:, kc:kc + 1],
                        start=(kc == 0), stop=(kc == 15)), r=[key, "csb"], w=[pmk])
        wstream(blocks, cons)
        S.op("dve", lambda e, l=l, pm=pm: e.tensor_tensor(out=modsb[:, l * 96:(l + 1) * 96], in0=pm[:, 0:96],
                                                       in1=adab_s[:, l * 96:(l + 1) * 96], op=ALU.add),
             r=[pmk, "adab_s"], w=[("mod", l)])
        S.op("dve", lambda e, l=l: e.scalar_tensor_tensor(out=gam1[:, l * 16:(l + 1) * 16],
                                                         in0=modsb[:, l * 96 + 16:l * 96 + 32], scalar=1.0,
                                                         in1=nmw_s[:, l * 16:(l + 1) * 16], op0=ALU.add, op1=ALU.mult),
             r=[("mod", l), "nmw_s"], w=[("gam1", l)])
        S.op("dve", lambda e, l=l: e.scalar_tensor_tensor(out=gam2[:, l * 16:(l + 1) * 16],
                                                         in0=modsb[:, l * 96 + 64:l * 96 + 80], scalar=1.0,
                                                         in1=nfw_s[:, l * 16:(l + 1) * 16], op0=ALU.add, op1=ALU.mult),
             r=[("mod", l), "nfw_s"], w=[("gam2", l)])

    def modcol(l, which, c):
        base = l * 96 + which * 16 + c
        return modsb[:, base:base + 1]

    xs = S.sb([128, 16, TT], F32, "xs")
    hb = S.sb([128, 16, TT], BF16, "hb")
    act = S.sb([128, NFC, TT], BF16, "act")
    rstd = S.sb([128, TT], F32, "rstd")
    tmpr = Ring([(S.sb([128, TT], F32, f"tmp{i}"), ("tmp", i)) for i in range(3)])
    vst3 = S.sb([128, 4, 2048], BF16, "vst3")
    stgf = Ring([(S.sb([128, TT], F32, f"stgf{i}"), ("stgf", i)) for i in range(2)])

    def norm_tile(gam_ap, sh_ap, l, rdeps, out_fn=None):
        for g in range(4):
            S.op("act", lambda e, g=g: e.activation(out=act[:, 4 * g:4 * g + 4, :], in_=xs[:, 4 * g:4 * g + 4, :],
                                                   func=AF.Square),
                 r=[("xs", c) for c in range(4 * g, 4 * g + 4)], w=[("act", c) for c in range(4 * g, 4 * g + 4)])
        pss, pssk = psum()
        for c in range(16):
            S.op("pe", lambda e, c=c: e.matmul(pss[:, :], onesm[:, :], act[:, c, :], start=(c == 0), stop=(c == 15)),
                 r=[("act", c), "onesm"], w=[pssk])
        S.op("act", lambda e: e.activation(out=rstd[:], in_=pss[:, :], func=AF.Ln, bias=epsc[:, 0:1], scale=1.0),
             r=[pssk, "epsc"], w=["rstd"])
        S.op("act", lambda e: e.activation(out=rstd[:], in_=rstd[:], func=AF.Exp, scale=-0.5), r=["rstd"], w=["rstd"])
        for c in range(16):
            t, tk = tmpr.next()
            S.op("dve", lambda e, c=c, t=t: e.tensor_tensor(out=t[:], in0=xs[:, c, :], in1=rstd[:], op=ALU.mult),
                 r=[("xs", c), "rstd"], w=[tk])
            if out_fn is None:
                S.op("act", lambda e, c=c, t=t: e.activation(out=hb[:, c, :], in_=t[:], func=AF.Identity,
                                                            bias=sh_ap[:, c:c + 1], scale=gam_ap[:, c:c + 1]),
                     r=[tk] + rdeps, w=[("hb", c)])
            else:
                out_fn(c, t, tk)

    def load_x(src, tt):
        S.op("sp", lambda e: e.dma_start(out=xs[:, :, :],
                                         in_=src[:, TS(tt, TT)].rearrange("(c p) t -> p c t", p=128)),
             r=[("xd",)], w=[("xs", c) for c in range(16)], dma=True)

    evac_rr = Ring(["act", "dve"])

    def evac_bf16(ps_ap, psk, dst_ap, dstk, scale=None, func=None, extra_r=()):
        if func is not None:
            S.op("act", lambda e: e.activation(out=dst_ap, in_=ps_ap, func=func), r=[psk] + list(extra_r), w=[dstk])
            return
        eng = evac_rr.next()
        if eng == "act":
            S.op("act", lambda e: e.activation(out=dst_ap, in_=ps_ap, func=AF.Identity,
                                               scale=(1.0 if scale is None else scale)), r=[psk] + list(extra_r), w=[dstk])
        else:
            if scale is None:
                S.op("dve", lambda e: e.tensor_copy(out=dst_ap, in_=ps_ap), r=[psk] + list(extra_r), w=[dstk])
            else:
                S.op("dve", lambda e: e.tensor_scalar(out=dst_ap, in0=ps_ap, scalar1=scale, scalar2=None, op0=ALU.mult),
                     r=[psk] + list(extra_r), w=[dstk])

    def phase_inproj(l, x_src):
        mixer = l % 2
        j = l // 2
        if mixer == 0:
            W = mw_in[j]
            fm_blocks = 4
            qscale = 256 ** -0.5
            nq_chunks = 8
            tm_c0, tm_blocks = 1024, 10
        else:
            W = fw_in[j]
            fm_blocks = 8
            qscale = 128 ** -0.5
            nq_chunks = 16
            tm_c0, tm_blocks = 4096, 4
        def body(tt):
            load_x(x_src, tt)
            norm_tile(gam1[:, l * 16:(l + 1) * 16], modsb[:, l * 96:l * 96 + 16], l, [("gam1", l), ("mod", l)])
            blocks = []
            for bi in range(fm_blocks):
                blocks.append(lambda bi=bi: wload(W, 0, 16, bi * 512, 512))
            for bi in range(tm_blocks):
                blocks.append(lambda bi=bi: wload(W, 0, 16, tm_c0 + bi * 512, 512))
            blocks.append(lambda: wload(W, 0, 16, 6144, 8 if mixer == 0 else 16))

            def cons(jb, b, key):
                if jb < fm_blocks:
                    for oc4 in range(4):
                        oc = jb * 4 + oc4
                        p, pk = psum()
                        for kc in range(16):
                            S.op("pe", lambda e, p=p, b=b, kc=kc, oc4=oc4: e.matmul(
                                p[:, :], b[:, kc, oc4 * 128:(oc4 + 1) * 128], hb[:, kc, :],
                                start=(kc == 0), stop=(kc == 15)), r=[key, ("hb", kc)], w=[pk])
                        evac_bf16(p[:, :], pk, act[:, oc, :], ("act", oc), scale=(qscale if oc < nq_chunks else None))
                    if jb == fm_blocks - 1:
                        nfm = fm_blocks * 4
                        S.op("sp", lambda e: e.dma_start(
                            out=qkT_d.rearrange("(c p) t -> p c t", p=128)[:, 0:nfm, TS(tt, TT)], in_=act[:, 0:nfm, :]),
                            r=[("act", c) for c in range(nfm)], w=[("qkT", c) for c in range(32)], dma=True)
                elif jb < fm_blocks + tm_blocks:
                    nb = jb - fm_blocks
                    for m in range(4):
                        p, pk = psum()
                        for kc in range(16):
                            S.op("pe", lambda e, p=p, b=b, kc=kc, m=m: e.matmul(
                                p[:, :], hb[:, kc, m * 128:(m + 1) * 128], b[:, kc, :],
                                start=(kc == 0), stop=(kc == 15)), r=[key, ("hb", kc)], w=[pk])
                        is_o = (mixer == 0 and nb >= 6)
                        s4 = nb % 4
                        evac_bf16(p[:, :], pk, vst3[:, m, s4 * 512:(s4 + 1) * 512], ("vst", m, s4),
                                  func=(AF.Sigmoid if is_o else None))
                    if mixer == 0:
                        s4 = nb % 4
                        S.op("sp", lambda e, nb=nb, s4=s4: e.dma_start(
                            out=tmb[nb].rearrange("(a m p) c -> a p m c", m=4, p=128)[TS(tt, 1)]
                            .rearrange("a p m c -> (a p) m c"), in_=vst3[:, :, s4 * 512:(s4 + 1) * 512]),
                            r=[("vst", m, s4) for m in range(4)], w=[("tm",)], dma=True)
                    elif nb == tm_blocks - 1:
                        S.op("sp", lambda e: e.dma_start(
                            out=tm_d.rearrange("(a m p) c -> a p m c", m=4, p=128)[TS(tt, 1)]
                            .rearrange("a p m c -> (a p) m c"), in_=vst3[:, :, :]),
                            r=[("vst", m, x) for m in range(4) for x in range(4)], w=[("tm",)], dma=True)
                else:
                    if mixer == 0:
                        for gi in range(2):
                            p, pk = psum()
                            for kc in range(16):
                                S.op("pe", lambda e, p=p, b=b, kc=kc, gi=gi: e.matmul(
                                    p[0:4, :], b[:, kc, gi * 4:gi * 4 + 4], hb[:, kc, :],
                                    start=(kc == 0), stop=(kc == 15)), r=[key, ("hb", kc)], w=[pk])
                            sf, sfk = stgf.next()
                            S.op("dve", lambda e, p=p, sf=sf: e.tensor_copy(out=sf[0:4, :], in_=p[0:4, :]), r=[pk], w=[sfk])
                            dst = gate_d if gi == 0 else gate2_d
                            S.op("sp", lambda e, sf=sf, dst=dst: e.dma_start(
                                out=dst[0:4, TS(tt, TT)], in_=sf[0:4, :]), r=[sfk], w=[("gate", gi)], dma=True)
                    else:
                        p, pk = psum()
                        for kc in range(16):
                            S.op("pe", lambda e, p=p, b=b, kc=kc: e.matmul(
                                p[0:16, :], b[:, kc, 0:16], hb[:, kc, :],
                                start=(kc == 0), stop=(kc == 15)), r=[key, ("hb", kc)], w=[pk])
                        sf, sfk = stgf.next()
                        S.op("dve", lambda e, p=p, sf=sf: e.tensor_copy(out=sf[0:16, :], in_=p[0:16, :]), r=[pk], w=[sfk])
                        S.op("sp", lambda e, sf=sf: e.dma_start(
                            out=gate_d[0:16, TS(tt, TT)], in_=sf[0:16, :]), r=[sfk], w=[("gate", 0)], dma=True)
            wstream(blocks, cons)
        S.loop(NT, body, reset_w)

    def phase_post(l, x_src, last):
        mixer = l % 2
        j = l // 2
        Wo = mw_out[j] if mixer == 0 else fw_out[j]
        def fin_factory(tt):
            def fin(c, t, tk):
                S.op("act", lambda e, c=c, t=t: e.activation(out=xs[:, c, :], in_=t[:], func=AF.Identity,
                                                            scale=fnw_s[:, c:c + 1]), r=[tk, "fnw_s"], w=[("xs", c)])
                if c == 15:
                    S.op("sp", lambda e: e.dma_start(
                        out=outT[:, TS(tt, TT)].rearrange("(c p) t -> p c t", p=128), in_=xs[:, :, :]),
                        r=[("xs", x) for x in range(16)], w=[("out",)], dma=True)
            return fin

        def body(tt):
            load_x(x_src, tt)
            S.op("sp", lambda e: e.dma_start(
                out=act[:, 16:32, :], in_=oT_d[:, TS(tt, TT)].rearrange("(c p) t -> p c t", p=128)),
                r=[("oT",)], w=[("act", c) for c in range(16, 32)], dma=True)
            blocks = [(lambda ob=ob: wload(Wo, 0, 16, ob * 512, 512)) for ob in range(4)]

            def cons_o(ob, b, key):
                for oc4 in range(4):
                    oc = ob * 4 + oc4
                    p, pk = psum()
                    for kc in range(16):
                        S.op("pe", lambda e, p=p, b=b, kc=kc, oc4=oc4: e.matmul(
                            p[:, :], b[:, kc, oc4 * 128:(oc4 + 1) * 128], act[:, 16 + kc, :],
                            start=(kc == 0), stop=(kc == 15)), r=[key, ("act", 16 + kc)], w=[pk])
                    S.op("dve", lambda e, p=p, oc=oc: e.scalar_tensor_tensor(
                        out=xs[:, oc, :], in0=p[:, :], scalar=modcol(l, 2, oc), in1=xs[:, oc, :],
                        op0=ALU.mult, op1=ALU.add), r=[pk, ("mod", l), ("xs", oc)], w=[("xs", oc)])
            wstream(blocks, cons_o)
            import os as _os
            if _os.environ.get("DBG_SKIP_FFN"):
                norm_tile(None, None, l, [], out_fn=fin_factory(tt))
                return
            norm_tile(gam2[:, l * 16:(l + 1) * 16], modsb[:, l * 96 + 48:l * 96 + 64], l, [("gam2", l), ("mod", l)])
            blocks = []
            for fb in range(22):
                blocks.append(lambda fb=fb: wload(w_gu[l], 0, 16, fb * 256, 256))
                blocks.append(lambda fb=fb: wload(w_gu[l], 0, 16, DFF + fb * 256, 256))
            hold = {}

            def cons_gu(jb, b, key):
                if jb % 2 == 0:
                    hold["g"] = (b, key)
                    return
                fb = jb // 2
                gb, gk = hold["g"]
                ub, uk = b, key
                for f2 in range(2):
                    fc = fb * 2 + f2
                    pg, pgk = psum()
                    pu, puk = psum()
                    for kc in range(16):
                        S.op("pe", lambda e, pg=pg, gb=gb, kc=kc, f2=f2: e.matmul(
                            pg[:, :], gb[:, kc, f2 * 128:(f2 + 1) * 128], hb[:, kc, :],
                            start=(kc == 0), stop=(kc == 15)), r=[gk, ("hb", kc)], w=[pgk])
                    for kc in range(16):
                        S.op("pe", lambda e, pu=pu, ub=ub, kc=kc, f2=f2: e.matmul(
                            pu[:, :], ub[:, kc, f2 * 128:(f2 + 1) * 128], hb[:, kc, :],
                            start=(kc == 0), stop=(kc == 15)), r=[uk, ("hb", kc)], w=[puk])
                    t, tk = tmpr.next()
                    S.op("act", lambda e, pg=pg, t=t: e.activation(out=t[:], in_=pg[:, :], func=AF.Silu), r=[pgk], w=[tk])
                    S.op("dve", lambda e, pu=pu, t=t, fc=fc: e.tensor_tensor(out=act[:, fc, :], in0=t[:], in1=pu[:, :],
                                                                           op=ALU.mult), r=[tk, puk], w=[("act", fc)])
            wstream(blocks, cons_gu, la=1)
            for cb in range(4):
                accs = [psum() for _ in range(4)]
                blocks = [(lambda kg=kg, cb=cb: wload(w_dn[l], kg * 16, (16 if kg < 2 else 12), cb * 512, 512)) for kg in range(3)]

                def cons_d(kg, b, key, accs=accs, cb=cb):
                    nk = 16 if kg < 2 else 12
                    for oc4 in range(4):
                        p, pk = accs[oc4]
                        for kc in range(nk):
                            S.op("pe", lambda e, p=p, b=b, kc=kc, oc4=oc4, kg=kg, nk=nk: e.matmul(
                                p[:, :], b[:, kc, oc4 * 128:(oc4 + 1) * 128], act[:, kg * 16 + kc, :],
                                start=(kg == 0 and kc == 0), stop=(kg == 2 and kc == nk - 1)),
                                r=[key, ("act", kg * 16 + kc)], w=[pk])
                wstream(blocks, cons_d)
                for oc4 in range(4):
                    oc = cb * 4 + oc4
                    p, pk = accs[oc4]
                    S.op("dve", lambda e, p=p, oc=oc: e.scalar_tensor_tensor(
                        out=xs[:, oc, :], in0=p[:, :], scalar=modcol(l, 5, oc), in1=xs[:, oc, :],
                        op0=ALU.mult, op1=ALU.add), r=[pk, ("mod", l), ("xs", oc)], w=[("xs", oc)])
            if not last:
                S.op("sp", lambda e: e.dma_start(
                    out=xs_d[:, TS(tt, TT)].rearrange("(c p) t -> p c t", p=128), in_=xs[:, :, :]),
                    r=[("xs", c) for c in range(16)], w=[("xd",)], dma=True)
            else:
                norm_tile(None, None, l, [], out_fn=fin_factory(tt))
        S.loop(NT, body, reset_w)

    def phase_fox(l):
        j = l // 2
        if True:
            S.reset(m0)

            def sb2(shape, dt, name):
                return S.sb(shape, dt, name)

            rS, rT, rO, rM = Ring([0, 1]), Ring([2, 3]), Ring([4, 5]), Ring([6, 7])

            def bank(ring):
                i_ = ring.next()
                return psb[i_], ("ps", i_)
            fz = sb2([16, T], F32, "fz")
            nbf = sb2([16, 1], F32, "nbf")
            bfs = sb2([16, 1], F32, "bfs")
            ncb = sb2([128, T], F32, "ncb")
            qh = [sb2([128, T], BF16, f"qh{i}") for i in range(1)]
            kh = [sb2([128, T], BF16, f"kh{i}") for i in range(1)]
            vh = [sb2([128, NQ, 129], BF16, f"vh{i}") for i in range(1)]
            sq = sb2([128, T], BF16, "sqb")
            q2 = sb2([128, NQ], F32, "q2")
            km = sb2([128, 16], F32, "km")
            km1 = sb2([128, 1], F32, "km1")
            negm = sb2([128, NQ], F32, "negm")
            ssb = Ring([(sb2([128, 512], F32, f"ssb{i}"), ("ssb", i)) for i in range(4)])
            pbf = Ring([(sb2([128, 512], BF16, f"pbf{i}"), ("pbf", i)) for i in range(3)])
            oTh = [sb2([128, T], BF16, f"oTh{i}") for i in range(1)]

            S.op("sp", lambda e: e.dma_start(out=fz[:, :], in_=gate_d[0:16, :]),
                 r=[("gate", 0)], w=["fz"], dma=True)
            S.op("sp", lambda e: e.dma_start(out=bfs[:, :], in_=fbf[j * 16:(j + 1) * 16, 0:1]), w=["bfs"], dma=True)
            S.op("dve", lambda e: e.tensor_scalar(out=nbf[:], in0=bfs[:], scalar1=-1.0, scalar2=None, op0=ALU.mult),
                 r=["bfs"], w=["nbf"])
            S.op("act", lambda e: e.activation(out=fz[:, :], in_=fz[:, :], func=AF.Exp, bias=nbf[:, 0:1], scale=-1.0),
                 r=["fz", "nbf"], w=["fz"])
            S.op("act", lambda e: e.activation(out=fz[:, :], in_=fz[:, :], func=AF.Ln, bias=onec[0:16, 0:1], scale=1.0),
                 r=["fz", "onec"], w=["fz"])
            S.op("dve", lambda e: e.memset(ncb[0:16, :], 1.0), w=["ncb"])
            SEG = 1024
            for sg in range(T // SEG):
                a0, a1 = sg * SEG, (sg + 1) * SEG
                S.op("dve", lambda e, a0=a0, a1=a1, sg=sg: e.tensor_tensor_scan(
                    out=fz[:, a0:a1], data0=ncb[0:16, a0:a1], data1=fz[:, a0:a1],
                    initial=(0.0 if sg == 0 else fz[:, a0 - 1:a0]), op0=ALU.mult, op1=ALU.add),
                    r=["ncb", "fz"], w=["fz"])
            S.op("sp", lambda e: e.dma_start(out=ncum_d[:, :], in_=fz[:, :]), r=["fz"], w=["ncum_d"], dma=True)
            nqt = sb2([NQ, 128], F32, "nqt")
            maskT = sb2([128, 4 * 512], F32, "maskT")
            S.op("act", lambda e: e.dma_start(out=maskT[:, :], in_=consts[:, 2560:2560 + 2048]), w=["maskT"], dma=True)
            ncq = sb2([128, NQ], F32, "ncq")

            def load_head(h):
                i = 0
                S.op("act", lambda e: e.dma_start(out=qh[i][:, :], in_=qkT_d[TS(h, 128), :]),
                     r=[("qkT", oc) for oc in range(32)], w=[("qh", i)], dma=True)
                S.op("act", lambda e: e.dma_start(out=kh[i][:, :], in_=qkT_d[D:2 * D, :][TS(h, 128), :]),
                     r=[("qkT", oc) for oc in range(32)], w=[("kh", i)], dma=True)
                S.op("act", lambda e: e.dma_start(
                    out=vh[i][:, :, 0:128],
                    in_=tm_d[:, TS(h, 128)].rearrange("(j p) d -> p j d", p=128)),
                    r=[("tm",)], w=[("vh", i)], dma=True)
                S.op("pool", lambda e: e.memset(vh[i][:, :, 128:129], 1.0), r=[], w=[("vh1", i)])
                S.op("act", lambda e: e.dma_start(out=ncb[:, :], in_=ncum_d[TS(h, 1), :].to_broadcast([128, T])),
                     r=["ncum_d"], w=["ncb"], dma=True)
                S.op("act", lambda e: e.dma_start(out=nqt[:, :],
                                                 in_=ncum_d[TS(h, 1), :].rearrange("o (q p) -> (o q) p", p=128)),
                     r=["ncum_d"], w=["nqt"], dma=True)
                p, pk = bank(rM)
                S.op("pe", lambda e, p=p: e.transpose(p[:, 0:NQ], nqt[0:NQ, :], ident_f[0:NQ, 0:NQ]),
                     r=["nqt", "ident_f"], w=[pk])
                S.op("dve", lambda e, p=p: e.tensor_copy(out=ncq[:, :], in_=p[:, 0:NQ]), r=[pk], w=["ncq"])

            def head_body(h):
                i = 0
                load_head(h)
                S.op("act", lambda e, i=i: e.activation(out=sq[:, :], in_=kh[i][:, :], func=AF.Square),
                     r=[("kh", i)], w=["sq"])
                for t4 in range(T // 512):
                    p, pk = bank(rM)
                    S.op("pe", lambda e, p=p, t4=t4: e.matmul(p[:, :], ones1[:, :], sq[:, t4 * 512:(t4 + 1) * 512],
                                                             start=True, stop=True), r=["sq", "ones1"], w=[pk])
                    S.op("dve", lambda e, p=p, t4=t4: e.tensor_reduce(out=km[:, t4:t4 + 1], in_=p[:, :], axis=AX.X,
                                                                     op=ALU.max), r=[pk], w=[("km", t4)])
                S.op("dve", lambda e: e.tensor_reduce(out=km1[:, 0:1], in_=km[:, 0:T // 512], axis=AX.X, op=ALU.max),
                     r=[("km", t4) for t4 in range(T // 512)], w=["km1"])
                S.op("dve", lambda e: e.tensor_scalar(out=km1[:, 0:1], in0=km1[:, 0:1], scalar1=1.05, scalar2=None,
                                                      op0=ALU.mult), r=["km1"], w=["km1"])
                S.op("act", lambda e, i=i: e.activation(out=sq[:, :], in_=qh[i][:, :], func=AF.Square),
                     r=[("qh", i)], w=["sq"])
                for t4 in range(T // 512):
                    p, pk = bank(rM)
                    S.op("pe", lambda e, p=p, t4=t4: e.matmul(p[:, :], ones1[:, :], sq[:, t4 * 512:(t4 + 1) * 512],
                                                             start=True, stop=True), r=["sq", "ones1"], w=[pk])
                    mq, mqk = ssb.next()
                    S.op("act", lambda e, p=p, mq=mq: e.activation(out=mq[:, :], in_=p[:, :], func=AF.Sqrt,
                                                                  scale=km1[:, 0:1]), r=[pk, "km1"], w=[mqk])
                    S.op("dve", lambda e, mq=mq, t4=t4: e.scalar_tensor_tensor(
                        out=ncb[:, t4 * 512:(t4 + 1) * 512], in0=ncb[:, t4 * 512:(t4 + 1) * 512], scalar=-1.0,
                        in1=mq[:, :], op0=ALU.mult, op1=ALU.subtract), r=[mqk, "ncb"], w=["ncb"])
                steps = []
                for qt in range(T // 512):
                    nkb = 4 * qt + 4
                    for kb_ in range(nkb):
                        steps.append(dict(qt=qt, kb=kb_, first=(kb_ == 0), last=(kb_ == nkb - 1),
                                          diag=(kb_ - 4 * qt if kb_ >= 4 * qt else None)))
                po = {}

                def st_qk(s):
                    p, pk = bank(rS)
                    qt, kb_ = s["qt"], s["kb"]
                    S.op("pe", lambda e: e.matmul(p[:, :], kh[i][:, kb_ * 128:(kb_ + 1) * 128],
                                                  qh[i][:, qt * 512:(qt + 1) * 512], start=True, stop=True),
                         r=[("qh", i), ("kh", i)], w=[pk])
                    sb_, sbk = ssb.next()
                    S.op("dve", lambda e: e.tensor_tensor(out=sb_[:, :], in0=p[:, :], in1=ncb[:, qt * 512:(qt + 1) * 512],
                                                          op=ALU.add), r=[pk, "ncb"], w=[sbk])
                    if s["diag"] is not None:
                        dj = s["diag"]
                        S.op("pool", lambda e: e.tensor_tensor(out=sb_[:, :], in0=sb_[:, :],
                                                               in1=maskT[:, dj * 512:(dj + 1) * 512], op=ALU.add),
                             r=[sbk, "maskT"], w=[sbk])
                    pb, pbk = pbf.next()
                    s["pb"], s["pbk"] = pb, pbk
                    S.op("act", lambda e: e.activation(out=pb[:, :], in_=sb_[:, :], func=AF.Exp,
                                                       bias=ncq[:, kb_:kb_ + 1], scale=1.0),
                         r=[sbk, "ncq"], w=[pbk])

                def st_pv(s):
                    qt, kb_ = s["qt"], s["kb"]
                    if s["first"]:
                        po["o"], po["ok"] = bank(rO)
                        po["d"], po["dk"] = bank(rT)
                    pO, pOk, pD, pDk = po["o"], po["ok"], po["d"], po["dk"]
                    pb = s["pb"]
                    S.op("pe", lambda e: e.matmul(pO[:, :], vh[i][:, kb_, 0:128], pb[:, :],
                                                  start=s["first"], stop=s["last"]),
                         r=[s["pbk"], ("vh", i)], w=[pOk])
                    S.op("pe", lambda e: e.matmul(pD[:, :], ones1[:, :], pb[:, :],
                                                  start=s["first"], stop=s["last"]),
                         r=[s["pbk"], "ones1"], w=[pDk])
                    if s["last"]:
                        rd, rdk = ssb.next()
                        S.op("dve", lambda e: e.reciprocal(out=rd[:, :], in_=pD[:, :]), r=[pDk], w=[rdk])
                        S.op("dve", lambda e: e.tensor_tensor(out=oTh[i][:, qt * 512:(qt + 1) * 512], in0=pO[:, :],
                                                              in1=rd[:, :], op=ALU.mult),
                             r=[pOk, rdk], w=[("oTh", i, qt)])

                ns = len(steps)
                for n in range(ns + 2):
                    if n < ns:
                        st_qk(steps[n])
                    if 0 <= n - 2 < ns:
                        st_pv(steps[n - 2])
                S.op("act", lambda e, i=i: e.dma_start(out=oT_d[TS(h, 128), :], in_=oTh[i][:, :]),
                     r=[("oTh", i, qt) for qt in range(T // 512)], w=[("oT",)], dma=True)
            S.loop(16, head_body)
            S.barrier()

    def phase_mlstm(l):
        j = l // 2
        S.reset(m0)
        A = S.sb([4, T], F32, "gA")
        B = S.sb([4, T], F32, "gB")
        bg = S.sb([4, 2], F32, "bg")
        bg15 = S.sb([4, 2], F32, "bg15")
        ref = S.sb([4, NQ], F32, "ref")
        Gn = S.sb([4, NQ], F32, "Gn")
        Gp = S.sb([4, NQ], F32, "Gp")
        r1 = S.sb([4, NQ], F32, "r1")
        eT = S.sb([128, NQ * 4], F32, "eT")
        flT = S.sb([128, NQ * 4], F32, "flT")
        r1b = S.sb([128, 4 * NQ], F32, "r1b")
        nwb = S.sb([128, 2048], F32, "nwb")
        Cs = S.sb([128, 2, 512], F32, "Cs")
        Cb = S.sb([128, 2, 512], BF16, "Cb")
        ns = S.sb([128, 2], F32, "ns")
        nb_ = S.sb([128, 2], BF16, "nb_")
        qTb = [S.sb([128, 2, 512], BF16, f"qTb{i}") for i in range(2)]
        kTb = [S.sb([128, 2, 512], BF16, f"kTb{i}") for i in range(2)]
        ktm = [S.sb([128, 4, 256], BF16, f"ktm{i}") for i in range(2)]
        vb = [S.sb([128, 4, 512], BF16, f"vb{i}") for i in range(2)]
        ob = [S.sb([128, 4, 512], BF16, f"ob{i}") for i in range(2)]
        yTb = [S.sb([128, 4, 512], BF16, f"yTb{i}") for i in range(2)]
        PT = Ring([(S.sb([128, 128], BF16, f"PT{i}"), ("PT", i)) for i in range(2)])
        Kt = Ring([(S.sb([128, 256], BF16, f"Kt{i}"), ("Kt", i)) for i in range(2)])
        d1 = Ring([(S.sb([128, 4], F32, f"d1{i}"), ("d1", i)) for i in range(2)])
        junk = S.sb([128, 512], BF16, "junk")
        ytmp = Ring([(S.sb([128, 512], F32, f"ytmp{i}"), ("ytmp", i)) for i in range(2)])
        ybf = Ring([(S.sb([128, 512], BF16, f"ybf{i}"), ("ybf", i)) for i in range(2)])

        gdeps = [("gate", gi) for gi in range(2)]
        S.op("sp", lambda e: e.dma_start(out=A[:, :], in_=gate_d[0:4, :]), r=gdeps, w=["gA"], dma=True)
        S.op("sp", lambda e: e.dma_start(out=B[:, :], in_=gate2_d[0:4, :]), r=gdeps, w=["gB"], dma=True)
        S.op("sp", lambda e: e.dma_start(out=bg[:, :], in_=mbg[:, 2 * j:2 * j + 2]), w=["bg"], dma=True)
        S.op("sp", lambda e: e.dma_start(out=nwb[:, :], in_=mnw[:, j * 2048:(j + 1) * 2048]), w=["nwb"], dma=True)
        S.op("dve", lambda e: e.tensor_scalar(out=bg15[:, :], in0=bg[:, :], scalar1=1.0 / 15.0, scalar2=None,
                                              op0=ALU.mult), r=["bg"], w=["bg15"])
        S.op("act", lambda e: e.activation(out=A[:, :], in_=A[:, :], func=AF.Tanh, bias=bg15[:, 0:1], scale=1.0 / 15.0),
             r=["gA", "bg15"], w=["gA"])
        S.op("act", lambda e: e.activation(out=B[:, :], in_=B[:, :], func=AF.Tanh, bias=bg15[:, 1:2], scale=1.0 / 15.0),
             r=["gB", "bg15"], w=["gB"])
        S.op("act", lambda e: e.activation(out=B[:, :], in_=B[:, :], func=AF.Exp, scale=-15.0), r=["gB"], w=["gB"])
        S.op("act", lambda e: e.activation(out=B[:, :], in_=B[:, :], func=AF.Ln, bias=onec[0:4, 0:1], scale=1.0), r=["gB", "onec"], w=["gB"])
        onesT = S.sb([4, T], F32, "onesT")
        S.op("pool", lambda e: e.memset(onesT[:, :], 1.0), w=["onesT"])
        SEG = 1024
        for sg in range(T // SEG):
            a0, a1 = sg * SEG, (sg + 1) * SEG
            S.op("dve", lambda e, a0=a0, a1=a1, sg=sg: e.tensor_tensor_scan(
                out=B[:, a0:a1], data0=onesT[:, a0:a1], data1=B[:, a0:a1],
                initial=(0.0 if sg == 0 else B[:, a0 - 1:a0]), op0=ALU.mult, op1=ALU.add),
                r=["gB", "onesT"], w=["gB"])
        S.op("dve", lambda e: e.scalar_tensor_tensor(out=A[:, :], in0=A[:, :], scalar=15.0, in1=B[:, :],
                                                    op0=ALU.mult, op1=ALU.add), r=["gA", "gB"], w=["gA"])
        A3 = A.rearrange("p (c s) -> p c s", c=NQ, s=128)
        B3 = B.rearrange("p (c s) -> p c s", c=NQ, s=128)
        S.op("dve", lambda e: e.tensor_reduce(out=ref[:, :], in_=A3, axis=AX.X, op=ALU.max), r=["gA"], w=["ref"])
        S.op("dve", lambda e: e.tensor_tensor_scan(out=Gn[:, :], data0=ref[:, :], data1=ref[:, :], initial=0.0,
                                                  op0=ALU.max, op1=ALU.max), r=["ref"], w=["Gn"])
        S.op("dve", lambda e: e.memset(Gp[:, 0:1], 0.0), w=["Gp0"])
        if NQ > 1:
            S.op("dve", lambda e: e.tensor_copy(out=Gp[:, 1:NQ], in_=Gn[:, 0:NQ - 1]), r=["Gn"], w=["Gp1"])
        S.op("dve", lambda e: e.tensor_tensor(out=r1[:, :], in0=Gp[:, :], in1=Gn[:, :], op=ALU.subtract),
             r=["Gn", "Gp0", "Gp1"], w=["r1"])
        S.op("act", lambda e: e.activation(out=r1[:, :], in_=r1[:, :], func=AF.Exp), r=["r1"], w=["r1"])
        for c in range(NQ):
            S.op("dve", lambda e, c=c: e.tensor_scalar(out=A[:, c * 128:(c + 1) * 128], in0=A[:, c * 128:(c + 1) * 128],
                                                      scalar1=Gn[:, c:c + 1], scalar2=None, op0=ALU.subtract),
                 r=["gA", "Gn"], w=["gA"])
            S.op("pool", lambda e, c=c: e.tensor_scalar(out=B[:, c * 128:(c + 1) * 128], in0=B[:, c * 128:(c + 1) * 128],
                                                       scalar1=Gn[:, c:c + 1], scalar2=None, op0=ALU.subtract),
                 r=["gB", "Gn"], w=["gB"])
        S.op("act", lambda e: e.activation(out=A[:, :], in_=A[:, :], func=AF.Exp), r=["gA"], w=["gA"])
        S.op("act", lambda e: e.activation(out=B[:, :], in_=B[:, :], func=AF.Exp), r=["gB"], w=["gB"])
        for (src, dst, dk, sk) in ((A, eT, "eT", "gA"), (B, flT, "flT", "gB")):
            for g0 in range(0, NQ, 64):
                p, pk = psum()
                n_in = min(64, NQ - g0)
                for c in range(g0, g0 + n_in):
                    S.op("pe", lambda e, p=p, c=c, g0=g0, src=src: e.transpose(
                        p[:, (c - g0) * 4:(c - g0 + 1) * 4], src[0:4, c * 128:(c + 1) * 128], ident_f[0:4, 0:4]),
                        r=[sk, "ident_f"], w=[pk])
                S.op("dve", lambda e, p=p, g0=g0, n_in=n_in, dst=dst: e.tensor_copy(
                    out=dst[:, g0 * 4:(g0 + n_in) * 4], in_=p[:, 0:n_in * 4]), r=[pk], w=[dk])
        p, pk = psum()
        for h in range(4):
            S.op("pe", lambda e, p=p, h=h: e.matmul(p[:, h * NQ:(h + 1) * NQ], sel16[0:4, h * 128:(h + 1) * 128],
                                                   r1[0:4, 0:NQ], start=True, stop=True), r=["sel16", "r1"], w=[pk])
        S.op("dve", lambda e, p=p: e.tensor_copy(out=r1b[:, :], in_=p[:, 0:4 * NQ]), r=[pk], w=["r1b"])

        NB = T // 512
        for h in range(4):
            S.op("dve", lambda e: e.memset(Cs[:, :, :], 0.0), w=["Cs"])
            S.op("dve", lambda e: e.memset(ns[:, :], 0.0), w=["ns"])
            for tb in range(NB):
                bi = (h * NB + tb) % 2
                t0 = tb * 512
                S.op("sp", lambda e, bi=bi, t0=t0, h=h: e.dma_start(
                    out=qTb[bi][:, :, :], in_=qkT_d[h * 256:(h + 1) * 256, t0:t0 + 512].rearrange("(c p) t -> p c t", p=128)),
                    r=[("qkT", oc) for oc in range(16)], w=[("qTb", bi)], dma=True)
                S.op("sp", lambda e, bi=bi, t0=t0, h=h: e.dma_start(
                    out=kTb[bi][:, :, :], in_=qkT_d[1024 + h * 256:1024 + (h + 1) * 256, t0:t0 + 512].rearrange("(c p) t -> p c t", p=128)),
                    r=[("qkT", oc) for oc in range(16)], w=[("kTb", bi)], dma=True)
                S.op("sp", lambda e, bi=bi, t0=t0, h=h: e.dma_start(
                    out=ktm[bi][:, :, :], in_=tmb[h // 2][t0:t0 + 512, (h % 2) * 256:(h % 2 + 1) * 256].rearrange("(j p) d -> p j d", p=128)),
                    r=[("tm",)], w=[("ktm", bi)], dma=True)
                S.op("sp", lambda e, bi=bi, t0=t0, h=h: e.dma_start(
                    out=vb[bi][:, :, :], in_=tmb[2 + h][t0:t0 + 512, :].rearrange("(j p) d -> p j d", p=128)),
                    r=[("tm",)], w=[("vb", bi)], dma=True)
                S.op("sp", lambda e, bi=bi, t0=t0, h=h: e.dma_start(
                    out=ob[bi][:, :, :], in_=tmb[6 + h][t0:t0 + 512, :].rearrange("(j p) d -> p j d", p=128)),
                    r=[("tm",)], w=[("ob", bi)], dma=True)
                for cc in range(4):
                    c = tb * 4 + cc
                    ecol = eT[:, c * 4 + h:c * 4 + h + 1]
                    fcol = flT[:, c * 4 + h:c * 4 + h + 1]
                    rcol = r1b[:, h * NQ + c:h * NQ + c + 1]
                    tsl = slice(cc * 128, (cc + 1) * 128)
                    p1, p1k = psum()
                    for c2 in range(2):
                        S.op("pe", lambda e, c2=c2, p1=p1, bi=bi, tsl=tsl: e.matmul(
                            p1[:, 0:128], kTb[bi][:, c2, tsl], qTb[bi][:, c2, tsl], start=(c2 == 0), stop=(c2 == 1)),
                            r=[("kTb", bi), ("qTb", bi)], w=[p1k])
                    pt, ptk = PT.next()
                    S.op("dve", lambda e, p1=p1, pt=pt, ecol=ecol: e.scalar_tensor_tensor(
                        out=pt[:, :], in0=p1[:, 0:128], scalar=ecol, in1=mask01[:, :], op0=ALU.mult, op1=ALU.mult),
                        r=[p1k, "eT", "mask01"], w=[ptk])
                    S.op("dve", lambda e, rcol=rcol: e.tensor_scalar(out=Cs[:, :, :], in0=Cs[:, :, :], scalar1=rcol,
                                                                    scalar2=None, op0=ALU.mult), r=["Cs", "r1b"], w=["Cs"])
                    S.op("act", lambda e: e.activation(out=Cb[:, :, :], in_=Cs[:, :, :], func=AF.Identity), r=["Cs"], w=["Cb"])
                    S.op("dve", lambda e, rcol=rcol: e.tensor_scalar(out=ns[:, :], in0=ns[:, :], scalar1=rcol,
                                                                    scalar2=None, op0=ALU.mult), r=["ns", "r1b"], w=["ns"])
                    S.op("dve", lambda e: e.tensor_copy(out=nb_[:, :], in_=ns[:, :]), r=["ns"], w=["nb_"])
                    p2, p2k = psum()
                    for c2 in range(2):
                        S.op("pe", lambda e, c2=c2, p2=p2, bi=bi, tsl=tsl: e.matmul(
                            p2[:, :], qTb[bi][:, c2, tsl], Cb[:, c2, :], start=(c2 == 0), stop=False),
                            r=[("qTb", bi), "Cb"], w=[p2k])
                    S.op("pe", lambda e, p2=p2, pt=pt, bi=bi, cc=cc: e.matmul(
                        p2[:, :], pt[:, :], vb[bi][:, cc, :], start=False, stop=True), r=[ptk, ("vb", bi)], w=[p2k])
                    p3, p3k = psum()
                    for c2 in range(2):
                        S.op("pe", lambda e, c2=c2, p3=p3, bi=bi, tsl=tsl: e.matmul(
                            p3[:, 0:1], qTb[bi][:, c2, tsl], nb_[:, c2:c2 + 1], start=(c2 == 0), stop=False),
                            r=[("qTb", bi), "nb_"], w=[p3k])
                    S.op("pe", lambda e, p3=p3, pt=pt: e.matmul(p3[:, 0:1], pt[:, :], ones1[:, 0:1], start=False, stop=True),
                         r=[ptk, "ones1"], w=[p3k])
                    kt_, ktk = Kt.next()
                    S.op("pool", lambda e, kt_=kt_, bi=bi, cc=cc, ecol=ecol: e.tensor_scalar(
                        out=kt_[:, :], in0=ktm[bi][:, cc, :], scalar1=ecol, scalar2=None, op0=ALU.mult),
                        r=[("ktm", bi), "eT"], w=[ktk])
                    p5, p5k = psum()
                    for c2 in range(2):
                        p4, p4k = psum()
                        S.op("pe", lambda e, c2=c2, p4=p4, kt_=kt_, bi=bi, cc=cc: e.matmul(
                            p4[:, :], kt_[:, c2 * 128:(c2 + 1) * 128], vb[bi][:, cc, :], start=True, stop=True),
                            r=[ktk, ("vb", bi)], w=[p4k])
                        S.op("pe", lambda e, c2=c2, p5=p5, kt_=kt_: e.matmul(
                            p5[:, c2:c2 + 1], kt_[:, c2 * 128:(c2 + 1) * 128], ones1[:, 0:1], start=True, stop=True),
                            r=[ktk, "ones1"], w=[p5k])
                        S.op("dve", lambda e, c2=c2, p4=p4: e.tensor_tensor(out=Cs[:, c2, :], in0=Cs[:, c2, :], in1=p4[:, :],
                                                                          op=ALU.add), r=[p4k, "Cs", "Cb"], w=["Cs"])
                    S.op("dve", lambda e, p5=p5: e.tensor_tensor(out=ns[:, :], in0=ns[:, :], in1=p5[:, 0:2], op=ALU.add),
                         r=[p5k, "ns", "nb_"], w=["ns"])
                    dd, ddk = d1.next()
                    S.op("act", lambda e, dd=dd, p3=p3: e.activation(out=dd[:, 0:1], in_=p3[:, 0:1], func=AF.Abs),
                         r=[p3k], w=[(ddk, 0)])
                    S.op("dve", lambda e, dd=dd, fcol=fcol: e.tensor_scalar(
                        out=dd[:, 0:1], in0=dd[:, 0:1], scalar1=fcol, scalar2=None, op0=ALU.max),
                        r=[(ddk, 0), "flT"], w=[(ddk, 0)])
                    S.op("dve", lambda e, dd=dd: e.reciprocal(out=dd[:, 1:2], in_=dd[:, 0:1]), r=[(ddk, 0)], w=[(ddk, 1)])
                    S.op("pool", lambda e, dd=dd: e.memset(dd[:, 2:3], 0.0), w=[(ddk, 2)])
                    S.op("act", lambda e, dd=dd, p2=p2: e.activation(out=junk[:, :], in_=p2[:, :], func=AF.Square,
                                                                    scale=dd[:, 1:2], accum_out=dd[:, 2:3]),
                         r=[p2k, (ddk, 1), (ddk, 2)], w=[(ddk, 2), "junk"])
                    S.op("act", lambda e, dd=dd: e.activation(out=dd[:, 3:4], in_=dd[:, 2:3], func=AF.Ln,
                                                             bias=epsc[:, 0:1], scale=1.0 / 512.0),
                         r=[(ddk, 2), "epsc"], w=[(ddk, 3)])
                    S.op("act", lambda e, dd=dd: e.activation(out=dd[:, 3:4], in_=dd[:, 3:4], func=AF.Exp, scale=-0.5),
                         r=[(ddk, 3)], w=[(ddk, 3)])
                    S.op("dve", lambda e, dd=dd: e.tensor_tensor(out=dd[:, 3:4], in0=dd[:, 3:4], in1=dd[:, 1:2],
                                                                op=ALU.mult), r=[(ddk, 3), (ddk, 1)], w=[(ddk, 3)])
                    yt, ytk = ytmp.next()
                    S.op("dve", lambda e, yt=yt, p2=p2, dd=dd, h=h: e.scalar_tensor_tensor(
                        out=yt[:, :], in0=p2[:, :], scalar=dd[:, 3:4], in1=nwb[:, h * 512:(h + 1) * 512],
                        op0=ALU.mult, op1=ALU.mult), r=[p2k, (ddk, 3), "nwb"], w=[ytk])
                    yb, ybk = ybf.next()
                    S.op("pool", lambda e, yb=yb, yt=yt, bi=bi, cc=cc: e.tensor_tensor(
                        out=yb[:, :], in0=yt[:, :], in1=ob[bi][:, cc, :], op=ALU.mult), r=[ytk, ("ob", bi)], w=[ybk])
                    p6, p6k = psum()
                    p6v = p6[:, 0:256].bitcast(BF16)
                    for d4 in range(4):
                        S.op("pe", lambda e, d4=d4, p6v=p6v, yb=yb: e.transpose(
                            p6v[:, d4 * 128:(d4 + 1) * 128], yb[:, d4 * 128:(d4 + 1) * 128], ident_b[:, :]),
                            r=[ybk, "ident_b"], w=[p6k])
                    S.op("act", lambda e, p6v=p6v, bi=bi, tsl=tsl: e.activation(
                        out=yTb[bi][:, :, tsl], in_=p6v.rearrange("p (d t) -> p d t", d=4, t=128), func=AF.Identity),
                        r=[p6k], w=[("yTb", bi, cc)])
                S.op("sp", lambda e, bi=bi, t0=t0, h=h: e.dma_start(
                    out=oT_d[h * 512:(h + 1) * 512, t0:t0 + 512].rearrange("(d p) t -> p d t", p=128), in_=yTb[bi][:, :, :]),
                    r=[("yTb", bi, cc) for cc in range(4)], w=[("oT",)], dma=True)
        S.barrier()

    x_src = xT_in
    for l in range(depth):
        phase_inproj(l, x_src)
        S.barrier()
        if l % 2 == 1:
            phase_fox(l)
        else:
            phase_mlstm(l)
        phase_post(l, x_src, last=(l == depth - 1))
        S.barrier()
        x_src = xs_d
    S.op("sp", None)
    S.emit()
    return nc, stack


def _consts():
    c = np.zeros((128, 512 + 2048), np.float32)
    c[:, 0:128] = np.eye(128, dtype=np.float32)
    s = np.arange(128)
    c[:, 128:256] = (s[:, None] <= s[None, :]).astype(np.float32)
    c[:, 256:384] = np.where(s[None, :] <= s[:, None], 0.0, NEG)
    for h in range(16):
        c[h, 512 + h * 128:512 + (h + 1) * 128] = 1.0
    m = np.zeros((128, 4 * 512), np.float32)
    ql = np.arange(512)
    for j in range(4):
        m[:, j * 512:(j + 1) * 512] = np.where(j * 128 + s[:, None] <= ql[None, :], 0.0, NEG)
    return np.concatenate([c, m], axis=1)


def _col(v):
    v = np.asarray(v, np.float32)
    return np.ascontiguousarray(v.reshape(-1, 128).T)


def make_in_map(b, T, depth, x, c, ada_w, ada_b, norm_mix_w, norm_ffn_w, mlstm_w_in, mlstm_b_gates,
                mlstm_norm_w, mlstm_w_out, fox_w_in, fox_b_f, fox_w_out, ffn_w_gate_up, ffn_w_down, final_norm_w):
    n_ml = (depth + 1) // 2
    n_fx = depth // 2
    m = {}
    m["xT"] = np.ascontiguousarray(np.asarray(x[b], np.float32).T)
    m["cT"] = _col(c[b])
    m["ada_w"] = np.ascontiguousarray(ada_w[:depth], dtype=np.float32)
    m["ada_bT"] = np.concatenate([_col(ada_b[l]) for l in range(depth)], axis=1)
    m["nmw"] = np.concatenate([_col(norm_mix_w[l]) for l in range(depth)], axis=1)
    m["nfw"] = np.concatenate([_col(norm_ffn_w[l]) for l in range(depth)], axis=1)
    m["fnw"] = _col(final_norm_w)
    m["mw_in"] = np.ascontiguousarray(mlstm_w_in[:max(n_ml, 1)], dtype=np.float32)
    bgs = np.asarray(mlstm_b_gates, np.float32)[:max(n_ml, 1)]
    m["mbg"] = np.ascontiguousarray(np.concatenate([np.stack([bg[0:4], bg[4:8]], axis=1) for bg in bgs], axis=1))
    m["mnw"] = np.ascontiguousarray(np.concatenate(
        [np.broadcast_to(np.asarray(w, np.float32)[None, :], (128, 2048)) for w in mlstm_norm_w[:max(n_ml, 1)]], axis=1))
    m["mw_out"] = np.ascontiguousarray(mlstm_w_out[:max(n_ml, 1)], dtype=np.float32)
    m["fw_in"] = np.ascontiguousarray(fox_w_in[:max(n_fx, 1)], dtype=np.float32)
    m["fbf"] = np.ascontiguousarray(np.asarray(fox_b_f, np.float32)[:max(n_fx, 1)].reshape(-1, 1))
    m["fw_out"] = np.ascontiguousarray(fox_w_out[:max(n_fx, 1)], dtype=np.float32)
    m["w_gu"] = np.ascontiguousarray(ffn_w_gate_up[:depth], dtype=np.float32)
    m["w_dn"] = np.ascontiguousarray(ffn_w_down[:depth], dtype=np.float32)
    m["consts"] = _consts()
    return m


_CACHE = {}


def kernel(**inputs):
    x = np.asarray(inputs["x"])
    Bn, T, _ = x.shape
    depth = np.asarray(inputs["ada_w"]).shape[0]
    key = (T, depth)
    if key not in _CACHE:
        _CACHE[key] = build(T, depth)
    nc, _stack = _CACHE[key]
    arrs = {k: np.asarray(v) for k, v in inputs.items()}
    in_maps = [make_in_map(b, T, depth, **arrs) for b in range(Bn)]
    res = run_bass_kernel_spmd(nc, in_maps, core_ids=list(range(Bn)))
    out = np.stack([np.ascontiguousarray(r["outT"].T) for r in res.results], axis=0)
    return out.astype(np.float32)
```

```python
from contextlib import ExitStack
import numpy as np
import ml_dtypes
import concourse.bass as bass
import concourse.mybir as mybir
from concourse.bass_utils import run_bass_kernel_spmd

F32 = mybir.dt.float32
BF16 = mybir.dt.bfloat16
AF = mybir.ActivationFunctionType
ALU = mybir.AluOpType
AX = mybir.AxisListType

D = 2048
NC_ = 16
DFF = 5632
NFC = 44
EPS = 1e-6
TT = 512
NEG = -30000.0
R_DMA = 8


class LoopVar:
    cur = None


LV = LoopVar()


def TS(idx, size, off=0):
    if isinstance(idx, LoopVar):
        if isinstance(idx.cur, int):
            return slice(idx.cur * size + off, idx.cur * size + off + size)
        assert off == 0
        return bass.ts(idx.cur, size)
    return slice(idx * size + off, idx * size + off + size)


def DS(idx, stride, off, size):
    if isinstance(idx, LoopVar):
        return bass.ds(idx.cur * stride + off, size)
    return slice(idx * stride + off, idx * stride + off + size)


class Ring:
    registry = []

    def __init__(self, items):
        self.items = items
        self.i = 0
        Ring.registry.append(self)

    def next(self):
        x = self.items[self.i % len(self.items)]
        self.i += 1
        return x


ENGS = ["pe", "act", "dve", "pool", "sp"]
QUEUES = ["sp", "pool", "act"]


class Sch:
    ARENA = 188 * 1024

    def __init__(self, nc, stack):
        self.nc = nc
        self.stack = stack
        self.ops = []
        self.lw = {}
        self.lr = {}
        self.barrier_deps = set()
        self.last_on = {}
        self.dma_hist = {q: [] for q in QUEUES}
        self.n_t = 0
        self.regions = []
        self._after_barrier = set(ENGS)
        Ring.registry.clear()

    def sb(self, shape, dt, name=None):
        if not hasattr(self, "arena"):
            self.arena = self.stack.enter_context(self.nc.sbuf_tensor("arena", [128, self.ARENA], mybir.dt.uint8))
            self.top = 0
        esz = 4 if dt == F32 else 2
        n = 1
        for d in shape[1:]:
            n *= d
        nbytes = (n * esz + 63) // 64 * 64
        off = self.top
        self.top += nbytes
        assert self.top <= self.ARENA, f"SBUF arena overflow: {self.top} ({name})"
        ap = self.arena[0:shape[0], off:off + n * esz].bitcast(dt)
        if len(shape) == 3:
            ap = ap.rearrange("p (a b) -> p a b", a=shape[1], b=shape[2])
        return ap

    def mark(self):
        return self.top

    def reset(self, m):
        self.top = m

    def ps(self, shape, dt=F32, name=None):
        self.n_t += 1
        return self.stack.enter_context(self.nc.psum_tensor(name or f"p{self.n_t}", list(shape), dt))

    def op(self, eng, fn, r=(), w=(), dma=False):
        idx = len(self.ops)
        deps = set(self.barrier_deps) if eng not in self._after_barrier else set()
        self._after_barrier.add(eng)
        for k in r:
            x = self.lw.get(k)
            if x is not None:
                deps.add(x)
        for k in w:
            x = self.lw.get(k)
            if x is not None:
                deps.add(x)
            for y in self.lr.get(k, ()):
                deps.add(y)
        for k in r:
            self.lr.setdefault(k, []).append(idx)
        for k in w:
            self.lw[k] = idx
            self.lr[k] = []
        deps.discard(idx)
        self.ops.append(dict(eng=eng, fn=fn, deps=deps, dma=dma))
        if dma:
            self.dma_hist[eng].append(idx)
        else:
            self.last_on[eng] = idx
        return idx

    def barrier(self):
        deps = set(self.last_on.values())
        for q, h in self.dma_hist.items():
            deps.update(h[-R_DMA:])
        self.barrier_deps = deps
        self._after_barrier = set()

    def loop(self, N, body, reset_fn=None):
        if N == 1:
            body(0)
            return
        self.barrier()
        reg = dict(N=N, s0=len(self.ops), bdeps=set(self.barrier_deps))
        for rg in Ring.registry:
            rg.i = 0
        if reset_fn:
            reset_fn()
        body(LV)
        reg["s1"] = len(self.ops)
        for rg in Ring.registry:
            rg.i = 0
        if reset_fn:
            reset_fn()
        body(LV)
        reg["s2"] = len(self.ops)
        assert reg["s2"] - reg["s1"] == reg["s1"] - reg["s0"], "loop body not iteration-invariant"
        for a, b in zip(range(reg["s0"], reg["s1"]), range(reg["s1"], reg["s2"])):
            assert self.ops[a]["eng"] == self.ops[b]["eng"] and self.ops[a]["dma"] == self.ops[b]["dma"]
        self.regions.append(reg)
        self.barrier()

    def emit(self):
        nc = self.nc
        ops = self.ops
        n = len(ops)
        reg_of = [None] * n
        copy_of = [0] * n
        for reg in self.regions:
            for i in range(reg["s0"], reg["s1"]):
                reg_of[i] = reg
                copy_of[i] = 1
            for i in range(reg["s1"], reg["s2"]):
                reg_of[i] = reg
                copy_of[i] = 2

        def twin(i):
            reg = reg_of[i]
            return i + (reg["s1"] - reg["s0"]) if copy_of[i] == 1 else i

        def pe_pe(d, i):
            return ops[d]["eng"] == "pe" and ops[i]["eng"] == "pe" and not ops[d]["dma"] and not ops[i]["dma"]

        sig = [False] * n
        for i, o in enumerate(ops):
            for d in o["deps"]:
                if pe_pe(d, i):
                    continue
                sig[twin(d)] = True
        cnt = {e: 0 for e in ENGS}
        rr = {q: 0 for q in QUEUES}
        tot = {q: [0] * R_DMA for q in QUEUES}
        i = 0
        while i < n:
            reg = reg_of[i]
            if reg is None:
                o = ops[i]
                e = o["eng"]
                if o["dma"]:
                    s = rr[e] % R_DMA
                    rr[e] += 1
                    tot[e][s] += 1
                    o["slot"] = s
                    o["c"], o["k"] = 16 * tot[e][s], 0
                elif sig[i]:
                    cnt[e] += 1
                    o["c"], o["k"] = cnt[e], 0
                i += 1
                continue
            N = reg["N"]
            body = range(reg["s1"], reg["s2"])
            delta = {e: 0 for e in ENGS}
            for j in body:
                if not ops[j]["dma"] and sig[j]:
                    delta[ops[j]["eng"]] += 1
            run_c = {e: 0 for e in ENGS}
            rr0 = dict(rr)
            cslot = {q: [0] * R_DMA for q in QUEUES}
            for j in body:
                if ops[j]["dma"]:
                    q = ops[j]["eng"]
                    s = rr0[q] % R_DMA
                    rr0[q] += 1
                    ops[j]["slot"] = s
                    ops[j]["m"] = cslot[q][s]
                    cslot[q][s] += 1
            for j in body:
                o = ops[j]
                e = o["eng"]
                if o["dma"]:
                    s = o["slot"]
                    o["c"] = 16 * (tot[e][s] + o["m"] + 1)
                    o["k"] = 16 * cslot[e][s]
                elif sig[j]:
                    run_c[e] += 1
                    o["c"], o["k"] = cnt[e] + run_c[e], delta[e]
            for e in ENGS:
                cnt[e] += N * delta[e]
            for q in QUEUES:
                for s in range(R_DMA):
                    tot[q][s] += N * cslot[q][s]
            reg["cslot"] = cslot
            off = reg["s1"] - reg["s0"]
            for j0 in range(reg["s0"], reg["s1"]):
                t = ops[j0 + off]
                if "c" in t:
                    ops[j0]["c"], ops[j0]["k"] = t["c"], 0
                if "slot" in t:
                    ops[j0]["slot"] = t["slot"]
            i = reg["s2"]

        csem = {e: self.stack.enter_context(nc.semaphore(f"c_{e}")) for e in ENGS}
        dsem = {q: [self.stack.enter_context(nc.semaphore(f"d_{q}{s}")) for s in range(R_DMA)] for q in QUEUES}

        def semkey(p):
            return ("d", p["eng"], p["slot"]) if p["dma"] else ("c", p["eng"])

        def dep_target(d, i):
            t = twin(d)
            p = ops[t]
            key = semkey(p)
            if reg_of[i] is not None and reg_of[i] is reg_of[d]:
                if copy_of[i] == 1:
                    return key, p["c"], 0
                if copy_of[d] == 1:
                    return key, p["c"] - p["k"], p["k"]
                return key, p["c"], p["k"]
            if reg_of[d] is not None:
                return key, p["c"] + (reg_of[d]["N"] - 1) * p["k"], 0
            return key, p["c"], 0

        tmpregs = {}
        itregs = {}

        def emit_waits(eobj, waits, waited, it):
            for (key, k), c in waits.items():
                prev = waited.get((key, k))
                if prev is not None and prev >= c:
                    continue
                waited[(key, k)] = c
                sem = csem[key[1]] if key[0] == "c" else dsem[key[1]][key[2]]
                if k == 0 or it is None:
                    eobj.wait_ge(sem, c)
                else:
                    rg = tmpregs.get(id(eobj))
                    if rg is None:
                        rg = eobj.alloc_register()
                        tmpregs[id(eobj)] = rg
                    itr = waited.get("__itr")
                    if itr is None:
                        itr = eobj.to_reg(it)
                        waited["__itr"] = itr
                    eobj.reg_mul(rg, itr, k)
                    eobj.reg_add(rg, rg, c)
                    eobj.wait_ge(sem, rg)

        def collect(i, ename):
            o = ops[i]
            waits = {}
            for d in o["deps"]:
                if pe_pe(d, i):
                    continue
                key, c, k = dep_target(d, i)
                if waits.get((key, k), -10 ** 9) < c:
                    waits[(key, k)] = c
            if o["dma"]:
                key = ("d", ename, o["slot"])
                c, k = o["c"] - 16, o["k"]
                if copy_of[i] == 1:
                    k = 0
                if c > 0 or k > 0:
                    if waits.get((key, k), -10 ** 9) < c:
                        waits[(key, k)] = c
            return waits

        def emit_op(i, ename, eobj, waited, it):
            o = ops[i]
            emit_waits(eobj, collect(i, ename), waited, it)
            if o["fn"] is None:
                return
            ins = o["fn"](eobj)
            if o["dma"]:
                ins.then_inc(dsem[ename][o["slot"]], 16)
            elif sig[twin(i)]:
                ins.then_inc(csem[ename], 1)

        def run(ename, eobj):
            waited = {}
            i = 0
            while i < n:
                reg = reg_of[i]
                if reg is None:
                    if ops[i]["eng"] == ename:
                        emit_op(i, ename, eobj, waited, None)
                    i += 1
                    continue
                LV.cur = 0
                for j in range(reg["s0"], reg["s1"]):
                    if ops[j]["eng"] == ename:
                        emit_op(j, ename, eobj, waited, None)
                LV.cur = None
                mine = [j for j in range(reg["s1"], reg["s2"]) if ops[j]["eng"] == ename]
                if mine:
                    with eobj.Fori(1, reg["N"]) as iv:
                        LV.cur = iv
                        w2 = {}
                        for j in mine:
                            emit_op(j, ename, eobj, w2, iv)
                    LV.cur = None
                i = reg["s2"]

        with nc.Block() as block:
            @block.tensor
            def _(e):
                run("pe", e)

            @block.scalar
            def _(e):
                run("act", e)

            @block.vector
            def _(e):
                run("dve", e)

            @block.gpsimd
            def _(e):
                run("pool", e)

            @block.sync
            def _(e):
                run("sp", e)


def build(T, depth, dbg=False):
    NT = T // TT
    NQ = T // 128
    n_ml = (depth + 1) // 2
    n_fx = depth // 2
    nc = bass.Bass("TRN2", target_bir_lowering=False)
    stack = ExitStack()
    S = Sch(nc, stack)

    def dram(name, shape, dt, kind="Internal"):
        return nc.dram_tensor(name, list(shape), dt, kind=kind).ap()

    xT_in = dram("xT", [D, T], F32, "ExternalInput")
    cT = dram("cT", [128, 16], F32, "ExternalInput")
    ada_w = dram("ada_w", [depth, D, 6 * D], F32, "ExternalInput")
    ada_bT = dram("ada_bT", [128, depth * 96], F32, "ExternalInput")
    nmw = dram("nmw", [128, depth * 16], F32, "ExternalInput")
    nfw = dram("nfw", [128, depth * 16], F32, "ExternalInput")
    fnw = dram("fnw", [128, 16], F32, "ExternalInput")
    mw_in = dram("mw_in", [max(n_ml, 1), D, 6152], F32, "ExternalInput")
    mbg = dram("mbg", [4, max(n_ml, 1) * 2], F32, "ExternalInput")
    mnw = dram("mnw", [128, max(n_ml, 1) * 2048], F32, "ExternalInput")
    mw_out = dram("mw_out", [max(n_ml, 1), D, D], F32, "ExternalInput")
    fw_in = dram("fw_in", [max(n_fx, 1), D, 6160], F32, "ExternalInput")
    fbf = dram("fbf", [16 * max(n_fx, 1), 1], F32, "ExternalInput")
    fw_out = dram("fw_out", [max(n_fx, 1), D, D], F32, "ExternalInput")
    w_gu = dram("w_gu", [depth, D, 2 * DFF], F32, "ExternalInput")
    w_dn = dram("w_dn", [depth, DFF, D], F32, "ExternalInput")
    consts = dram("consts", [128, 2560 + 2048], F32, "ExternalInput")
    outT = dram("outT", [D, T], F32, "ExternalOutput")

    xs_d = dram("xs_d", [D, T], F32)
    qkT_d = dram("qkT_d", [2 * D, T], BF16)
    tm_d = dram("tm_d", [T, 2048], BF16)
    tmb = [dram(f"tmb{i}", [T, 512], BF16) for i in range(10)]
    gate_d = dram("gate_d", [16, T], F32)
    gate2_d = dram("gate2_d", [4, T], F32)
    oT_d = dram("oT_d", [D, T], BF16)
    ncum_d = dram("ncum_d", [16, T], F32)
    if dbg:
        dbg_h = dram("dbg_h", [D, T], F32, "ExternalOutput")

    ident_f = S.sb([128, 128], F32, "ident_f")
    ident_b = S.sb([128, 128], BF16, "ident_b")
    mask01 = S.sb([128, 128], F32, "mask01")
    maskneg = S.sb([128, 128], F32, "maskneg")
    onesm = S.sb([128, 128], BF16, "onesm")
    ones1 = S.sb([128, 128], BF16, "ones1")
    sel16 = S.sb([16, 4 * 128], F32, "sel16")
    modsb = S.sb([128, depth * 96], F32, "modsb")
    gam1 = S.sb([128, depth * 16], F32, "gam1")
    gam2 = S.sb([128, depth * 16], F32, "gam2")
    nmw_s = S.sb([128, depth * 16], F32, "nmw_s")
    nfw_s = S.sb([128, depth * 16], F32, "nfw_s")
    fnw_s = S.sb([128, 16], F32, "fnw_s")
    zero_c = S.sb([128, 16], F32, "zero_c")
    epsc = S.sb([128, 1], F32, "epsc")
    onec = S.sb([128, 1], F32, "onec")

    S.op("pool", lambda e: e.dma_start(out=ident_f[:], in_=consts[:, 0:128]), w=["ident_f"], dma=True)
    S.op("pool", lambda e: e.dma_start(out=ident_b[:], in_=consts[:, 0:128]), w=["ident_b"], dma=True)
    S.op("pool", lambda e: e.dma_start(out=mask01[:], in_=consts[:, 128:256]), w=["mask01"], dma=True)
    S.op("pool", lambda e: e.dma_start(out=maskneg[:], in_=consts[:, 256:384]), w=["maskneg"], dma=True)
    S.op("pool", lambda e: e.dma_start(out=sel16[:], in_=consts[0:16, 512:512 + 512]), w=["sel16"], dma=True)
    S.op("pool", lambda e: e.dma_start(out=nmw_s[:], in_=nmw[:, :]), w=["nmw_s"], dma=True)
    S.op("pool", lambda e: e.dma_start(out=nfw_s[:], in_=nfw[:, :]), w=["nfw_s"], dma=True)
    S.op("pool", lambda e: e.dma_start(out=fnw_s[:], in_=fnw[:, :]), w=["fnw_s"], dma=True)
    S.op("dve", lambda e: e.memset(onesm[:], 1.0 / D), w=["onesm"])
    S.op("dve", lambda e: e.memset(ones1[:], 1.0), w=["ones1"])
    S.op("dve", lambda e: e.memset(zero_c[:], 0.0), w=["zero_c"])
    S.op("dve", lambda e: e.memset(epsc[:], EPS), w=["epsc"])
    S.op("dve", lambda e: e.memset(onec[:], 1.0), w=["onec"])

    wblk = {}

    class WRef:
        def __init__(self, name, i):
            self.name, self.i = name, i

    class WMat:
        def __init__(self, name):
            self.name = name

        def __getitem__(self, i):
            return WRef(self.name, i)

    def conv(name, src, n, blocks, slot):
        dst = dram(name, [n * len(blocks), 128, slot], BF16)
        for i in range(n):
            for bi_, (k0, nk, c0, ncb_) in enumerate(blocks):
                d = dst[i * len(blocks) + bi_, :, 0:nk * ncb_]
                wblk[(name, i, k0, c0)] = (d, nk, ncb_)
                S.op("pool", lambda e, i=i, k0=k0, nk=nk, c0=c0, ncb_=ncb_, d=d: e.dma_start(
                    out=d.rearrange("p (k c) -> p k c", k=nk, c=ncb_),
                    in_=src[i, k0 * 128:(k0 + nk) * 128, c0:c0 + ncb_].rearrange("(k p) c -> p k c", p=128)),
                    w=[("wconv", name, i)], dma=True)
        return WMat(name)

    in_blocks = [(0, 16, c0, 512) for c0 in range(0, 6144, 512)]
    mw_in = conv("mw_in_b", mw_in, max(n_ml, 1), in_blocks + [(0, 16, 6144, 8)], 16 * 512)
    fw_in = conv("fw_in_b", fw_in, max(n_fx, 1), in_blocks + [(0, 16, 6144, 16)], 16 * 512)
    out_blocks = [(0, 16, c0, 512) for c0 in range(0, D, 512)]
    mw_out = conv("mw_out_b", mw_out, max(n_ml, 1), out_blocks, 16 * 512)
    fw_out = conv("fw_out_b", fw_out, max(n_fx, 1), out_blocks, 16 * 512)
    gu_blocks = [(0, 16, c0, 256) for c0 in range(0, 2 * DFF, 256)]
    w_gu = conv("w_gu_b", w_gu, depth, gu_blocks, 16 * 256)
    dn_blocks = [(kg * 16, (16 if kg < 2 else 12), cb * 512, 512) for kg in range(3) for cb in range(4)]
    w_dn = conv("w_dn_b", w_dn, depth, dn_blocks, 16 * 512)
    S.barrier()

    m0 = S.mark()
    NWB = 3
    wbufs = [S.sb([128, 16, 512], BF16, f"wb{i}") for i in range(NWB)]
    wstate = dict(n=0)

    def wload(src2d, k0, nk, c0, ncols, dcol=0):
        i = wstate["n"] % NWB
        wstate["n"] += 1
        b = wbufs[i]
        if isinstance(src2d, WRef):
            d, nk_, nc_ = wblk[(src2d.name, src2d.i, k0, c0)]
            assert nk_ == nk and nc_ == ncols, (src2d.name, k0, c0, nk, ncols)
            src = d.rearrange("p (k c) -> p k c", k=nk, c=ncols)
        else:
            src = src2d[k0 * 128:(k0 + nk) * 128, c0:c0 + ncols].rearrange("(k p) c -> p k c", p=128)
        S.op("pool", lambda e: e.dma_start(out=b[:, 0:nk, dcol:dcol + ncols], in_=src), w=[("wb", i)], dma=True)
        return b, ("wb", i)

    def wload2(src2d, k0, nk, c0, c1, ncols):
        i = wstate["n"] % NWB
        wstate["n"] += 1
        b = wbufs[i]
        s0 = src2d[k0 * 128:(k0 + nk) * 128, c0:c0 + ncols].rearrange("(k p) c -> p k c", p=128)
        s1 = src2d[k0 * 128:(k0 + nk) * 128, c1:c1 + ncols].rearrange("(k p) c -> p k c", p=128)
        S.op("pool", lambda e: e.dma_start(out=b[:, 0:nk, 0:ncols], in_=s0), w=[("wb", i)], dma=True)
        S.op("pool", lambda e: e.dma_start(out=b[:, 0:nk, ncols:2 * ncols], in_=s1), w=[("wb", i, 1)], r=[("wb", i)], dma=True)
        return b, ("wb", i, 1)

    def reset_w():
        wstate["n"] = 0

    def wstream(blocks, consume, la=NWB - 1):
        issued = []
        for j in range(len(blocks)):
            while len(issued) < min(len(blocks), j + la + 1):
                issued.append(blocks[len(issued)]())
            b, key = issued[j]
            consume(j, b, key)

    psb = [S.ps([128, 512], F32, f"psb{i}") for i in range(8)]
    psring = Ring(list(range(8)))

    def psum():
        i = psring.next()
        return psb[i], ("ps", i)

    cs32 = S.sb([128, 16], F32, "cs32")
    csb = S.sb([128, 16], BF16, "csb")
    adab_s = S.sb([128, depth * 96], F32, "adab_s")
    S.op("pool", lambda e: e.dma_start(out=cs32[:], in_=cT[:, :]), w=["cs32"], dma=True)
    S.op("pool", lambda e: e.dma_start(out=adab_s[:], in_=ada_bT[:, :]), w=["adab_s"], dma=True)
    S.op("act", lambda e: e.activation(out=csb[:], in_=cs32[:], func=AF.Silu), r=["cs32"], w=["csb"])
    for l in range(depth):
        pm, pmk = psum()
        blocks = [(lambda l=l, bi=bi: wload(ada_w[l], 0, 16, bi * 512, 512)) for bi in range(24)]

        def cons(j, b, key, pm=pm, pmk=pmk):
            for j4 in range(4):
                col = j * 4 + j4
                for kc in range(16):
                    S.op("pe", lambda e, b=b, kc=kc, j4=j4, col=col: e.matmul(
                        pm[:, col:col + 1], b[:, kc, j4 * 128:(j4 + 1) * 128], csb[:, kc:kc + 1],
                        start=(kc == 0), stop=(kc == 15)), r=[key, "csb"], w=[pmk])
        wstream(blocks, cons)
        S.op("dve", lambda e, l=l, pm=pm: e.tensor_tensor(out=modsb[:, l * 96:(l + 1) * 96], in0=pm[:, 0:96],
                                                       in1=adab_s[:, l * 96:(l + 1) * 96], op=ALU.add),
             r=[pmk, "adab_s"], w=[("mod", l)])
        S.op("dve", lambda e, l=l: e.scalar_tensor_tensor(out=gam1[:, l * 16:(l + 1) * 16],
                                                         in0=modsb[:, l * 96 + 16:l * 96 + 32], scalar=1.0,
                                                         in1=nmw_s[:, l * 16:(l + 1) * 16], op0=ALU.add, op1=ALU.mult),
             r=[("mod", l), "nmw_s"], w=[("gam1", l)])
        S.op("dve", lambda e, l=l: e.scalar_tensor_tensor(out=gam2[:, l * 16:(l + 1) * 16],
                                                         in0=modsb[:, l * 96 + 64:l * 96 + 80], scalar=1.0,
                                                         in1=nfw_s[:, l * 16:(l + 1) * 16], op0=ALU.add, op1=ALU.mult),
             r=[("mod", l), "nfw_s"], w=[("gam2", l)])

    def modcol(l, which, c):
        base = l * 96 + which * 16 + c
        return modsb[:, base:base + 1]

    xs = S.sb([128, 16, TT], F32, "xs")
    hb = S.sb([128, 16, TT], BF16, "hb")
    act = S.sb([128, NFC, TT], BF16, "act")
    rstd = S.sb([128, TT], F32, "rstd")
    tmpr = Ring([(S.sb([128, TT], F32, f"tmp{i}"), ("tmp", i)) for i in range(3)])
    vst3 = S.sb([128, 4, 2048], BF16, "vst3")
    stgf = Ring([(S.sb([128, TT], F32, f"stgf{i}"), ("stgf", i)) for i in range(2)])

    def norm_tile(gam_ap, sh_ap, l, rdeps, out_fn=None):
        for g in range(4):
            S.op("act", lambda e, g=g: e.activation(out=act[:, 4 * g:4 * g + 4, :], in_=xs[:, 4 * g:4 * g + 4, :],
                                                   func=AF.Square),
                 r=[("xs", c) for c in range(4 * g, 4 * g + 4)], w=[("act", c) for c in range(4 * g, 4 * g + 4)])
        pss, pssk = psum()
        for c in range(16):
            S.op("pe", lambda e, c=c: e.matmul(pss[:, :], onesm[:, :], act[:, c, :], start=(c == 0), stop=(c == 15)),
                 r=[("act", c), "onesm"], w=[pssk])
        S.op("act", lambda e: e.activation(out=rstd[:], in_=pss[:, :], func=AF.Ln, bias=epsc[:, 0:1], scale=1.0),
             r=[pssk, "epsc"], w=["rstd"])
        S.op("act", lambda e: e.activation(out=rstd[:], in_=rstd[:], func=AF.Exp, scale=-0.5), r=["rstd"], w=["rstd"])
        for c in range(16):
            t, tk = tmpr.next()
            S.op("dve", lambda e, c=c, t=t: e.tensor_tensor(out=t[:], in0=xs[:, c, :], in1=rstd[:], op=ALU.mult),
                 r=[("xs", c), "rstd"], w=[tk])
            if out_fn is None:
                S.op("act", lambda e, c=c, t=t: e.activation(out=hb[:, c, :], in_=t[:], func=AF.Identity,
                                                            bias=sh_ap[:, c:c + 1], scale=gam_ap[:, c:c + 1]),
                     r=[tk] + rdeps, w=[("hb", c)])
            else:
                out_fn(c, t, tk)

    def load_x(src, tt):
        S.op("sp", lambda e: e.dma_start(out=xs[:, :, :],
                                         in_=src[:, TS(tt, TT)].rearrange("(c p) t -> p c t", p=128)),
             r=[("xd",)], w=[("xs", c) for c in range(16)], dma=True)

    evac_rr = Ring(["act", "dve"])

    def evac_bf16(ps_ap, psk, dst_ap, dstk, scale=None, func=None, extra_r=()):
        if func is not None:
            S.op("act", lambda e: e.activation(out=dst_ap, in_=ps_ap, func=func), r=[psk] + list(extra_r), w=[dstk])
            return
        eng = evac_rr.next()
        if eng == "act":
            S.op("act", lambda e: e.activation(out=dst_ap, in_=ps_ap, func=AF.Identity,
                                               scale=(1.0 if scale is None else scale)), r=[psk] + list(extra_r), w=[dstk])
        else:
            if scale is None:
                S.op("dve", lambda e: e.tensor_copy(out=dst_ap, in_=ps_ap), r=[psk] + list(extra_r), w=[dstk])
            else:
                S.op("dve", lambda e: e.tensor_scalar(out=dst_ap, in0=ps_ap, scalar1=scale, scalar2=None, op0=ALU.mult),
                     r=[psk] + list(extra_r), w=[dstk])

    def phase_inproj(l, x_src):
        mixer = l % 2
        j = l // 2
        if mixer == 0:
            W = mw_in[j]
            fm_blocks = 4
            qscale = 256 ** -0.5
            nq_chunks = 8
            tm_c0, tm_blocks = 1024, 10
        else:
            W = fw_in[j]
            fm_blocks = 8
            qscale = 128 ** -0.5
            nq_chunks = 16
            tm_c0, tm_blocks = 4096, 4
        def body(tt):
            load_x(x_src, tt)
            norm_tile(gam1[:, l * 16:(l + 1) * 16], modsb[:, l * 96:l * 96 + 16], l, [("gam1", l), ("mod", l)])
            blocks = []
            for bi in range(fm_blocks):
                blocks.append(lambda bi=bi: wload(W, 0, 16, bi * 512, 512))
            for bi in range(tm_blocks):
                blocks.append(lambda bi=bi: wload(W, 0, 16, tm_c0 + bi * 512, 512))
            blocks.append(lambda: wload(W, 0, 16, 6144, 8 if mixer == 0 else 16))

            def cons(jb, b, key):
                if jb < fm_blocks:
                    for oc4 in range(4):
                        oc = jb * 4 + oc4
                        p, pk = psum()
                        for kc in range(16):
                            S.op("pe", lambda e, p=p, b=b, kc=kc, oc4=oc4: e.matmul(
                                p[:, :], b[:, kc, oc4 * 128:(oc4 + 1) * 128], hb[:, kc, :],
                                start=(kc == 0), stop=(kc == 15)), r=[key, ("hb", kc)], w=[pk])
                        evac_bf16(p[:, :], pk, act[:, oc, :], ("act", oc), scale=(qscale if oc < nq_chunks else None))
                    if jb == fm_blocks - 1:
                        nfm = fm_blocks * 4
                        S.op("sp", lambda e: e.dma_start(
                            out=qkT_d.rearrange("(c p) t -> p c t", p=128)[:, 0:nfm, TS(tt, TT)], in_=act[:, 0:nfm, :]),
                            r=[("act", c) for c in range(nfm)], w=[("qkT", c) for c in range(32)], dma=True)
                elif jb < fm_blocks + tm_blocks:
                    nb = jb - fm_blocks
                    for m in range(4):
                        p, pk = psum()
                        for kc in range(16):
                            S.op("pe", lambda e, p=p, b=b, kc=kc, m=m: e.matmul(
                                p[:, :], hb[:, kc, m * 128:(m + 1) * 128], b[:, kc, :],
                                start=(kc == 0), stop=(kc == 15)), r=[key, ("hb", kc)], w=[pk])
                        is_o = (mixer == 0 and nb >= 6)
                        s4 = nb % 4
                        evac_bf16(p[:, :], pk, vst3[:, m, s4 * 512:(s4 + 1) * 512], ("vst", m, s4),
                                  func=(AF.Sigmoid if is_o else None))
                    if mixer == 0:
                        s4 = nb % 4
                        S.op("sp", lambda e, nb=nb, s4=s4: e.dma_start(
                            out=tmb[nb].rearrange("(a m p) c -> a p m c", m=4, p=128)[TS(tt, 1)]
                            .rearrange("a p m c -> (a p) m c"), in_=vst3[:, :, s4 * 512:(s4 + 1) * 512]),
                            r=[("vst", m, s4) for m in range(4)], w=[("tm",)], dma=True)
                    elif nb == tm_blocks - 1:
                        S.op("sp", lambda e: e.dma_start(
                            out=tm_d.rearrange("(a m p) c -> a p m c", m=4, p=128)[TS(tt, 1)]
                            .rearrange("a p m c -> (a p) m c"), in_=vst3[:, :, :]),
                            r=[("vst", m, x) for m in range(4) for x in range(4)], w=[("tm",)], dma=True)
                else:
                    if mixer == 0:
                        for gi in range(2):
                            p, pk = psum()
                            for kc in range(16):
                                S.op("pe", lambda e, p=p, b=b, kc=kc, gi=gi: e.matmul(
                                    p[0:4, :], b[:, kc, gi * 4:gi * 4 + 4], hb[:, kc, :],
                                    start=(kc == 0), stop=(kc == 15)), r=[key, ("hb", kc)], w=[pk])
                            sf, sfk = stgf.next()
                            S.op("dve", lambda e, p=p, sf=sf: e.tensor_copy(out=sf[0:4, :], in_=p[0:4, :]), r=[pk], w=[sfk])
                            dst = gate_d if gi == 0 else gate2_d
                            S.op("sp", lambda e, sf=sf, dst=dst: e.dma_start(
                                out=dst[0:4, TS(tt, TT)], in_=sf[0:4, :]), r=[sfk], w=[("gate", gi)], dma=True)
                    else:
                        p, pk = psum()
                        for kc in range(16):
                            S.op("pe", lambda e, p=p, b=b, kc=kc: e.matmul(
                                p[0:16, :], b[:, kc, 0:16], hb[:, kc, :],
                                start=(kc == 0), stop=(kc == 15)), r=[key, ("hb", kc)], w=[pk])
                        sf, sfk = stgf.next()
                        S.op("dve", lambda e, p=p, sf=sf: e.tensor_copy(out=sf[0:16, :], in_=p[0:16, :]), r=[pk], w=[sfk])
                        S.op("sp", lambda e, sf=sf: e.dma_start(
                            out=gate_d[0:16, TS(tt, TT)], in_=sf[0:16, :]), r=[sfk], w=[("gate", 0)], dma=True)
            wstream(blocks, cons)
        S.loop(NT, body, reset_w)

    def phase_post(l, x_src, last):
        mixer = l % 2
        j = l // 2
        Wo = mw_out[j] if mixer == 0 else fw_out[j]
        def fin_factory(tt):
            def fin(c, t, tk):
                S.op("act", lambda e, c=c, t=t: e.activation(out=xs[:, c, :], in_=t[:], func=AF.Identity,
                                                            scale=fnw_s[:, c:c + 1]), r=[tk, "fnw_s"], w=[("xs", c)])
                if c == 15:
                    S.op("sp", lambda e: e.dma_start(
                        out=outT[:, TS(tt, TT)].rearrange("(c p) t -> p c t", p=128), in_=xs[:, :, :]),
                        r=[("xs", x) for x in range(16)], w=[("out",)], dma=True)
            return fin

        def body(tt):
            load_x(x_src, tt)
            S.op("sp", lambda e: e.dma_start(
                out=act[:, 16:32, :], in_=oT_d[:, TS(tt, TT)].rearrange("(c p) t -> p c t", p=128)),
                r=[("oT",)], w=[("act", c) for c in range(16, 32)], dma=True)
            blocks = [(lambda ob=ob: wload(Wo, 0, 16, ob * 512, 512)) for ob in range(4)]

            def cons_o(ob, b, key):
                for oc4 in range(4):
                    oc = ob * 4 + oc4
                    p, pk = psum()
                    for kc in range(16):
                        S.op("pe", lambda e, p=p, b=b, kc=kc, oc4=oc4: e.matmul(
                            p[:, :], b[:, kc, oc4 * 128:(oc4 + 1) * 128], act[:, 16 + kc, :],
                            start=(kc == 0), stop=(kc == 15)), r=[key, ("act", 16 + kc)], w=[pk])
                    S.op("dve", lambda e, p=p, oc=oc: e.scalar_tensor_tensor(
                        out=xs[:, oc, :], in0=p[:, :], scalar=modcol(l, 2, oc), in1=xs[:, oc, :],
                        op0=ALU.mult, op1=ALU.add), r=[pk, ("mod", l), ("xs", oc)], w=[("xs", oc)])
            wstream(blocks, cons_o)
            import os as _os
            if _os.environ.get("DBG_SKIP_FFN"):
                norm_tile(None, None, l, [], out_fn=fin_factory(tt))
                return
            norm_tile(gam2[:, l * 16:(l + 1) * 16], modsb[:, l * 96 + 48:l * 96 + 64], l, [("gam2", l), ("mod", l)])
            blocks = []
            for fb in range(22):
                blocks.append(lambda fb=fb: wload(w_gu[l], 0, 16, fb * 256, 256))
                blocks.append(lambda fb=fb: wload(w_gu[l], 0, 16, DFF + fb * 256, 256))
            hold = {}

            def cons_gu(jb, b, key):
                if jb % 2 == 0:
                    hold["g"] = (b, key)
                    return
                fb = jb // 2
                gb, gk = hold["g"]
                ub, uk = b, key
                for f2 in range(2):
                    fc = fb * 2 + f2
                    pg, pgk = psum()
                    pu, puk = psum()
                    for kc in range(16):
                        S.op("pe", lambda e, pg=pg, gb=gb, kc=kc, f2=f2: e.matmul(
                            pg[:, :], gb[:, kc, f2 * 128:(f2 + 1) * 128], hb[:, kc, :],
                            start=(kc == 0), stop=(kc == 15)), r=[gk, ("hb", kc)], w=[pgk])
                    for kc in range(16):
                        S.op("pe", lambda e, pu=pu, ub=ub, kc=kc, f2=f2: e.matmul(
                            pu[:, :], ub[:, kc, f2 * 128:(f2 + 1) * 128], hb[:, kc, :],
                            start=(kc == 0), stop=(kc == 15)), r=[uk, ("hb", kc)], w=[puk])
                    t, tk = tmpr.next()
                    S.op("act", lambda e, pg=pg, t=t: e.activation(out=t[:], in_=pg[:, :], func=AF.Silu), r=[pgk], w=[tk])
                    S.op("dve", lambda e, pu=pu, t=t, fc=fc: e.tensor_tensor(out=act[:, fc, :], in0=t[:], in1=pu[:, :],
                                                                           op=ALU.mult), r=[tk, puk], w=[("act", fc)])
            wstream(blocks, cons_gu, la=1)
            for cb in range(4):
                accs = [psum() for _ in range(4)]
                blocks = [(lambda kg=kg, cb=cb: wload(w_dn[l], kg * 16, (16 if kg < 2 else 12), cb * 512, 512)) for kg in range(3)]

                def cons_d(kg, b, key, accs=accs, cb=cb):
                    nk = 16 if kg < 2 else 12
                    for oc4 in range(4):
                        p, pk = accs[oc4]
                        for kc in range(nk):
                            S.op("pe", lambda e, p=p, b=b, kc=kc, oc4=oc4, kg=kg, nk=nk: e.matmul(
                                p[:, :], b[:, kc, oc4 * 128:(oc4 + 1) * 128], act[:, kg * 16 + kc, :],
                                start=(kg == 0 and kc == 0), stop=(kg == 2 and kc == nk - 1)),
                                r=[key, ("act", kg * 16 + kc)], w=[pk])
                wstream(blocks, cons_d)
                for oc4 in range(4):
                    oc = cb * 4 + oc4
                    p, pk = accs[oc4]
                    S.op("dve", lambda e, p=p, oc=oc: e.scalar_tensor_tensor(
                        out=xs[:, oc, :], in0=p[:, :], scalar=modcol(l, 5, oc), in1=xs[:, oc, :],
                        op0=ALU.mult, op1=ALU.add), r=[pk, ("mod", l), ("xs", oc)], w=[("xs", oc)])
            if not last:
                S.op("sp", lambda e: e.dma_start(
                    out=xs_d[:, TS(tt, TT)].rearrange("(c p) t -> p c t", p=128), in_=xs[:, :, :]),
                    r=[("xs", c) for c in range(16)], w=[("xd",)], dma=True)
            else:
                norm_tile(None, None, l, [], out_fn=fin_factory(tt))
        S.loop(NT, body, reset_w)

    def phase_fox(l):
        j = l // 2
        if True:
            S.reset(m0)

            def sb2(shape, dt, name):
                return S.sb(shape, dt, name)

            rS, rT, rO, rM = Ring([0, 1]), Ring([2, 3]), Ring([4, 5]), Ring([6, 7])

            def bank(ring):
                i_ = ring.next()
                return psb[i_], ("ps", i_)
            fz = sb2([16, T], F32, "fz")
            nbf = sb2([16, 1], F32, "nbf")
            bfs = sb2([16, 1], F32, "bfs")
            ncb = sb2([128, T], F32, "ncb")
            qh = [sb2([128, T], BF16, f"qh{i}") for i in range(1)]
            kh = [sb2([128, T], BF16, f"kh{i}") for i in range(1)]
            vh = [sb2([128, NQ, 129], BF16, f"vh{i}") for i in range(1)]
            sq = sb2([128, T], BF16, "sqb")
            q2 = sb2([128, NQ], F32, "q2")
            km = sb2([128, 16], F32, "km")
            km1 = sb2([128, 1], F32, "km1")
            negm = sb2([128, NQ], F32, "negm")
            ssb = Ring([(sb2([128, 512], F32, f"ssb{i}"), ("ssb", i)) for i in range(4)])
            pbf = Ring([(sb2([128, 512], BF16, f"pbf{i}"), ("pbf", i)) for i in range(6)])
            oTh = [sb2([128, T], BF16, f"oTh{i}") for i in range(1)]

            S.op("sp", lambda e: e.dma_start(out=fz[:, :], in_=gate_d[0:16, :]),
                 r=[("gate", 0)], w=["fz"], dma=True)
            S.op("sp", lambda e: e.dma_start(out=bfs[:, :], in_=fbf[j * 16:(j + 1) * 16, 0:1]), w=["bfs"], dma=True)
            S.op("dve", lambda e: e.tensor_scalar(out=nbf[:], in0=bfs[:], scalar1=-1.0, scalar2=None, op0=ALU.mult),
                 r=["bfs"], w=["nbf"])
            S.op("act", lambda e: e.activation(out=fz[:, :], in_=fz[:, :], func=AF.Exp, bias=nbf[:, 0:1], scale=-1.0),
                 r=["fz", "nbf"], w=["fz"])
            S.op("act", lambda e: e.activation(out=fz[:, :], in_=fz[:, :], func=AF.Ln, bias=onec[0:16, 0:1], scale=1.0),
                 r=["fz", "onec"], w=["fz"])
            S.op("dve", lambda e: e.memset(ncb[0:16, :], 1.0), w=["ncb"])
            SEG = 1024
            for sg in range(T // SEG):
                a0, a1 = sg * SEG, (sg + 1) * SEG
                S.op("dve", lambda e, a0=a0, a1=a1, sg=sg: e.tensor_tensor_scan(
                    out=fz[:, a0:a1], data0=ncb[0:16, a0:a1], data1=fz[:, a0:a1],
                    initial=(0.0 if sg == 0 else fz[:, a0 - 1:a0]), op0=ALU.mult, op1=ALU.add),
                    r=["ncb", "fz"], w=["fz"])
            S.op("sp", lambda e: e.dma_start(out=ncum_d[:, :], in_=fz[:, :]), r=["fz"], w=["ncum_d"], dma=True)
            nqt = sb2([NQ, 128], F32, "nqt")
            maskT = sb2([128, 4 * 512], F32, "maskT")
            S.op("act", lambda e: e.dma_start(out=maskT[:, :], in_=consts[:, 2560:2560 + 2048]), w=["maskT"], dma=True)
            ncq = sb2([128, NQ], F32, "ncq")

            def load_head(h):
                i = 0
                S.op("act", lambda e: e.dma_start(out=qh[i][:, :], in_=qkT_d[TS(h, 128), :]),
                     r=[("qkT", oc) for oc in range(32)], w=[("qh", i)], dma=True)
                S.op("act", lambda e: e.dma_start(out=kh[i][:, :], in_=qkT_d[D:2 * D, :][TS(h, 128), :]),
                     r=[("qkT", oc) for oc in range(32)], w=[("kh", i)], dma=True)
                S.op("act", lambda e: e.dma_start(
                    out=vh[i][:, :, 0:128],
                    in_=tm_d[:, TS(h, 128)].rearrange("(j p) d -> p j d", p=128)),
                    r=[("tm",)], w=[("vh", i)], dma=True)
                S.op("pool", lambda e: e.memset(vh[i][:, :, 128:129], 1.0), r=[], w=[("vh1", i)])
                S.op("act", lambda e: e.dma_start(out=ncb[:, :], in_=ncum_d[TS(h, 1), :].to_broadcast([128, T])),
                     r=["ncum_d"], w=["ncb"], dma=True)
                S.op("act", lambda e: e.dma_start(out=nqt[:, :],
                                                 in_=ncum_d[TS(h, 1), :].rearrange("o (q p) -> (o q) p", p=128)),
                     r=["ncum_d"], w=["nqt"], dma=True)
                p, pk = bank(rM)
                S.op("pe", lambda e, p=p: e.transpose(p[:, 0:NQ], nqt[0:NQ, :], ident_f[0:NQ, 0:NQ]),
                     r=["nqt", "ident_f"], w=[pk])
                S.op("dve", lambda e, p=p: e.tensor_copy(out=ncq[:, :], in_=p[:, 0:NQ]), r=[pk], w=["ncq"])

            def head_body(h):
                i = 0
                load_head(h)
                S.op("act", lambda e, i=i: e.activation(out=sq[:, :], in_=kh[i][:, :], func=AF.Square),
                     r=[("kh", i)], w=["sq"])
                for t4 in range(T // 512):
                    p, pk = bank(rM)
                    S.op("pe", lambda e, p=p, t4=t4: e.matmul(p[:, :], ones1[:, :], sq[:, t4 * 512:(t4 + 1) * 512],
                                                             start=True, stop=True), r=["sq", "ones1"], w=[pk])
                    S.op("dve", lambda e, p=p, t4=t4: e.tensor_reduce(out=km[:, t4:t4 + 1], in_=p[:, :], axis=AX.X,
                                                                     op=ALU.max), r=[pk], w=[("km", t4)])
                S.op("dve", lambda e: e.tensor_reduce(out=km1[:, 0:1], in_=km[:, 0:T // 512], axis=AX.X, op=ALU.max),
                     r=[("km", t4) for t4 in range(T // 512)], w=["km1"])
                S.op("dve", lambda e: e.tensor_scalar(out=km1[:, 0:1], in0=km1[:, 0:1], scalar1=1.05, scalar2=None,
                                                      op0=ALU.mult), r=["km1"], w=["km1"])
                S.op("act", lambda e, i=i: e.activation(out=sq[:, :], in_=qh[i][:, :], func=AF.Square),
                     r=[("qh", i)], w=["sq"])
                for t4 in range(T // 512):
                    p, pk = bank(rM)
                    S.op("pe", lambda e, p=p, t4=t4: e.matmul(p[:, :], ones1[:, :], sq[:, t4 * 512:(t4 + 1) * 512],
                                                             start=True, stop=True), r=["sq", "ones1"], w=[pk])
                    mq, mqk = ssb.next()
                    S.op("act", lambda e, p=p, mq=mq: e.activation(out=mq[:, :], in_=p[:, :], func=AF.Sqrt,
                                                                  scale=km1[:, 0:1]), r=[pk, "km1"], w=[mqk])
                    S.op("dve", lambda e, mq=mq, t4=t4: e.scalar_tensor_tensor(
                        out=ncb[:, t4 * 512:(t4 + 1) * 512], in0=ncb[:, t4 * 512:(t4 + 1) * 512], scalar=-1.0,
                        in1=mq[:, :], op0=ALU.mult, op1=ALU.subtract), r=[mqk, "ncb"], w=["ncb"])
                steps = []
                for qt in range(T // 512):
                    nkb = 4 * qt + 4
                    for kb_ in range(nkb):
                        steps.append(dict(qt=qt, kb=kb_, first=(kb_ == 0), last=(kb_ == nkb - 1),
                                          diag=(kb_ - 4 * qt if kb_ >= 4 * qt else None)))
                po = {}

                def st_qk(s):
                    p, pk = bank(rS)
                    qt, kb_ = s["qt"], s["kb"]
                    S.op("pe", lambda e: e.matmul(p[:, :], kh[i][:, kb_ * 128:(kb_ + 1) * 128],
                                                  qh[i][:, qt * 512:(qt + 1) * 512], start=True, stop=True),
                         r=[("qh", i), ("kh", i)], w=[pk])
                    sb_, sbk = ssb.next()
                    S.op("dve", lambda e: e.tensor_tensor(out=sb_[:, :], in0=p[:, :], in1=ncb[:, qt * 512:(qt + 1) * 512],
                                                          op=ALU.add), r=[pk, "ncb"], w=[sbk])
                    if s["diag"] is not None:
                        dj = s["diag"]
                        S.op("pool", lambda e: e.tensor_tensor(out=sb_[:, :], in0=sb_[:, :],
                                                               in1=maskT[:, dj * 512:(dj + 1) * 512], op=ALU.add),
                             r=[sbk, "maskT"], w=[sbk])
                    pb, pbk = pbf.next()
                    s["pb"], s["pbk"] = pb, pbk
                    S.op("act", lambda e: e.activation(out=pb[:, :], in_=sb_[:, :], func=AF.Exp,
                                                       bias=ncq[:, kb_:kb_ + 1], scale=1.0),
                         r=[sbk, "ncq"], w=[pbk])

                def st_pv(s):
                    qt, kb_ = s["qt"], s["kb"]
                    if s["first"]:
                        po["o"], po["ok"] = bank(rO)
                        po["d"], po["dk"] = bank(rT)
                    pO, pOk, pD, pDk = po["o"], po["ok"], po["d"], po["dk"]
                    pb = s["pb"]
                    S.op("pe", lambda e: e.matmul(pO[:, :], vh[i][:, kb_, 0:128], pb[:, :],
                                                  start=s["first"], stop=s["last"]),
                         r=[s["pbk"], ("vh", i)], w=[pOk])
                    S.op("pe", lambda e: e.matmul(pD[:, :], ones1[:, :], pb[:, :],
                                                  start=s["first"], stop=s["last"]),
                         r=[s["pbk"], "ones1"], w=[pDk])
                    if s["last"]:
                        rd, rdk = ssb.next()
                        S.op("dve", lambda e: e.reciprocal(out=rd[:, :], in_=pD[:, :]), r=[pDk], w=[rdk])
                        S.op("dve", lambda e: e.tensor_tensor(out=oTh[i][:, qt * 512:(qt + 1) * 512], in0=pO[:, :],
                                                              in1=rd[:, :], op=ALU.mult),
                             r=[pOk, rdk], w=[("oTh", i, qt)])

                ns = len(steps)
                SKEW = 4
                for n in range(ns + SKEW):
                    if n < ns:
                        st_qk(steps[n])
                    if 0 <= n - SKEW < ns:
                        st_pv(steps[n - SKEW])
                S.op("act", lambda e, i=i: e.dma_start(out=oT_d[TS(h, 128), :], in_=oTh[i][:, :]),
                     r=[("oTh", i, qt) for qt in range(T // 512)], w=[("oT",)], dma=True)
            S.loop(16, head_body)
            S.barrier()

    def phase_mlstm(l):
        j = l // 2
        S.reset(m0)
        A = S.sb([4, T], F32, "gA")
        B = S.sb([4, T], F32, "gB")
        bg = S.sb([4, 2], F32, "bg")
        bg15 = S.sb([4, 2], F32, "bg15")
        ref = S.sb([4, NQ], F32, "ref")
        Gn = S.sb([4, NQ], F32, "Gn")
        Gp = S.sb([4, NQ], F32, "Gp")
        r1 = S.sb([4, NQ], F32, "r1")
        eT = S.sb([128, NQ * 4], F32, "eT")
        flT = S.sb([128, NQ * 4], F32, "flT")
        r1b = S.sb([128, 4 * NQ], F32, "r1b")
        nwb = S.sb([128, 2048], F32, "nwb")
        Cs = S.sb([128, 2, 512], F32, "Cs")
        Cb = S.sb([128, 2, 512], BF16, "Cb")
        ns = S.sb([128, 2], F32, "ns")
        nb_ = S.sb([128, 2], BF16, "nb_")
        qTb = [S.sb([128, 2, 512], BF16, f"qTb{i}") for i in range(2)]
        kTb = [S.sb([128, 2, 512], BF16, f"kTb{i}") for i in range(2)]
        ktm = [S.sb([128, 4, 256], BF16, f"ktm{i}") for i in range(2)]
        vb = [S.sb([128, 4, 512], BF16, f"vb{i}") for i in range(2)]
        ob = [S.sb([128, 4, 512], BF16, f"ob{i}") for i in range(2)]
        yTb = [S.sb([128, 4, 512], BF16, f"yTb{i}") for i in range(2)]
        PT = Ring([(S.sb([128, 128], BF16, f"PT{i}"), ("PT", i)) for i in range(2)])
        Kt = Ring([(S.sb([128, 256], BF16, f"Kt{i}"), ("Kt", i)) for i in range(2)])
        d1 = Ring([(S.sb([128, 4], F32, f"d1{i}"), ("d1", i)) for i in range(2)])
        junk = S.sb([128, 512], BF16, "junk")
        ytmp = Ring([(S.sb([128, 512], F32, f"ytmp{i}"), ("ytmp", i)) for i in range(2)])
        ybf = Ring([(S.sb([128, 512], BF16, f"ybf{i}"), ("ybf", i)) for i in range(2)])

        gdeps = [("gate", gi) for gi in range(2)]
        S.op("sp", lambda e: e.dma_start(out=A[:, :], in_=gate_d[0:4, :]), r=gdeps, w=["gA"], dma=True)
        S.op("sp", lambda e: e.dma_start(out=B[:, :], in_=gate2_d[0:4, :]), r=gdeps, w=["gB"], dma=True)
        S.op("sp", lambda e: e.dma_start(out=bg[:, :], in_=mbg[:, 2 * j:2 * j + 2]), w=["bg"], dma=True)
        S.op("sp", lambda e: e.dma_start(out=nwb[:, :], in_=mnw[:, j * 2048:(j + 1) * 2048]), w=["nwb"], dma=True)
        S.op("dve", lambda e: e.tensor_scalar(out=bg15[:, :], in0=bg[:, :], scalar1=1.0 / 15.0, scalar2=None,
                                              op0=ALU.mult), r=["bg"], w=["bg15"])
        S.op("act", lambda e: e.activation(out=A[:, :], in_=A[:, :], func=AF.Tanh, bias=bg15[:, 0:1], scale=1.0 / 15.0),
             r=["gA", "bg15"], w=["gA"])
        S.op("act", lambda e: e.activation(out=B[:, :], in_=B[:, :], func=AF.Tanh, bias=bg15[:, 1:2], scale=1.0 / 15.0),
             r=["gB", "bg15"], w=["gB"])
        S.op("act", lambda e: e.activation(out=B[:, :], in_=B[:, :], func=AF.Exp, scale=-15.0), r=["gB"], w=["gB"])
        S.op("act", lambda e: e.activation(out=B[:, :], in_=B[:, :], func=AF.Ln, bias=onec[0:4, 0:1], scale=1.0), r=["gB", "onec"], w=["gB"])
        onesT = S.sb([4, T], F32, "onesT")
        S.op("pool", lambda e: e.memset(onesT[:, :], 1.0), w=["onesT"])
        SEG = 1024
        for sg in range(T // SEG):
            a0, a1 = sg * SEG, (sg + 1) * SEG
            S.op("dve", lambda e, a0=a0, a1=a1, sg=sg: e.tensor_tensor_scan(
                out=B[:, a0:a1], data0=onesT[:, a0:a1], data1=B[:, a0:a1],
                initial=(0.0 if sg == 0 else B[:, a0 - 1:a0]), op0=ALU.mult, op1=ALU.add),
                r=["gB", "onesT"], w=["gB"])
        S.op("dve", lambda e: e.scalar_tensor_tensor(out=A[:, :], in0=A[:, :], scalar=15.0, in1=B[:, :],
                                                    op0=ALU.mult, op1=ALU.add), r=["gA", "gB"], w=["gA"])
        A3 = A.rearrange("p (c s) -> p c s", c=NQ, s=128)
        B3 = B.rearrange("p (c s) -> p c s", c=NQ, s=128)
        S.op("dve", lambda e: e.tensor_reduce(out=ref[:, :], in_=A3, axis=AX.X, op=ALU.max), r=["gA"], w=["ref"])
        S.op("dve", lambda e: e.tensor_tensor_scan(out=Gn[:, :], data0=ref[:, :], data1=ref[:, :], initial=0.0,
                                                  op0=ALU.max, op1=ALU.max), r=["ref"], w=["Gn"])
        S.op("dve", lambda e: e.memset(Gp[:, 0:1], 0.0), w=["Gp0"])
        if NQ > 1:
            S.op("dve", lambda e: e.tensor_copy(out=Gp[:, 1:NQ], in_=Gn[:, 0:NQ - 1]), r=["Gn"], w=["Gp1"])
        S.op("dve", lambda e: e.tensor_tensor(out=r1[:, :], in0=Gp[:, :], in1=Gn[:, :], op=ALU.subtract),
             r=["Gn", "Gp0", "Gp1"], w=["r1"])
        S.op("act", lambda e: e.activation(out=r1[:, :], in_=r1[:, :], func=AF.Exp), r=["r1"], w=["r1"])
        for c in range(NQ):
            S.op("dve", lambda e, c=c: e.tensor_scalar(out=A[:, c * 128:(c + 1) * 128], in0=A[:, c * 128:(c + 1) * 128],
                                                      scalar1=Gn[:, c:c + 1], scalar2=None, op0=ALU.subtract),
                 r=["gA", "Gn"], w=["gA"])
            S.op("pool", lambda e, c=c: e.tensor_scalar(out=B[:, c * 128:(c + 1) * 128], in0=B[:, c * 128:(c + 1) * 128],
                                                       scalar1=Gn[:, c:c + 1], scalar2=None, op0=ALU.subtract),
                 r=["gB", "Gn"], w=["gB"])
        S.op("act", lambda e: e.activation(out=A[:, :], in_=A[:, :], func=AF.Exp), r=["gA"], w=["gA"])
        S.op("act", lambda e: e.activation(out=B[:, :], in_=B[:, :], func=AF.Exp), r=["gB"], w=["gB"])
        for (src, dst, dk, sk) in ((A, eT, "eT", "gA"), (B, flT, "flT", "gB")):
            for g0 in range(0, NQ, 64):
                p, pk = psum()
                n_in = min(64, NQ - g0)
                for c in range(g0, g0 + n_in):
                    S.op("pe", lambda e, p=p, c=c, g0=g0, src=src: e.transpose(
                        p[:, (c - g0) * 4:(c - g0 + 1) * 4], src[0:4, c * 128:(c + 1) * 128], ident_f[0:4, 0:4]),
                        r=[sk, "ident_f"], w=[pk])
                S.op("dve", lambda e, p=p, g0=g0, n_in=n_in, dst=dst: e.tensor_copy(
                    out=dst[:, g0 * 4:(g0 + n_in) * 4], in_=p[:, 0:n_in * 4]), r=[pk], w=[dk])
        p, pk = psum()
        for h in range(4):
            S.op("pe", lambda e, p=p, h=h: e.matmul(p[:, h * NQ:(h + 1) * NQ], sel16[0:4, h * 128:(h + 1) * 128],
                                                   r1[0:4, 0:NQ], start=True, stop=True), r=["sel16", "r1"], w=[pk])
        S.op("dve", lambda e, p=p: e.tensor_copy(out=r1b[:, :], in_=p[:, 0:4 * NQ]), r=[pk], w=["r1b"])

        NB = T // 512
        for h in range(4):
            S.op("dve", lambda e: e.memset(Cs[:, :, :], 0.0), w=["Cs"])
            S.op("dve", lambda e: e.memset(ns[:, :], 0.0), w=["ns"])
            for tb in range(NB):
                bi = (h * NB + tb) % 2
                t0 = tb * 512
                S.op("sp", lambda e, bi=bi, t0=t0, h=h: e.dma_start(
                    out=qTb[bi][:, :, :], in_=qkT_d[h * 256:(h + 1) * 256, t0:t0 + 512].rearrange("(c p) t -> p c t", p=128)),
                    r=[("qkT", oc) for oc in range(16)], w=[("qTb", bi)], dma=True)
                S.op("sp", lambda e, bi=bi, t0=t0, h=h: e.dma_start(
                    out=kTb[bi][:, :, :], in_=qkT_d[1024 + h * 256:1024 + (h + 1) * 256, t0:t0 + 512].rearrange("(c p) t -> p c t", p=128)),
                    r=[("qkT", oc) for oc in range(16)], w=[("kTb", bi)], dma=True)
                S.op("sp", lambda e, bi=bi, t0=t0, h=h: e.dma_start(
                    out=ktm[bi][:, :, :], in_=tmb[h // 2][t0:t0 + 512, (h % 2) * 256:(h % 2 + 1) * 256].rearrange("(j p) d -> p j d", p=128)),
                    r=[("tm",)], w=[("ktm", bi)], dma=True)
                S.op("sp", lambda e, bi=bi, t0=t0, h=h: e.dma_start(
                    out=vb[bi][:, :, :], in_=tmb[2 + h][t0:t0 + 512, :].rearrange("(j p) d -> p j d", p=128)),
                    r=[("tm",)], w=[("vb", bi)], dma=True)
                S.op("sp", lambda e, bi=bi, t0=t0, h=h: e.dma_start(
                    out=ob[bi][:, :, :], in_=tmb[6 + h][t0:t0 + 512, :].rearrange("(j p) d -> p j d", p=128)),
                    r=[("tm",)], w=[("ob", bi)], dma=True)
                for cc in range(4):
                    c = tb * 4 + cc
                    ecol = eT[:, c * 4 + h:c * 4 + h + 1]
                    fcol = flT[:, c * 4 + h:c * 4 + h + 1]
                    rcol = r1b[:, h * NQ + c:h * NQ + c + 1]
                    tsl = slice(cc * 128, (cc + 1) * 128)
                    p1, p1k = psum()
                    for c2 in range(2):
                        S.op("pe", lambda e, c2=c2, p1=p1, bi=bi, tsl=tsl: e.matmul(
                            p1[:, 0:128], kTb[bi][:, c2, tsl], qTb[bi][:, c2, tsl], start=(c2 == 0), stop=(c2 == 1)),
                            r=[("kTb", bi), ("qTb", bi)], w=[p1k])
                    pt, ptk = PT.next()
                    S.op("dve", lambda e, p1=p1, pt=pt, ecol=ecol: e.scalar_tensor_tensor(
                        out=pt[:, :], in0=p1[:, 0:128], scalar=ecol, in1=mask01[:, :], op0=ALU.mult, op1=ALU.mult),
                        r=[p1k, "eT", "mask01"], w=[ptk])
                    S.op("dve", lambda e, rcol=rcol: e.tensor_scalar(out=Cs[:, :, :], in0=Cs[:, :, :], scalar1=rcol,
                                                                    scalar2=None, op0=ALU.mult), r=["Cs", "r1b"], w=["Cs"])
                    S.op("act", lambda e: e.activation(out=Cb[:, :, :], in_=Cs[:, :, :], func=AF.Identity), r=["Cs"], w=["Cb"])
                    S.op("dve", lambda e, rcol=rcol: e.tensor_scalar(out=ns[:, :], in0=ns[:, :], scalar1=rcol,
                                                                    scalar2=None, op0=ALU.mult), r=["ns", "r1b"], w=["ns"])
                    S.op("dve", lambda e: e.tensor_copy(out=nb_[:, :], in_=ns[:, :]), r=["ns"], w=["nb_"])
                    p2, p2k = psum()
                    for c2 in range(2):
                        S.op("pe", lambda e, c2=c2, p2=p2, bi=bi, tsl=tsl: e.matmul(
                            p2[:, :], qTb[bi][:, c2, tsl], Cb[:, c2, :], start=(c2 == 0), stop=False),
                            r=[("qTb", bi), "Cb"], w=[p2k])
                    S.op("pe", lambda e, p2=p2, pt=pt, bi=bi, cc=cc: e.matmul(
                        p2[:, :], pt[:, :], vb[bi][:, cc, :], start=False, stop=True), r=[ptk, ("vb", bi)], w=[p2k])
                    p3, p3k = psum()
                    for c2 in range(2):
                        S.op("pe", lambda e, c2=c2, p3=p3, bi=bi, tsl=tsl: e.matmul(
                            p3[:, 0:1], qTb[bi][:, c2, tsl], nb_[:, c2:c2 + 1], start=(c2 == 0), stop=False),
                            r=[("qTb", bi), "nb_"], w=[p3k])
                    S.op("pe", lambda e, p3=p3, pt=pt: e.matmul(p3[:, 0:1], pt[:, :], ones1[:, 0:1], start=False, stop=True),
                         r=[ptk, "ones1"], w=[p3k])
                    kt_, ktk = Kt.next()
                    S.op("pool", lambda e, kt_=kt_, bi=bi, cc=cc, ecol=ecol: e.tensor_scalar(
                        out=kt_[:, :], in0=ktm[bi][:, cc, :], scalar1=ecol, scalar2=None, op0=ALU.mult),
                        r=[("ktm", bi), "eT"], w=[ktk])
                    p5, p5k = psum()
                    for c2 in range(2):
                        p4, p4k = psum()
                        S.op("pe", lambda e, c2=c2, p4=p4, kt_=kt_, bi=bi, cc=cc: e.matmul(
                            p4[:, :], kt_[:, c2 * 128:(c2 + 1) * 128], vb[bi][:, cc, :], start=True, stop=True),
                            r=[ktk, ("vb", bi)], w=[p4k])
                        S.op("pe", lambda e, c2=c2, p5=p5, kt_=kt_: e.matmul(
                            p5[:, c2:c2 + 1], kt_[:, c2 * 128:(c2 + 1) * 128], ones1[:, 0:1], start=True, stop=True),
                            r=[ktk, "ones1"], w=[p5k])
                        S.op("dve", lambda e, c2=c2, p4=p4: e.tensor_tensor(out=Cs[:, c2, :], in0=Cs[:, c2, :], in1=p4[:, :],
                                                                          op=ALU.add), r=[p4k, "Cs", "Cb"], w=["Cs"])
                    S.op("dve", lambda e, p5=p5: e.tensor_tensor(out=ns[:, :], in0=ns[:, :], in1=p5[:, 0:2], op=ALU.add),
                         r=[p5k, "ns", "nb_"], w=["ns"])
                    dd, ddk = d1.next()
                    S.op("act", lambda e, dd=dd, p3=p3: e.activation(out=dd[:, 0:1], in_=p3[:, 0:1], func=AF.Abs),
                         r=[p3k], w=[(ddk, 0)])
                    S.op("dve", lambda e, dd=dd, fcol=fcol: e.tensor_scalar(
                        out=dd[:, 0:1], in0=dd[:, 0:1], scalar1=fcol, scalar2=None, op0=ALU.max),
                        r=[(ddk, 0), "flT"], w=[(ddk, 0)])
                    S.op("dve", lambda e, dd=dd: e.reciprocal(out=dd[:, 1:2], in_=dd[:, 0:1]), r=[(ddk, 0)], w=[(ddk, 1)])
                    S.op("pool", lambda e, dd=dd: e.memset(dd[:, 2:3], 0.0), w=[(ddk, 2)])
                    S.op("act", lambda e, dd=dd, p2=p2: e.activation(out=junk[:, :], in_=p2[:, :], func=AF.Square,
                                                                    scale=dd[:, 1:2], accum_out=dd[:, 2:3]),
                         r=[p2k, (ddk, 1), (ddk, 2)], w=[(ddk, 2), "junk"])
                    S.op("act", lambda e, dd=dd: e.activation(out=dd[:, 3:4], in_=dd[:, 2:3], func=AF.Ln,
                                                             bias=epsc[:, 0:1], scale=1.0 / 512.0),
                         r=[(ddk, 2), "epsc"], w=[(ddk, 3)])
                    S.op("act", lambda e, dd=dd: e.activation(out=dd[:, 3:4], in_=dd[:, 3:4], func=AF.Exp, scale=-0.5),
                         r=[(ddk, 3)], w=[(ddk, 3)])
                    S.op("dve", lambda e, dd=dd: e.tensor_tensor(out=dd[:, 3:4], in0=dd[:, 3:4], in1=dd[:, 1:2],
                                                                op=ALU.mult), r=[(ddk, 3), (ddk, 1)], w=[(ddk, 3)])
                    yt, ytk = ytmp.next()
                    S.op("dve", lambda e, yt=yt, p2=p2, dd=dd, h=h: e.scalar_tensor_tensor(
                        out=yt[:, :], in0=p2[:, :], scalar=dd[:, 3:4], in1=nwb[:, h * 512:(h + 1) * 512],
                        op0=ALU.mult, op1=ALU.mult), r=[p2k, (ddk, 3), "nwb"], w=[ytk])
                    yb, ybk = ybf.next()
                    S.op("pool", lambda e, yb=yb, yt=yt, bi=bi, cc=cc: e.tensor_tensor(
                        out=yb[:, :], in0=yt[:, :], in1=ob[bi][:, cc, :], op=ALU.mult), r=[ytk, ("ob", bi)], w=[ybk])
                    p6, p6k = psum()
                    p6v = p6[:, 0:256].bitcast(BF16)
                    for d4 in range(4):
                        S.op("pe", lambda e, d4=d4, p6v=p6v, yb=yb: e.transpose(
                            p6v[:, d4 * 128:(d4 + 1) * 128], yb[:, d4 * 128:(d4 + 1) * 128], ident_b[:, :]),
                            r=[ybk, "ident_b"], w=[p6k])
                    S.op("act", lambda e, p6v=p6v, bi=bi, tsl=tsl: e.activation(
                        out=yTb[bi][:, :, tsl], in_=p6v.rearrange("p (d t) -> p d t", d=4, t=128), func=AF.Identity),
                        r=[p6k], w=[("yTb", bi, cc)])
                S.op("sp", lambda e, bi=bi, t0=t0, h=h: e.dma_start(
                    out=oT_d[h * 512:(h + 1) * 512, t0:t0 + 512].rearrange("(d p) t -> p d t", p=128), in_=yTb[bi][:, :, :]),
                    r=[("yTb", bi, cc) for cc in range(4)], w=[("oT",)], dma=True)
        S.barrier()

    x_src = xT_in
    for l in range(depth):
        phase_inproj(l, x_src)
        S.barrier()
        if l % 2 == 1:
            phase_fox(l)
        else:
            phase_mlstm(l)
        phase_post(l, x_src, last=(l == depth - 1))
        S.barrier()
        x_src = xs_d
    S.op("sp", None)
    S.emit()
    return nc, stack


def _consts():
    c = np.zeros((128, 512 + 2048), np.float32)
    c[:, 0:128] = np.eye(128, dtype=np.float32)
    s = np.arange(128)
    c[:, 128:256] = (s[:, None] <= s[None, :]).astype(np.float32)
    c[:, 256:384] = np.where(s[None, :] <= s[:, None], 0.0, NEG)
    for h in range(16):
        c[h, 512 + h * 128:512 + (h + 1) * 128] = 1.0
    m = np.zeros((128, 4 * 512), np.float32)
    ql = np.arange(512)
    for j in range(4):
        m[:, j * 512:(j + 1) * 512] = np.where(j * 128 + s[:, None] <= ql[None, :], 0.0, NEG)
    return np.concatenate([c, m], axis=1)


def _col(v):
    v = np.asarray(v, np.float32)
    return np.ascontiguousarray(v.reshape(-1, 128).T)


def make_in_map(b, T, depth, x, c, ada_w, ada_b, norm_mix_w, norm_ffn_w, mlstm_w_in, mlstm_b_gates,
                mlstm_norm_w, mlstm_w_out, fox_w_in, fox_b_f, fox_w_out, ffn_w_gate_up, ffn_w_down, final_norm_w):
    n_ml = (depth + 1) // 2
    n_fx = depth // 2
    m = {}
    m["xT"] = np.ascontiguousarray(np.asarray(x[b], np.float32).T)
    m["cT"] = _col(c[b])
    m["ada_w"] = np.ascontiguousarray(ada_w[:depth], dtype=np.float32)
    m["ada_bT"] = np.concatenate([_col(ada_b[l]) for l in range(depth)], axis=1)
    m["nmw"] = np.concatenate([_col(norm_mix_w[l]) for l in range(depth)], axis=1)
    m["nfw"] = np.concatenate([_col(norm_ffn_w[l]) for l in range(depth)], axis=1)
    m["fnw"] = _col(final_norm_w)
    m["mw_in"] = np.ascontiguousarray(mlstm_w_in[:max(n_ml, 1)], dtype=np.float32)
    bgs = np.asarray(mlstm_b_gates, np.float32)[:max(n_ml, 1)]
    m["mbg"] = np.ascontiguousarray(np.concatenate([np.stack([bg[0:4], bg[4:8]], axis=1) for bg in bgs], axis=1))
    m["mnw"] = np.ascontiguousarray(np.concatenate(
        [np.broadcast_to(np.asarray(w, np.float32)[None, :], (128, 2048)) for w in mlstm_norm_w[:max(n_ml, 1)]], axis=1))
    m["mw_out"] = np.ascontiguousarray(mlstm_w_out[:max(n_ml, 1)], dtype=np.float32)
    m["fw_in"] = np.ascontiguousarray(fox_w_in[:max(n_fx, 1)], dtype=np.float32)
    m["fbf"] = np.ascontiguousarray(np.asarray(fox_b_f, np.float32)[:max(n_fx, 1)].reshape(-1, 1))
    m["fw_out"] = np.ascontiguousarray(fox_w_out[:max(n_fx, 1)], dtype=np.float32)
    m["w_gu"] = np.ascontiguousarray(ffn_w_gate_up[:depth], dtype=np.float32)
    m["w_dn"] = np.ascontiguousarray(ffn_w_down[:depth], dtype=np.float32)
    m["consts"] = _consts()
    return m


_CACHE = {}


def kernel(**inputs):
    x = np.asarray(inputs["x"])
    Bn, T, _ = x.shape
    depth = np.asarray(inputs["ada_w"]).shape[0]
    key = (T, depth)
    if key not in _CACHE:
        _CACHE[key] = build(T, depth)
    nc, _stack = _CACHE[key]
    arrs = {k: np.asarray(v) for k, v in inputs.items()}
    in_maps = [make_in_map(b, T, depth, **arrs) for b in range(Bn)]
    res = run_bass_kernel_spmd(nc, in_maps, core_ids=list(range(Bn)))
    out = np.stack([np.ascontiguousarray(r["outT"].T) for r in res.results], axis=0)
    return out.astype(np.float32)
```

```python
from contextlib import ExitStack
import numpy as np
import ml_dtypes
import concourse.bass as bass
import concourse.mybir as mybir
from concourse.bass_utils import run_bass_kernel_spmd

F32 = mybir.dt.float32
BF16 = mybir.dt.bfloat16
AF = mybir.ActivationFunctionType
ALU = mybir.AluOpType
AX = mybir.AxisListType

D = 2048
NC_ = 16
DFF = 5632
NFC = 44
EPS = 1e-6
TT = 512
NEG = -30000.0
R_DMA = 8


class LoopVar:
    cur = None


LV = LoopVar()


def TS(idx, size, off=0):
    if isinstance(idx, LoopVar):
        if isinstance(idx.cur, int):
            return slice(idx.cur * size + off, idx.cur * size + off + size)
        assert off == 0
        return bass.ts(idx.cur, size)
    return slice(idx * size + off, idx * size + off + size)


def DS(idx, stride, off, size):
    if isinstance(idx, LoopVar):
        return bass.ds(idx.cur * stride + off, size)
    return slice(idx * stride + off, idx * stride + off + size)


class Ring:
    registry = []

    def __init__(self, items):
        self.items = items
        self.i = 0
        Ring.registry.append(self)

    def next(self):
        x = self.items[self.i % len(self.items)]
        self.i += 1
        return x


ENGS = ["pe", "act", "dve", "pool", "sp"]
QUEUES = ["sp", "pool", "act"]


class Sch:
    ARENA = 188 * 1024

    def __init__(self, nc, stack):
        self.nc = nc
        self.stack = stack
        self.ops = []
        self.lw = {}
        self.lr = {}
        self.barrier_deps = set()
        self.last_on = {}
        self.dma_hist = {q: [] for q in QUEUES}
        self.n_t = 0
        self.regions = []
        self._after_barrier = set(ENGS)
        Ring.registry.clear()

    def sb(self, shape, dt, name=None):
        if not hasattr(self, "arena"):
            self.arena = self.stack.enter_context(self.nc.sbuf_tensor("arena", [128, self.ARENA], mybir.dt.uint8))
            self.top = 0
        esz = 4 if dt == F32 else 2
        n = 1
        for d in shape[1:]:
            n *= d
        nbytes = (n * esz + 63) // 64 * 64
        off = self.top
        self.top += nbytes
        assert self.top <= self.ARENA, f"SBUF arena overflow: {self.top} ({name})"
        ap = self.arena[0:shape[0], off:off + n * esz].bitcast(dt)
        if len(shape) == 3:
            ap = ap.rearrange("p (a b) -> p a b", a=shape[1], b=shape[2])
        return ap

    def mark(self):
        return self.top

    def reset(self, m):
        self.top = m

    def ps(self, shape, dt=F32, name=None):
        self.n_t += 1
        return self.stack.enter_context(self.nc.psum_tensor(name or f"p{self.n_t}", list(shape), dt))

    def op(self, eng, fn, r=(), w=(), dma=False):
        idx = len(self.ops)
        deps = set(self.barrier_deps) if eng not in self._after_barrier else set()
        self._after_barrier.add(eng)
        for k in r:
            x = self.lw.get(k)
            if x is not None:
                deps.add(x)
        for k in w:
            x = self.lw.get(k)
            if x is not None:
                deps.add(x)
            for y in self.lr.get(k, ()):
                deps.add(y)
        for k in r:
            self.lr.setdefault(k, []).append(idx)
        for k in w:
            self.lw[k] = idx
            self.lr[k] = []
        deps.discard(idx)
        self.ops.append(dict(eng=eng, fn=fn, deps=deps, dma=dma))
        if dma:
            self.dma_hist[eng].append(idx)
        else:
            self.last_on[eng] = idx
        return idx

    def barrier(self):
        deps = set(self.last_on.values())
        for q, h in self.dma_hist.items():
            deps.update(h[-R_DMA:])
        self.barrier_deps = deps
        self._after_barrier = set()

    def loop(self, N, body, reset_fn=None):
        if N == 1:
            body(0)
            return
        self.barrier()
        reg = dict(N=N, s0=len(self.ops), bdeps=set(self.barrier_deps))
        for rg in Ring.registry:
            rg.i = 0
        if reset_fn:
            reset_fn()
        body(LV)
        reg["s1"] = len(self.ops)
        for rg in Ring.registry:
            rg.i = 0
        if reset_fn:
            reset_fn()
        body(LV)
        reg["s2"] = len(self.ops)
        assert reg["s2"] - reg["s1"] == reg["s1"] - reg["s0"], "loop body not iteration-invariant"
        for a, b in zip(range(reg["s0"], reg["s1"]), range(reg["s1"], reg["s2"])):
            assert self.ops[a]["eng"] == self.ops[b]["eng"] and self.ops[a]["dma"] == self.ops[b]["dma"]
        self.regions.append(reg)
        self.barrier()

    def emit(self):
        nc = self.nc
        ops = self.ops
        n = len(ops)
        reg_of = [None] * n
        copy_of = [0] * n
        for reg in self.regions:
            for i in range(reg["s0"], reg["s1"]):
                reg_of[i] = reg
                copy_of[i] = 1
            for i in range(reg["s1"], reg["s2"]):
                reg_of[i] = reg
                copy_of[i] = 2

        def twin(i):
            reg = reg_of[i]
            return i + (reg["s1"] - reg["s0"]) if copy_of[i] == 1 else i

        def pe_pe(d, i):
            return ops[d]["eng"] == "pe" and ops[i]["eng"] == "pe" and not ops[d]["dma"] and not ops[i]["dma"]

        sig = [False] * n
        for i, o in enumerate(ops):
            for d in o["deps"]:
                if pe_pe(d, i):
                    continue
                sig[twin(d)] = True
        cnt = {e: 0 for e in ENGS}
        rr = {q: 0 for q in QUEUES}
        tot = {q: [0] * R_DMA for q in QUEUES}
        i = 0
        while i < n:
            reg = reg_of[i]
            if reg is None:
                o = ops[i]
                e = o["eng"]
                if o["dma"]:
                    s = rr[e] % R_DMA
                    rr[e] += 1
                    tot[e][s] += 1
                    o["slot"] = s
                    o["c"], o["k"] = 16 * tot[e][s], 0
                elif sig[i]:
                    cnt[e] += 1
                    o["c"], o["k"] = cnt[e], 0
                i += 1
                continue
            N = reg["N"]
            body = range(reg["s1"], reg["s2"])
            delta = {e: 0 for e in ENGS}
            for j in body:
                if not ops[j]["dma"] and sig[j]:
                    delta[ops[j]["eng"]] += 1
            run_c = {e: 0 for e in ENGS}
            rr0 = dict(rr)
            cslot = {q: [0] * R_DMA for q in QUEUES}
            for j in body:
                if ops[j]["dma"]:
                    q = ops[j]["eng"]
                    s = rr0[q] % R_DMA
                    rr0[q] += 1
                    ops[j]["slot"] = s
                    ops[j]["m"] = cslot[q][s]
                    cslot[q][s] += 1
            for j in body:
                o = ops[j]
                e = o["eng"]
                if o["dma"]:
                    s = o["slot"]
                    o["c"] = 16 * (tot[e][s] + o["m"] + 1)
                    o["k"] = 16 * cslot[e][s]
                elif sig[j]:
                    run_c[e] += 1
                    o["c"], o["k"] = cnt[e] + run_c[e], delta[e]
            for e in ENGS:
                cnt[e] += N * delta[e]
            for q in QUEUES:
                for s in range(R_DMA):
                    tot[q][s] += N * cslot[q][s]
            reg["cslot"] = cslot
            off = reg["s1"] - reg["s0"]
            for j0 in range(reg["s0"], reg["s1"]):
                t = ops[j0 + off]
                if "c" in t:
                    ops[j0]["c"], ops[j0]["k"] = t["c"], 0
                if "slot" in t:
                    ops[j0]["slot"] = t["slot"]
            i = reg["s2"]

        csem = {e: self.stack.enter_context(nc.semaphore(f"c_{e}")) for e in ENGS}
        dsem = {q: [self.stack.enter_context(nc.semaphore(f"d_{q}{s}")) for s in range(R_DMA)] for q in QUEUES}

        def semkey(p):
            return ("d", p["eng"], p["slot"]) if p["dma"] else ("c", p["eng"])

        def dep_target(d, i):
            t = twin(d)
            p = ops[t]
            key = semkey(p)
            if reg_of[i] is not None and reg_of[i] is reg_of[d]:
                if copy_of[i] == 1:
                    return key, p["c"], 0
                if copy_of[d] == 1:
                    return key, p["c"] - p["k"], p["k"]
                return key, p["c"], p["k"]
            if reg_of[d] is not None:
                return key, p["c"] + (reg_of[d]["N"] - 1) * p["k"], 0
            return key, p["c"], 0

        tmpregs = {}
        itregs = {}

        def emit_waits(eobj, waits, waited, it):
            for (key, k), c in waits.items():
                prev = waited.get((key, k))
                if prev is not None and prev >= c:
                    continue
                waited[(key, k)] = c
                sem = csem[key[1]] if key[0] == "c" else dsem[key[1]][key[2]]
                if k == 0 or it is None:
                    eobj.wait_ge(sem, c)
                else:
                    rg = tmpregs.get(id(eobj))
                    if rg is None:
                        rg = eobj.alloc_register()
                        tmpregs[id(eobj)] = rg
                    itr = waited.get("__itr")
                    if itr is None:
                        itr = eobj.to_reg(it)
                        waited["__itr"] = itr
                    eobj.reg_mul(rg, itr, k)
                    eobj.reg_add(rg, rg, c)
                    eobj.wait_ge(sem, rg)

        def collect(i, ename):
            o = ops[i]
            waits = {}
            for d in o["deps"]:
                if pe_pe(d, i):
                    continue
                key, c, k = dep_target(d, i)
                if waits.get((key, k), -10 ** 9) < c:
                    waits[(key, k)] = c
            if o["dma"]:
                key = ("d", ename, o["slot"])
                c, k = o["c"] - 16, o["k"]
                if copy_of[i] == 1:
                    k = 0
                if c > 0 or k > 0:
                    if waits.get((key, k), -10 ** 9) < c:
                        waits[(key, k)] = c
            return waits

        def emit_op(i, ename, eobj, waited, it):
            o = ops[i]
            emit_waits(eobj, collect(i, ename), waited, it)
            if o["fn"] is None:
                return
            ins = o["fn"](eobj)
            if o["dma"]:
                ins.then_inc(dsem[ename][o["slot"]], 16)
            elif sig[twin(i)]:
                ins.then_inc(csem[ename], 1)

        def run(ename, eobj):
            waited = {}
            i = 0
            while i < n:
                reg = reg_of[i]
                if reg is None:
                    if ops[i]["eng"] == ename:
                        emit_op(i, ename, eobj, waited, None)
                    i += 1
                    continue
                LV.cur = 0
                for j in range(reg["s0"], reg["s1"]):
                    if ops[j]["eng"] == ename:
                        emit_op(j, ename, eobj, waited, None)
                LV.cur = None
                mine = [j for j in range(reg["s1"], reg["s2"]) if ops[j]["eng"] == ename]
                if mine:
                    with eobj.Fori(1, reg["N"]) as iv:
                        LV.cur = iv
                        w2 = {}
                        for j in mine:
                            emit_op(j, ename, eobj, w2, iv)
                    LV.cur = None
                i = reg["s2"]

        with nc.Block() as block:
            @block.tensor
            def _(e):
                run("pe", e)

            @block.scalar
            def _(e):
                run("act", e)

            @block.vector
            def _(e):
                run("dve", e)

            @block.gpsimd
            def _(e):
                run("pool", e)

            @block.sync
            def _(e):
                run("sp", e)


def build(T, depth, dbg=False):
    NT = T // TT
    NQ = T // 128
    n_ml = (depth + 1) // 2
    n_fx = depth // 2
    nc = bass.Bass("TRN2", target_bir_lowering=False)
    stack = ExitStack()
    S = Sch(nc, stack)

    def dram(name, shape, dt, kind="Internal"):
        return nc.dram_tensor(name, list(shape), dt, kind=kind).ap()

    xT_in = dram("xT", [D, T], F32, "ExternalInput")
    cT = dram("cT", [128, 16], F32, "ExternalInput")
    ada_w = dram("ada_w", [depth, D, 6 * D], F32, "ExternalInput")
    ada_bT = dram("ada_bT", [128, depth * 96], F32, "ExternalInput")
    nmw = dram("nmw", [128, depth * 16], F32, "ExternalInput")
    nfw = dram("nfw", [128, depth * 16], F32, "ExternalInput")
    fnw = dram("fnw", [128, 16], F32, "ExternalInput")
    mw_in = dram("mw_in", [max(n_ml, 1), D, 6152], F32, "ExternalInput")
    mbg = dram("mbg", [4, max(n_ml, 1) * 2], F32, "ExternalInput")
    mnw = dram("mnw", [128, max(n_ml, 1) * 2048], F32, "ExternalInput")
    mw_out = dram("mw_out", [max(n_ml, 1), D, D], F32, "ExternalInput")
    fw_in = dram("fw_in", [max(n_fx, 1), D, 6160], F32, "ExternalInput")
    fbf = dram("fbf", [16 * max(n_fx, 1), 1], F32, "ExternalInput")
    fw_out = dram("fw_out", [max(n_fx, 1), D, D], F32, "ExternalInput")
    w_gu = dram("w_gu", [depth, D, 2 * DFF], F32, "ExternalInput")
    w_dn = dram("w_dn", [depth, DFF, D], F32, "ExternalInput")
    consts = dram("consts", [128, 2560 + 2048], F32, "ExternalInput")
    outT = dram("outT", [D, T], F32, "ExternalOutput")

    xs_d = dram("xs_d", [D, T], F32)
    qkT_d = dram("qkT_d", [2 * D, T], BF16)
    tm_d = dram("tm_d", [T, 2048], BF16)
    tmb = [dram(f"tmb{i}", [T, 512], BF16) for i in range(10)]
    gate_d = dram("gate_d", [16, T], F32)
    gate2_d = dram("gate2_d", [4, T], F32)
    oT_d = dram("oT_d", [D, T], BF16)
    ncum_d = dram("ncum_d", [16, T], F32)
    if dbg:
        dbg_h = dram("dbg_h", [D, T], F32, "ExternalOutput")

    ident_f = S.sb([128, 128], F32, "ident_f")
    ident_b = S.sb([128, 128], BF16, "ident_b")
    mask01 = S.sb([128, 128], F32, "mask01")
    maskneg = S.sb([128, 128], F32, "maskneg")
    onesm = S.sb([128, 128], BF16, "onesm")
    ones1 = S.sb([128, 128], BF16, "ones1")
    sel16 = S.sb([16, 4 * 128], F32, "sel16")
    modsb = S.sb([128, depth * 96], F32, "modsb")
    gam1 = S.sb([128, depth * 16], F32, "gam1")
    gam2 = S.sb([128, depth * 16], F32, "gam2")
    nmw_s = S.sb([128, depth * 16], F32, "nmw_s")
    nfw_s = S.sb([128, depth * 16], F32, "nfw_s")
    fnw_s = S.sb([128, 16], F32, "fnw_s")
    zero_c = S.sb([128, 16], F32, "zero_c")
    epsc = S.sb([128, 1], F32, "epsc")
    onec = S.sb([128, 1], F32, "onec")

    S.op("pool", lambda e: e.dma_start(out=ident_f[:], in_=consts[:, 0:128]), w=["ident_f"], dma=True)
    S.op("pool", lambda e: e.dma_start(out=ident_b[:], in_=consts[:, 0:128]), w=["ident_b"], dma=True)
    S.op("pool", lambda e: e.dma_start(out=mask01[:], in_=consts[:, 128:256]), w=["mask01"], dma=True)
    S.op("pool", lambda e: e.dma_start(out=maskneg[:], in_=consts[:, 256:384]), w=["maskneg"], dma=True)
    S.op("pool", lambda e: e.dma_start(out=sel16[:], in_=consts[0:16, 512:512 + 512]), w=["sel16"], dma=True)
    S.op("pool", lambda e: e.dma_start(out=nmw_s[:], in_=nmw[:, :]), w=["nmw_s"], dma=True)
    S.op("pool", lambda e: e.dma_start(out=nfw_s[:], in_=nfw[:, :]), w=["nfw_s"], dma=True)
    S.op("pool", lambda e: e.dma_start(out=fnw_s[:], in_=fnw[:, :]), w=["fnw_s"], dma=True)
    S.op("dve", lambda e: e.memset(onesm[:], 1.0 / D), w=["onesm"])
    S.op("dve", lambda e: e.memset(ones1[:], 1.0), w=["ones1"])
    S.op("dve", lambda e: e.memset(zero_c[:], 0.0), w=["zero_c"])
    S.op("dve", lambda e: e.memset(epsc[:], EPS), w=["epsc"])
    S.op("dve", lambda e: e.memset(onec[:], 1.0), w=["onec"])

    wblk = {}

    class WRef:
        def __init__(self, name, i):
            self.name, self.i = name, i

    class WMat:
        def __init__(self, name):
            self.name = name

        def __getitem__(self, i):
            return WRef(self.name, i)

    conv_jobs = {}

    def conv(name, src, n, blocks, slot):
        dst = dram(name, [n * len(blocks), 128, slot], BF16)
        for i in range(n):
            jobs = []
            for bi_, (k0, nk, c0, ncb_) in enumerate(blocks):
                d = dst[i * len(blocks) + bi_, :, 0:nk * ncb_]
                wblk[(name, i, k0, c0)] = (d, nk, ncb_)
                jobs.append((d, src, i, k0, nk, c0, ncb_))
            conv_jobs[(name, i)] = jobs
        return WMat(name)

    def emit_conv(name, i):
        for (d, src, i_, k0, nk, c0, ncb_) in conv_jobs.pop((name, i), []):
            S.op("pool", lambda e, i_=i_, k0=k0, nk=nk, c0=c0, ncb_=ncb_, d=d, src=src: e.dma_start(
                out=d.rearrange("p (k c) -> p k c", k=nk, c=ncb_),
                in_=src[i_, k0 * 128:(k0 + nk) * 128, c0:c0 + ncb_].rearrange("(k p) c -> p k c", p=128)),
                w=[("wconv", name, i_)], dma=True)

    def emit_conv_layer(l):
        if l >= depth:
            return
        if l % 2 == 0:
            emit_conv("mw_in_b", l // 2)
            emit_conv("mw_out_b", l // 2)
        else:
            emit_conv("fw_in_b", l // 2)
            emit_conv("fw_out_b", l // 2)
        emit_conv("w_gu_b", l)
        emit_conv("w_dn_b", l)

    in_blocks = [(0, 16, c0, 512) for c0 in range(0, 6144, 512)]
    mw_in = conv("mw_in_b", mw_in, max(n_ml, 1), in_blocks + [(0, 16, 6144, 8)], 16 * 512)
    fw_in = conv("fw_in_b", fw_in, max(n_fx, 1), in_blocks + [(0, 16, 6144, 16)], 16 * 512)
    out_blocks = [(0, 16, c0, 512) for c0 in range(0, D, 512)]
    mw_out = conv("mw_out_b", mw_out, max(n_ml, 1), out_blocks, 16 * 512)
    fw_out = conv("fw_out_b", fw_out, max(n_fx, 1), out_blocks, 16 * 512)
    gu_blocks = [(0, 16, c0, 256) for c0 in range(0, 2 * DFF, 256)]
    w_gu = conv("w_gu_b", w_gu, depth, gu_blocks, 16 * 256)
    dn_blocks = [(kg * 16, (16 if kg < 2 else 12), cb * 512, 512) for kg in range(3) for cb in range(4)]
    w_dn = conv("w_dn_b", w_dn, depth, dn_blocks, 16 * 512)
    emit_conv_layer(0)
    S.barrier()

    m0 = S.mark()
    NWB = 3
    wbufs = [S.sb([128, 16, 512], BF16, f"wb{i}") for i in range(NWB)]
    wstate = dict(n=0)

    def wload(src2d, k0, nk, c0, ncols, dcol=0):
        i = wstate["n"] % NWB
        wstate["n"] += 1
        b = wbufs[i]
        if isinstance(src2d, WRef):
            d, nk_, nc_ = wblk[(src2d.name, src2d.i, k0, c0)]
            assert nk_ == nk and nc_ == ncols, (src2d.name, k0, c0, nk, ncols)
            src = d.rearrange("p (k c) -> p k c", k=nk, c=ncols)
        else:
            src = src2d[k0 * 128:(k0 + nk) * 128, c0:c0 + ncols].rearrange("(k p) c -> p k c", p=128)
        S.op("pool", lambda e: e.dma_start(out=b[:, 0:nk, dcol:dcol + ncols], in_=src), w=[("wb", i)], dma=True)
        return b, ("wb", i)

    def wload2(src2d, k0, nk, c0, c1, ncols):
        i = wstate["n"] % NWB
        wstate["n"] += 1
        b = wbufs[i]
        s0 = src2d[k0 * 128:(k0 + nk) * 128, c0:c0 + ncols].rearrange("(k p) c -> p k c", p=128)
        s1 = src2d[k0 * 128:(k0 + nk) * 128, c1:c1 + ncols].rearrange("(k p) c -> p k c", p=128)
        S.op("pool", lambda e: e.dma_start(out=b[:, 0:nk, 0:ncols], in_=s0), w=[("wb", i)], dma=True)
        S.op("pool", lambda e: e.dma_start(out=b[:, 0:nk, ncols:2 * ncols], in_=s1), w=[("wb", i, 1)], r=[("wb", i)], dma=True)
        return b, ("wb", i, 1)

    def reset_w():
        wstate["n"] = 0

    def wstream(blocks, consume, la=NWB - 1):
        issued = []
        for j in range(len(blocks)):
            while len(issued) < min(len(blocks), j + la + 1):
                issued.append(blocks[len(issued)]())
            b, key = issued[j]
            consume(j, b, key)

    psb = [S.ps([128, 512], F32, f"psb{i}") for i in range(8)]
    psring = Ring(list(range(8)))

    def psum():
        i = psring.next()
        return psb[i], ("ps", i)

    cs32 = S.sb([128, 16], F32, "cs32")
    csb = S.sb([128, 16], BF16, "csb")
    adab_s = S.sb([128, depth * 96], F32, "adab_s")
    S.op("pool", lambda e: e.dma_start(out=cs32[:], in_=cT[:, :]), w=["cs32"], dma=True)
    S.op("pool", lambda e: e.dma_start(out=adab_s[:], in_=ada_bT[:, :]), w=["adab_s"], dma=True)
    S.op("act", lambda e: e.activation(out=csb[:], in_=cs32[:], func=AF.Silu), r=["cs32"], w=["csb"])
    for l in range(depth):
        pm, pmk = psum()
        blocks = [(lambda l=l, bi=bi: wload(ada_w[l], 0, 16, bi * 512, 512)) for bi in range(24)]

        def cons(j, b, key, pm=pm, pmk=pmk):
            for j4 in range(4):
                col = j * 4 + j4
                for kc in range(16):
                    S.op("pe", lambda e, b=b, kc=kc, j4=j4, col=col: e.matmul(
                        pm[:, col:col + 1], b[:, kc, j4 * 128:(j4 + 1) * 128], csb[:, kc:kc + 1],
                        start=(kc == 0), stop=(kc == 15)), r=[key, "csb"], w=[pmk])
        wstream(blocks, cons)
        S.op("dve", lambda e, l=l, pm=pm: e.tensor_tensor(out=modsb[:, l * 96:(l + 1) * 96], in0=pm[:, 0:96],
                                                       in1=adab_s[:, l * 96:(l + 1) * 96], op=ALU.add),
             r=[pmk, "adab_s"], w=[("mod", l)])
        S.op("dve", lambda e, l=l: e.scalar_tensor_tensor(out=gam1[:, l * 16:(l + 1) * 16],
                                                         in0=modsb[:, l * 96 + 16:l * 96 + 32], scalar=1.0,
                                                         in1=nmw_s[:, l * 16:(l + 1) * 16], op0=ALU.add, op1=ALU.mult),
             r=[("mod", l), "nmw_s"], w=[("gam1", l)])
        S.op("dve", lambda e, l=l: e.scalar_tensor_tensor(out=gam2[:, l * 16:(l + 1) * 16],
                                                         in0=modsb[:, l * 96 + 64:l * 96 + 80], scalar=1.0,
                                                         in1=nfw_s[:, l * 16:(l + 1) * 16], op0=ALU.add, op1=ALU.mult),
             r=[("mod", l), "nfw_s"], w=[("gam2", l)])

    def modcol(l, which, c):
        base = l * 96 + which * 16 + c
        return modsb[:, base:base + 1]

    xs = S.sb([128, 16, TT], F32, "xs")
    hb = S.sb([128, 16, TT], BF16, "hb")
    act = S.sb([128, NFC, TT], BF16, "act")
    rstd = S.sb([128, TT], F32, "rstd")
    tmpr = Ring([(S.sb([128, TT], F32, f"tmp{i}"), ("tmp", i)) for i in range(3)])
    vst3 = S.sb([128, 4, 2048], BF16, "vst3")
    stgf = Ring([(S.sb([128, TT], F32, f"stgf{i}"), ("stgf", i)) for i in range(2)])

    def norm_tile(gam_ap, sh_ap, l, rdeps, out_fn=None):
        for g in range(4):
            S.op("act", lambda e, g=g: e.activation(out=act[:, 4 * g:4 * g + 4, :], in_=xs[:, 4 * g:4 * g + 4, :],
                                                   func=AF.Square),
                 r=[("xs", c) for c in range(4 * g, 4 * g + 4)], w=[("act", c) for c in range(4 * g, 4 * g + 4)])
        pss, pssk = psum()
        for c in range(16):
            S.op("pe", lambda e, c=c: e.matmul(pss[:, :], onesm[:, :], act[:, c, :], start=(c == 0), stop=(c == 15)),
                 r=[("act", c), "onesm"], w=[pssk])
        S.op("act", lambda e: e.activation(out=rstd[:], in_=pss[:, :], func=AF.Ln, bias=epsc[:, 0:1], scale=1.0),
             r=[pssk, "epsc"], w=["rstd"])
        S.op("act", lambda e: e.activation(out=rstd[:], in_=rstd[:], func=AF.Exp, scale=-0.5), r=["rstd"], w=["rstd"])
        for c in range(16):
            t, tk = tmpr.next()
            S.op("dve", lambda e, c=c, t=t: e.tensor_tensor(out=t[:], in0=xs[:, c, :], in1=rstd[:], op=ALU.mult),
                 r=[("xs", c), "rstd"], w=[tk])
            if out_fn is None:
                S.op("act", lambda e, c=c, t=t: e.activation(out=hb[:, c, :], in_=t[:], func=AF.Identity,
                                                            bias=sh_ap[:, c:c + 1], scale=gam_ap[:, c:c + 1]),
                     r=[tk] + rdeps, w=[("hb", c)])
            else:
                out_fn(c, t, tk)

    def load_x(src, tt):
        S.op("sp", lambda e: e.dma_start(out=xs[:, :, :],
                                         in_=src[:, TS(tt, TT)].rearrange("(c p) t -> p c t", p=128)),
             r=[("xd",)], w=[("xs", c) for c in range(16)], dma=True)

    evac_rr = Ring(["act", "dve"])

    def evac_bf16(ps_ap, psk, dst_ap, dstk, scale=None, func=None, extra_r=()):
        if func is not None:
            S.op("act", lambda e: e.activation(out=dst_ap, in_=ps_ap, func=func), r=[psk] + list(extra_r), w=[dstk])
            return
        eng = evac_rr.next()
        if eng == "act":
            S.op("act", lambda e: e.activation(out=dst_ap, in_=ps_ap, func=AF.Identity,
                                               scale=(1.0 if scale is None else scale)), r=[psk] + list(extra_r), w=[dstk])
        else:
            if scale is None:
                S.op("dve", lambda e: e.tensor_copy(out=dst_ap, in_=ps_ap), r=[psk] + list(extra_r), w=[dstk])
            else:
                S.op("dve", lambda e: e.tensor_scalar(out=dst_ap, in0=ps_ap, scalar1=scale, scalar2=None, op0=ALU.mult),
                     r=[psk] + list(extra_r), w=[dstk])

    def phase_inproj(l, x_src):
        mixer = l % 2
        j = l // 2
        if mixer == 0:
            W = mw_in[j]
            fm_blocks = 4
            qscale = 256 ** -0.5
            nq_chunks = 8
            tm_c0, tm_blocks = 1024, 10
        else:
            W = fw_in[j]
            fm_blocks = 8
            qscale = 128 ** -0.5
            nq_chunks = 16
            tm_c0, tm_blocks = 4096, 4
        def body(tt):
            load_x(x_src, tt)
            norm_tile(gam1[:, l * 16:(l + 1) * 16], modsb[:, l * 96:l * 96 + 16], l, [("gam1", l), ("mod", l)])
            blocks = []
            for bi in range(fm_blocks):
                blocks.append(lambda bi=bi: wload(W, 0, 16, bi * 512, 512))
            for bi in range(tm_blocks):
                blocks.append(lambda bi=bi: wload(W, 0, 16, tm_c0 + bi * 512, 512))
            blocks.append(lambda: wload(W, 0, 16, 6144, 8 if mixer == 0 else 16))

            def cons(jb, b, key):
                if jb < fm_blocks:
                    for oc4 in range(4):
                        oc = jb * 4 + oc4
                        p, pk = psum()
                        for kc in range(16):
                            S.op("pe", lambda e, p=p, b=b, kc=kc, oc4=oc4: e.matmul(
                                p[:, :], b[:, kc, oc4 * 128:(oc4 + 1) * 128], hb[:, kc, :],
                                start=(kc == 0), stop=(kc == 15)), r=[key, ("hb", kc)], w=[pk])
                        evac_bf16(p[:, :], pk, act[:, oc, :], ("act", oc), scale=(qscale if oc < nq_chunks else None))
                    if jb == fm_blocks - 1:
                        nfm = fm_blocks * 4
                        S.op("sp", lambda e: e.dma_start(
                            out=qkT_d.rearrange("(c p) t -> p c t", p=128)[:, 0:nfm, TS(tt, TT)], in_=act[:, 0:nfm, :]),
                            r=[("act", c) for c in range(nfm)], w=[("qkT", c) for c in range(32)], dma=True)
                elif jb < fm_blocks + tm_blocks:
                    nb = jb - fm_blocks
                    for m in range(4):
                        p, pk = psum()
                        for kc in range(16):
                            S.op("pe", lambda e, p=p, b=b, kc=kc, m=m: e.matmul(
                                p[:, :], hb[:, kc, m * 128:(m + 1) * 128], b[:, kc, :],
                                start=(kc == 0), stop=(kc == 15)), r=[key, ("hb", kc)], w=[pk])
                        is_o = (mixer == 0 and nb >= 6)
                        s4 = nb % 4
                        evac_bf16(p[:, :], pk, vst3[:, m, s4 * 512:(s4 + 1) * 512], ("vst", m, s4),
                                  func=(AF.Sigmoid if is_o else None))
                    if mixer == 0:
                        s4 = nb % 4
                        S.op("sp", lambda e, nb=nb, s4=s4: e.dma_start(
                            out=tmb[nb].rearrange("(a m p) c -> a p m c", m=4, p=128)[TS(tt, 1)]
                            .rearrange("a p m c -> (a p) m c"), in_=vst3[:, :, s4 * 512:(s4 + 1) * 512]),
                            r=[("vst", m, s4) for m in range(4)], w=[("tm",)], dma=True)
                    elif nb == tm_blocks - 1:
                        S.op("sp", lambda e: e.dma_start(
                            out=tm_d.rearrange("(a m p) c -> a p m c", m=4, p=128)[TS(tt, 1)]
                            .rearrange("a p m c -> (a p) m c"), in_=vst3[:, :, :]),
                            r=[("vst", m, x) for m in range(4) for x in range(4)], w=[("tm",)], dma=True)
                else:
                    if mixer == 0:
                        for gi in range(2):
                            p, pk = psum()
                            for kc in range(16):
                                S.op("pe", lambda e, p=p, b=b, kc=kc, gi=gi: e.matmul(
                                    p[0:4, :], b[:, kc, gi * 4:gi * 4 + 4], hb[:, kc, :],
                                    start=(kc == 0), stop=(kc == 15)), r=[key, ("hb", kc)], w=[pk])
                            sf, sfk = stgf.next()
                            S.op("dve", lambda e, p=p, sf=sf: e.tensor_copy(out=sf[0:4, :], in_=p[0:4, :]), r=[pk], w=[sfk])
                            dst = gate_d if gi == 0 else gate2_d
                            S.op("sp", lambda e, sf=sf, dst=dst: e.dma_start(
                                out=dst[0:4, TS(tt, TT)], in_=sf[0:4, :]), r=[sfk], w=[("gate", gi)], dma=True)
                    else:
                        p, pk = psum()
                        for kc in range(16):
                            S.op("pe", lambda e, p=p, b=b, kc=kc: e.matmul(
                                p[0:16, :], b[:, kc, 0:16], hb[:, kc, :],
                                start=(kc == 0), stop=(kc == 15)), r=[key, ("hb", kc)], w=[pk])
                        sf, sfk = stgf.next()
                        S.op("dve", lambda e, p=p, sf=sf: e.tensor_copy(out=sf[0:16, :], in_=p[0:16, :]), r=[pk], w=[sfk])
                        S.op("sp", lambda e, sf=sf: e.dma_start(
                            out=gate_d[0:16, TS(tt, TT)], in_=sf[0:16, :]), r=[sfk], w=[("gate", 0)], dma=True)
            wstream(blocks, cons)
        S.loop(NT, body, reset_w)

    def phase_post(l, x_src, last):
        mixer = l % 2
        j = l // 2
        Wo = mw_out[j] if mixer == 0 else fw_out[j]
        def fin_factory(tt):
            def fin(c, t, tk):
                S.op("act", lambda e, c=c, t=t: e.activation(out=xs[:, c, :], in_=t[:], func=AF.Identity,
                                                            scale=fnw_s[:, c:c + 1]), r=[tk, "fnw_s"], w=[("xs", c)])
                if c == 15:
                    S.op("sp", lambda e: e.dma_start(
                        out=outT[:, TS(tt, TT)].rearrange("(c p) t -> p c t", p=128), in_=xs[:, :, :]),
                        r=[("xs", x) for x in range(16)], w=[("out",)], dma=True)
            return fin

        def body(tt):
            load_x(x_src, tt)
            S.op("sp", lambda e: e.dma_start(
                out=act[:, 16:32, :], in_=oT_d[:, TS(tt, TT)].rearrange("(c p) t -> p c t", p=128)),
                r=[("oT",)], w=[("act", c) for c in range(16, 32)], dma=True)
            blocks = [(lambda ob=ob: wload(Wo, 0, 16, ob * 512, 512)) for ob in range(4)]

            def cons_o(ob, b, key):
                for oc4 in range(4):
                    oc = ob * 4 + oc4
                    p, pk = psum()
                    for kc in range(16):
                        S.op("pe", lambda e, p=p, b=b, kc=kc, oc4=oc4: e.matmul(
                            p[:, :], b[:, kc, oc4 * 128:(oc4 + 1) * 128], act[:, 16 + kc, :],
                            start=(kc == 0), stop=(kc == 15)), r=[key, ("act", 16 + kc)], w=[pk])
                    S.op("dve", lambda e, p=p, oc=oc: e.scalar_tensor_tensor(
                        out=xs[:, oc, :], in0=p[:, :], scalar=modcol(l, 2, oc), in1=xs[:, oc, :],
                        op0=ALU.mult, op1=ALU.add), r=[pk, ("mod", l), ("xs", oc)], w=[("xs", oc)])
            wstream(blocks, cons_o)
            import os as _os
            if _os.environ.get("DBG_SKIP_FFN"):
                norm_tile(None, None, l, [], out_fn=fin_factory(tt))
                return
            norm_tile(gam2[:, l * 16:(l + 1) * 16], modsb[:, l * 96 + 48:l * 96 + 64], l, [("gam2", l), ("mod", l)])
            blocks = []
            for fb in range(22):
                blocks.append(lambda fb=fb: wload(w_gu[l], 0, 16, fb * 256, 256))
                blocks.append(lambda fb=fb: wload(w_gu[l], 0, 16, DFF + fb * 256, 256))
            hold = {}

            def cons_gu(jb, b, key):
                if jb % 2 == 0:
                    hold["g"] = (b, key)
                    return
                fb = jb // 2
                gb, gk = hold["g"]
                ub, uk = b, key
                for f2 in range(2):
                    fc = fb * 2 + f2
                    pg, pgk = psum()
                    pu, puk = psum()
                    for kc in range(16):
                        S.op("pe", lambda e, pg=pg, gb=gb, kc=kc, f2=f2: e.matmul(
                            pg[:, :], gb[:, kc, f2 * 128:(f2 + 1) * 128], hb[:, kc, :],
                            start=(kc == 0), stop=(kc == 15)), r=[gk, ("hb", kc)], w=[pgk])
                    for kc in range(16):
                        S.op("pe", lambda e, pu=pu, ub=ub, kc=kc, f2=f2: e.matmul(
                            pu[:, :], ub[:, kc, f2 * 128:(f2 + 1) * 128], hb[:, kc, :],
                            start=(kc == 0), stop=(kc == 15)), r=[uk, ("hb", kc)], w=[puk])
                    t, tk = tmpr.next()
                    S.op("act", lambda e, pg=pg, t=t: e.activation(out=t[:], in_=pg[:, :], func=AF.Silu), r=[pgk], w=[tk])
                    S.op("dve", lambda e, pu=pu, t=t, fc=fc: e.tensor_tensor(out=act[:, fc, :], in0=t[:], in1=pu[:, :],
                                                                           op=ALU.mult), r=[tk, puk], w=[("act", fc)])
            wstream(blocks, cons_gu, la=1)
            for cb in range(4):
                accs = [psum() for _ in range(4)]
                blocks = [(lambda kg=kg, cb=cb: wload(w_dn[l], kg * 16, (16 if kg < 2 else 12), cb * 512, 512)) for kg in range(3)]

                def cons_d(kg, b, key, accs=accs, cb=cb):
                    nk = 16 if kg < 2 else 12
                    for oc4 in range(4):
                        p, pk = accs[oc4]
                        for kc in range(nk):
                            S.op("pe", lambda e, p=p, b=b, kc=kc, oc4=oc4, kg=kg, nk=nk: e.matmul(
                                p[:, :], b[:, kc, oc4 * 128:(oc4 + 1) * 128], act[:, kg * 16 + kc, :],
                                start=(kg == 0 and kc == 0), stop=(kg == 2 and kc == nk - 1)),
                                r=[key, ("act", kg * 16 + kc)], w=[pk])
                wstream(blocks, cons_d)
                for oc4 in range(4):
                    oc = cb * 4 + oc4
                    p, pk = accs[oc4]
                    S.op("dve", lambda e, p=p, oc=oc: e.scalar_tensor_tensor(
                        out=xs[:, oc, :], in0=p[:, :], scalar=modcol(l, 5, oc), in1=xs[:, oc, :],
                        op0=ALU.mult, op1=ALU.add), r=[pk, ("mod", l), ("xs", oc)], w=[("xs", oc)])
            if not last:
                S.op("sp", lambda e: e.dma_start(
                    out=xs_d[:, TS(tt, TT)].rearrange("(c p) t -> p c t", p=128), in_=xs[:, :, :]),
                    r=[("xs", c) for c in range(16)], w=[("xd",)], dma=True)
            else:
                norm_tile(None, None, l, [], out_fn=fin_factory(tt))
        S.loop(NT, body, reset_w)

    def phase_fox(l):
        j = l // 2
        if True:
            S.reset(m0)

            def sb2(shape, dt, name):
                return S.sb(shape, dt, name)

            rS, rT, rO, rM = Ring([0, 1]), Ring([2, 3]), Ring([4, 5]), Ring([6, 7])

            def bank(ring):
                i_ = ring.next()
                return psb[i_], ("ps", i_)
            fz = sb2([16, T], F32, "fz")
            nbf = sb2([16, 1], F32, "nbf")
            bfs = sb2([16, 1], F32, "bfs")
            ncb = sb2([128, T], F32, "ncb")
            qh = [sb2([128, T], BF16, f"qh{i}") for i in range(1)]
            kh = [sb2([128, T], BF16, f"kh{i}") for i in range(1)]
            vh = [sb2([128, NQ, 129], BF16, f"vh{i}") for i in range(1)]
            sq = sb2([128, T], BF16, "sqb")
            q2 = sb2([128, NQ], F32, "q2")
            km = sb2([128, 16], F32, "km")
            km1 = sb2([128, 1], F32, "km1")
            negm = sb2([128, NQ], F32, "negm")
            ssb = Ring([(sb2([128, 512], F32, f"ssb{i}"), ("ssb", i)) for i in range(4)])
            pbf = Ring([(sb2([128, 512], BF16, f"pbf{i}"), ("pbf", i)) for i in range(6)])
            oTh = [sb2([128, T], BF16, f"oTh{i}") for i in range(1)]

            S.op("sp", lambda e: e.dma_start(out=fz[:, :], in_=gate_d[0:16, :]),
                 r=[("gate", 0)], w=["fz"], dma=True)
            S.op("sp", lambda e: e.dma_start(out=bfs[:, :], in_=fbf[j * 16:(j + 1) * 16, 0:1]), w=["bfs"], dma=True)
            S.op("dve", lambda e: e.tensor_scalar(out=nbf[:], in0=bfs[:], scalar1=-1.0, scalar2=None, op0=ALU.mult),
                 r=["bfs"], w=["nbf"])
            S.op("act", lambda e: e.activation(out=fz[:, :], in_=fz[:, :], func=AF.Exp, bias=nbf[:, 0:1], scale=-1.0),
                 r=["fz", "nbf"], w=["fz"])
            S.op("act", lambda e: e.activation(out=fz[:, :], in_=fz[:, :], func=AF.Ln, bias=onec[0:16, 0:1], scale=1.0),
                 r=["fz", "onec"], w=["fz"])
            S.op("dve", lambda e: e.memset(ncb[0:16, :], 1.0), w=["ncb"])
            SEG = 1024
            for sg in range(T // SEG):
                a0, a1 = sg * SEG, (sg + 1) * SEG
                S.op("dve", lambda e, a0=a0, a1=a1, sg=sg: e.tensor_tensor_scan(
                    out=fz[:, a0:a1], data0=ncb[0:16, a0:a1], data1=fz[:, a0:a1],
                    initial=(0.0 if sg == 0 else fz[:, a0 - 1:a0]), op0=ALU.mult, op1=ALU.add),
                    r=["ncb", "fz"], w=["fz"])
            S.op("sp", lambda e: e.dma_start(out=ncum_d[:, :], in_=fz[:, :]), r=["fz"], w=["ncum_d"], dma=True)
            nqt = sb2([NQ, 128], F32, "nqt")
            maskT = sb2([128, 4 * 512], F32, "maskT")
            S.op("act", lambda e: e.dma_start(out=maskT[:, :], in_=consts[:, 2560:2560 + 2048]), w=["maskT"], dma=True)
            ncq = sb2([128, NQ], F32, "ncq")

            def load_head(h):
                i = 0
                S.op("act", lambda e: e.dma_start(out=qh[i][:, :], in_=qkT_d[TS(h, 128), :]),
                     r=[("qkT", oc) for oc in range(32)], w=[("qh", i)], dma=True)
                S.op("act", lambda e: e.dma_start(out=kh[i][:, :], in_=qkT_d[D:2 * D, :][TS(h, 128), :]),
                     r=[("qkT", oc) for oc in range(32)], w=[("kh", i)], dma=True)
                S.op("act", lambda e: e.dma_start(
                    out=vh[i][:, :, 0:128],
                    in_=tm_d[:, TS(h, 128)].rearrange("(j p) d -> p j d", p=128)),
                    r=[("tm",)], w=[("vh", i)], dma=True)
                S.op("pool", lambda e: e.memset(vh[i][:, :, 128:129], 1.0), r=[], w=[("vh1", i)])
                S.op("act", lambda e: e.dma_start(out=ncb[:, :], in_=ncum_d[TS(h, 1), :].to_broadcast([128, T])),
                     r=["ncum_d"], w=["ncb"], dma=True)
                S.op("act", lambda e: e.dma_start(out=nqt[:, :],
                                                 in_=ncum_d[TS(h, 1), :].rearrange("o (q p) -> (o q) p", p=128)),
                     r=["ncum_d"], w=["nqt"], dma=True)
                p, pk = bank(rM)
                S.op("pe", lambda e, p=p: e.transpose(p[:, 0:NQ], nqt[0:NQ, :], ident_f[0:NQ, 0:NQ]),
                     r=["nqt", "ident_f"], w=[pk])
                S.op("dve", lambda e, p=p: e.tensor_copy(out=ncq[:, :], in_=p[:, 0:NQ]), r=[pk], w=["ncq"])

            def head_body(h):
                i = 0
                load_head(h)
                S.op("act", lambda e, i=i: e.activation(out=sq[:, :], in_=kh[i][:, :], func=AF.Square),
                     r=[("kh", i)], w=["sq"])
                for t4 in range(T // 512):
                    p, pk = bank(rM)
                    S.op("pe", lambda e, p=p, t4=t4: e.matmul(p[:, :], ones1[:, :], sq[:, t4 * 512:(t4 + 1) * 512],
                                                             start=True, stop=True), r=["sq", "ones1"], w=[pk])
                    S.op("dve", lambda e, p=p, t4=t4: e.tensor_reduce(out=km[:, t4:t4 + 1], in_=p[:, :], axis=AX.X,
                                                                     op=ALU.max), r=[pk], w=[("km", t4)])
                S.op("dve", lambda e: e.tensor_reduce(out=km1[:, 0:1], in_=km[:, 0:T // 512], axis=AX.X, op=ALU.max),
                     r=[("km", t4) for t4 in range(T // 512)], w=["km1"])
                S.op("dve", lambda e: e.tensor_scalar(out=km1[:, 0:1], in0=km1[:, 0:1], scalar1=1.05, scalar2=None,
                                                      op0=ALU.mult), r=["km1"], w=["km1"])
                S.op("act", lambda e, i=i: e.activation(out=sq[:, :], in_=qh[i][:, :], func=AF.Square),
                     r=[("qh", i)], w=["sq"])
                for t4 in range(T // 512):
                    p, pk = bank(rM)
                    S.op("pe", lambda e, p=p, t4=t4: e.matmul(p[:, :], ones1[:, :], sq[:, t4 * 512:(t4 + 1) * 512],
                                                             start=True, stop=True), r=["sq", "ones1"], w=[pk])
                    mq, mqk = ssb.next()
                    S.op("act", lambda e, p=p, mq=mq: e.activation(out=mq[:, :], in_=p[:, :], func=AF.Sqrt,
                                                                  scale=km1[:, 0:1]), r=[pk, "km1"], w=[mqk])
                    S.op("dve", lambda e, mq=mq, t4=t4: e.scalar_tensor_tensor(
                        out=ncb[:, t4 * 512:(t4 + 1) * 512], in0=ncb[:, t4 * 512:(t4 + 1) * 512], scalar=-1.0,
                        in1=mq[:, :], op0=ALU.mult, op1=ALU.subtract), r=[mqk, "ncb"], w=["ncb"])
                steps = []
                for qt in range(T // 512):
                    nkb = 4 * qt + 4
                    for kb_ in range(nkb):
                        steps.append(dict(qt=qt, kb=kb_, first=(kb_ == 0), last=(kb_ == nkb - 1),
                                          diag=(kb_ - 4 * qt if kb_ >= 4 * qt else None)))
                po = {}

                def st_qk(s):
                    p, pk = bank(rS)
                    qt, kb_ = s["qt"], s["kb"]
                    S.op("pe", lambda e: e.matmul(p[:, :], kh[i][:, kb_ * 128:(kb_ + 1) * 128],
                                                  qh[i][:, qt * 512:(qt + 1) * 512], start=True, stop=True),
                         r=[("qh", i), ("kh", i)], w=[pk])
                    sb_, sbk = ssb.next()
                    S.op("dve", lambda e: e.tensor_tensor(out=sb_[:, :], in0=p[:, :], in1=ncb[:, qt * 512:(qt + 1) * 512],
                                                          op=ALU.add), r=[pk, "ncb"], w=[sbk])
                    if s["diag"] is not None:
                        dj = s["diag"]
                        S.op("pool", lambda e: e.tensor_tensor(out=sb_[:, :], in0=sb_[:, :],
                                                               in1=maskT[:, dj * 512:(dj + 1) * 512], op=ALU.add),
                             r=[sbk, "maskT"], w=[sbk])
                    pb, pbk = pbf.next()
                    s["pb"], s["pbk"] = pb, pbk
                    S.op("act", lambda e: e.activation(out=pb[:, :], in_=sb_[:, :], func=AF.Exp,
                                                       bias=ncq[:, kb_:kb_ + 1], scale=1.0),
                         r=[sbk, "ncq"], w=[pbk])

                def st_pv(s):
                    qt, kb_ = s["qt"], s["kb"]
                    if s["first"]:
                        po["o"], po["ok"] = bank(rO)
                        po["d"], po["dk"] = bank(rT)
                    pO, pOk, pD, pDk = po["o"], po["ok"], po["d"], po["dk"]
                    pb = s["pb"]
                    S.op("pe", lambda e: e.matmul(pO[:, :], vh[i][:, kb_, 0:128], pb[:, :],
                                                  start=s["first"], stop=s["last"]),
                         r=[s["pbk"], ("vh", i)], w=[pOk])
                    S.op("pe", lambda e: e.matmul(pD[:, :], ones1[:, :], pb[:, :],
                                                  start=s["first"], stop=s["last"]),
                         r=[s["pbk"], "ones1"], w=[pDk])
                    if s["last"]:
                        rd, rdk = ssb.next()
                        S.op("dve", lambda e: e.reciprocal(out=rd[:, :], in_=pD[:, :]), r=[pDk], w=[rdk])
                        S.op("dve", lambda e: e.tensor_tensor(out=oTh[i][:, qt * 512:(qt + 1) * 512], in0=pO[:, :],
                                                              in1=rd[:, :], op=ALU.mult),
                             r=[pOk, rdk], w=[("oTh", i, qt)])

                ns = len(steps)
                SKEW = 4
                for n in range(ns + SKEW):
                    if n < ns:
                        st_qk(steps[n])
                    if 0 <= n - SKEW < ns:
                        st_pv(steps[n - SKEW])
                S.op("act", lambda e, i=i: e.dma_start(out=oT_d[TS(h, 128), :], in_=oTh[i][:, :]),
                     r=[("oTh", i, qt) for qt in range(T // 512)], w=[("oT",)], dma=True)
            S.loop(16, head_body)
            S.barrier()

    def phase_mlstm(l):
        j = l // 2
        S.reset(m0)
        A = S.sb([4, T], F32, "gA")
        B = S.sb([4, T], F32, "gB")
        bg = S.sb([4, 2], F32, "bg")
        bg15 = S.sb([4, 2], F32, "bg15")
        ref = S.sb([4, NQ], F32, "ref")
        Gn = S.sb([4, NQ], F32, "Gn")
        Gp = S.sb([4, NQ], F32, "Gp")
        r1 = S.sb([4, NQ], F32, "r1")
        eT = S.sb([128, NQ * 4], F32, "eT")
        flT = S.sb([128, NQ * 4], F32, "flT")
        r1b = S.sb([128, 4 * NQ], F32, "r1b")
        nwb = S.sb([128, 2048], F32, "nwb")
        Cs = S.sb([128, 2, 512], F32, "Cs")
        Cb = S.sb([128, 2, 512], BF16, "Cb")
        ns = S.sb([128, 2], F32, "ns")
        nb_ = S.sb([128, 2], BF16, "nb_")
        qTb = [S.sb([128, 2, 512], BF16, f"qTb{i}") for i in range(2)]
        kTb = [S.sb([128, 2, 512], BF16, f"kTb{i}") for i in range(2)]
        ktm = [S.sb([128, 4, 256], BF16, f"ktm{i}") for i in range(2)]
        vb = [S.sb([128, 4, 512], BF16, f"vb{i}") for i in range(2)]
        ob = [S.sb([128, 4, 512], BF16, f"ob{i}") for i in range(2)]
        yTb = [S.sb([128, 4, 512], BF16, f"yTb{i}") for i in range(2)]
        PT = Ring([(S.sb([128, 128], BF16, f"PT{i}"), ("PT", i)) for i in range(2)])
        Kt = Ring([(S.sb([128, 256], BF16, f"Kt{i}"), ("Kt", i)) for i in range(2)])
        d1 = Ring([(S.sb([128, 4], F32, f"d1{i}"), ("d1", i)) for i in range(2)])
        junk = S.sb([128, 512], BF16, "junk")
        ytmp = Ring([(S.sb([128, 512], F32, f"ytmp{i}"), ("ytmp", i)) for i in range(2)])
        ybf = Ring([(S.sb([128, 512], BF16, f"ybf{i}"), ("ybf", i)) for i in range(2)])

        gdeps = [("gate", gi) for gi in range(2)]
        S.op("sp", lambda e: e.dma_start(out=A[:, :], in_=gate_d[0:4, :]), r=gdeps, w=["gA"], dma=True)
        S.op("sp", lambda e: e.dma_start(out=B[:, :], in_=gate2_d[0:4, :]), r=gdeps, w=["gB"], dma=True)
        S.op("sp", lambda e: e.dma_start(out=bg[:, :], in_=mbg[:, 2 * j:2 * j + 2]), w=["bg"], dma=True)
        S.op("sp", lambda e: e.dma_start(out=nwb[:, :], in_=mnw[:, j * 2048:(j + 1) * 2048]), w=["nwb"], dma=True)
        S.op("dve", lambda e: e.tensor_scalar(out=bg15[:, :], in0=bg[:, :], scalar1=1.0 / 15.0, scalar2=None,
                                              op0=ALU.mult), r=["bg"], w=["bg15"])
        S.op("act", lambda e: e.activation(out=A[:, :], in_=A[:, :], func=AF.Tanh, bias=bg15[:, 0:1], scale=1.0 / 15.0),
             r=["gA", "bg15"], w=["gA"])
        S.op("act", lambda e: e.activation(out=B[:, :], in_=B[:, :], func=AF.Tanh, bias=bg15[:, 1:2], scale=1.0 / 15.0),
             r=["gB", "bg15"], w=["gB"])
        S.op("act", lambda e: e.activation(out=B[:, :], in_=B[:, :], func=AF.Exp, scale=-15.0), r=["gB"], w=["gB"])
        S.op("act", lambda e: e.activation(out=B[:, :], in_=B[:, :], func=AF.Ln, bias=onec[0:4, 0:1], scale=1.0), r=["gB", "onec"], w=["gB"])
        onesT = S.sb([4, T], F32, "onesT")
        S.op("pool", lambda e: e.memset(onesT[:, :], 1.0), w=["onesT"])
        SEG = 1024
        for sg in range(T // SEG):
            a0, a1 = sg * SEG, (sg + 1) * SEG
            S.op("dve", lambda e, a0=a0, a1=a1, sg=sg: e.tensor_tensor_scan(
                out=B[:, a0:a1], data0=onesT[:, a0:a1], data1=B[:, a0:a1],
                initial=(0.0 if sg == 0 else B[:, a0 - 1:a0]), op0=ALU.mult, op1=ALU.add),
                r=["gB", "onesT"], w=["gB"])
        S.op("dve", lambda e: e.scalar_tensor_tensor(out=A[:, :], in0=A[:, :], scalar=15.0, in1=B[:, :],
                                                    op0=ALU.mult, op1=ALU.add), r=["gA", "gB"], w=["gA"])
        A3 = A.rearrange("p (c s) -> p c s", c=NQ, s=128)
        B3 = B.rearrange("p (c s) -> p c s", c=NQ, s=128)
        S.op("dve", lambda e: e.tensor_reduce(out=ref[:, :], in_=A3, axis=AX.X, op=ALU.max), r=["gA"], w=["ref"])
        S.op("dve", lambda e: e.tensor_tensor_scan(out=Gn[:, :], data0=ref[:, :], data1=ref[:, :], initial=0.0,
                                                  op0=ALU.max, op1=ALU.max), r=["ref"], w=["Gn"])
        S.op("dve", lambda e: e.memset(Gp[:, 0:1], 0.0), w=["Gp0"])
        if NQ > 1:
            S.op("dve", lambda e: e.tensor_copy(out=Gp[:, 1:NQ], in_=Gn[:, 0:NQ - 1]), r=["Gn"], w=["Gp1"])
        S.op("dve", lambda e: e.tensor_tensor(out=r1[:, :], in0=Gp[:, :], in1=Gn[:, :], op=ALU.subtract),
             r=["Gn", "Gp0", "Gp1"], w=["r1"])
        S.op("act", lambda e: e.activation(out=r1[:, :], in_=r1[:, :], func=AF.Exp), r=["r1"], w=["r1"])
        for c in range(NQ):
            S.op("dve", lambda e, c=c: e.tensor_scalar(out=A[:, c * 128:(c + 1) * 128], in0=A[:, c * 128:(c + 1) * 128],
                                                      scalar1=Gn[:, c:c + 1], scalar2=None, op0=ALU.subtract),
                 r=["gA", "Gn"], w=["gA"])
            S.op("pool", lambda e, c=c: e.tensor_scalar(out=B[:, c * 128:(c + 1) * 128], in0=B[:, c * 128:(c + 1) * 128],
                                                       scalar1=Gn[:, c:c + 1], scalar2=None, op0=ALU.subtract),
                 r=["gB", "Gn"], w=["gB"])
        S.op("act", lambda e: e.activation(out=A[:, :], in_=A[:, :], func=AF.Exp), r=["gA"], w=["gA"])
        S.op("act", lambda e: e.activation(out=B[:, :], in_=B[:, :], func=AF.Exp), r=["gB"], w=["gB"])
        for (src, dst, dk, sk) in ((A, eT, "eT", "gA"), (B, flT, "flT", "gB")):
            for g0 in range(0, NQ, 64):
                p, pk = psum()
                n_in = min(64, NQ - g0)
                for c in range(g0, g0 + n_in):
                    S.op("pe", lambda e, p=p, c=c, g0=g0, src=src: e.transpose(
                        p[:, (c - g0) * 4:(c - g0 + 1) * 4], src[0:4, c * 128:(c + 1) * 128], ident_f[0:4, 0:4]),
                        r=[sk, "ident_f"], w=[pk])
                S.op("dve", lambda e, p=p, g0=g0, n_in=n_in, dst=dst: e.tensor_copy(
                    out=dst[:, g0 * 4:(g0 + n_in) * 4], in_=p[:, 0:n_in * 4]), r=[pk], w=[dk])
        p, pk = psum()
        for h in range(4):
            S.op("pe", lambda e, p=p, h=h: e.matmul(p[:, h * NQ:(h + 1) * NQ], sel16[0:4, h * 128:(h + 1) * 128],
                                                   r1[0:4, 0:NQ], start=True, stop=True), r=["sel16", "r1"], w=[pk])
        S.op("dve", lambda e, p=p: e.tensor_copy(out=r1b[:, :], in_=p[:, 0:4 * NQ]), r=[pk], w=["r1b"])

        NB = T // 512
        for h in range(4):
            S.op("dve", lambda e: e.memset(Cs[:, :, :], 0.0), w=["Cs"])
            S.op("dve", lambda e: e.memset(ns[:, :], 0.0), w=["ns"])
            for tb in range(NB):
                bi = (h * NB + tb) % 2
                t0 = tb * 512
                S.op("sp", lambda e, bi=bi, t0=t0, h=h: e.dma_start(
                    out=qTb[bi][:, :, :], in_=qkT_d[h * 256:(h + 1) * 256, t0:t0 + 512].rearrange("(c p) t -> p c t", p=128)),
                    r=[("qkT", oc) for oc in range(16)], w=[("qTb", bi)], dma=True)
                S.op("sp", lambda e, bi=bi, t0=t0, h=h: e.dma_start(
                    out=kTb[bi][:, :, :], in_=qkT_d[1024 + h * 256:1024 + (h + 1) * 256, t0:t0 + 512].rearrange("(c p) t -> p c t", p=128)),
                    r=[("qkT", oc) for oc in range(16)], w=[("kTb", bi)], dma=True)
                S.op("sp", lambda e, bi=bi, t0=t0, h=h: e.dma_start(
                    out=ktm[bi][:, :, :], in_=tmb[h // 2][t0:t0 + 512, (h % 2) * 256:(h % 2 + 1) * 256].rearrange("(j p) d -> p j d", p=128)),
                    r=[("tm",)], w=[("ktm", bi)], dma=True)
                S.op("sp", lambda e, bi=bi, t0=t0, h=h: e.dma_start(
                    out=vb[bi][:, :, :], in_=tmb[2 + h][t0:t0 + 512, :].rearrange("(j p) d -> p j d", p=128)),
                    r=[("tm",)], w=[("vb", bi)], dma=True)
                S.op("sp", lambda e, bi=bi, t0=t0, h=h: e.dma_start(
                    out=ob[bi][:, :, :], in_=tmb[6 + h][t0:t0 + 512, :].rearrange("(j p) d -> p j d", p=128)),
                    r=[("tm",)], w=[("ob", bi)], dma=True)
                for cc in range(4):
                    c = tb * 4 + cc
                    ecol = eT[:, c * 4 + h:c * 4 + h + 1]
                    fcol = flT[:, c * 4 + h:c * 4 + h + 1]
                    rcol = r1b[:, h * NQ + c:h * NQ + c + 1]
                    tsl = slice(cc * 128, (cc + 1) * 128)
                    p1, p1k = psum()
                    for c2 in range(2):
                        S.op("pe", lambda e, c2=c2, p1=p1, bi=bi, tsl=tsl: e.matmul(
                            p1[:, 0:128], kTb[bi][:, c2, tsl], qTb[bi][:, c2, tsl], start=(c2 == 0), stop=(c2 == 1)),
                            r=[("kTb", bi), ("qTb", bi)], w=[p1k])
                    pt, ptk = PT.next()
                    S.op("dve", lambda e, p1=p1, pt=pt, ecol=ecol: e.scalar_tensor_tensor(
                        out=pt[:, :], in0=p1[:, 0:128], scalar=ecol, in1=mask01[:, :], op0=ALU.mult, op1=ALU.mult),
                        r=[p1k, "eT", "mask01"], w=[ptk])
                    S.op("dve", lambda e, rcol=rcol: e.tensor_scalar(out=Cs[:, :, :], in0=Cs[:, :, :], scalar1=rcol,
                                                                    scalar2=None, op0=ALU.mult), r=["Cs", "r1b"], w=["Cs"])
                    S.op("act", lambda e: e.activation(out=Cb[:, :, :], in_=Cs[:, :, :], func=AF.Identity), r=["Cs"], w=["Cb"])
                    S.op("dve", lambda e, rcol=rcol: e.tensor_scalar(out=ns[:, :], in0=ns[:, :], scalar1=rcol,
                                                                    scalar2=None, op0=ALU.mult), r=["ns", "r1b"], w=["ns"])
                    S.op("dve", lambda e: e.tensor_copy(out=nb_[:, :], in_=ns[:, :]), r=["ns"], w=["nb_"])
                    p2, p2k = psum()
                    for c2 in range(2):
                        S.op("pe", lambda e, c2=c2, p2=p2, bi=bi, tsl=tsl: e.matmul(
                            p2[:, :], qTb[bi][:, c2, tsl], Cb[:, c2, :], start=(c2 == 0), stop=False),
                            r=[("qTb", bi), "Cb"], w=[p2k])
                    S.op("pe", lambda e, p2=p2, pt=pt, bi=bi, cc=cc: e.matmul(
                        p2[:, :], pt[:, :], vb[bi][:, cc, :], start=False, stop=True), r=[ptk, ("vb", bi)], w=[p2k])
                    p3, p3k = psum()
                    for c2 in range(2):
                        S.op("pe", lambda e, c2=c2, p3=p3, bi=bi, tsl=tsl: e.matmul(
                            p3[:, 0:1], qTb[bi][:, c2, tsl], nb_[:, c2:c2 + 1], start=(c2 == 0), stop=False),
                            r=[("qTb", bi), "nb_"], w=[p3k])
                    S.op("pe", lambda e, p3=p3, pt=pt: e.matmul(p3[:, 0:1], pt[:, :], ones1[:, 0:1], start=False, stop=True),
                         r=[ptk, "ones1"], w=[p3k])
                    kt_, ktk = Kt.next()
                    S.op("pool", lambda e, kt_=kt_, bi=bi, cc=cc, ecol=ecol: e.tensor_scalar(
                        out=kt_[:, :], in0=ktm[bi][:, cc, :], scalar1=ecol, scalar2=None, op0=ALU.mult),
                        r=[("ktm", bi), "eT"], w=[ktk])
                    p5, p5k = psum()
                    for c2 in range(2):
                        p4, p4k = psum()
                        S.op("pe", lambda e, c2=c2, p4=p4, kt_=kt_, bi=bi, cc=cc: e.matmul(
                            p4[:, :], kt_[:, c2 * 128:(c2 + 1) * 128], vb[bi][:, cc, :], start=True, stop=True),
                            r=[ktk, ("vb", bi)], w=[p4k])
                        S.op("pe", lambda e, c2=c2, p5=p5, kt_=kt_: e.matmul(
                            p5[:, c2:c2 + 1], kt_[:, c2 * 128:(c2 + 1) * 128], ones1[:, 0:1], start=True, stop=True),
                            r=[ktk, "ones1"], w=[p5k])
                        S.op("dve", lambda e, c2=c2, p4=p4: e.tensor_tensor(out=Cs[:, c2, :], in0=Cs[:, c2, :], in1=p4[:, :],
                                                                          op=ALU.add), r=[p4k, "Cs", "Cb"], w=["Cs"])
                    S.op("dve", lambda e, p5=p5: e.tensor_tensor(out=ns[:, :], in0=ns[:, :], in1=p5[:, 0:2], op=ALU.add),
                         r=[p5k, "ns", "nb_"], w=["ns"])
                    dd, ddk = d1.next()
                    S.op("act", lambda e, dd=dd, p3=p3: e.activation(out=dd[:, 0:1], in_=p3[:, 0:1], func=AF.Abs),
                         r=[p3k], w=[(ddk, 0)])
                    S.op("dve", lambda e, dd=dd, fcol=fcol: e.tensor_scalar(
                        out=dd[:, 0:1], in0=dd[:, 0:1], scalar1=fcol, scalar2=None, op0=ALU.max),
                        r=[(ddk, 0), "flT"], w=[(ddk, 0)])
                    S.op("dve", lambda e, dd=dd: e.reciprocal(out=dd[:, 1:2], in_=dd[:, 0:1]), r=[(ddk, 0)], w=[(ddk, 1)])
                    S.op("pool", lambda e, dd=dd: e.memset(dd[:, 2:3], 0.0), w=[(ddk, 2)])
                    S.op("act", lambda e, dd=dd, p2=p2: e.activation(out=junk[:, :], in_=p2[:, :], func=AF.Square,
                                                                    scale=dd[:, 1:2], accum_out=dd[:, 2:3]),
                         r=[p2k, (ddk, 1), (ddk, 2)], w=[(ddk, 2), "junk"])
                    S.op("act", lambda e, dd=dd: e.activation(out=dd[:, 3:4], in_=dd[:, 2:3], func=AF.Ln,
                                                             bias=epsc[:, 0:1], scale=1.0 / 512.0),
                         r=[(ddk, 2), "epsc"], w=[(ddk, 3)])
                    S.op("act", lambda e, dd=dd: e.activation(out=dd[:, 3:4], in_=dd[:, 3:4], func=AF.Exp, scale=-0.5),
                         r=[(ddk, 3)], w=[(ddk, 3)])
                    S.op("dve", lambda e, dd=dd: e.tensor_tensor(out=dd[:, 3:4], in0=dd[:, 3:4], in1=dd[:, 1:2],
                                                                op=ALU.mult), r=[(ddk, 3), (ddk, 1)], w=[(ddk, 3)])
                    yt, ytk = ytmp.next()
                    S.op("dve", lambda e, yt=yt, p2=p2, dd=dd, h=h: e.scalar_tensor_tensor(
                        out=yt[:, :], in0=p2[:, :], scalar=dd[:, 3:4], in1=nwb[:, h * 512:(h + 1) * 512],
                        op0=ALU.mult, op1=ALU.mult), r=[p2k, (ddk, 3), "nwb"], w=[ytk])
                    yb, ybk = ybf.next()
                    S.op("pool", lambda e, yb=yb, yt=yt, bi=bi, cc=cc: e.tensor_tensor(
                        out=yb[:, :], in0=yt[:, :], in1=ob[bi][:, cc, :], op=ALU.mult), r=[ytk, ("ob", bi)], w=[ybk])
                    p6, p6k = psum()
                    p6v = p6[:, 0:256].bitcast(BF16)
                    for d4 in range(4):
                        S.op("pe", lambda e, d4=d4, p6v=p6v, yb=yb: e.transpose(
                            p6v[:, d4 * 128:(d4 + 1) * 128], yb[:, d4 * 128:(d4 + 1) * 128], ident_b[:, :]),
                            r=[ybk, "ident_b"], w=[p6k])
                    S.op("act", lambda e, p6v=p6v, bi=bi, tsl=tsl: e.activation(
                        out=yTb[bi][:, :, tsl], in_=p6v.rearrange("p (d t) -> p d t", d=4, t=128), func=AF.Identity),
                        r=[p6k], w=[("yTb", bi, cc)])
                S.op("sp", lambda e, bi=bi, t0=t0, h=h: e.dma_start(
                    out=oT_d[h * 512:(h + 1) * 512, t0:t0 + 512].rearrange("(d p) t -> p d t", p=128), in_=yTb[bi][:, :, :]),
                    r=[("yTb", bi, cc) for cc in range(4)], w=[("oT",)], dma=True)
        S.barrier()

    x_src = xT_in
    for l in range(depth):
        phase_inproj(l, x_src)
        S.barrier()
        emit_conv_layer(l + 1)
        if l % 2 == 1:
            phase_fox(l)
        else:
            phase_mlstm(l)
        phase_post(l, x_src, last=(l == depth - 1))
        S.barrier()
        x_src = xs_d
    S.op("sp", None)
    S.emit()
    return nc, stack


def _consts():
    c = np.zeros((128, 512 + 2048), np.float32)
    c[:, 0:128] = np.eye(128, dtype=np.float32)
    s = np.arange(128)
    c[:, 128:256] = (s[:, None] <= s[None, :]).astype(np.float32)
    c[:, 256:384] = np.where(s[None, :] <= s[:, None], 0.0, NEG)
    for h in range(16):
        c[h, 512 + h * 128:512 + (h + 1) * 128] = 1.0
    m = np.zeros((128, 4 * 512), np.float32)
    ql = np.arange(512)
    for j in range(4):
        m[:, j * 512:(j + 1) * 512] = np.where(j * 128 + s[:, None] <= ql[None, :], 0.0, NEG)
    return np.concatenate([c, m], axis=1)


def _col(v):
    v = np.asarray(v, np.float32)
    return np.ascontiguousarray(v.reshape(-1, 128).T)


def make_in_map(b, T, depth, x, c, ada_w, ada_b, norm_mix_w, norm_ffn_w, mlstm_w_in, mlstm_b_gates,
                mlstm_norm_w, mlstm_w_out, fox_w_in, fox_b_f, fox_w_out, ffn_w_gate_up, ffn_w_down, final_norm_w):
    n_ml = (depth + 1) // 2
    n_fx = depth // 2
    m = {}
    m["xT"] = np.ascontiguousarray(np.asarray(x[b], np.float32).T)
    m["cT"] = _col(c[b])
    m["ada_w"] = np.ascontiguousarray(ada_w[:depth], dtype=np.float32)
    m["ada_bT"] = np.concatenate([_col(ada_b[l]) for l in range(depth)], axis=1)
    m["nmw"] = np.concatenate([_col(norm_mix_w[l]) for l in range(depth)], axis=1)
    m["nfw"] = np.concatenate([_col(norm_ffn_w[l]) for l in range(depth)], axis=1)
    m["fnw"] = _col(final_norm_w)
    m["mw_in"] = np.ascontiguousarray(mlstm_w_in[:max(n_ml, 1)], dtype=np.float32)
    bgs = np.asarray(mlstm_b_gates, np.float32)[:max(n_ml, 1)]
    m["mbg"] = np.ascontiguousarray(np.concatenate([np.stack([bg[0:4], bg[4:8]], axis=1) for bg in bgs], axis=1))
    m["mnw"] = np.ascontiguousarray(np.concatenate(
        [np.broadcast_to(np.asarray(w, np.float32)[None, :], (128, 2048)) for w in mlstm_norm_w[:max(n_ml, 1)]], axis=1))
    m["mw_out"] = np.ascontiguousarray(mlstm_w_out[:max(n_ml, 1)], dtype=np.float32)
    m["fw_in"] = np.ascontiguousarray(fox_w_in[:max(n_fx, 1)], dtype=np.float32)
    m["fbf"] = np.ascontiguousarray(np.asarray(fox_b_f, np.float32)[:max(n_fx, 1)].reshape(-1, 1))
    m["fw_out"] = np.ascontiguousarray(fox_w_out[:max(n_fx, 1)], dtype=np.float32)
    m["w_gu"] = np.ascontiguousarray(ffn_w_gate_up[:depth], dtype=np.float32)
    m["w_dn"] = np.ascontiguousarray(ffn_w_down[:depth], dtype=np.float32)
    m["consts"] = _consts()
    return m


_CACHE = {}


def kernel(**inputs):
    x = np.asarray(inputs["x"])
    Bn, T, _ = x.shape
    depth = np.asarray(inputs["ada_w"]).shape[0]
    key = (T, depth)
    if key not in _CACHE:
        _CACHE[key] = build(T, depth)
    nc, _stack = _CACHE[key]
    arrs = {k: np.asarray(v) for k, v in inputs.items()}
    in_maps = [make_in_map(b, T, depth, **arrs) for b in range(Bn)]
    res = run_bass_kernel_spmd(nc, in_maps, core_ids=list(range(Bn)))
    out = np.stack([np.ascontiguousarray(r["outT"].T) for r in res.results], axis=0)
    return out.astype(np.float32)
```

```python
from contextlib import ExitStack
import numpy as np
import ml_dtypes
import concourse.bass as bass
import concourse.mybir as mybir
from concourse.bass_utils import run_bass_kernel_spmd

F32 = mybir.dt.float32
BF16 = mybir.dt.bfloat16
AF = mybir.ActivationFunctionType
ALU = mybir.AluOpType
AX = mybir.AxisListType

D = 2048
NC_ = 16
DFF = 5632
NFC = 44
EPS = 1e-6
TT = 512
NEG = -30000.0
R_DMA = 8


class LoopVar:
    cur = None


LV = LoopVar()


def TS(idx, size, off=0):
    if isinstance(idx, LoopVar):
        if isinstance(idx.cur, int):
            return slice(idx.cur * size + off, idx.cur * size + off + size)
        assert off == 0
        return bass.ts(idx.cur, size)
    return slice(idx * size + off, idx * size + off + size)


def DS(idx, stride, off, size):
    if isinstance(idx, LoopVar):
        return bass.ds(idx.cur * stride + off, size)
    return slice(idx * stride + off, idx * stride + off + size)


class Ring:
    registry = []

    def __init__(self, items):
        self.items = items
        self.i = 0
        Ring.registry.append(self)

    def next(self):
        x = self.items[self.i % len(self.items)]
        self.i += 1
        return x


ENGS = ["pe", "act", "dve", "pool", "sp"]
QUEUES = ["sp", "pool", "act"]


class Sch:
    ARENA = 188 * 1024

    def __init__(self, nc, stack):
        self.nc = nc
        self.stack = stack
        self.ops = []
        self.lw = {}
        self.lr = {}
        self.barrier_deps = set()
        self.last_on = {}
        self.dma_hist = {q: [] for q in QUEUES}
        self.n_t = 0
        self.regions = []
        self._after_barrier = set(ENGS)
        Ring.registry.clear()

    def sb(self, shape, dt, name=None):
        if not hasattr(self, "arena"):
            self.arena = self.stack.enter_context(self.nc.sbuf_tensor("arena", [128, self.ARENA], mybir.dt.uint8))
            self.top = 0
        esz = 4 if dt == F32 else 2
        n = 1
        for d in shape[1:]:
            n *= d
        nbytes = (n * esz + 63) // 64 * 64
        off = self.top
        self.top += nbytes
        assert self.top <= self.ARENA, f"SBUF arena overflow: {self.top} ({name})"
        ap = self.arena[0:shape[0], off:off + n * esz].bitcast(dt)
        if len(shape) == 3:
            ap = ap.rearrange("p (a b) -> p a b", a=shape[1], b=shape[2])
        return ap

    def mark(self):
        return self.top

    def reset(self, m):
        self.top = m

    def ps(self, shape, dt=F32, name=None):
        self.n_t += 1
        return self.stack.enter_context(self.nc.psum_tensor(name or f"p{self.n_t}", list(shape), dt))

    def op(self, eng, fn, r=(), w=(), dma=False):
        idx = len(self.ops)
        deps = set(self.barrier_deps) if eng not in self._after_barrier else set()
        self._after_barrier.add(eng)
        for k in r:
            x = self.lw.get(k)
            if x is not None:
                deps.add(x)
        for k in w:
            x = self.lw.get(k)
            if x is not None:
                deps.add(x)
            for y in self.lr.get(k, ()):
                deps.add(y)
        for k in r:
            self.lr.setdefault(k, []).append(idx)
        for k in w:
            self.lw[k] = idx
            self.lr[k] = []
        deps.discard(idx)
        self.ops.append(dict(eng=eng, fn=fn, deps=deps, dma=dma))
        if dma:
            self.dma_hist[eng].append(idx)
        else:
            self.last_on[eng] = idx
        return idx

    def barrier(self):
        deps = set(self.last_on.values())
        for q, h in self.dma_hist.items():
            deps.update(h[-R_DMA:])
        self.barrier_deps = deps
        self._after_barrier = set()

    def loop(self, N, body, reset_fn=None):
        if N == 1:
            body(0)
            return
        self.barrier()
        reg = dict(N=N, s0=len(self.ops), bdeps=set(self.barrier_deps))
        for rg in Ring.registry:
            rg.i = 0
        if reset_fn:
            reset_fn()
        body(LV)
        reg["s1"] = len(self.ops)
        for rg in Ring.registry:
            rg.i = 0
        if reset_fn:
            reset_fn()
        body(LV)
        reg["s2"] = len(self.ops)
        assert reg["s2"] - reg["s1"] == reg["s1"] - reg["s0"], "loop body not iteration-invariant"
        for a, b in zip(range(reg["s0"], reg["s1"]), range(reg["s1"], reg["s2"])):
            assert self.ops[a]["eng"] == self.ops[b]["eng"] and self.ops[a]["dma"] == self.ops[b]["dma"]
        self.regions.append(reg)
        self.barrier()

    def emit(self):
        nc = self.nc
        ops = self.ops
        n = len(ops)
        reg_of = [None] * n
        copy_of = [0] * n
        for reg in self.regions:
            for i in range(reg["s0"], reg["s1"]):
                reg_of[i] = reg
                copy_of[i] = 1
            for i in range(reg["s1"], reg["s2"]):
                reg_of[i] = reg
                copy_of[i] = 2

        def twin(i):
            reg = reg_of[i]
            return i + (reg["s1"] - reg["s0"]) if copy_of[i] == 1 else i

        def pe_pe(d, i):
            return ops[d]["eng"] == "pe" and ops[i]["eng"] == "pe" and not ops[d]["dma"] and not ops[i]["dma"]

        sig = [False] * n
        for i, o in enumerate(ops):
            for d in o["deps"]:
                if pe_pe(d, i):
                    continue
                sig[twin(d)] = True
        cnt = {e: 0 for e in ENGS}
        rr = {q: 0 for q in QUEUES}
        tot = {q: [0] * R_DMA for q in QUEUES}
        i = 0
        while i < n:
            reg = reg_of[i]
            if reg is None:
                o = ops[i]
                e = o["eng"]
                if o["dma"]:
                    s = rr[e] % R_DMA
                    rr[e] += 1
                    tot[e][s] += 1
                    o["slot"] = s
                    o["c"], o["k"] = 16 * tot[e][s], 0
                elif sig[i]:
                    cnt[e] += 1
                    o["c"], o["k"] = cnt[e], 0
                i += 1
                continue
            N = reg["N"]
            body = range(reg["s1"], reg["s2"])
            delta = {e: 0 for e in ENGS}
            for j in body:
                if not ops[j]["dma"] and sig[j]:
                    delta[ops[j]["eng"]] += 1
            run_c = {e: 0 for e in ENGS}
            rr0 = dict(rr)
            cslot = {q: [0] * R_DMA for q in QUEUES}
            for j in body:
                if ops[j]["dma"]:
                    q = ops[j]["eng"]
                    s = rr0[q] % R_DMA
                    rr0[q] += 1
                    ops[j]["slot"] = s
                    ops[j]["m"] = cslot[q][s]
                    cslot[q][s] += 1
            for j in body:
                o = ops[j]
                e = o["eng"]
                if o["dma"]:
                    s = o["slot"]
                    o["c"] = 16 * (tot[e][s] + o["m"] + 1)
                    o["k"] = 16 * cslot[e][s]
                elif sig[j]:
                    run_c[e] += 1
                    o["c"], o["k"] = cnt[e] + run_c[e], delta[e]
            for e in ENGS:
                cnt[e] += N * delta[e]
            for q in QUEUES:
                for s in range(R_DMA):
                    tot[q][s] += N * cslot[q][s]
            reg["cslot"] = cslot
            off = reg["s1"] - reg["s0"]
            for j0 in range(reg["s0"], reg["s1"]):
                t = ops[j0 + off]
                if "c" in t:
                    ops[j0]["c"], ops[j0]["k"] = t["c"], 0
                if "slot" in t:
                    ops[j0]["slot"] = t["slot"]
            i = reg["s2"]

        csem = {e: self.stack.enter_context(nc.semaphore(f"c_{e}")) for e in ENGS}
        dsem = {q: [self.stack.enter_context(nc.semaphore(f"d_{q}{s}")) for s in range(R_DMA)] for q in QUEUES}

        def semkey(p):
            return ("d", p["eng"], p["slot"]) if p["dma"] else ("c", p["eng"])

        def dep_target(d, i):
            t = twin(d)
            p = ops[t]
            key = semkey(p)
            if reg_of[i] is not None and reg_of[i] is reg_of[d]:
                if copy_of[i] == 1:
                    return key, p["c"], 0
                if copy_of[d] == 1:
                    return key, p["c"] - p["k"], p["k"]
                return key, p["c"], p["k"]
            if reg_of[d] is not None:
                return key, p["c"] + (reg_of[d]["N"] - 1) * p["k"], 0
            return key, p["c"], 0

        tmpregs = {}
        itregs = {}

        def emit_waits(eobj, waits, waited, it):
            for (key, k), c in waits.items():
                prev = waited.get((key, k))
                if prev is not None and prev >= c:
                    continue
                waited[(key, k)] = c
                sem = csem[key[1]] if key[0] == "c" else dsem[key[1]][key[2]]
                if k == 0 or it is None:
                    eobj.wait_ge(sem, c)
                else:
                    rg = tmpregs.get(id(eobj))
                    if rg is None:
                        rg = eobj.alloc_register()
                        tmpregs[id(eobj)] = rg
                    itr = waited.get("__itr")
                    if itr is None:
                        itr = eobj.to_reg(it)
                        waited["__itr"] = itr
                    eobj.reg_mul(rg, itr, k)
                    eobj.reg_add(rg, rg, c)
                    eobj.wait_ge(sem, rg)

        def collect(i, ename):
            o = ops[i]
            waits = {}
            for d in o["deps"]:
                if pe_pe(d, i):
                    continue
                key, c, k = dep_target(d, i)
                if waits.get((key, k), -10 ** 9) < c:
                    waits[(key, k)] = c
            if o["dma"]:
                key = ("d", ename, o["slot"])
                c, k = o["c"] - 16, o["k"]
                if copy_of[i] == 1:
                    k = 0
                if c > 0 or k > 0:
                    if waits.get((key, k), -10 ** 9) < c:
                        waits[(key, k)] = c
            return waits

        def emit_op(i, ename, eobj, waited, it):
            o = ops[i]
            emit_waits(eobj, collect(i, ename), waited, it)
            if o["fn"] is None:
                return
            ins = o["fn"](eobj)
            if o["dma"]:
                ins.then_inc(dsem[ename][o["slot"]], 16)
            elif sig[twin(i)]:
                ins.then_inc(csem[ename], 1)

        def run(ename, eobj):
            waited = {}
            i = 0
            while i < n:
                reg = reg_of[i]
                if reg is None:
                    if ops[i]["eng"] == ename:
                        emit_op(i, ename, eobj, waited, None)
                    i += 1
                    continue
                LV.cur = 0
                for j in range(reg["s0"], reg["s1"]):
                    if ops[j]["eng"] == ename:
                        emit_op(j, ename, eobj, waited, None)
                LV.cur = None
                mine = [j for j in range(reg["s1"], reg["s2"]) if ops[j]["eng"] == ename]
                if mine:
                    with eobj.Fori(1, reg["N"]) as iv:
                        LV.cur = iv
                        w2 = {}
                        for j in mine:
                            emit_op(j, ename, eobj, w2, iv)
                    LV.cur = None
                i = reg["s2"]

        with nc.Block() as block:
            @block.tensor
            def _(e):
                run("pe", e)

            @block.scalar
            def _(e):
                run("act", e)

            @block.vector
            def _(e):
                run("dve", e)

            @block.gpsimd
            def _(e):
                run("pool", e)

            @block.sync
            def _(e):
                run("sp", e)


def build(T, depth, dbg=False):
    NT = T // TT
    NQ = T // 128
    n_ml = (depth + 1) // 2
    n_fx = depth // 2
    nc = bass.Bass("TRN2", target_bir_lowering=False)
    stack = ExitStack()
    S = Sch(nc, stack)

    def dram(name, shape, dt, kind="Internal"):
        return nc.dram_tensor(name, list(shape), dt, kind=kind).ap()

    xT_in = dram("xT", [D, T], F32, "ExternalInput")
    cT = dram("cT", [128, 16], F32, "ExternalInput")
    ada_w = dram("ada_w", [depth, D, 6 * D], F32, "ExternalInput")
    ada_bT = dram("ada_bT", [128, depth * 96], F32, "ExternalInput")
    nmw = dram("nmw", [128, depth * 16], F32, "ExternalInput")
    nfw = dram("nfw", [128, depth * 16], F32, "ExternalInput")
    fnw = dram("fnw", [128, 16], F32, "ExternalInput")
    mw_in = dram("mw_in", [max(n_ml, 1), D, 6152], F32, "ExternalInput")
    mbg = dram("mbg", [4, max(n_ml, 1) * 2], F32, "ExternalInput")
    mnw = dram("mnw", [128, max(n_ml, 1) * 2048], F32, "ExternalInput")
    mw_out = dram("mw_out", [max(n_ml, 1), D, D], F32, "ExternalInput")
    fw_in = dram("fw_in", [max(n_fx, 1), D, 6160], F32, "ExternalInput")
    fbf = dram("fbf", [16 * max(n_fx, 1), 1], F32, "ExternalInput")
    fw_out = dram("fw_out", [max(n_fx, 1), D, D], F32, "ExternalInput")
    w_gu = dram("w_gu", [depth, D, 2 * DFF], F32, "ExternalInput")
    w_dn = dram("w_dn", [depth, DFF, D], F32, "ExternalInput")
    consts = dram("consts", [128, 2560 + 2048], F32, "ExternalInput")
    outT = dram("outT", [D, T], F32, "ExternalOutput")

    xs_d = dram("xs_d", [D, T], F32)
    qkT_d = dram("qkT_d", [2 * D, T], BF16)
    tm_d = dram("tm_d", [T, 2048], BF16)
    tmb = [dram(f"tmb{i}", [T, 512], BF16) for i in range(10)]
    gate_d = dram("gate_d", [16, T], F32)
    gate2_d = dram("gate2_d", [4, T], F32)
    oT_d = dram("oT_d", [D, T], BF16)
    ncum_d = dram("ncum_d", [16, T], F32)
    if dbg:
        dbg_h = dram("dbg_h", [D, T], F32, "ExternalOutput")

    ident_f = S.sb([128, 128], F32, "ident_f")
    ident_b = S.sb([128, 128], BF16, "ident_b")
    mask01 = S.sb([128, 128], F32, "mask01")
    maskneg = S.sb([128, 128], F32, "maskneg")
    onesm = S.sb([128, 128], BF16, "onesm")
    ones1 = S.sb([128, 128], BF16, "ones1")
    sel16 = S.sb([16, 4 * 128], F32, "sel16")
    modsb = S.sb([128, depth * 96], F32, "modsb")
    gam1 = S.sb([128, depth * 16], F32, "gam1")
    gam2 = S.sb([128, depth * 16], F32, "gam2")
    nmw_s = S.sb([128, depth * 16], F32, "nmw_s")
    nfw_s = S.sb([128, depth * 16], F32, "nfw_s")
    fnw_s = S.sb([128, 16], F32, "fnw_s")
    zero_c = S.sb([128, 16], F32, "zero_c")
    epsc = S.sb([128, 1], F32, "epsc")
    onec = S.sb([128, 1], F32, "onec")

    S.op("pool", lambda e: e.dma_start(out=ident_f[:], in_=consts[:, 0:128]), w=["ident_f"], dma=True)
    S.op("pool", lambda e: e.dma_start(out=ident_b[:], in_=consts[:, 0:128]), w=["ident_b"], dma=True)
    S.op("pool", lambda e: e.dma_start(out=mask01[:], in_=consts[:, 128:256]), w=["mask01"], dma=True)
    S.op("pool", lambda e: e.dma_start(out=maskneg[:], in_=consts[:, 256:384]), w=["maskneg"], dma=True)
    S.op("pool", lambda e: e.dma_start(out=sel16[:], in_=consts[0:16, 512:512 + 512]), w=["sel16"], dma=True)
    S.op("pool", lambda e: e.dma_start(out=nmw_s[:], in_=nmw[:, :]), w=["nmw_s"], dma=True)
    S.op("pool", lambda e: e.dma_start(out=nfw_s[:], in_=nfw[:, :]), w=["nfw_s"], dma=True)
    S.op("pool", lambda e: e.dma_start(out=fnw_s[:], in_=fnw[:, :]), w=["fnw_s"], dma=True)
    S.op("dve", lambda e: e.memset(onesm[:], 1.0 / D), w=["onesm"])
    S.op("dve", lambda e: e.memset(ones1[:], 1.0), w=["ones1"])
    S.op("dve", lambda e: e.memset(zero_c[:], 0.0), w=["zero_c"])
    S.op("dve", lambda e: e.memset(epsc[:], EPS), w=["epsc"])
    S.op("dve", lambda e: e.memset(onec[:], 1.0), w=["onec"])

    wblk = {}

    class WRef:
        def __init__(self, name, i):
            self.name, self.i = name, i

    class WMat:
        def __init__(self, name):
            self.name = name

        def __getitem__(self, i):
            return WRef(self.name, i)

    conv_jobs = {}

    def conv(name, src, n, blocks, slot):
        dst = dram(name, [n * len(blocks), 128, slot], BF16)
        for i in range(n):
            jobs = []
            for bi_, (k0, nk, c0, ncb_) in enumerate(blocks):
                d = dst[i * len(blocks) + bi_, :, 0:nk * ncb_]
                wblk[(name, i, k0, c0)] = (d, nk, ncb_)
                jobs.append((d, src, i, k0, nk, c0, ncb_))
            conv_jobs[(name, i)] = jobs
        return WMat(name)

    def emit_conv(name, i):
        for (d, src, i_, k0, nk, c0, ncb_) in conv_jobs.pop((name, i), []):
            S.op("pool", lambda e, i_=i_, k0=k0, nk=nk, c0=c0, ncb_=ncb_, d=d, src=src: e.dma_start(
                out=d.rearrange("p (k c) -> p k c", k=nk, c=ncb_),
                in_=src[i_, k0 * 128:(k0 + nk) * 128, c0:c0 + ncb_].rearrange("(k p) c -> p k c", p=128)),
                w=[("wconv", name, i_)], dma=True)

    def emit_conv_layer(l):
        if l >= depth:
            return
        if l % 2 == 0:
            emit_conv("mw_in_b", l // 2)
            emit_conv("mw_out_b", l // 2)
        else:
            emit_conv("fw_in_b", l // 2)
            emit_conv("fw_out_b", l // 2)
        emit_conv("w_gu_b", l)
        emit_conv("w_dn_b", l)

    in_blocks = [(0, 16, c0, 512) for c0 in range(0, 6144, 512)]
    mw_in = conv("mw_in_b", mw_in, max(n_ml, 1), in_blocks + [(0, 16, 6144, 8)], 16 * 512)
    fw_in = conv("fw_in_b", fw_in, max(n_fx, 1), in_blocks + [(0, 16, 6144, 16)], 16 * 512)
    out_blocks = [(0, 16, c0, 512) for c0 in range(0, D, 512)]
    mw_out = conv("mw_out_b", mw_out, max(n_ml, 1), out_blocks, 16 * 512)
    fw_out = conv("fw_out_b", fw_out, max(n_fx, 1), out_blocks, 16 * 512)
    gu_blocks = [(0, 16, c0, 256) for c0 in range(0, 2 * DFF, 256)]
    w_gu = conv("w_gu_b", w_gu, depth, gu_blocks, 16 * 256)
    dn_blocks = [(kg * 16, (16 if kg < 2 else 12), cb * 512, 512) for kg in range(3) for cb in range(4)]
    w_dn = conv("w_dn_b", w_dn, depth, dn_blocks, 16 * 512)
    emit_conv_layer(0)
    S.barrier()

    m0 = S.mark()
    NWB = 3
    wbufs = [S.sb([128, 16, 512], BF16, f"wb{i}") for i in range(NWB)]
    wstate = dict(n=0)

    def wload(src2d, k0, nk, c0, ncols, dcol=0):
        i = wstate["n"] % NWB
        wstate["n"] += 1
        b = wbufs[i]
        if isinstance(src2d, WRef):
            d, nk_, nc_ = wblk[(src2d.name, src2d.i, k0, c0)]
            assert nk_ == nk and nc_ == ncols, (src2d.name, k0, c0, nk, ncols)
            src = d.rearrange("p (k c) -> p k c", k=nk, c=ncols)
        else:
            src = src2d[k0 * 128:(k0 + nk) * 128, c0:c0 + ncols].rearrange("(k p) c -> p k c", p=128)
        S.op("pool", lambda e: e.dma_start(out=b[:, 0:nk, dcol:dcol + ncols], in_=src), w=[("wb", i)], dma=True)
        return b, ("wb", i)

    def wload2(src2d, k0, nk, c0, c1, ncols):
        i = wstate["n"] % NWB
        wstate["n"] += 1
        b = wbufs[i]
        s0 = src2d[k0 * 128:(k0 + nk) * 128, c0:c0 + ncols].rearrange("(k p) c -> p k c", p=128)
        s1 = src2d[k0 * 128:(k0 + nk) * 128, c1:c1 + ncols].rearrange("(k p) c -> p k c", p=128)
        S.op("pool", lambda e: e.dma_start(out=b[:, 0:nk, 0:ncols], in_=s0), w=[("wb", i)], dma=True)
        S.op("pool", lambda e: e.dma_start(out=b[:, 0:nk, ncols:2 * ncols], in_=s1), w=[("wb", i, 1)], r=[("wb", i)], dma=True)
        return b, ("wb", i, 1)

    def reset_w():
        wstate["n"] = 0

    def wstream(blocks, consume, la=NWB - 1):
        issued = []
        for j in range(len(blocks)):
            while len(issued) < min(len(blocks), j + la + 1):
                issued.append(blocks[len(issued)]())
            b, key = issued[j]
            consume(j, b, key)

    psb = [S.ps([128, 512], F32, f"psb{i}") for i in range(8)]
    psring = Ring(list(range(8)))

    def psum():
        i = psring.next()
        return psb[i], ("ps", i)

    cs32 = S.sb([128, 16], F32, "cs32")
    csb = S.sb([128, 16], BF16, "csb")
    adab_s = S.sb([128, depth * 96], F32, "adab_s")
    S.op("pool", lambda e: e.dma_start(out=cs32[:], in_=cT[:, :]), w=["cs32"], dma=True)
    S.op("pool", lambda e: e.dma_start(out=adab_s[:], in_=ada_bT[:, :]), w=["adab_s"], dma=True)
    S.op("act", lambda e: e.activation(out=csb[:], in_=cs32[:], func=AF.Silu), r=["cs32"], w=["csb"])
    for l in range(depth):
        pm, pmk = psum()
        blocks = [(lambda l=l, bi=bi: wload(ada_w[l], 0, 16, bi * 512, 512)) for bi in range(24)]

        def cons(j, b, key, pm=pm, pmk=pmk):
            for j4 in range(4):
                col = j * 4 + j4
                for kc in range(16):
                    S.op("pe", lambda e, b=b, kc=kc, j4=j4, col=col: e.matmul(
                        pm[:, col:col + 1], b[:, kc, j4 * 128:(j4 + 1) * 128], csb[:, kc:kc + 1],
                        start=(kc == 0), stop=(kc == 15)), r=[key, "csb"], w=[pmk])
        wstream(blocks, cons)
        S.op("dve", lambda e, l=l, pm=pm: e.tensor_tensor(out=modsb[:, l * 96:(l + 1) * 96], in0=pm[:, 0:96],
                                                       in1=adab_s[:, l * 96:(l + 1) * 96], op=ALU.add),
             r=[pmk, "adab_s"], w=[("mod", l)])
        S.op("dve", lambda e, l=l: e.scalar_tensor_tensor(out=gam1[:, l * 16:(l + 1) * 16],
                                                         in0=modsb[:, l * 96 + 16:l * 96 + 32], scalar=1.0,
                                                         in1=nmw_s[:, l * 16:(l + 1) * 16], op0=ALU.add, op1=ALU.mult),
             r=[("mod", l), "nmw_s"], w=[("gam1", l)])
        S.op("dve", lambda e, l=l: e.scalar_tensor_tensor(out=gam2[:, l * 16:(l + 1) * 16],
                                                         in0=modsb[:, l * 96 + 64:l * 96 + 80], scalar=1.0,
                                                         in1=nfw_s[:, l * 16:(l + 1) * 16], op0=ALU.add, op1=ALU.mult),
             r=[("mod", l), "nfw_s"], w=[("gam2", l)])

    def modcol(l, which, c):
        base = l * 96 + which * 16 + c
        return modsb[:, base:base + 1]

    xs = S.sb([128, 16, TT], F32, "xs")
    hb = S.sb([128, 16, TT], BF16, "hb")
    act = S.sb([128, NFC, TT], BF16, "act")
    rstd = S.sb([128, TT], F32, "rstd")
    tmpr = Ring([(S.sb([128, TT], F32, f"tmp{i}"), ("tmp", i)) for i in range(3)])
    vst3 = S.sb([128, 4, 2048], BF16, "vst3")
    stgf = Ring([(S.sb([128, TT], F32, f"stgf{i}"), ("stgf", i)) for i in range(2)])

    def norm_tile(gam_ap, sh_ap, l, rdeps, out_fn=None):
        for g in range(4):
            S.op("act", lambda e, g=g: e.activation(out=act[:, 4 * g:4 * g + 4, :], in_=xs[:, 4 * g:4 * g + 4, :],
                                                   func=AF.Square),
                 r=[("xs", c) for c in range(4 * g, 4 * g + 4)], w=[("act", c) for c in range(4 * g, 4 * g + 4)])
        pss, pssk = psum()
        for c in range(16):
            S.op("pe", lambda e, c=c: e.matmul(pss[:, :], onesm[:, :], act[:, c, :], start=(c == 0), stop=(c == 15)),
                 r=[("act", c), "onesm"], w=[pssk])
        S.op("act", lambda e: e.activation(out=rstd[:], in_=pss[:, :], func=AF.Ln, bias=epsc[:, 0:1], scale=1.0),
             r=[pssk, "epsc"], w=["rstd"])
        S.op("act", lambda e: e.activation(out=rstd[:], in_=rstd[:], func=AF.Exp, scale=-0.5), r=["rstd"], w=["rstd"])
        for c in range(16):
            t, tk = tmpr.next()
            S.op("dve", lambda e, c=c, t=t: e.tensor_tensor(out=t[:], in0=xs[:, c, :], in1=rstd[:], op=ALU.mult),
                 r=[("xs", c), "rstd"], w=[tk])
            if out_fn is None:
                S.op("act", lambda e, c=c, t=t: e.activation(out=hb[:, c, :], in_=t[:], func=AF.Identity,
                                                            bias=sh_ap[:, c:c + 1], scale=gam_ap[:, c:c + 1]),
                     r=[tk] + rdeps, w=[("hb", c)])
            else:
                out_fn(c, t, tk)

    def load_x(src, tt):
        S.op("sp", lambda e: e.dma_start(out=xs[:, :, :],
                                         in_=src[:, TS(tt, TT)].rearrange("(c p) t -> p c t", p=128)),
             r=[("xd",)], w=[("xs", c) for c in range(16)], dma=True)

    evac_rr = Ring(["act", "dve"])

    def evac_bf16(ps_ap, psk, dst_ap, dstk, scale=None, func=None, extra_r=()):
        if func is not None:
            S.op("act", lambda e: e.activation(out=dst_ap, in_=ps_ap, func=func), r=[psk] + list(extra_r), w=[dstk])
            return
        eng = evac_rr.next()
        if eng == "act":
            S.op("act", lambda e: e.activation(out=dst_ap, in_=ps_ap, func=AF.Identity,
                                               scale=(1.0 if scale is None else scale)), r=[psk] + list(extra_r), w=[dstk])
        else:
            if scale is None:
                S.op("dve", lambda e: e.tensor_copy(out=dst_ap, in_=ps_ap), r=[psk] + list(extra_r), w=[dstk])
            else:
                S.op("dve", lambda e: e.tensor_scalar(out=dst_ap, in0=ps_ap, scalar1=scale, scalar2=None, op0=ALU.mult),
                     r=[psk] + list(extra_r), w=[dstk])

    def phase_inproj(l, x_src):
        mixer = l % 2
        j = l // 2
        if mixer == 0:
            W = mw_in[j]
            fm_blocks = 4
            qscale = 256 ** -0.5
            nq_chunks = 8
            tm_c0, tm_blocks = 1024, 10
        else:
            W = fw_in[j]
            fm_blocks = 8
            qscale = 128 ** -0.5
            nq_chunks = 16
            tm_c0, tm_blocks = 4096, 4
        def body(tt):
            load_x(x_src, tt)
            norm_tile(gam1[:, l * 16:(l + 1) * 16], modsb[:, l * 96:l * 96 + 16], l, [("gam1", l), ("mod", l)])
            blocks = []
            for bi in range(fm_blocks):
                blocks.append(lambda bi=bi: wload(W, 0, 16, bi * 512, 512))
            for bi in range(tm_blocks):
                blocks.append(lambda bi=bi: wload(W, 0, 16, tm_c0 + bi * 512, 512))
            blocks.append(lambda: wload(W, 0, 16, 6144, 8 if mixer == 0 else 16))

            def cons(jb, b, key):
                if jb < fm_blocks:
                    for oc4 in range(4):
                        oc = jb * 4 + oc4
                        p, pk = psum()
                        for kc in range(16):
                            S.op("pe", lambda e, p=p, b=b, kc=kc, oc4=oc4: e.matmul(
                                p[:, :], b[:, kc, oc4 * 128:(oc4 + 1) * 128], hb[:, kc, :],
                                start=(kc == 0), stop=(kc == 15)), r=[key, ("hb", kc)], w=[pk])
                        evac_bf16(p[:, :], pk, act[:, oc, :], ("act", oc), scale=(qscale if oc < nq_chunks else None))
                    if jb == fm_blocks - 1:
                        nfm = fm_blocks * 4
                        S.op("sp", lambda e: e.dma_start(
                            out=qkT_d.rearrange("(c p) t -> p c t", p=128)[:, 0:nfm, TS(tt, TT)], in_=act[:, 0:nfm, :]),
                            r=[("act", c) for c in range(nfm)], w=[("qkT", c) for c in range(32)], dma=True)
                elif jb < fm_blocks + tm_blocks:
                    nb = jb - fm_blocks
                    for m in range(4):
                        p, pk = psum()
                        for kc in range(16):
                            S.op("pe", lambda e, p=p, b=b, kc=kc, m=m: e.matmul(
                                p[:, :], hb[:, kc, m * 128:(m + 1) * 128], b[:, kc, :],
                                start=(kc == 0), stop=(kc == 15)), r=[key, ("hb", kc)], w=[pk])
                        is_o = (mixer == 0 and nb >= 6)
                        s4 = nb % 4
                        evac_bf16(p[:, :], pk, vst3[:, m, s4 * 512:(s4 + 1) * 512], ("vst", m, s4),
                                  func=(AF.Sigmoid if is_o else None))
                    if mixer == 0:
                        s4 = nb % 4
                        S.op("sp", lambda e, nb=nb, s4=s4: e.dma_start(
                            out=tmb[nb].rearrange("(a m p) c -> a p m c", m=4, p=128)[TS(tt, 1)]
                            .rearrange("a p m c -> (a p) m c"), in_=vst3[:, :, s4 * 512:(s4 + 1) * 512]),
                            r=[("vst", m, s4) for m in range(4)], w=[("tm",)], dma=True)
                    elif nb == tm_blocks - 1:
                        S.op("sp", lambda e: e.dma_start(
                            out=tm_d.rearrange("(a m p) c -> a p m c", m=4, p=128)[TS(tt, 1)]
                            .rearrange("a p m c -> (a p) m c"), in_=vst3[:, :, :]),
                            r=[("vst", m, x) for m in range(4) for x in range(4)], w=[("tm",)], dma=True)
                else:
                    if mixer == 0:
                        for gi in range(2):
                            p, pk = psum()
                            for kc in range(16):
                                S.op("pe", lambda e, p=p, b=b, kc=kc, gi=gi: e.matmul(
                                    p[0:4, :], b[:, kc, gi * 4:gi * 4 + 4], hb[:, kc, :],
                                    start=(kc == 0), stop=(kc == 15)), r=[key, ("hb", kc)], w=[pk])
                            sf, sfk = stgf.next()
                            S.op("dve", lambda e, p=p, sf=sf: e.tensor_copy(out=sf[0:4, :], in_=p[0:4, :]), r=[pk], w=[sfk])
                            dst = gate_d if gi == 0 else gate2_d
                            S.op("sp", lambda e, sf=sf, dst=dst: e.dma_start(
                                out=dst[0:4, TS(tt, TT)], in_=sf[0:4, :]), r=[sfk], w=[("gate", gi)], dma=True)
                    else:
                        p, pk = psum()
                        for kc in range(16):
                            S.op("pe", lambda e, p=p, b=b, kc=kc: e.matmul(
                                p[0:16, :], b[:, kc, 0:16], hb[:, kc, :],
                                start=(kc == 0), stop=(kc == 15)), r=[key, ("hb", kc)], w=[pk])
                        sf, sfk = stgf.next()
                        S.op("dve", lambda e, p=p, sf=sf: e.tensor_copy(out=sf[0:16, :], in_=p[0:16, :]), r=[pk], w=[sfk])
                        S.op("sp", lambda e, sf=sf: e.dma_start(
                            out=gate_d[0:16, TS(tt, TT)], in_=sf[0:16, :]), r=[sfk], w=[("gate", 0)], dma=True)
            wstream(blocks, cons)
        S.loop(NT, body, reset_w)

    def phase_post(l, x_src, last):
        mixer = l % 2
        j = l // 2
        Wo = mw_out[j] if mixer == 0 else fw_out[j]
        def fin_factory(tt):
            def fin(c, t, tk):
                S.op("act", lambda e, c=c, t=t: e.activation(out=xs[:, c, :], in_=t[:], func=AF.Identity,
                                                            scale=fnw_s[:, c:c + 1]), r=[tk, "fnw_s"], w=[("xs", c)])
                if c == 15:
                    S.op("sp", lambda e: e.dma_start(
                        out=outT[:, TS(tt, TT)].rearrange("(c p) t -> p c t", p=128), in_=xs[:, :, :]),
                        r=[("xs", x) for x in range(16)], w=[("out",)], dma=True)
            return fin

        def body(tt):
            load_x(x_src, tt)
            S.op("sp", lambda e: e.dma_start(
                out=act[:, 16:32, :], in_=oT_d[:, TS(tt, TT)].rearrange("(c p) t -> p c t", p=128)),
                r=[("oT",)], w=[("act", c) for c in range(16, 32)], dma=True)
            blocks = [(lambda ob=ob: wload(Wo, 0, 16, ob * 512, 512)) for ob in range(4)]

            def cons_o(ob, b, key):
                for oc4 in range(4):
                    oc = ob * 4 + oc4
                    p, pk = psum()
                    for kc in range(16):
                        S.op("pe", lambda e, p=p, b=b, kc=kc, oc4=oc4: e.matmul(
                            p[:, :], b[:, kc, oc4 * 128:(oc4 + 1) * 128], act[:, 16 + kc, :],
                            start=(kc == 0), stop=(kc == 15)), r=[key, ("act", 16 + kc)], w=[pk])
                    S.op("dve", lambda e, p=p, oc=oc: e.scalar_tensor_tensor(
                        out=xs[:, oc, :], in0=p[:, :], scalar=modcol(l, 2, oc), in1=xs[:, oc, :],
                        op0=ALU.mult, op1=ALU.add), r=[pk, ("mod", l), ("xs", oc)], w=[("xs", oc)])
            wstream(blocks, cons_o)
            import os as _os
            if _os.environ.get("DBG_SKIP_FFN"):
                norm_tile(None, None, l, [], out_fn=fin_factory(tt))
                return
            norm_tile(gam2[:, l * 16:(l + 1) * 16], modsb[:, l * 96 + 48:l * 96 + 64], l, [("gam2", l), ("mod", l)])
            blocks = []
            for fb in range(22):
                blocks.append(lambda fb=fb: wload(w_gu[l], 0, 16, fb * 256, 256))
                blocks.append(lambda fb=fb: wload(w_gu[l], 0, 16, DFF + fb * 256, 256))
            hold = {}

            def cons_gu(jb, b, key):
                if jb % 2 == 0:
                    hold["g"] = (b, key)
                    return
                fb = jb // 2
                gb, gk = hold["g"]
                ub, uk = b, key
                for f2 in range(2):
                    fc = fb * 2 + f2
                    pg, pgk = psum()
                    pu, puk = psum()
                    for kc in range(16):
                        S.op("pe", lambda e, pg=pg, gb=gb, kc=kc, f2=f2: e.matmul(
                            pg[:, :], gb[:, kc, f2 * 128:(f2 + 1) * 128], hb[:, kc, :],
                            start=(kc == 0), stop=(kc == 15)), r=[gk, ("hb", kc)], w=[pgk])
                    for kc in range(16):
                        S.op("pe", lambda e, pu=pu, ub=ub, kc=kc, f2=f2: e.matmul(
                            pu[:, :], ub[:, kc, f2 * 128:(f2 + 1) * 128], hb[:, kc, :],
                            start=(kc == 0), stop=(kc == 15)), r=[uk, ("hb", kc)], w=[puk])
                    t, tk = tmpr.next()
                    S.op("act", lambda e, pg=pg, t=t: e.activation(out=t[:], in_=pg[:, :], func=AF.Silu), r=[pgk], w=[tk])
                    S.op("dve", lambda e, pu=pu, t=t, fc=fc: e.tensor_tensor(out=act[:, fc, :], in0=t[:], in1=pu[:, :],
                                                                           op=ALU.mult), r=[tk, puk], w=[("act", fc)])
            wstream(blocks, cons_gu, la=1)
            for cb in range(4):
                accs = [psum() for _ in range(4)]
                blocks = [(lambda kg=kg, cb=cb: wload(w_dn[l], kg * 16, (16 if kg < 2 else 12), cb * 512, 512)) for kg in range(3)]

                def cons_d(kg, b, key, accs=accs, cb=cb):
                    nk = 16 if kg < 2 else 12
                    for oc4 in range(4):
                        p, pk = accs[oc4]
                        for kc in range(nk):
                            S.op("pe", lambda e, p=p, b=b, kc=kc, oc4=oc4, kg=kg, nk=nk: e.matmul(
                                p[:, :], b[:, kc, oc4 * 128:(oc4 + 1) * 128], act[:, kg * 16 + kc, :],
                                start=(kg == 0 and kc == 0), stop=(kg == 2 and kc == nk - 1)),
                                r=[key, ("act", kg * 16 + kc)], w=[pk])
                wstream(blocks, cons_d)
                for oc4 in range(4):
                    oc = cb * 4 + oc4
                    p, pk = accs[oc4]
                    S.op("dve", lambda e, p=p, oc=oc: e.scalar_tensor_tensor(
                        out=xs[:, oc, :], in0=p[:, :], scalar=modcol(l, 5, oc), in1=xs[:, oc, :],
                        op0=ALU.mult, op1=ALU.add), r=[pk, ("mod", l), ("xs", oc)], w=[("xs", oc)])
            if not last:
                S.op("sp", lambda e: e.dma_start(
                    out=xs_d[:, TS(tt, TT)].rearrange("(c p) t -> p c t", p=128), in_=xs[:, :, :]),
                    r=[("xs", c) for c in range(16)], w=[("xd",)], dma=True)
            else:
                norm_tile(None, None, l, [], out_fn=fin_factory(tt))
        S.loop(NT, body, reset_w)

    def phase_fox(l):
        j = l // 2
        if True:
            S.reset(m0)

            def sb2(shape, dt, name):
                return S.sb(shape, dt, name)

            rS, rT, rO, rM = Ring([0, 1]), Ring([2, 3]), Ring([4, 5]), Ring([6, 7])

            def bank(ring):
                i_ = ring.next()
                return psb[i_], ("ps", i_)
            fz = sb2([16, T], F32, "fz")
            nbf = sb2([16, 1], F32, "nbf")
            bfs = sb2([16, 1], F32, "bfs")
            ncb = sb2([128, T], F32, "ncb")
            qh = [sb2([128, T], BF16, f"qh{i}") for i in range(1)]
            kh = [sb2([128, T], BF16, f"kh{i}") for i in range(1)]
            vh = [sb2([128, NQ, 129], BF16, f"vh{i}") for i in range(1)]
            sq = sb2([128, T], BF16, "sqb")
            q2 = sb2([128, NQ], F32, "q2")
            km = sb2([128, 16], F32, "km")
            km1 = sb2([128, 1], F32, "km1")
            negm = sb2([128, NQ], F32, "negm")
            ssb = Ring([(sb2([128, 512], F32, f"ssb{i}"), ("ssb", i)) for i in range(4)])
            pbf = Ring([(sb2([128, 512], BF16, f"pbf{i}"), ("pbf", i)) for i in range(6)])
            oTh = [sb2([128, T], BF16, f"oTh{i}") for i in range(1)]

            S.op("sp", lambda e: e.dma_start(out=fz[:, :], in_=gate_d[0:16, :]),
                 r=[("gate", 0)], w=["fz"], dma=True)
            S.op("sp", lambda e: e.dma_start(out=bfs[:, :], in_=fbf[j * 16:(j + 1) * 16, 0:1]), w=["bfs"], dma=True)
            S.op("dve", lambda e: e.tensor_scalar(out=nbf[:], in0=bfs[:], scalar1=-1.0, scalar2=None, op0=ALU.mult),
                 r=["bfs"], w=["nbf"])
            S.op("act", lambda e: e.activation(out=fz[:, :], in_=fz[:, :], func=AF.Exp, bias=nbf[:, 0:1], scale=-1.0),
                 r=["fz", "nbf"], w=["fz"])
            S.op("act", lambda e: e.activation(out=fz[:, :], in_=fz[:, :], func=AF.Ln, bias=onec[0:16, 0:1], scale=1.0),
                 r=["fz", "onec"], w=["fz"])
            S.op("dve", lambda e: e.memset(ncb[0:16, :], 1.0), w=["ncb"])
            SEG = 1024
            for sg in range(T // SEG):
                a0, a1 = sg * SEG, (sg + 1) * SEG
                S.op("dve", lambda e, a0=a0, a1=a1, sg=sg: e.tensor_tensor_scan(
                    out=fz[:, a0:a1], data0=ncb[0:16, a0:a1], data1=fz[:, a0:a1],
                    initial=(0.0 if sg == 0 else fz[:, a0 - 1:a0]), op0=ALU.mult, op1=ALU.add),
                    r=["ncb", "fz"], w=["fz"])
            S.op("sp", lambda e: e.dma_start(out=ncum_d[:, :], in_=fz[:, :]), r=["fz"], w=["ncum_d"], dma=True)
            nqt = sb2([NQ, 128], F32, "nqt")
            maskT = sb2([128, 4 * 512], F32, "maskT")
            S.op("act", lambda e: e.dma_start(out=maskT[:, :], in_=consts[:, 2560:2560 + 2048]), w=["maskT"], dma=True)
            ncq = sb2([128, NQ], F32, "ncq")

            def load_head(h):
                i = 0
                S.op("act", lambda e: e.dma_start(out=qh[i][:, :], in_=qkT_d[TS(h, 128), :]),
                     r=[("qkT", oc) for oc in range(32)], w=[("qh", i)], dma=True)
                S.op("act", lambda e: e.dma_start(out=kh[i][:, :], in_=qkT_d[D:2 * D, :][TS(h, 128), :]),
                     r=[("qkT", oc) for oc in range(32)], w=[("kh", i)], dma=True)
                S.op("act", lambda e: e.dma_start(
                    out=vh[i][:, :, 0:128],
                    in_=tm_d[:, TS(h, 128)].rearrange("(j p) d -> p j d", p=128)),
                    r=[("tm",)], w=[("vh", i)], dma=True)
                S.op("pool", lambda e: e.memset(vh[i][:, :, 128:129], 1.0), r=[], w=[("vh1", i)])
                S.op("act", lambda e: e.dma_start(out=ncb[:, :], in_=ncum_d[TS(h, 1), :].to_broadcast([128, T])),
                     r=["ncum_d"], w=["ncb"], dma=True)
                S.op("act", lambda e: e.dma_start(out=nqt[:, :],
                                                 in_=ncum_d[TS(h, 1), :].rearrange("o (q p) -> (o q) p", p=128)),
                     r=["ncum_d"], w=["nqt"], dma=True)
                p, pk = bank(rM)
                S.op("pe", lambda e, p=p: e.transpose(p[:, 0:NQ], nqt[0:NQ, :], ident_f[0:NQ, 0:NQ]),
                     r=["nqt", "ident_f"], w=[pk])
                S.op("dve", lambda e, p=p: e.tensor_copy(out=ncq[:, :], in_=p[:, 0:NQ]), r=[pk], w=["ncq"])

            def head_body(h):
                i = 0
                load_head(h)
                S.op("act", lambda e, i=i: e.activation(out=sq[:, :], in_=kh[i][:, :], func=AF.Square),
                     r=[("kh", i)], w=["sq"])
                for t4 in range(T // 512):
                    p, pk = bank(rM)
                    S.op("pe", lambda e, p=p, t4=t4: e.matmul(p[:, :], ones1[:, :], sq[:, t4 * 512:(t4 + 1) * 512],
                                                             start=True, stop=True), r=["sq", "ones1"], w=[pk])
                    S.op("dve", lambda e, p=p, t4=t4: e.tensor_reduce(out=km[:, t4:t4 + 1], in_=p[:, :], axis=AX.X,
                                                                     op=ALU.max), r=[pk], w=[("km", t4)])
                S.op("dve", lambda e: e.tensor_reduce(out=km1[:, 0:1], in_=km[:, 0:T // 512], axis=AX.X, op=ALU.max),
                     r=[("km", t4) for t4 in range(T // 512)], w=["km1"])
                S.op("dve", lambda e: e.tensor_scalar(out=km1[:, 0:1], in0=km1[:, 0:1], scalar1=1.05, scalar2=None,
                                                      op0=ALU.mult), r=["km1"], w=["km1"])
                S.op("act", lambda e, i=i: e.activation(out=sq[:, :], in_=qh[i][:, :], func=AF.Square),
                     r=[("qh", i)], w=["sq"])
                for t4 in range(T // 512):
                    p, pk = bank(rM)
                    S.op("pe", lambda e, p=p, t4=t4: e.matmul(p[:, :], ones1[:, :], sq[:, t4 * 512:(t4 + 1) * 512],
                                                             start=True, stop=True), r=["sq", "ones1"], w=[pk])
                    mq, mqk = ssb.next()
                    S.op("act", lambda e, p=p, mq=mq: e.activation(out=mq[:, :], in_=p[:, :], func=AF.Sqrt,
                                                                  scale=km1[:, 0:1]), r=[pk, "km1"], w=[mqk])
                    S.op("dve", lambda e, mq=mq, t4=t4: e.scalar_tensor_tensor(
                        out=ncb[:, t4 * 512:(t4 + 1) * 512], in0=ncb[:, t4 * 512:(t4 + 1) * 512], scalar=-1.0,
                        in1=mq[:, :], op0=ALU.mult, op1=ALU.subtract), r=[mqk, "ncb"], w=["ncb"])
                steps = []
                for qt in range(T // 512):
                    nkb = 4 * qt + 4
                    for kb_ in range(nkb):
                        steps.append(dict(qt=qt, kb=kb_, first=(kb_ == 0), last=(kb_ == nkb - 1),
                                          diag=(kb_ - 4 * qt if kb_ >= 4 * qt else None)))
                po = {}

                def st_qk(s):
                    p, pk = bank(rS)
                    qt, kb_ = s["qt"], s["kb"]
                    c0 = s["diag"] * 128 if s["diag"] is not None else 0
                    s["c0"] = c0
                    S.op("pe", lambda e: e.matmul(p[:, c0:512], kh[i][:, kb_ * 128:(kb_ + 1) * 128],
                                                  qh[i][:, qt * 512 + c0:(qt + 1) * 512], start=True, stop=True),
                         r=[("qh", i), ("kh", i)], w=[pk])
                    sb_, sbk = ssb.next()
                    S.op("dve", lambda e: e.tensor_tensor(out=sb_[:, c0:512], in0=p[:, c0:512],
                                                          in1=ncb[:, qt * 512 + c0:(qt + 1) * 512],
                                                          op=ALU.add), r=[pk, "ncb"], w=[sbk])
                    if s["diag"] is not None:
                        dj = s["diag"]
                        S.op("pool", lambda e: e.tensor_tensor(out=sb_[:, c0:512], in0=sb_[:, c0:512],
                                                               in1=maskT[:, dj * 512 + c0:(dj + 1) * 512], op=ALU.add),
                             r=[sbk, "maskT"], w=[sbk])
                    pb, pbk = pbf.next()
                    s["pb"], s["pbk"] = pb, pbk
                    S.op("act", lambda e: e.activation(out=pb[:, c0:512], in_=sb_[:, c0:512], func=AF.Exp,
                                                       bias=ncq[:, kb_:kb_ + 1], scale=1.0),
                         r=[sbk, "ncq"], w=[pbk])

                def st_pv(s):
                    qt, kb_ = s["qt"], s["kb"]
                    if s["first"]:
                        po["o"], po["ok"] = bank(rO)
                        po["d"], po["dk"] = bank(rT)
                    pO, pOk, pD, pDk = po["o"], po["ok"], po["d"], po["dk"]
                    pb = s["pb"]
                    c0 = s["c0"]
                    S.op("pe", lambda e: e.matmul(pO[:, c0:512], vh[i][:, kb_, 0:128], pb[:, c0:512],
                                                  start=s["first"], stop=s["last"]),
                         r=[s["pbk"], ("vh", i)], w=[pOk])
                    S.op("pe", lambda e: e.matmul(pD[:, c0:512], ones1[:, :], pb[:, c0:512],
                                                  start=s["first"], stop=s["last"]),
                         r=[s["pbk"], "ones1"], w=[pDk])
                    if s["last"]:
                        rd, rdk = ssb.next()
                        S.op("dve", lambda e: e.reciprocal(out=rd[:, :], in_=pD[:, :]), r=[pDk], w=[rdk])
                        S.op("dve", lambda e: e.tensor_tensor(out=oTh[i][:, qt * 512:(qt + 1) * 512], in0=pO[:, :],
                                                              in1=rd[:, :], op=ALU.mult),
                             r=[pOk, rdk], w=[("oTh", i, qt)])

                ns = len(steps)
                SKEW = 4
                for n in range(ns + SKEW):
                    if n < ns:
                        st_qk(steps[n])
                    if 0 <= n - SKEW < ns:
                        st_pv(steps[n - SKEW])
                S.op("act", lambda e, i=i: e.dma_start(out=oT_d[TS(h, 128), :], in_=oTh[i][:, :]),
                     r=[("oTh", i, qt) for qt in range(T // 512)], w=[("oT",)], dma=True)
            S.loop(16, head_body)
            S.barrier()

    def phase_mlstm(l):
        j = l // 2
        S.reset(m0)
        A = S.sb([4, T], F32, "gA")
        B = S.sb([4, T], F32, "gB")
        bg = S.sb([4, 2], F32, "bg")
        bg15 = S.sb([4, 2], F32, "bg15")
        ref = S.sb([4, NQ], F32, "ref")
        Gn = S.sb([4, NQ], F32, "Gn")
        Gp = S.sb([4, NQ], F32, "Gp")
        r1 = S.sb([4, NQ], F32, "r1")
        eT = S.sb([128, NQ * 4], F32, "eT")
        flT = S.sb([128, NQ * 4], F32, "flT")
        r1b = S.sb([128, 4 * NQ], F32, "r1b")
        nwb = S.sb([128, 2048], F32, "nwb")
        Cs = S.sb([128, 2, 512], F32, "Cs")
        Cb = S.sb([128, 2, 512], BF16, "Cb")
        ns = S.sb([128, 2], F32, "ns")
        nb_ = S.sb([128, 2], BF16, "nb_")
        qTb = [S.sb([128, 2, 512], BF16, f"qTb{i}") for i in range(2)]
        kTb = [S.sb([128, 2, 512], BF16, f"kTb{i}") for i in range(2)]
        ktm = [S.sb([128, 4, 256], BF16, f"ktm{i}") for i in range(2)]
        vb = [S.sb([128, 4, 512], BF16, f"vb{i}") for i in range(2)]
        ob = [S.sb([128, 4, 512], BF16, f"ob{i}") for i in range(2)]
        yTb = [S.sb([128, 4, 512], BF16, f"yTb{i}") for i in range(2)]
        PT = Ring([(S.sb([128, 128], BF16, f"PT{i}"), ("PT", i)) for i in range(2)])
        Kt = Ring([(S.sb([128, 256], BF16, f"Kt{i}"), ("Kt", i)) for i in range(2)])
        d1 = Ring([(S.sb([128, 4], F32, f"d1{i}"), ("d1", i)) for i in range(2)])
        junk = S.sb([128, 512], BF16, "junk")
        ytmp = Ring([(S.sb([128, 512], F32, f"ytmp{i}"), ("ytmp", i)) for i in range(2)])
        ybf = Ring([(S.sb([128, 512], BF16, f"ybf{i}"), ("ybf", i)) for i in range(2)])

        gdeps = [("gate", gi) for gi in range(2)]
        S.op("sp", lambda e: e.dma_start(out=A[:, :], in_=gate_d[0:4, :]), r=gdeps, w=["gA"], dma=True)
        S.op("sp", lambda e: e.dma_start(out=B[:, :], in_=gate2_d[0:4, :]), r=gdeps, w=["gB"], dma=True)
        S.op("sp", lambda e: e.dma_start(out=bg[:, :], in_=mbg[:, 2 * j:2 * j + 2]), w=["bg"], dma=True)
        S.op("sp", lambda e: e.dma_start(out=nwb[:, :], in_=mnw[:, j * 2048:(j + 1) * 2048]), w=["nwb"], dma=True)
        S.op("dve", lambda e: e.tensor_scalar(out=bg15[:, :], in0=bg[:, :], scalar1=1.0 / 15.0, scalar2=None,
                                              op0=ALU.mult), r=["bg"], w=["bg15"])
        S.op("act", lambda e: e.activation(out=A[:, :], in_=A[:, :], func=AF.Tanh, bias=bg15[:, 0:1], scale=1.0 / 15.0),
             r=["gA", "bg15"], w=["gA"])
        S.op("act", lambda e: e.activation(out=B[:, :], in_=B[:, :], func=AF.Tanh, bias=bg15[:, 1:2], scale=1.0 / 15.0),
             r=["gB", "bg15"], w=["gB"])
        S.op("act", lambda e: e.activation(out=B[:, :], in_=B[:, :], func=AF.Exp, scale=-15.0), r=["gB"], w=["gB"])
        S.op("act", lambda e: e.activation(out=B[:, :], in_=B[:, :], func=AF.Ln, bias=onec[0:4, 0:1], scale=1.0), r=["gB", "onec"], w=["gB"])
        onesT = S.sb([4, T], F32, "onesT")
        S.op("pool", lambda e: e.memset(onesT[:, :], 1.0), w=["onesT"])
        SEG = 1024
        for sg in range(T // SEG):
            a0, a1 = sg * SEG, (sg + 1) * SEG
            S.op("dve", lambda e, a0=a0, a1=a1, sg=sg: e.tensor_tensor_scan(
                out=B[:, a0:a1], data0=onesT[:, a0:a1], data1=B[:, a0:a1],
                initial=(0.0 if sg == 0 else B[:, a0 - 1:a0]), op0=ALU.mult, op1=ALU.add),
                r=["gB", "onesT"], w=["gB"])
        S.op("dve", lambda e: e.scalar_tensor_tensor(out=A[:, :], in0=A[:, :], scalar=15.0, in1=B[:, :],
                                                    op0=ALU.mult, op1=ALU.add), r=["gA", "gB"], w=["gA"])
        A3 = A.rearrange("p (c s) -> p c s", c=NQ, s=128)
        B3 = B.rearrange("p (c s) -> p c s", c=NQ, s=128)
        S.op("dve", lambda e: e.tensor_reduce(out=ref[:, :], in_=A3, axis=AX.X, op=ALU.max), r=["gA"], w=["ref"])
        S.op("dve", lambda e: e.tensor_tensor_scan(out=Gn[:, :], data0=ref[:, :], data1=ref[:, :], initial=0.0,
                                                  op0=ALU.max, op1=ALU.max), r=["ref"], w=["Gn"])
        S.op("dve", lambda e: e.memset(Gp[:, 0:1], 0.0), w=["Gp0"])
        if NQ > 1:
            S.op("dve", lambda e: e.tensor_copy(out=Gp[:, 1:NQ], in_=Gn[:, 0:NQ - 1]), r=["Gn"], w=["Gp1"])
        S.op("dve", lambda e: e.tensor_tensor(out=r1[:, :], in0=Gp[:, :], in1=Gn[:, :], op=ALU.subtract),
             r=["Gn", "Gp0", "Gp1"], w=["r1"])
        S.op("act", lambda e: e.activation(out=r1[:, :], in_=r1[:, :], func=AF.Exp), r=["r1"], w=["r1"])
        for c in range(NQ):
            S.op("dve", lambda e, c=c: e.tensor_scalar(out=A[:, c * 128:(c + 1) * 128], in0=A[:, c * 128:(c + 1) * 128],
                                                      scalar1=Gn[:, c:c + 1], scalar2=None, op0=ALU.subtract),
                 r=["gA", "Gn"], w=["gA"])
            S.op("pool", lambda e, c=c: e.tensor_scalar(out=B[:, c * 128:(c + 1) * 128], in0=B[:, c * 128:(c + 1) * 128],
                                                       scalar1=Gn[:, c:c + 1], scalar2=None, op0=ALU.subtract),
                 r=["gB", "Gn"], w=["gB"])
        S.op("act", lambda e: e.activation(out=A[:, :], in_=A[:, :], func=AF.Exp), r=["gA"], w=["gA"])
        S.op("act", lambda e: e.activation(out=B[:, :], in_=B[:, :], func=AF.Exp), r=["gB"], w=["gB"])
        for (src, dst, dk, sk) in ((A, eT, "eT", "gA"), (B, flT, "flT", "gB")):
            for g0 in range(0, NQ, 64):
                p, pk = psum()
                n_in = min(64, NQ - g0)
                for c in range(g0, g0 + n_in):
                    S.op("pe", lambda e, p=p, c=c, g0=g0, src=src: e.transpose(
                        p[:, (c - g0) * 4:(c - g0 + 1) * 4], src[0:4, c * 128:(c + 1) * 128], ident_f[0:4, 0:4]),
                        r=[sk, "ident_f"], w=[pk])
                S.op("dve", lambda e, p=p, g0=g0, n_in=n_in, dst=dst: e.tensor_copy(
                    out=dst[:, g0 * 4:(g0 + n_in) * 4], in_=p[:, 0:n_in * 4]), r=[pk], w=[dk])
        p, pk = psum()
        for h in range(4):
            S.op("pe", lambda e, p=p, h=h: e.matmul(p[:, h * NQ:(h + 1) * NQ], sel16[0:4, h * 128:(h + 1) * 128],
                                                   r1[0:4, 0:NQ], start=True, stop=True), r=["sel16", "r1"], w=[pk])
        S.op("dve", lambda e, p=p: e.tensor_copy(out=r1b[:, :], in_=p[:, 0:4 * NQ]), r=[pk], w=["r1b"])

        NB = T // 512
        for h in range(4):
            S.op("dve", lambda e: e.memset(Cs[:, :, :], 0.0), w=["Cs"])
            S.op("dve", lambda e: e.memset(ns[:, :], 0.0), w=["ns"])
            for tb in range(NB):
                bi = (h * NB + tb) % 2
                t0 = tb * 512
                S.op("sp", lambda e, bi=bi, t0=t0, h=h: e.dma_start(
                    out=qTb[bi][:, :, :], in_=qkT_d[h * 256:(h + 1) * 256, t0:t0 + 512].rearrange("(c p) t -> p c t", p=128)),
                    r=[("qkT", oc) for oc in range(16)], w=[("qTb", bi)], dma=True)
                S.op("sp", lambda e, bi=bi, t0=t0, h=h: e.dma_start(
                    out=kTb[bi][:, :, :], in_=qkT_d[1024 + h * 256:1024 + (h + 1) * 256, t0:t0 + 512].rearrange("(c p) t -> p c t", p=128)),
                    r=[("qkT", oc) for oc in range(16)], w=[("kTb", bi)], dma=True)
                S.op("sp", lambda e, bi=bi, t0=t0, h=h: e.dma_start(
                    out=ktm[bi][:, :, :], in_=tmb[h // 2][t0:t0 + 512, (h % 2) * 256:(h % 2 + 1) * 256].rearrange("(j p) d -> p j d", p=128)),
                    r=[("tm",)], w=[("ktm", bi)], dma=True)
                S.op("sp", lambda e, bi=bi, t0=t0, h=h: e.dma_start(
                    out=vb[bi][:, :, :], in_=tmb[2 + h][t0:t0 + 512, :].rearrange("(j p) d -> p j d", p=128)),
                    r=[("tm",)], w=[("vb", bi)], dma=True)
                S.op("sp", lambda e, bi=bi, t0=t0, h=h: e.dma_start(
                    out=ob[bi][:, :, :], in_=tmb[6 + h][t0:t0 + 512, :].rearrange("(j p) d -> p j d", p=128)),
                    r=[("tm",)], w=[("ob", bi)], dma=True)
                for cc in range(4):
                    c = tb * 4 + cc
                    ecol = eT[:, c * 4 + h:c * 4 + h + 1]
                    fcol = flT[:, c * 4 + h:c * 4 + h + 1]
                    rcol = r1b[:, h * NQ + c:h * NQ + c + 1]
                    tsl = slice(cc * 128, (cc + 1) * 128)
                    p1, p1k = psum()
                    for c2 in range(2):
                        S.op("pe", lambda e, c2=c2, p1=p1, bi=bi, tsl=tsl: e.matmul(
                            p1[:, 0:128], kTb[bi][:, c2, tsl], qTb[bi][:, c2, tsl], start=(c2 == 0), stop=(c2 == 1)),
                            r=[("kTb", bi), ("qTb", bi)], w=[p1k])
                    pt, ptk = PT.next()
                    S.op("dve", lambda e, p1=p1, pt=pt, ecol=ecol: e.scalar_tensor_tensor(
                        out=pt[:, :], in0=p1[:, 0:128], scalar=ecol, in1=mask01[:, :], op0=ALU.mult, op1=ALU.mult),
                        r=[p1k, "eT", "mask01"], w=[ptk])
                    S.op("dve", lambda e, rcol=rcol: e.tensor_scalar(out=Cs[:, :, :], in0=Cs[:, :, :], scalar1=rcol,
                                                                    scalar2=None, op0=ALU.mult), r=["Cs", "r1b"], w=["Cs"])
                    S.op("act", lambda e: e.activation(out=Cb[:, :, :], in_=Cs[:, :, :], func=AF.Identity), r=["Cs"], w=["Cb"])
                    S.op("dve", lambda e, rcol=rcol: e.tensor_scalar(out=ns[:, :], in0=ns[:, :], scalar1=rcol,
                                                                    scalar2=None, op0=ALU.mult), r=["ns", "r1b"], w=["ns"])
                    S.op("dve", lambda e: e.tensor_copy(out=nb_[:, :], in_=ns[:, :]), r=["ns"], w=["nb_"])
                    p2, p2k = psum()
                    for c2 in range(2):
                        S.op("pe", lambda e, c2=c2, p2=p2, bi=bi, tsl=tsl: e.matmul(
                            p2[:, :], qTb[bi][:, c2, tsl], Cb[:, c2, :], start=(c2 == 0), stop=False),
                            r=[("qTb", bi), "Cb"], w=[p2k])
                    S.op("pe", lambda e, p2=p2, pt=pt, bi=bi, cc=cc: e.matmul(
                        p2[:, :], pt[:, :], vb[bi][:, cc, :], start=False, stop=True), r=[ptk, ("vb", bi)], w=[p2k])
                    p3, p3k = psum()
                    for c2 in range(2):
                        S.op("pe", lambda e, c2=c2, p3=p3, bi=bi, tsl=tsl: e.matmul(
                            p3[:, 0:1], qTb[bi][:, c2, tsl], nb_[:, c2:c2 + 1], start=(c2 == 0), stop=False),
                            r=[("qTb", bi), "nb_"], w=[p3k])
                    S.op("pe", lambda e, p3=p3, pt=pt: e.matmul(p3[:, 0:1], pt[:, :], ones1[:, 0:1], start=False, stop=True),
                         r=[ptk, "ones1"], w=[p3k])
                    kt_, ktk = Kt.next()
                    S.op("pool", lambda e, kt_=kt_, bi=bi, cc=cc, ecol=ecol: e.tensor_scalar(
                        out=kt_[:, :], in0=ktm[bi][:, cc, :], scalar1=ecol, scalar2=None, op0=ALU.mult),
                        r=[("ktm", bi), "eT"], w=[ktk])
                    p5, p5k = psum()
                    for c2 in range(2):
                        p4, p4k = psum()
                        S.op("pe", lambda e, c2=c2, p4=p4, kt_=kt_, bi=bi, cc=cc: e.matmul(
                            p4[:, :], kt_[:, c2 * 128:(c2 + 1) * 128], vb[bi][:, cc, :], start=True, stop=True),
                            r=[ktk, ("vb", bi)], w=[p4k])
                        S.op("pe", lambda e, c2=c2, p5=p5, kt_=kt_: e.matmul(
                            p5[:, c2:c2 + 1], kt_[:, c2 * 128:(c2 + 1) * 128], ones1[:, 0:1], start=True, stop=True),
                            r=[ktk, "ones1"], w=[p5k])
                        S.op("dve", lambda e, c2=c2, p4=p4: e.tensor_tensor(out=Cs[:, c2, :], in0=Cs[:, c2, :], in1=p4[:, :],
                                                                          op=ALU.add), r=[p4k, "Cs", "Cb"], w=["Cs"])
                    S.op("dve", lambda e, p5=p5: e.tensor_tensor(out=ns[:, :], in0=ns[:, :], in1=p5[:, 0:2], op=ALU.add),
                         r=[p5k, "ns", "nb_"], w=["ns"])
                    dd, ddk = d1.next()
                    S.op("act", lambda e, dd=dd, p3=p3: e.activation(out=dd[:, 0:1], in_=p3[:, 0:1], func=AF.Abs),
                         r=[p3k], w=[(ddk, 0)])
                    S.op("dve", lambda e, dd=dd, fcol=fcol: e.tensor_scalar(
                        out=dd[:, 0:1], in0=dd[:, 0:1], scalar1=fcol, scalar2=None, op0=ALU.max),
                        r=[(ddk, 0), "flT"], w=[(ddk, 0)])
                    S.op("dve", lambda e, dd=dd: e.reciprocal(out=dd[:, 1:2], in_=dd[:, 0:1]), r=[(ddk, 0)], w=[(ddk, 1)])
                    S.op("pool", lambda e, dd=dd: e.memset(dd[:, 2:3], 0.0), w=[(ddk, 2)])
                    S.op("act", lambda e, dd=dd, p2=p2: e.activation(out=junk[:, :], in_=p2[:, :], func=AF.Square,
                                                                    scale=dd[:, 1:2], accum_out=dd[:, 2:3]),
                         r=[p2k, (ddk, 1), (ddk, 2)], w=[(ddk, 2), "junk"])
                    S.op("act", lambda e, dd=dd: e.activation(out=dd[:, 3:4], in_=dd[:, 2:3], func=AF.Ln,
                                                             bias=epsc[:, 0:1], scale=1.0 / 512.0),
                         r=[(ddk, 2), "epsc"], w=[(ddk, 3)])
                    S.op("act", lambda e, dd=dd: e.activation(out=dd[:, 3:4], in_=dd[:, 3:4], func=AF.Exp, scale=-0.5),
                         r=[(ddk, 3)], w=[(ddk, 3)])
                    S.op("dve", lambda e, dd=dd: e.tensor_tensor(out=dd[:, 3:4], in0=dd[:, 3:4], in1=dd[:, 1:2],
                                                                op=ALU.mult), r=[(ddk, 3), (ddk, 1)], w=[(ddk, 3)])
                    yt, ytk = ytmp.next()
                    S.op("dve", lambda e, yt=yt, p2=p2, dd=dd, h=h: e.scalar_tensor_tensor(
                        out=yt[:, :], in0=p2[:, :], scalar=dd[:, 3:4], in1=nwb[:, h * 512:(h + 1) * 512],
                        op0=ALU.mult, op1=ALU.mult), r=[p2k, (ddk, 3), "nwb"], w=[ytk])
                    yb, ybk = ybf.next()
                    S.op("pool", lambda e, yb=yb, yt=yt, bi=bi, cc=cc: e.tensor_tensor(
                        out=yb[:, :], in0=yt[:, :], in1=ob[bi][:, cc, :], op=ALU.mult), r=[ytk, ("ob", bi)], w=[ybk])
                    p6, p6k = psum()
                    p6v = p6[:, 0:256].bitcast(BF16)
                    for d4 in range(4):
                        S.op("pe", lambda e, d4=d4, p6v=p6v, yb=yb: e.transpose(
                            p6v[:, d4 * 128:(d4 + 1) * 128], yb[:, d4 * 128:(d4 + 1) * 128], ident_b[:, :]),
                            r=[ybk, "ident_b"], w=[p6k])
                    S.op("act", lambda e, p6v=p6v, bi=bi, tsl=tsl: e.activation(
                        out=yTb[bi][:, :, tsl], in_=p6v.rearrange("p (d t) -> p d t", d=4, t=128), func=AF.Identity),
                        r=[p6k], w=[("yTb", bi, cc)])
                S.op("sp", lambda e, bi=bi, t0=t0, h=h: e.dma_start(
                    out=oT_d[h * 512:(h + 1) * 512, t0:t0 + 512].rearrange("(d p) t -> p d t", p=128), in_=yTb[bi][:, :, :]),
                    r=[("yTb", bi, cc) for cc in range(4)], w=[("oT",)], dma=True)
        S.barrier()

    x_src = xT_in
    for l in range(depth):
        phase_inproj(l, x_src)
        S.barrier()
        emit_conv_layer(l + 1)
        if l % 2 == 1:
            phase_fox(l)
        else:
            phase_mlstm(l)
        phase_post(l, x_src, last=(l == depth - 1))
        S.barrier()
        x_src = xs_d
    S.op("sp", None)
    S.emit()
    return nc, stack


def _consts():
    c = np.zeros((128, 512 + 2048), np.float32)
    c[:, 0:128] = np.eye(128, dtype=np.float32)
    s = np.arange(128)
    c[:, 128:256] = (s[:, None] <= s[None, :]).astype(np.float32)
    c[:, 256:384] = np.where(s[None, :] <= s[:, None], 0.0, NEG)
    for h in range(16):
        c[h, 512 + h * 128:512 + (h + 1) * 128] = 1.0
    m = np.zeros((128, 4 * 512), np.float32)
    ql = np.arange(512)
    for j in range(4):
        m[:, j * 512:(j + 1) * 512] = np.where(j * 128 + s[:, None] <= ql[None, :], 0.0, NEG)
    return np.concatenate([c, m], axis=1)


def _col(v):
    v = np.asarray(v, np.float32)
    return np.ascontiguousarray(v.reshape(-1, 128).T)


def make_in_map(b, T, depth, x, c, ada_w, ada_b, norm_mix_w, norm_ffn_w, mlstm_w_in, mlstm_b_gates,
                mlstm_norm_w, mlstm_w_out, fox_w_in, fox_b_f, fox_w_out, ffn_w_gate_up, ffn_w_down, final_norm_w):
    n_ml = (depth + 1) // 2
    n_fx = depth // 2
    m = {}
    m["xT"] = np.ascontiguousarray(np.asarray(x[b], np.float32).T)
    m["cT"] = _col(c[b])
    m["ada_w"] = np.ascontiguousarray(ada_w[:depth], dtype=np.float32)
    m["ada_bT"] = np.concatenate([_col(ada_b[l]) for l in range(depth)], axis=1)
    m["nmw"] = np.concatenate([_col(norm_mix_w[l]) for l in range(depth)], axis=1)
    m["nfw"] = np.concatenate([_col(norm_ffn_w[l]) for l in range(depth)], axis=1)
    m["fnw"] = _col(final_norm_w)
    m["mw_in"] = np.ascontiguousarray(mlstm_w_in[:max(n_ml, 1)], dtype=np.float32)
    bgs = np.asarray(mlstm_b_gates, np.float32)[:max(n_ml, 1)]
    m["mbg"] = np.ascontiguousarray(np.concatenate([np.stack([bg[0:4], bg[4:8]], axis=1) for bg in bgs], axis=1))
    m["mnw"] = np.ascontiguousarray(np.concatenate(
        [np.broadcast_to(np.asarray(w, np.float32)[None, :], (128, 2048)) for w in mlstm_norm_w[:max(n_ml, 1)]], axis=1))
    m["mw_out"] = np.ascontiguousarray(mlstm_w_out[:max(n_ml, 1)], dtype=np.float32)
    m["fw_in"] = np.ascontiguousarray(fox_w_in[:max(n_fx, 1)], dtype=np.float32)
    m["fbf"] = np.ascontiguousarray(np.asarray(fox_b_f, np.float32)[:max(n_fx, 1)].reshape(-1, 1))
    m["fw_out"] = np.ascontiguousarray(fox_w_out[:max(n_fx, 1)], dtype=np.float32)
    m["w_gu"] = np.ascontiguousarray(ffn_w_gate_up[:depth], dtype=np.float32)
    m["w_dn"] = np.ascontiguousarray(ffn_w_down[:depth], dtype=np.float32)
    m["consts"] = _consts()
    return m


_CACHE = {}


def kernel(**inputs):
    x = np.asarray(inputs["x"])
    Bn, T, _ = x.shape
    depth = np.asarray(inputs["ada_w"]).shape[0]
    key = (T, depth)
    if key not in _CACHE:
        _CACHE[key] = build(T, depth)
    nc, _stack = _CACHE[key]
    arrs = {k: np.asarray(v) for k, v in inputs.items()}
    in_maps = [make_in_map(b, T, depth, **arrs) for b in range(Bn)]
    res = run_bass_kernel_spmd(nc, in_maps, core_ids=list(range(Bn)))
    out = np.stack([np.ascontiguousarray(r["outT"].T) for r in res.results], axis=0)
    return out.astype(np.float32)
```
